# Optimizing a Trainium2 kernel written in Bass

```python
import math
import jax, jax.numpy as jnp
from jax import lax
import numpy as np

D_MODEL = 1024
BATCH = 32
SEQ = 256
DEPTH = 1
DEC_BATCH = 4
DEC_SEQ = 1024
PAST_LEN = 256

GRID_W = 64
N_HEADS_ATTN = 8
HEAD_DIM = 64
ATTN_WIDTH = N_HEADS_ATTN * HEAD_DIM
HY_CH = D_MODEL - ATTN_WIDTH
MIX_WIDTH = ATTN_WIDTH + HY_CH
IN_COLS = 3 * ATTN_WIDTH + 3 * HY_CH
WIN_ROWS = 8
WIN_COLS = 16
D_FF = 2816
HY_ORDER = 2
FILT_FREQS = 8
FILT_EMB = 2 * FILT_FREQS + 1
FILT_HID = 64
DECAY_TARGET = 1e-2
FAST_DECAY_PCT = 0.3
SLOW_DECAY_PCT = 1.5
MAX_DECAY = math.log(DECAY_TARGET) / FAST_DECAY_PCT
MIN_DECAY = math.log(DECAY_TARGET) / SLOW_DECAY_PCT
Q_BLOCK = 128
EPS = 1e-6
NEG = -1e30

kernel_name = "hymba_natten_hyena_prefix_dit_step"


def _rmsnorm(x, g):
    xf = x.astype(jnp.float32)
    y = xf * lax.rsqrt(jnp.mean(xf * xf, axis=-1, keepdims=True) + EPS)
    return (y * g.astype(jnp.float32)).astype(x.dtype)


def _adaln(cvec, w_ada, b_ada):
    m = jax.nn.silu(cvec) @ w_ada + b_ada
    return [t[:, None, :] for t in jnp.split(m, 6, axis=-1)]


def _dwconv3(x, w, b):
    xp = jnp.pad(x, ((0, 0), (1, 1), (0, 0)))
    return xp[:, :-2] * w[0] + xp[:, 1:-1] * w[1] + xp[:, 2:] * w[2] + b


def _filter_spectrum(L, w1, b1, w2, b2, w3, b3, freq):
    t = jnp.arange(L, dtype=jnp.float32) / L
    f = jnp.arange(1, FILT_FREQS + 1, dtype=jnp.float32)
    ang = 2.0 * math.pi * t[:, None] * f[None, :]
    z = jnp.concatenate([t[:, None], jnp.cos(ang), jnp.sin(ang)], axis=-1)
    h = jnp.sin(freq * (z @ w1 + b1))
    h = jnp.sin(freq * (h @ w2 + b2))
    h = (h @ w3 + b3).astype(jnp.float32).reshape(L, HY_ORDER, 2, HY_CH)
    deltas = jnp.abs(jnp.linspace(MIN_DECAY, MAX_DECAY, HY_CH, dtype=jnp.float32))
    decay = jnp.exp(-t[:, None] * deltas[None, :])
    h = h * decay[:, None, None, :]
    fwd, bwd = h[:, :, 0], h[:, :, 1]
    k_full = jnp.concatenate([fwd, jnp.zeros((1, HY_ORDER, HY_CH), jnp.float32),
                              jnp.flip(bwd[1:], axis=0)], axis=0)
    return jnp.fft.rfft(k_full, axis=0)


def _long_conv(u, k_f, bias):
    L = u.shape[1]
    uf = u.astype(jnp.float32)
    y = jnp.fft.irfft(jnp.fft.rfft(uf, n=2 * L, axis=1) * k_f[None], n=2 * L, axis=1)[:, :L]
    return (y + uf * bias.astype(jnp.float32)).astype(u.dtype)


def _hyena(hy, k_f, filt_bias):
    x1, x2, v = jnp.split(hy, 3, axis=-1)
    z = x1 * _long_conv(v, k_f[:, 0], filt_bias[0])
    return x2 * _long_conv(z, k_f[:, 1], filt_bias[1])


def _ctx_attention(q, k, v):
    B, H, L, dh = q.shape
    nb = L // Q_BLOCK
    qb = q.reshape(B, H, nb, Q_BLOCK, dh).transpose(2, 0, 1, 3, 4)
    scale = HEAD_DIM ** -0.5

    def blk(qi):
        s = jnp.einsum('bhqd,bhkd->bhqk', qi, k).astype(jnp.float32) * scale
        p = jax.nn.softmax(s, axis=-1).astype(v.dtype)
        return jnp.einsum('bhqk,bhkd->bhqd', p, v)

    o = lax.map(blk, qb)
    return o.transpose(1, 2, 0, 3, 4).reshape(B, H, L, dh)


def _neighbourhood_attention(q, k, v, k_ctx, v_ctx, rpb):
    B, H, N, dh = q.shape
    R = N // GRID_W
    W = GRID_W
    wr = min(WIN_ROWS, R)
    scale = HEAD_DIM ** -0.5
    qg = q.reshape(B, H, R, W, dh)
    kg = k.reshape(B, H, R, W, dh)
    vg = v.reshape(B, H, R, W, dh)
    r = jnp.arange(R)
    row_start = jnp.clip(r - wr // 2, 0, R - wr)
    row_idx = row_start[:, None] + jnp.arange(wr)[None, :]
    k_band = kg[:, :, row_idx]
    v_band = vg[:, :, row_idx]
    col = jnp.arange(W)
    col_start = jnp.clip(col - WIN_COLS // 2, 0, W - WIN_COLS)
    col_in = (col[None, :] >= col_start[:, None]) & (col[None, :] < col_start[:, None] + WIN_COLS)
    dr = row_idx - r[:, None] + (WIN_ROWS - 1)
    dc = jnp.clip(col[None, :] - col[:, None], -(WIN_COLS - 1), WIN_COLS - 1) + (WIN_COLS - 1)
    bias = rpb[:, dr[:, None, :, None], dc[None, :, None, :]].astype(jnp.float32)
    s_loc = jnp.einsum('bhrqd,bhrikd->bhrqik', qg, k_band).astype(jnp.float32) * scale
    s_loc = jnp.where(col_in[None, None, None, :, None, :], s_loc + bias[None], NEG)
    s_loc = s_loc.reshape(B, H, R, W, wr * W)
    s_ctx = jnp.einsum('bhrqd,bhkd->bhrqk', qg, k_ctx).astype(jnp.float32) * scale
    p = jax.nn.softmax(jnp.concatenate([s_loc, s_ctx], axis=-1), axis=-1).astype(v.dtype)
    p_loc = p[..., :wr * W].reshape(B, H, R, W, wr, W)
    p_ctx = p[..., wr * W:]
    o = (jnp.einsum('bhrqik,bhrikd->bhrqd', p_loc, v_band)
         + jnp.einsum('bhrqk,bhkd->bhrqd', p_ctx, v_ctx))
    return o.reshape(B, H, N, dh)


def _heads(t):
    B, L, _ = t.shape
    return t.reshape(B, L, N_HEADS_ATTN, HEAD_DIM).transpose(0, 2, 1, 3)


def _layer(x, mods, attend, norm1_g, w_in, hy_conv_w, hy_conv_b, k_f, filt_bias,
           grp_norm_g, w_out, norm2_g, w_up, ffn_conv_w, ffn_conv_b, w_down):
    sh1, sc1, g1, sh2, sc2, g2 = mods
    B, L, _ = x.shape
    h = _rmsnorm(x, norm1_g) * (1.0 + sc1) + sh1
    proj = h @ w_in
    q = _heads(proj[..., :ATTN_WIDTH])
    k = _heads(proj[..., ATTN_WIDTH:2 * ATTN_WIDTH])
    v = _heads(proj[..., 2 * ATTN_WIDTH:3 * ATTN_WIDTH])
    hy = _dwconv3(proj[..., 3 * ATTN_WIDTH:], hy_conv_w, hy_conv_b)
    a = attend(q, k, v).transpose(0, 2, 1, 3).reshape(B, L, ATTN_WIDTH)
    hyo = _hyena(hy, k_f, filt_bias)
    merged = jnp.concatenate([_rmsnorm(a, grp_norm_g[:ATTN_WIDTH]),
                              _rmsnorm(hyo, grp_norm_g[ATTN_WIDTH:])], axis=-1)
    x = x + g1 * (merged @ w_out)
    h2 = _rmsnorm(x, norm2_g) * (1.0 + sc2) + sh2
    u = _dwconv3(h2 @ w_up, ffn_conv_w, ffn_conv_b)
    gate, up = jnp.split(u, 2, axis=-1)
    x = x + g2 * ((jax.nn.silu(gate) * up) @ w_down)
    return x, k, v


def setup_inputs(seed: int = 0) -> dict:
    key = jax.random.key(seed)
    ks = jax.random.split(key, 32)
    n = jax.random.normal
    f32 = jnp.float32
    D = D_MODEL
    return {
        "x_prompt": n(ks[0], (BATCH, SEQ, D), f32),
        "x_sample": n(ks[1], (DEC_BATCH, DEC_SEQ, D), f32),
        "cache_ctx_k": n(ks[2], (DEC_BATCH, DEPTH, N_HEADS_ATTN, PAST_LEN, HEAD_DIM), f32),
        "cache_ctx_v": n(ks[3], (DEC_BATCH, DEPTH, N_HEADS_ATTN, PAST_LEN, HEAD_DIM), f32),
        "c": n(ks[4], (DEC_BATCH, D), f32),
        "c_ctx": n(ks[5], (D,), f32),
        "w_ada": n(ks[6], (DEPTH, D, 6 * D), f32) * (0.5 * D ** -0.5),
        "b_ada": n(ks[7], (DEPTH, 6 * D), f32) * 0.02,
        "norm1_g": 1.0 + 0.05 * n(ks[8], (DEPTH, D), f32),
        "w_in": n(ks[9], (DEPTH, D, IN_COLS), f32) * D ** -0.5,
        "rpb": n(ks[10], (DEPTH, N_HEADS_ATTN, 2 * WIN_ROWS - 1, 2 * WIN_COLS - 1), f32) * 0.1,
        "hy_conv_w": n(ks[11], (DEPTH, 3, 3 * HY_CH), f32) * (3.0 ** -0.5),
        "hy_conv_b": n(ks[12], (DEPTH, 3 * HY_CH), f32) * 0.02,
        "filt_w1": n(ks[13], (DEPTH, FILT_EMB, FILT_HID), f32) * FILT_EMB ** -0.5,
        "filt_b1": n(ks[14], (DEPTH, FILT_HID), f32) * 0.1,
        "filt_w2": n(ks[15], (DEPTH, FILT_HID, FILT_HID), f32) * FILT_HID ** -0.5,
        "filt_b2": n(ks[16], (DEPTH, FILT_HID), f32) * 0.1,
        "filt_w3": n(ks[17], (DEPTH, FILT_HID, HY_ORDER * 2 * HY_CH), f32) * (0.05 * FILT_HID ** -0.5),
        "filt_b3": n(ks[18], (DEPTH, HY_ORDER * 2 * HY_CH), f32) * 0.01,
        "filt_freq": 1.0 + 0.1 * n(ks[19], (DEPTH, FILT_HID), f32),
        "filt_bias": n(ks[20], (DEPTH, HY_ORDER, HY_CH), f32) * 0.1,
        "grp_norm_g": 1.0 + 0.05 * n(ks[21], (DEPTH, MIX_WIDTH), f32),
        "w_out": n(ks[22], (DEPTH, MIX_WIDTH, D), f32) * MIX_WIDTH ** -0.5,
        "norm2_g": 1.0 + 0.05 * n(ks[23], (DEPTH, D), f32),
        "w_up": n(ks[24], (DEPTH, D, 2 * D_FF), f32) * D ** -0.5,
        "ffn_conv_w": n(ks[25], (DEPTH, 3, 2 * D_FF), f32) * (3.0 ** -0.5),
        "ffn_conv_b": n(ks[26], (DEPTH, 2 * D_FF), f32) * 0.02,
        "w_down": n(ks[27], (DEPTH, D_FF, D), f32) * D_FF ** -0.5,
        "final_g": 1.0 + 0.05 * n(ks[28], (D,), f32),
    }


def reference(x_prompt, x_sample, cache_ctx_k, cache_ctx_v, c, c_ctx, w_ada, b_ada, norm1_g,
              w_in, rpb, hy_conv_w, hy_conv_b, filt_w1, filt_b1, filt_w2, filt_b2, filt_w3,
              filt_b3, filt_freq, filt_bias, grp_norm_g, w_out, norm2_g, w_up, ffn_conv_w,
              ffn_conv_b, w_down, final_g):
    xc = x_prompt
    xs = x_sample
    L_ctx = xc.shape[1]
    L_lat = xs.shape[1]
    ks_out, vs_out = [], []
    for l in range(DEPTH):
        shared = (norm1_g[l], w_in[l], hy_conv_w[l], hy_conv_b[l])
        tail = (grp_norm_g[l], w_out[l], norm2_g[l], w_up[l], ffn_conv_w[l], ffn_conv_b[l], w_down[l])
        filt = (filt_w1[l], filt_b1[l], filt_w2[l], filt_b2[l], filt_w3[l], filt_b3[l], filt_freq[l])
        mods_ctx = _adaln(c_ctx[None, :], w_ada[l], b_ada[l])
        kf_ctx = _filter_spectrum(L_ctx, *filt)
        xc, k_c, v_c = _layer(xc, mods_ctx, _ctx_attention, *shared, kf_ctx, filt_bias[l], *tail)
        ks_out.append(k_c)
        vs_out.append(v_c)
        mods_lat = _adaln(c, w_ada[l], b_ada[l])
        kf_lat = _filter_spectrum(L_lat, *filt)
        k_cache = cache_ctx_k[:, l]
        v_cache = cache_ctx_v[:, l]
        rpb_l = rpb[l]
        attend_lat = lambda q, k, v: _neighbourhood_attention(q, k, v, k_cache, v_cache, rpb_l)
        xs, _, _ = _layer(xs, mods_lat, attend_lat, *shared, kf_lat, filt_bias[l], *tail)
    y_prompt = _rmsnorm(xc, final_g)
    y_sample = _rmsnorm(xs, final_g)
    state_ctx_k = jnp.stack(ks_out, axis=1)
    state_ctx_v = jnp.stack(vs_out, axis=1)
    return (y_prompt, y_sample, state_ctx_k, state_ctx_v)
```

```python
import numpy as np
import ml_dtypes
import concourse.bass as bass
import concourse.mybir as mybir
from concourse.bass_utils import run_bass_kernel_spmd

F32 = mybir.dt.float32
BF16 = mybir.dt.bfloat16
I32 = mybir.dt.int32
AF = mybir.ActivationFunctionType
ALU = mybir.AluOpType

D = 1024
NTOK = 1024
DFF = 2816
EPS = 1e-6
NEGM = -30000.0
SAME_ENGINE_SYNC = True


class T:
    __slots__ = ("h", "lw", "rd", "name")

    def __init__(self, h, name=""):
        self.h = h
        self.lw = None
        self.rd = {}
        self.name = name

    def __getitem__(self, idx):
        return self.h[idx]


class FW:
    def __init__(self, nc, n_dma_sems=24):
        self.nc = nc
        self.eng = {"pe": nc.tensor, "act": nc.scalar, "dve": nc.vector,
                    "pool": nc.gpsimd, "sp": nc.sync}
        self.sem = {k: nc.alloc_semaphore(name=f"s_{k}") for k in self.eng}
        self.cnt = {k: 0 for k in self.eng}
        self.seen = {k: {} for k in self.eng}
        self.dsems = [nc.alloc_semaphore(name=f"d_{i}") for i in range(n_dma_sems)]
        self.dcnt = [0] * n_dma_sems
        self.dnext = {"sp": 0, "pool": 0}
        self.drange = {"sp": (0, n_dma_sems // 2), "pool": (n_dma_sems // 2, n_dma_sems)}

    def _wait(self, e, dep):
        kind, k, v = dep
        if kind == "e" and k == e and (e == "pe" or not SAME_ENGINE_SYNC):
            return
        key = (kind, k)
        if self.seen[e].get(key, 0) >= v:
            return
        self.seen[e][key] = v
        s = self.sem[k] if kind == "e" else self.dsems[k]
        self.eng[e].wait_ge(s, v)

    def _deps(self, reads, writes):
        deps = []
        for t in reads:
            if t.lw is not None:
                deps.append(t.lw)
        for t in writes:
            if t.lw is not None:
                deps.append(t.lw)
            deps.extend((k[0], k[1], v) for k, v in t.rd.items())
        return deps

    def op(self, e, fn, reads=(), writes=(), inc=True):
        for d in self._deps(reads, writes):
            self._wait(e, d)
        ins = fn()
        if inc:
            self.cnt[e] += 1
            ins.then_inc(self.sem[e], 1)
            me = ("e", e, self.cnt[e])
        else:
            me = ("e", e, self.cnt[e] + 1)
        self._mark(me, reads, writes)
        return ins

    def _mark(self, me, reads, writes):
        key = (me[0], me[1])
        for t in reads:
            if t.rd.get(key, 0) < me[2]:
                t.rd[key] = me[2]
        for t in writes:
            t.lw = me
            t.rd = {}

    def dma(self, q, out, in_, reads=(), writes=(), **kw):
        for d in self._deps(reads, writes):
            self._wait(q, d)
        lo, hi = self.drange[q]
        i = lo + self.dnext[q]
        self.dnext[q] = (self.dnext[q] + 1) % (hi - lo)
        if self.dcnt[i] > 0:
            self._wait(q, ("d", i, self.dcnt[i]))
        self.dcnt[i] += 16
        ins = self.eng[q].dma_start(out=out, in_=in_, **kw)
        ins.then_inc(self.dsems[i], 16)
        me = ("d", i, self.dcnt[i])
        self._mark(me, reads, writes)
        return me

    def alias(self, new, olds):
        for o in olds:
            ds = list((k[0], k[1], v) for k, v in o.rd.items())
            if o.lw is not None:
                ds.append(o.lw)
            for d in ds:
                key = (d[0], d[1])
                if new.rd.get(key, 0) < d[2]:
                    new.rd[key] = d[2]

    def finish(self):
        for k in ("pe", "act", "dve", "pool"):
            if self.cnt[k] > 0:
                self._wait("sp", ("e", k, self.cnt[k]))
        for i, c in enumerate(self.dcnt):
            if c > 0:
                self._wait("sp", ("d", i, c))


KB = 1024


DEBUG = False
TAPS = []


def build_program():
    nc = bass.Bass("TRN2", target_bir_lowering=False)
    fw = FW(nc)
    del TAPS[:]

    def tap(name, t, ap, shape, dt=F32):
        if not DEBUG:
            return
        d = nc.dram_tensor("tap_" + name, list(shape), dt, kind="ExternalOutput").ap()
        fw.dma("sp", d, ap, reads=[t], writes=[T(None)])
        TAPS.append("tap_" + name)

    def din(name, shape, dt=F32):
        return nc.dram_tensor(name, list(shape), dt, kind="ExternalInput").ap()

    def dout(name, shape):
        return nc.dram_tensor(name, list(shape), F32, kind="ExternalOutput").ap()

    xT = {"S": din("xsT", [128, 8, NTOK]), "P": din("xpT", [128, 8, NTOK])}
    csil_d = din("csil", [128, 8, 2])
    wada_d = din("w_ada", [128, 8, 6144])
    bada_d = din("b_ada", [128, 48])
    gains_d = din("gains", [128, 32])
    win_d = din("w_in", [128, 8, 3072])
    hyc_d = din("hyc", [128, 12, 4])
    ffc_d = din("ffc", [128, 44, 4])
    wout_d = din("w_out", [128, 8, 1024])
    wup_d = din("w_up", [128, 8, 5632])
    wdn_d = din("w_down", [128, 8, 22 * 128])
    ckT_d = din("ckT", [128, 4, 256])
    cv_d = din("cv", [128, 2, 8, 64])
    tabE_d = din("tabE", [128, 8, 1024])
    tabI_d = din("tabI", [128, 8, 1024])
    w1b1_d = din("w1b1", [18, 64])
    w2_d = din("w2", [64, 64])
    fsm_d = din("fsm", [64, 4])
    w3b3_d = din("w3b3", [65, 2048])
    fb_d = din("fb", [1, 1024])
    zT_d = {1024: din("zT1024", [18, 1024]), 256: din("zT256", [18, 256])}
    dec_d = {1024: din("dec1024", [128, 8, 1024]), 256: din("dec256", [128, 2, 1024])}
    W_d = {1024: din("W1024", [128, 8, 2048], BF16), 256: din("W256", [128, 2, 512], BF16)}
    Wu_d = {1024: din("Wu1024", [128, 8, 2048], BF16), 256: din("Wu256", [128, 2, 512], BF16)}
    ident_d = din("ident", [128, 128], BF16)
    e0n_d = din("e0n", [128, 1])
    yT_o = {"S": dout("ysT", [128, 8, 512]), "P": dout("ypT", [128, 8, NTOK])}
    xo_d = din("xo", [128, 8, 514])
    sel_d = din("sel", [128, 8, 514], BF16)
    mrow_d = din("mrow", [1, 514])
    gbrow_d = din("gbrow", [1, 512])
    kT_o = dout("kT_o", [128, 4, NTOK])
    v_o = dout("v_o", [128, 8, 512])
    OUT = T(None, "outs")

    base = (nc.sbuf_base + 63) // 64 * 64
    top = nc.sbuf_top

    def sb(name, shape, dt, off):
        nb = int(np.prod(shape[1:])) * (2 if dt == BF16 else 4)
        assert base + off + nb <= top, (name, off, nb, top - base)
        return T(nc.alloc_sbuf_tensor_at(name, list(shape), dt, offset=base + off), name)

    O_SM = 0
    ident = sb("ident", [128, 128], BF16, 0)
    ones = sb("ones", [128, 128], BF16, 256)
    mods = sb("mods", [128, 2, 48], F32, 512)
    gains = sb("gains", [128, 32], F32, 896)
    gm = sb("gm", [128, 2, 2, 8], F32, 1024)
    hyc = sb("hyc", [128, 12, 4], F32, 1152)
    ffc = sb("ffc", [128, 44, 4], F32, 1344)
    csil = sb("csil", [128, 8, 2], F32, 2048)
    csb = sb("csb", [128, 8, 2], BF16, 2112)
    bada = sb("bada", [128, 48], F32, 2176)
    epsc = sb("epsc", [128, 1], F32, 2368)
    e0n = sb("e0n", [128, 1], F32, 3200)
    fsm = sb("fsm", [64, 8], F32, 2400)
    w1b1 = sb("w1b1", [18, 64], F32, 2432)
    w2s = sb("w2s", [64, 64], F32, 2688)
    smalls = sb("smalls", [128, 64], F32, 2944)
    O_W256 = 5 * KB
    O_K256 = 7 * KB
    O_A = 15 * KB
    O_B = 79 * KB
    O_C = 95 * KB
    O_D = 111 * KB
    O_E = 135 * KB
    Wm = {256: sb("W256", [128, 2, 512], BF16, O_W256), 1024: sb("W1024", [128, 8, 2048], BF16, O_A)}
    Ktab = {256: sb("K256", [128, 2, 2, 2, 512], BF16, O_K256),
            1024: sb("K1024", [128, 8, 2, 2, 512], BF16, O_A + 32 * KB)}
    wbuf = [sb(f"wbuf{i}", [128, 8, 512], BF16, O_D + i * 8 * KB) for i in range(3)]
    wb_i = [0]

    pinned = set()

    def next_wbuf():
        while True:
            t = wbuf[wb_i[0] % 3]
            wb_i[0] += 1
            if t.name not in pinned:
                return t

    class HalfView:
        def __init__(self, big, off):
            self.big, self.off = big, off

        def __getitem__(self, idx):
            if not isinstance(idx, tuple):
                idx = (idx, slice(None))
            ps_, cs_ = idx
            a = 0 if cs_.start is None else cs_.start
            b = 512 if cs_.stop is None else cs_.stop
            return self.big[ps_, self.off + a:self.off + b]

    psd = [nc.alloc_psum_tensor(f"psd{i}", [128, 1024], F32) for i in range(3)]
    psf = [T(HalfView(psd[i // 2], 512 * (i % 2)), f"psf{i}") for i in range(6)]
    psf.append(T(nc.alloc_psum_tensor("psf6", [128, 512], F32), "psf6"))
    psb = T(nc.alloc_psum_tensor("psb", [128, 1024], BF16), "psb")
    rot = {"i": 0, "banks": [0, 1, 2, 3, 4, 5]}

    def nextps():
        b = rot["banks"][rot["i"] % len(rot["banks"])]
        rot["i"] += 1
        return psf[b]

    def nextpair():
        if rot["i"] % 2:
            rot["i"] += 1
        b = rot["banks"][rot["i"] % len(rot["banks"])]
        assert b % 2 == 0
        rot["i"] += 2
        return psf[b], psf[b + 1], psd[b // 2]

    ACT = lambda fn, r, w: fw.op("act", fn, reads=r, writes=w)
    DVE = lambda fn, r, w: fw.op("dve", fn, reads=r, writes=w)
    POOL = lambda fn, r, w: fw.op("pool", fn, reads=r, writes=w)

    def mm(ps_ap, lhsT, rhs, start, stop, reads, writes, last):
        return fw.op("pe", lambda: nc.tensor.matmul(ps_ap, lhsT=lhsT, rhs=rhs, start=start, stop=stop),
                     reads=reads, writes=writes, inc=last)

    fw.dma("sp", ident[:], ident_d, writes=[ident])
    fw.dma("sp", e0n[:], e0n_d, writes=[e0n])
    fw.dma("sp", csil[:], csil_d, writes=[csil])
    fw.dma("sp", bada[:], bada_d, writes=[bada])
    fw.dma("sp", gains[:], gains_d, writes=[gains])
    fw.dma("sp", hyc[:], hyc_d, writes=[hyc])
    fw.dma("sp", ffc[:], ffc_d, writes=[ffc])
    fw.dma("sp", fsm[:, 0:4], fsm_d, writes=[fsm])
    fw.dma("sp", w1b1[:], w1b1_d, writes=[w1b1])
    fw.dma("sp", w2s[:], w2_d, writes=[w2s])
    DVE(lambda: nc.vector.memset(ones[:], 1.0), [], [ones])
    DVE(lambda: nc.vector.memset(epsc[:], EPS), [], [epsc])
    ACT(lambda: nc.scalar.activation(out=csb[:], in_=csil[:], func=AF.Silu), [csil], [csb])
    DVE(lambda: nc.vector.tensor_scalar(out=fsm[:, 4:5], in0=fsm[:, 1:2], scalar1=float(1.0 / (2 * np.pi)),
                                        scalar2=None, op0=ALU.mult), [fsm], [fsm])

    pm = psf[6]
    mods_state = {"blk": 0}

    def mods_mm(blk, wt):
        for j in range(4):
            cj = blk * 4 + j
            for kc in range(8):
                mm(pm[:, cj * 2:cj * 2 + 2], wt[:, kc, j * 128:(j + 1) * 128], csb[:, kc, :],
                   kc == 0, kc == 7, [wt, csb], [pm], kc == 7)

    late = {"slots": None, "pend": []}
    early = {"pend": []}

    def mods_early_issue():
        blk = mods_state["blk"]
        if blk < 4:
            wt = next_wbuf()
            fw.dma("pool", wt[:], wada_d[:, :, blk * 512:(blk + 1) * 512], writes=[wt])
            early["pend"].append((blk, wt))
            mods_state["blk"] += 1

    def mods_early_tick():
        if early["pend"]:
            b0, w0 = early["pend"].pop(0)
            mods_mm(b0, w0)
            mods_early_issue()

    def mods_late_tick():
        blk = mods_state["blk"]
        if len(late["pend"]) == 2 or (blk >= 12 and late["pend"]):
            b0, w0 = late["pend"].pop(0)
            mods_mm(b0, w0)
        if blk < 12:
            wt = late["slots"][blk % 2]
            fw.dma("pool", wt[:], wada_d[:, :, blk * 512:(blk + 1) * 512], writes=[wt])
            late["pend"].append((blk, wt))
            mods_state["blk"] += 1

    def mods_tick(limit=12):
        blk = mods_state["blk"]
        if blk >= limit:
            return
        mods_state["blk"] += 1
        wt = next_wbuf()
        fw.dma("pool", wt[:], wada_d[:, :, blk * 512:(blk + 1) * 512], writes=[wt])
        for j in range(4):
            cj = blk * 4 + j
            for kc in range(8):
                mm(pm[:, cj * 2:cj * 2 + 2], wt[:, kc, j * 128:(j + 1) * 128], csb[:, kc, :],
                   kc == 0, kc == 7, [wt, csb], [pm], kc == 7)

    def mods_finish():
        while mods_state["blk"] < 12:
            mods_tick()
    def mods_final(part):
        if part == 0:
            while early["pend"]:
                mods_early_tick()
            while mods_state["blk"] < 4:
                mods_tick()
            c0, c1 = 0, 16
        else:
            while mods_state["blk"] < 12 or late["pend"]:
                if late["slots"] is not None:
                    mods_late_tick()
                else:
                    mods_tick()
            c0, c1 = 16, 48
        pm3 = pm[:, 0:96].rearrange("p (c s) -> p c s", s=2)
        for s in range(2):
            DVE(lambda s=s: nc.vector.tensor_tensor(out=mods[:, s, c0:c1], in0=pm3[:, c0:c1, s], in1=bada[:, c0:c1], op=ALU.add),
                [pm, bada], [mods])
        w = part
        for s in range(2):
            DVE(lambda s=s, w=w: nc.vector.scalar_tensor_tensor(
                out=gm[:, s, w, :], in0=mods[:, s, (8 + 24 * w):(16 + 24 * w)], scalar=1.0,
                in1=gains[:, 8 * w:8 * w + 8], op0=ALU.add, op1=ALU.mult), [mods, gains], [gm])
        if part == 1:
            tap("mods", mods, mods[:], [128, 2, 48])
            tap("gm", gm, gm[:], [128, 2, 2, 8])

    def modcol(s, j, c):
        return mods[:, s, j * 8 + c:j * 8 + c + 1]

    def filter_phase(L, dead_in):
        Lc = L // 128
        nh = max(1, L // 512)
        dec = sb(f"dec{L}", [128, Lc, 1024], F32, O_B)
        fs = sb(f"fs{L}", [128, Lc, 2, 512], BF16, O_A if L == 1024 else O_E + 60 * KB)
        fd = sb(f"fd{L}", [128, Lc, 2, 512], BF16, O_A + 16 * KB if L == 1024 else O_E + 64 * KB)
        yv = sb(f"yv{L}", [64, L], F32, O_E + 32 * KB)
        ti = sb(f"ti{L}", [64, L], I32, O_E + 36 * KB)
        tf = sb(f"tf{L}", [64, L], F32, O_E + 40 * KB)
        h1 = sb(f"h1{L}", [64, L], F32, O_E + 44 * KB)
        h2 = sb(f"h2{L}", [65, L], BF16, O_E + 48 * KB)
        zT = sb(f"zT{L}", [18, L], F32, O_E + 52 * KB)
        fbs = sb(f"fbs{L}", [128, 1024], F32, O_E + 56 * KB)
        w3 = sb(f"w3{L}", [96, 2048], F32, O_C if L == 256 else O_E + 64 * KB)
        w3c = sb(f"w3c{L}", [96, 2048], BF16, O_C + 8 * KB if L == 256 else O_E + 60 * KB)
        w3b = sb(f"w3b{L}", [96, 2, 512], BF16, O_C + 12 * KB if L == 256 else O_E + 50 * KB)
        t1, t2 = w3c, w3b
        for t in [dec, fs, fd, yv, ti, tf, h1, h2, zT, fbs, w3c, w3b, w3]:
            fw.alias(t, dead_in)
        fw.dma("sp", zT[:], zT_d[L], writes=[zT])
        DVE(lambda: nc.vector.memset(w3[64:96, :], 0.0), [], [w3])
        fw.dma("sp", w3[0:65, :], w3b3_d, writes=[w3])
        for o in range(2):
            wf = w3[:, o * 1024:o * 1024 + 512]
            wb_ = w3[:, o * 1024 + 512:o * 1024 + 1024]
            DVE(lambda o=o, wf=wf, wb_=wb_: nc.vector.tensor_tensor(out=w3c[:, o * 1024:o * 1024 + 512], in0=wf, in1=wb_, op=ALU.add), [w3], [w3c])
            DVE(lambda o=o, wf=wf, wb_=wb_: nc.vector.tensor_tensor(out=w3c[:, o * 1024 + 512:o * 1024 + 1024], in0=wb_, in1=wf, op=ALU.subtract), [w3], [w3c])
            DVE(lambda o=o, wb_=wb_: nc.vector.tensor_copy(out=w3b[:, o, :], in_=wb_), [w3], [w3b])
        fw.dma("sp", dec[:], dec_d[L], writes=[dec])
        fw.dma("sp", fbs[:], fb_d.partition_broadcast(128), writes=[fbs])
        if L == 1024:
            prefetch_x("S", [])
        W = min(L, 512)

        def sin_layer(src_ps_list, dst, add_b2):
            for i, p in enumerate(src_ps_list):
                sl = slice(i * W, (i + 1) * W)
                if add_b2:
                    DVE(lambda p=p, sl=sl: nc.vector.tensor_scalar(out=yv[:, sl], in0=p[0:64, 0:W], scalar1=fsm[:, 0:1],
                                                                   scalar2=fsm[:, 4:5], op0=ALU.add, op1=ALU.mult),
                        [p, fsm], [yv])
                    DVE(lambda sl=sl: nc.vector.tensor_scalar(out=yv[:, sl], in0=yv[:, sl], scalar1=64.0, scalar2=None,
                                                              op0=ALU.add), [yv], [yv])
                else:
                    DVE(lambda p=p, sl=sl: nc.vector.tensor_scalar(out=yv[:, sl], in0=p[0:64, 0:W], scalar1=fsm[:, 4:5],
                                                                   scalar2=64.0, op0=ALU.mult, op1=ALU.add),
                        [p, fsm], [yv])
            DVE(lambda: nc.vector.tensor_copy(out=ti[:], in_=yv[:]), [yv], [ti])
            DVE(lambda: nc.vector.tensor_copy(out=tf[:], in_=ti[:]), [ti], [tf])
            DVE(lambda: nc.vector.tensor_tensor(out=yv[:], in0=yv[:], in1=tf[:], op=ALU.subtract), [yv, tf], [yv])
            DVE(lambda: nc.vector.tensor_scalar(out=tf[:], in0=yv[:], scalar1=0.5, scalar2=None, op0=ALU.is_gt), [yv], [tf])
            DVE(lambda: nc.vector.tensor_tensor(out=yv[:], in0=yv[:], in1=tf[:], op=ALU.subtract), [yv, tf], [yv])
            DVE(lambda: nc.vector.tensor_scalar(out=tf[:], in0=yv[:], scalar1=-0.5, scalar2=None, op0=ALU.is_lt), [yv], [tf])
            DVE(lambda: nc.vector.tensor_tensor(out=yv[:], in0=yv[:], in1=tf[:], op=ALU.add), [yv, tf], [yv])
            ACT(lambda: nc.scalar.activation(out=dst[0:64, :], in_=yv[:], func=AF.Sin, scale=float(2 * np.pi)), [yv], [dst])

        pl = []
        for i in range(nh):
            p = nextps()
            mm(p[0:64, 0:W], w1b1[:], zT[:, i * W:(i + 1) * W], True, True, [w1b1, zT], [p], True)
            pl.append(p)
        sin_layer(pl, h1, False)
        pl = []
        for i in range(nh):
            p = nextps()
            mm(p[0:64, 0:W], w2s[:], h1[:, i * W:(i + 1) * W], True, True, [w2s, h1], [p], True)
            pl.append(p)
        sin_layer(pl, h2, True)
        DVE(lambda: nc.vector.memset(h2[64:65, :], 1.0), [], [h2])
        for tc in range(Lc):
            if tc >= 1:
                mods_early_tick()
            pq = [nextps() for _ in range(4)]
            for q in range(4):
                mm(pq[q][:], h2[:, tc * 128:(tc + 1) * 128], w3c[0:65, q * 512:(q + 1) * 512], True, True, [h2, w3c], [pq[q]], True)
            for o in range(2):
                ps_, pd_ = pq[2 * o], pq[2 * o + 1]
                DVE(lambda ps_=ps_, o=o: nc.vector.tensor_tensor(out=fs[:, tc, o, :], in0=ps_[:], in1=dec[:, tc, 0:512], op=ALU.mult), [ps_, dec], [fs])
                DVE(lambda pd_=pd_, o=o: nc.vector.tensor_tensor(out=fd[:, tc, o, :], in0=pd_[:], in1=dec[:, tc, 0:512], op=ALU.mult), [pd_, dec], [fd])
            if tc == 0:
                for o in range(2):
                    pc = nextps()
                    mm(pc[:], h2[:, 0:128], w3b[0:65, o, :], True, True, [h2, w3b], [pc], True)
                    DVE(lambda o=o, pc=pc: nc.vector.scalar_tensor_tensor(out=fs[:, 0, o, :], in0=pc[:], scalar=e0n[:, 0:1], in1=fs[:, 0, o, :],
                                                                         op0=ALU.mult, op1=ALU.add), [fs, pc, e0n], [fs])
                    DVE(lambda o=o, pc=pc: nc.vector.scalar_tensor_tensor(out=fd[:, 0, o, :], in0=pc[:], scalar=e0n[:, 0:1], in1=fd[:, 0, o, :],
                                                                         op0=ALU.mult, op1=ALU.add), [fd, pc, e0n], [fd])
        Kt = Ktab[L]
        nblk = (2 * L) // 512
        for blk in range(nblk):
            wt = next_wbuf()
            fw.dma("sp", wt[:, 0:Lc, :], Wu_d[L][:, :, blk * 512:(blk + 1) * 512], writes=[wt])
            for j in range(4):
                fr = blk * 4 + j
                isI = fr >= Lc
                fc = fr - Lc if isI else fr
                src = fd if isI else fs
                for o in range(2):
                    p = nextps()
                    for tc in range(Lc):
                        mm(p[:], wt[:, tc, j * 128:(j + 1) * 128], src[:, tc, o, :], tc == 0, tc == Lc - 1, [wt, src], [p], tc == Lc - 1)
                    if isI:
                        ACT(lambda p=p, fc=fc, o=o: nc.scalar.copy(out=Kt[:, fc, 1, o, :], in_=p[:]), [p], [Kt])
                    else:
                        DVE(lambda p=p, fc=fc, o=o: nc.vector.scalar_tensor_tensor(
                            out=Kt[:, fc, 0, o, :], in0=fbs[:, o * 512:(o + 1) * 512], scalar=float(1.0 / L), in1=p[:],
                            op0=ALU.mult, op1=ALU.add), [p, fbs], [Kt])
        tap(f"h1_{L}", h1, h1[:], [64, L])
        tap(f"h2_{L}", h2, h2[:], [65, L])
        tap(f"fs_{L}", fs, fs[:], [128, Lc, 2, 512], BF16)
        tap(f"K_{L}", Kt, Kt[:], [128, Lc, 2, 2, 512], BF16)
        return [dec, fs, fd, yv, ti, tf, h1, h2, zT, fbs, t1, t2, w3]


    def rstd_from(ps_list, dst, dcount):
        for i, p in enumerate(ps_list):
            ACT(lambda p=p, i=i: nc.scalar.activation(out=dst[:, i * 512:(i + 1) * 512], in_=p[:], func=AF.Ln,
                                                      bias=epsc[:, 0:1], scale=float(1.0 / dcount)), [p, epsc], [dst])
        ACT(lambda: nc.scalar.activation(out=dst[:], in_=dst[:], func=AF.Exp, scale=-0.5), [dst], [dst])

    prefetched = {}
    final_dead = {}
    live = {}

    def s_tail(dead_all, mtok):
        sset = 1

        def sba(name, shape, dt, off):
            t = sb("So_" + name, shape, dt, off)
            fw.alias(t, dead_all)
            return t
        x1o = [sba(f"x1o{c}", [128, 514], F32, O_A + c * 2112) for c in range(8)]
        x2o = [sba(f"x2o{c}", [128, 512], F32, O_A + 17 * KB + c * 2048) for c in range(8)]
        xst = [sba(f"xst{i}", [128, 514], F32, O_A + 33 * KB + i * 2112) for i in range(2)]
        rso = sba("rso", [128, 514], F32, O_A + 38 * KB)
        sqo = [sba(f"sqo{i}", [128, 514], BF16, O_A + 41 * KB + i * 1088) for i in range(2)]
        yst = [sba(f"yst{i}", [128, 512], F32, O_A + 44 * KB + i * 2048) for i in range(4)]
        mrow = sba("mrow", [128, 514], F32, O_A + 52 * KB)
        tmpf = [sba(f"tmpf{i}", [128, 514], F32, O_A + 55 * KB + i * 2112) for i in range(2)]
        ost = [sba(f"ost{i}", [128, 512], F32, O_A + 60 * KB + i * 2048) for i in range(2)]
        selT = sba("sel", [128, 8, 514], BF16, O_B + 4 * KB)
        h2o = sba("h2o", [128, 8, 514], BF16, O_B + 4 * KB)
        mrgo = sba("mrgo", [128, 8, 514], BF16, O_E)
        actTo = sba("actTo", [128, 22, 512], BF16, O_E + 9 * KB)
        all_t = x1o + x2o + xst + [rso] + sqo + yst + [mrow] + tmpf + ost + [selT, h2o, mrgo, actTo]
        fw.dma("sp", selT[:], sel_d, writes=[selT])
        fw.dma("sp", mrow[:], mrow_d.partition_broadcast(128), writes=[mrow])
        rot["banks"] = [0, 1, 2, 3]
        rot["i"] = 0
        pst = (psf[4], psf[5], psd[2])

        def mm514(big, pa, pb2, lhs_fn, rhs_t, rhs_fn, n, reads):
            for k in range(n):
                mm(big[:, 0:512], lhs_fn(k), rhs_fn(k, 0, 512), k == 0, k == n - 1, reads, [pa], k == n - 1)
            for k in range(n):
                mm(big[:, 512:514], lhs_fn(k), rhs_fn(k, 512, 514), k == 0, k == n - 1, reads, [pb2], k == n - 1)

        for c in range(8):
            pa, pb2, big = nextpair()
            mm514(big, pa, pb2, lambda tk, c=c: mtok[:, tk, c * 128:(c + 1) * 128], selT,
                  lambda tk, a, b: selT[:, tk, a:b], 8, [mtok, selT])
            ACT(lambda c=c, big=big: nc.scalar.copy(out=mrgo[:, c, :], in_=big[:, 0:514]), [pa, pb2], [mrgo])
        fw.alias(h2o, [selT])

        def stats(c, src_t, width):
            s_ = sqo[c % 2]
            ACT(lambda: nc.scalar.activation(out=s_[:, 0:width], in_=src_t[:, 0:width], func=AF.Square), [src_t], [s_])
            mm(pst[2][:, 0:512], ones[:], s_[:, 0:512], c == 0, c == 7, [ones, s_], [pst[0]], True)
            if width > 512:
                mm(pst[2][:, 512:514], ones[:], s_[:, 512:514], c == 0, c == 7, [ones, s_], [pst[1]], True)

        def rstd_o(width):
            ACT(lambda: nc.scalar.activation(out=rso[:, 0:width], in_=pst[2][:, 0:width], func=AF.Ln, bias=epsc[:, 0:1], scale=float(1.0 / D)),
                [pst[0], pst[1], epsc], [rso])
            ACT(lambda: nc.scalar.activation(out=rso[:, 0:width], in_=rso[:, 0:width], func=AF.Exp, scale=-0.5), [rso], [rso])

        for b in range(2):
            wt = next_wbuf()
            fw.dma("pool", wt[:], wout_d[:, :, b * 512:(b + 1) * 512], writes=[wt])
            for j in range(4):
                cj = b * 4 + j
                xs = xst[cj % 2]
                fw.dma("sp", xs[:], xo_d[:, cj, :], writes=[xs])
                pa, pb2, big = nextpair()
                mm514(big, pa, pb2, lambda kc, j=j, wt=wt: wt[:, kc, j * 128:(j + 1) * 128], mrgo,
                      lambda kc, a, b_: mrgo[:, kc, a:b_], 8, [wt, mrgo])
                DVE(lambda cj=cj, big=big, xs=xs: nc.vector.scalar_tensor_tensor(
                    out=x1o[cj][:], in0=big[:, 0:514], scalar=modcol(sset, 2, cj), in1=xs[:], op0=ALU.mult, op1=ALU.add),
                    [pa, pb2, mods, xs], [x1o[cj]])
                if cj >= 1:
                    stats(cj - 1, x1o[cj - 1], 514)
        stats(7, x1o[7], 514)
        rstd_o(514)
        for c in range(8):
            tf_ = tmpf[c % 2]
            DVE(lambda c=c, tf_=tf_: nc.vector.tensor_tensor(out=tf_[:], in0=x1o[c][:], in1=rso[:], op=ALU.mult), [x1o[c], rso], [tf_])
            ACT(lambda c=c, tf_=tf_: nc.scalar.activation(out=tf_[:], in_=tf_[:], func=AF.Identity,
                                                          bias=modcol(sset, 3, c), scale=gm[:, sset, 1, c:c + 1]), [tf_, mods, gm], [tf_])
            DVE(lambda c=c, tf_=tf_: nc.vector.tensor_tensor(out=h2o[:, c, :], in0=tf_[:], in1=mrow[:], op=ALU.mult), [tf_, mrow], [h2o])
        items = []
        wts = {}
        for blk in range(11):
            for i in range(2):
                jj = 2 * blk + i
                for which, jcol in ((0, i), (1, 2 + i)):
                    fcx = which * 22 + jj
                    sy = yst[2 * which + (jj % 2)]
                    stt = {}
                    wc = ffc[:, fcx, :]

                    def A(blk=blk, i=i, which=which, jcol=jcol, sy=sy, stt=stt, wc=wc):
                        if i == 0 and which == 0:
                            wt = next_wbuf()
                            fw.dma("pool", wt[:], wup_d[:, :, blk * 512:(blk + 1) * 512], writes=[wt])
                            wts[blk] = wt
                        wt = wts[blk]
                        pa, pb2, big = nextpair()
                        mm514(big, pa, pb2, lambda kc: wt[:, kc, jcol * 128:(jcol + 1) * 128], h2o,
                              lambda kc, a, b_: h2o[:, kc, a:b_], 8, [wt, h2o])
                        stt["p"] = (pa, pb2, big)
                        ACT(lambda: nc.scalar.activation(out=sy[:], in_=big[:, 1:513], func=AF.Identity, bias=wc[:, 3:4], scale=wc[:, 1:2]),
                            [pa, pb2, ffc], [sy])

                    def B(sy=sy, stt=stt, wc=wc):
                        pa, pb2, big = stt["p"]
                        DVE(lambda: nc.vector.scalar_tensor_tensor(out=sy[:], in0=big[:, 0:512], scalar=wc[:, 0:1], in1=sy[:],
                                                                   op0=ALU.mult, op1=ALU.add), [sy, pa, ffc], [sy])
                        DVE(lambda: nc.vector.scalar_tensor_tensor(out=sy[:], in0=big[:, 2:514], scalar=wc[:, 2:3], in1=sy[:],
                                                                   op0=ALU.mult, op1=ALU.add), [sy, pa, pb2, ffc], [sy])

                    def C(which=which, jj=jj, sy=sy):
                        if which == 0:
                            ACT(lambda: nc.scalar.activation(out=sy[:], in_=sy[:], func=AF.Silu), [sy], [sy])
                        else:
                            gs = yst[jj % 2]
                            DVE(lambda: nc.vector.tensor_tensor(out=actTo[:, jj, :], in0=gs[:], in1=sy[:], op=ALU.mult), [gs, sy], [actTo])
                    items.append([A, B, C])
        n_it = len(items)
        for t_ in range(n_it + 2):
            for k in (2, 1, 0):
                ii = t_ - k
                if 0 <= ii < n_it:
                    items[ii][k]()
        for cj in range(8):
            wt = next_wbuf()
            wflat = wt[:].rearrange("p a b -> p (a b)")
            fw.dma("pool", wflat[:, 0:22 * 128], wdn_d[:, cj, :], writes=[wt])
            p = nextps()
            for kk in range(22):
                mm(p[:], wflat[:, kk * 128:(kk + 1) * 128], actTo[:, kk, :], kk == 0, kk == 21, [wt, actTo], [p], kk == 21)
            DVE(lambda p=p, cj=cj: nc.vector.scalar_tensor_tensor(
                out=x2o[cj][:], in0=p[:], scalar=modcol(sset, 5, cj), in1=x1o[cj][:, 1:513], op0=ALU.mult, op1=ALU.add),
                [p, mods, x1o[cj]], [x2o[cj]])
            if cj >= 1:
                stats(cj - 1, x2o[cj - 1], 512)
        stats(7, x2o[7], 512)
        live["S"] = all_t + dead_all
        prefetch_x("P", live["S"])
        yield "pre_S9"
        rstd_o(512)
        rot["banks"] = [0, 1, 2, 3, 4, 5]
        rot["i"] = 0
        for c in range(8):
            ft = ost[c % 2]
            DVE(lambda c=c, ft=ft: nc.vector.scalar_tensor_tensor(out=ft[:], in0=x2o[c][:], scalar=gains[:, 16 + c:17 + c],
                                                                 in1=rso[:, 0:512], op0=ALU.mult, op1=ALU.mult), [x2o[c], gains, rso], [ft])
            fw.dma("sp", yT_o["S"][:, c, :], ft[:], reads=[ft], writes=[OUT])
        final_dead["S"] = all_t + dead_all


    def prefetch_x(g2, deadl):
        xc = [sb(g2 + f"xc{c}", [128, NTOK], F32, O_E + c * 4 * KB) for c in range(8)]
        for t in xc:
            fw.alias(t, deadl)
        for c in range(8):
            fw.dma("sp", xc[c][:], xT[g2][:, c, :], writes=[xc[c]])
        prefetched[g2] = xc

    def group(g, dead_in):
        L = 1024 if g == "S" else 256
        nseq = NTOK // L
        Lc = L // 128
        sset = 1 if g == "S" else 0
        x_d = xT[g]
        x1T = sb(g + "x1T", [128, 4, NTOK], BF16, O_E)
        x2T = sb(g + "x2T", [128, 4, NTOK], BF16, O_E + 8 * KB)
        vtok = [sb(g + f"vtok{s}", [128, Lc, 512], BF16, O_E + 16 * KB + s * Lc * KB) for s in range(nseq)]
        nY = 16 // (2 * Lc)
        Ys = [sb(g + f"Y{k}", [128, 2 * Lc, 512], BF16, O_E + 24 * KB + k * 2 * Lc * KB) for k in range(nY)]
        Y = Ys[0]
        ysa = [sb(g + f"ysa{i}", [128, 512], F32, O_E + 40 * KB + i * 2 * KB) for i in range(2)]
        ysb = [sb(g + f"ysb{i}", [128, 512], F32, O_E + 44 * KB + i * 2 * KB) for i in range(2)]
        yt1 = sb(g + "yt1", [128, 512], F32, O_E + 48 * KB)
        yt2 = sb(g + "yt2", [128, 512], F32, O_E + 50 * KB)
        yt3 = sb(g + "yt3", [128, 512], F32, O_E + 66 * KB)
        yt4 = sb(g + "yt4", [128, 512], F32, O_E + 70 * KB)
        ystg = sb(g + "ystg", [128, NTOK], F32, O_E + 52 * KB)
        pstg = sb(g + "pstg", [128, NTOK], F32, O_E + 56 * KB)
        vT = sb(g + "vT", [128, NTOK], BF16, O_E + 60 * KB)
        vT2 = sb(g + "vT2", [128, NTOK], BF16, O_E + 68 * KB)
        vTs = [vT, vT2]
        rstd = sb(g + "rstd", [128, NTOK], F32, O_E + 40 * KB)
        xstg = [sb(g + f"xstg{i}", [128, NTOK], F32, O_E + 24 * KB + i * 4 * KB) for i in range(2)]
        sq = [sb(g + f"sq{i}", [128, NTOK], BF16, O_E + 32 * KB + i * 2 * KB) for i in range(2)]
        hT = sb(g + "hT", [128, 8, NTOK], BF16, O_B)
        mrg = sb(g + "mrg", [128, 8, NTOK], BF16, O_C)
        for t in [x1T, x2T, ystg, pstg, vT, vT2, rstd, hT, mrg] + Ys + vtok + ysa + ysb + [yt1, yt2, yt3, yt4] + xstg + sq:
            fw.alias(t, dead_in)

        if g not in prefetched:
            prefetch_x(g, dead_in)
        xc = prefetched[g]
        pss = [nextps(), nextps()]
        for c in range(8):
            s_ = sq[c % 2]
            ACT(lambda c=c, s_=s_: nc.scalar.activation(out=s_[:], in_=xc[c][:], func=AF.Square), [xc[c]], [s_])
            for tt in range(2):
                mm(pss[tt][:], ones[:], s_[:, tt * 512:(tt + 1) * 512], c == 0, c == 7, [ones, s_], [pss[tt]], True)
        rstd_from(pss, rstd, D)
        for c in range(8):
            DVE(lambda c=c: nc.vector.tensor_tensor(out=xc[c][:], in0=xc[c][:], in1=rstd[:], op=ALU.mult), [xc[c], rstd], [xc[c]])
            ACT(lambda c=c: nc.scalar.activation(out=hT[:, c, :], in_=xc[c][:], func=AF.Identity,
                                                 bias=modcol(sset, 0, c), scale=gm[:, sset, 0, c:c + 1]),
                [xc[c], mods, gm], [hT])
        for t in [x1T, x2T] + Ys + vtok + xstg:
            fw.alias(t, xc)
        for t in ysa:
            fw.alias(t, [rstd])
        yield "post_S1"
        if g == "P":
            for t in [x1T, x2T, ystg, pstg, vT, vT2, rstd, hT, mrg, yt1, yt2, yt3, yt4] + Ys + vtok + ysa + ysb + xstg + sq + xc:
                fw.alias(t, final_dead["S"])
            dead_in = dead_in + final_dead["S"]

        tap(g + "rstd", rstd, rstd[:], [128, NTOK])
        tap(g + "hT", hT, hT[:], [128, 8, NTOK], BF16)
        def dw_A(pa, pb2, big, wcols, stg_y):
            ACT(lambda: nc.scalar.activation(out=stg_y[:], in_=big[:, :], func=AF.Identity,
                                             bias=wcols[:, 3:4], scale=wcols[:, 1:2]), [pa, pb2, hyc, ffc], [stg_y])

        def dw_B(pa, pb2, big, wcols, stg_y):
            y3 = stg_y[:].rearrange("p (s l) -> p s l", l=L)
            p3 = big[:, :].rearrange("p (s l) -> p s l", l=L)
            DVE(lambda: nc.vector.scalar_tensor_tensor(out=y3[:, :, 1:L], in0=p3[:, :, 0:L - 1], scalar=wcols[:, 0:1],
                                                       in1=y3[:, :, 1:L], op0=ALU.mult, op1=ALU.add), [stg_y, pa, pb2, hyc, ffc], [stg_y])
            DVE(lambda: nc.vector.scalar_tensor_tensor(out=y3[:, :, 0:L - 1], in0=p3[:, :, 1:L], scalar=wcols[:, 2:3],
                                                       in1=y3[:, :, 0:L - 1], op0=ALU.mult, op1=ALU.add), [stg_y, pa, pb2, hyc, ffc], [stg_y])

        def run_pipeline(items):
            n = len(items)
            K = max(len(it) for it in items)
            for t in range(n + K - 1):
                for k in range(K - 1, -1, -1):
                    i = t - k
                    if 0 <= i < n and k < len(items[i]):
                        items[i][k]()

        def transposes_to_tok(srcT, src_ap_fn, dst_list, col0):
            for tk in range(8):
                fw.op("pe", lambda tk=tk: nc.tensor.transpose(psb[:, tk * 128:(tk + 1) * 128], src_ap_fn(tk), ident[:]),
                      reads=[srcT, ident], writes=[psb], inc=(tk == 7))
            for s in range(nseq):
                ACT(lambda s=s: nc.scalar.copy(out=dst_list[s][:, :, col0:col0 + 128],
                                               in_=psb[:, s * L:(s + 1) * L].rearrange("p (t c) -> p t c", c=128)),
                    [psb], [dst_list[s]])

        def proj_block(wt, j, writes_ps=None):
            pa, pb2, big = nextpair()
            for tt, p in enumerate((pa, pb2)):
                for kc in range(8):
                    mm(p[:], wt[:, kc, j * 128:(j + 1) * 128], hT[:, kc, tt * 512:(tt + 1) * 512], kc == 0, kc == 7, [wt, hT], [p], kc == 7)
            return pa, pb2, big

        items = []
        wts = {}
        for b in (3, 4, 5):
            for j in range(4):
                hc = (b - 3) * 4 + j
                yb = (ystg, pstg)[hc % 2]
                stt = {}

                def A(b=b, j=j, hc=hc, yb=yb, stt=stt):
                    if j == 0:
                        pinned.clear()
                        wt = next_wbuf()
                        fw.dma("pool", wt[:], win_d[:, :, b * 512:(b + 1) * 512], writes=[wt])
                        wts[b] = wt
                        pinned.add(wt.name)
                    stt["p"] = proj_block(wts[b], j)
                    dw_A(*stt["p"], hyc[:, hc, :], yb)

                def B(hc=hc, yb=yb, stt=stt):
                    dw_B(*stt["p"], hyc[:, hc, :], yb)

                def C(b=b, j=j, yb=yb):
                    if b == 3:
                        ACT(lambda: nc.scalar.copy(out=x1T[:, j, :], in_=yb[:]), [yb], [x1T])
                    elif b == 4:
                        ACT(lambda: nc.scalar.copy(out=x2T[:, j, :], in_=yb[:]), [yb], [x2T])
                    else:
                        vTj = vTs[j % 2]
                        ACT(lambda: nc.scalar.copy(out=vTj[:], in_=yb[:]), [yb], [vTj])
                        transposes_to_tok(vTj, lambda tk: vTj[:, tk * 128:(tk + 1) * 128], vtok, j * 128)
                items.append([A, B, C])
        run_pipeline(items)
        pinned.clear()

        tap(g + "x1T", x1T, x1T[:], [128, 4, NTOK], BF16)
        tap(g + "vtok0", vtok[0], vtok[0][:], [128, Lc, 512], BF16)
        pre_w = []
        for b in (0, 1, 2):
            wt = next_wbuf()
            fw.dma("pool", wt[:], win_d[:, :, b * 512:(b + 1) * 512], writes=[wt])
            pre_w.append(wt)
        def make_s2b(qT, kT, Vaug, kst):
            pieces = []
            for b in (0, 1):
                for j in range(4):
                    def piece(b=b, j=j):
                        wt = pre_w[b]
                        dstT = qT if b == 0 else kT
                        pa, pb2, _big = proj_block(wt, j)
                        for tt, p in enumerate((pa, pb2)):
                            sl = slice(tt * 512, (tt + 1) * 512)
                            if b == 1 and g == "P":
                                ks = kst[(2 * j + tt) % 4]
                                ACT(lambda p=p, ks=ks: nc.scalar.copy(out=ks[:], in_=p[:]), [p], [ks])
                                fw.dma("sp", kT_o[:, j, sl], ks[:], reads=[ks], writes=[OUT])
                                DVE(lambda ks=ks, sl=sl: nc.vector.tensor_copy(out=kT[:, j, sl], in_=ks[:]), [ks], [kT])
                            elif b == 0:
                                ACT(lambda p=p, sl=sl: nc.scalar.mul(out=dstT[:, j, sl], in_=p[:], mul=0.125), [p], [dstT])
                            else:
                                ACT(lambda p=p, sl=sl: nc.scalar.copy(out=dstT[:, j, sl], in_=p[:]), [p], [dstT])
                    pieces.append(piece)
            for tk in range(8):
                def piece(tk=tk):
                    wt = pre_w[2]
                    p = nextps()
                    for kc in range(8):
                        mm(p[:], hT[:, kc, tk * 128:(tk + 1) * 128], wt[:, kc, :], kc == 0, kc == 7, [hT, wt], [p], kc == 7)
                    p3 = p[:].rearrange("p (h d) -> p h d", d=64)
                    if g == "P":
                        ks = kst[tk % 4]
                        ACT(lambda: nc.scalar.copy(out=ks[:], in_=p[:]), [p], [ks])
                        fw.dma("sp", v_o[:, tk, :], ks[:], reads=[ks], writes=[OUT])
                        ACT(lambda: nc.scalar.copy(out=Vaug[:, tk, :, 0:64], in_=ks[:].rearrange("p (h d) -> p h d", d=64)), [ks], [Vaug])
                    else:
                        ACT(lambda: nc.scalar.copy(out=Vaug[:, tk, :, 0:64], in_=p3), [p], [Vaug])
                pieces.append(piece)
            return pieces

        s2b_pieces = []
        early_att = None
        if g == "P":
            curA = [O_A]

            def sba_(name, shape, dt):
                nb = int(np.prod(shape[1:])) * (2 if dt == BF16 else 4)
                t = sb(g + name, shape, dt, curA[0])
                curA[0] += (nb + 63) // 64 * 64
                fw.alias(t, dead_in)
                return t
            qT_ = sba_("qT", [128, 4, NTOK], BF16)
            kT_ = sba_("kT", [128, 4, NTOK], BF16)
            Vaug_ = sba_("Vaug", [128, 8, 8, 65], BF16)
            kst_ = [sba_(f"kst{i}", [128, 512], F32) for i in range(4)]
            POOL(lambda: nc.gpsimd.memset(Vaug_[:, :, :, 64:65], 1.0), [], [Vaug_])
            early_att = (qT_, kT_, Vaug_, kst_)
            s2b_pieces = make_s2b(*early_att)

        def s2b_hook():
            if s2b_pieces:
                s2b_pieces.pop(0)()

        Wt = Wm[L]
        Kt = Ktab[L]

        cm_i = [0]
        cm_t = [(ysa[0], ysb[0], yt1, yt2), (ysa[1], ysb[1], yt3, yt4)]
        if g == "S":
            late["slots"] = [sb("wadaA", [128, 8, 512], BF16, O_C), sb("wadaB", [128, 8, 512], BF16, O_C + 8 * KB)]
            for t in late["slots"]:
                fw.alias(t, dead_in)

        def conv_A(s, o):
            u = vtok[s]
            Y = Ys[s % nY]
            yb0 = 0
            for i in range(Lc):
                if g == "S":
                    mods_late_tick()
                pA, pB = nextps(), nextps()
                for (p, fr) in ((pA, i), (pB, Lc + i)):
                    for tc in range(Lc):
                        mm(p[:], Wt[:, tc, fr * 128:(fr + 1) * 128], u[:, tc, :], tc == 0, tc == Lc - 1, [Wt, u], [p], tc == Lc - 1)
                KR = Kt[:, i, 0, o, :]
                KI = Kt[:, i, 1, o, :]
                cm_i[0] += 1
                tA, tB, tC, tD = cm_t[cm_i[0] % 2]
                DVE(lambda pA=pA, tA=tA: nc.vector.tensor_tensor(out=tA[:], in0=pA[:], in1=KR, op=ALU.mult), [pA, Kt], [tA])
                DVE(lambda pB=pB, tB=tB: nc.vector.tensor_tensor(out=tB[:], in0=pB[:], in1=KI, op=ALU.mult), [pB, Kt], [tB])
                DVE(lambda pA=pA, tC=tC: nc.vector.tensor_tensor(out=tC[:], in0=pA[:], in1=KI, op=ALU.mult), [pA, Kt], [tC])
                DVE(lambda pB=pB, tD=tD: nc.vector.tensor_tensor(out=tD[:], in0=pB[:], in1=KR, op=ALU.mult), [pB, Kt], [tD])
                DVE(lambda i=i, tA=tA, tB=tB: nc.vector.tensor_tensor(out=Y[:, yb0 + i, :], in0=tA[:], in1=tB[:], op=ALU.subtract), [tA, tB], [Y])
                DVE(lambda i=i, tC=tC, tD=tD: nc.vector.tensor_tensor(out=Y[:, yb0 + Lc + i, :], in0=tC[:], in1=tD[:], op=ALU.add), [tC, tD], [Y])

        def conv_B(s, o, mulT):
            Y = Ys[s % nY]
            yb0 = 0
            Nn = min(L, 512)
            for cc in range(4):
                for th in range(L // Nn):
                    p = nextps()
                    for fr in range(2 * Lc):
                        col = (fr // Lc) * L + th * Nn
                        mm(p[:, 0:Nn], Y[:, yb0 + fr, cc * 128:(cc + 1) * 128], Wt[:, fr % Lc, col:col + Nn], fr == 0, fr == 2 * Lc - 1,
                           [Y, Wt], [p], fr == 2 * Lc - 1)
                    t0 = s * L + th * Nn
                    DVE(lambda p=p, cc=cc, t0=t0: nc.vector.tensor_tensor(out=mulT[:, cc, t0:t0 + Nn], in0=p[:, 0:Nn],
                                                                          in1=mulT[:, cc, t0:t0 + Nn], op=ALU.mult), [p, mulT], [mulT])

        def run_convs(o, mulT):
            conv_A(0, o)
            s2b_hook()
            for s_ in range(nseq):
                if s_ + 1 < nseq:
                    conv_A(s_ + 1, o)
                    s2b_hook()
                conv_B(s_, o, mulT)
                s2b_hook()

        run_convs(0, x1T)
        for cc in range(4):
            transposes_to_tok(x1T, lambda tk, cc=cc: x1T[:, cc, tk * 128:(tk + 1) * 128], vtok, cc * 128)
        run_convs(1, x2T)
        if g == "S":
            mods_final(1)
            fw.alias(mrg, late["slots"])
        for t in sq:
            fw.alias(t, Ys + xstg)
        fw.alias(rstd, ysa)
        pss = [nextps(), nextps()]
        for cc in range(4):
            s_ = sq[cc % 2]
            ACT(lambda s_=s_, cc=cc: nc.scalar.activation(out=s_[:], in_=x2T[:, cc, :], func=AF.Square), [x2T], [s_])
            for tt in range(2):
                mm(pss[tt][:], ones[:], s_[:, tt * 512:(tt + 1) * 512], cc == 0, cc == 3, [ones, s_], [pss[tt]], True)
        rstd_from(pss, rstd, 512)
        for cc in range(4):
            if g == "S":
                DVE(lambda cc=cc: nc.vector.scalar_tensor_tensor(out=x2T[:, cc, :], in0=x2T[:, cc, :], scalar=gains[:, 28 + cc:29 + cc],
                                                                 in1=rstd[:], op0=ALU.mult, op1=ALU.mult), [x2T, gains, rstd], [x2T])
                for tk in range(8):
                    fw.op("pe", lambda tk=tk, cc=cc: nc.tensor.transpose(psb[:, tk * 128:(tk + 1) * 128], x2T[:, cc, tk * 128:(tk + 1) * 128], ident[:]),
                          reads=[x2T, ident], writes=[psb], inc=(tk == 7))
                ACT(lambda cc=cc: nc.scalar.copy(out=mrg[:, :, 512 + cc * 128:512 + (cc + 1) * 128],
                                                 in_=psb[:, :].rearrange("p (t c) -> p t c", c=128)), [psb], [mrg])
            else:
                DVE(lambda cc=cc: nc.vector.scalar_tensor_tensor(out=mrg[:, 4 + cc, :], in0=x2T[:, cc, :], scalar=gains[:, 28 + cc:29 + cc],
                                                                 in1=rstd[:], op0=ALU.mult, op1=ALU.mult), [x2T, gains, rstd], [mrg])
        dead_h = [x1T, x2T, ystg, pstg, vT, vT2, rstd, yt1, yt2, yt3, yt4] + Ys + vtok + ysa + ysb + xstg + sq
        if g == "S":
            dead_h += [Wm[1024], Ktab[1024]]

        cur = [O_E]

        def sbe(name, shape, dt):
            nb = int(np.prod(shape[1:])) * (2 if dt == BF16 else 4)
            t = sb(g + name, shape, dt, cur[0])
            cur[0] += (nb + 63) // 64 * 64
            return t
        if early_att is not None:
            qT, kT, Vaug, kst = early_att
        else:
            qT = sbe("qT", [128, 4, NTOK], BF16)
            kT = sbe("kT", [128, 4, NTOK], BF16)
            Vaug = sbe("Vaug", [128, 8, 8, 65], BF16)
            kst = []
        ckT = sbe("ckT", [128, 4, 256], BF16)
        cVaug = sbe("cVaug", [128, 2, 8, 65], BF16)
        PT = [sbe(f"PT{i}", [128, 896], BF16) for i in range(2)]
        atok = sbe("atok", [128, 512], F32)
        an = sbe("an", [128, 512], BF16)
        if g == "S":
            tabs = {"E": sbe("tabE", [128, 8, 1024], BF16), "I": sbe("tabI", [128, 8, 1024], BF16)}
        else:
            PT.append(sbe("PT2", [128, 896], BF16))
            PT.append(sbe("PT3", [128, 896], BF16))
            tabs = {"E": PT[0], "I": PT[0]}
        att_t = [qT, kT, Vaug, ckT, cVaug, atok, an, tabs["E"], tabs["I"]] + PT + kst
        for t in att_t:
            fw.alias(t, dead_h + dead_in)
        if g == "S":
            fw.dma("pool", ckT[:], ckT_d, writes=[ckT])
            fw.dma("pool", cVaug[:, :, :, 0:64], cv_d, writes=[cVaug])
            fw.dma("pool", tabs["E"][:], tabE_d, writes=[tabs["E"]])
            fw.dma("pool", tabs["I"][:], tabI_d, writes=[tabs["I"]])
            POOL(lambda: nc.gpsimd.memset(cVaug[:, :, :, 64:65], 1.0), [], [cVaug])
            for nm in ("E", "I"):
                for hq in range(4):
                    ACT(lambda nm=nm, hq=hq: nc.scalar.activation(out=tabs[nm][:, 2 * hq:2 * hq + 2, :], in_=tabs[nm][:, 2 * hq:2 * hq + 2, :],
                                                                 func=AF.Exp), [tabs[nm]], [tabs[nm]])
        if early_att is None:
            POOL(lambda: nc.gpsimd.memset(Vaug[:, :, :, 64:65], 1.0), [], [Vaug])
            s2b_pieces = make_s2b(qT, kT, Vaug, kst)
        while s2b_pieces:
            s2b_pieces.pop(0)()

        gb = sb(g + "gb", [128, 512], F32, O_B)
        fw.alias(gb, [hT] + dead_in)
        if g == "S":
            fw.dma("sp", gb[:], gbrow_d.partition_broadcast(128), writes=[gb])
        rot["banks"] = [2, 3, 4, 5]
        rot["i"] = 0
        kp_of = {0: [0, 1, 2, 3], 1: [0, 1, 2, 3], 2: [0, 1, 2, 3, 4], 3: [1, 2, 3, 4, 5], 4: [2, 3, 4, 5, 6],
                 5: [3, 4, 5, 6, 7], 6: [4, 5, 6, 7], 7: [4, 5, 6, 7]}
        SC = 1.0
        O = [psf[0], psf[1]]
        Osb = sbe("Osb", [128, 520], F32)
        fw.alias(Osb, dead_h + dead_in)
        if g == "P":
            jobs = [(tk, h) for tk in range(8) for h in (0, 1, 4, 5)]
        else:
            jobs = [(tk, h) for tk in range(8) for h in range(8)]
        st = {}

        def slots_of(tk):
            if g == "P":
                s_ = tk // 2
                return [("k", 2 * s_, 0), ("k", 2 * s_ + 1, 0), ("k", 2 * s_, 2), ("k", 2 * s_ + 1, 2)]
            return [("b", kt, 0) for kt in reversed(kp_of[tk])] + [("c", 0, 0), ("c", 1, 0)]

        def emit_scores(ji):
            tk, h = jobs[ji]
            slots = slots_of(tk)
            ns = len(slots)
            pA = nextps()
            pB = nextps() if ns > 4 else None
            for si, (kind, kt, dh_) in enumerate(slots):
                hh_ = h + dh_
                c, pb_ = hh_ // 2, 64 * (hh_ % 2)
                q_ap = qT[pb_:pb_ + 64, c, tk * 128:(tk + 1) * 128]
                p = pA if si < 4 else pB
                o_ap = p[:, (si % 4) * 128:(si % 4) * 128 + 128]
                if kind == "c":
                    mm(o_ap, ckT[pb_:pb_ + 64, c, kt * 128:(kt + 1) * 128], q_ap, True, True, [ckT, qT], [p], True)
                else:
                    k_ap = kT[pb_:pb_ + 64, c, kt * 128:(kt + 1) * 128]
                    mm(o_ap, k_ap, q_ap, True, True, [kT, qT], [p], True)
            pt = PT[ji % len(PT)]
            n1 = min(ns, 4)
            ACT(lambda: nc.scalar.activation(out=pt[:, 0:n1 * 128], in_=pA[:, 0:n1 * 128], func=AF.Exp, scale=SC), [pA], [pt])
            if ns > 4:
                ACT(lambda: nc.scalar.activation(out=pt[:, 512:ns * 128], in_=pB[:, 0:(ns - 4) * 128], func=AF.Exp, scale=SC), [pB], [pt])
            if g == "S":
                nb = ns - 2
                tab = tabs["E"] if tk in (0, 1, 6, 7) else tabs["I"]
                kt0 = slots[0][1]
                i0 = 7 - (2 * kt0 - 2 * tk)
                DVE(lambda: nc.vector.tensor_tensor(out=pt[:, 0:nb * 128], in0=pt[:, 0:nb * 128],
                                                    in1=tab[:, h, i0 * 64:i0 * 64 + nb * 128], op=ALU.mult), [pt, tab], [pt])
            st[ji] = (pt, slots)

        def emit_pv(ji):
            tk, h = jobs[ji]
            pt, slots = st.pop(ji)
            ns = len(slots)
            for dh_ in sorted(set(sl_[2] for sl_ in slots)):
                hh_ = h + dh_
                ob = O[hh_ // 4]
                o_ap = ob[:, (hh_ % 4) * 65:(hh_ % 4) * 65 + 65]
                idx = [si for si, sl_ in enumerate(slots) if sl_[2] == dh_]
                for n_, si in enumerate(idx):
                    kind, kt, _ = slots[si]
                    if kind == "c":
                        v_ap, vt = cVaug[:, kt, hh_, :], cVaug
                    else:
                        v_ap, vt = Vaug[:, kt, hh_, :], Vaug
                    mm(o_ap, pt[:, si * 128:(si + 1) * 128], v_ap, n_ == 0, n_ == len(idx) - 1, [pt, vt], [ob], n_ == len(idx) - 1)

        def emit_tail_a(tk):
            for hb in range(2):
                ACT(lambda hb=hb: nc.scalar.copy(out=Osb[:, hb * 260:(hb + 1) * 260], in_=O[hb][:, 0:260]), [O[hb]], [Osb])
            o3 = Osb[:].rearrange("p (h d) -> p h d", d=65)
            DVE(lambda: nc.vector.reciprocal(out=smalls[:, 0:8], in_=o3[:, :, 64]), [Osb], [smalls])
            for hh in range(8):
                DVE(lambda hh=hh: nc.vector.tensor_scalar(out=atok[:, hh * 64:(hh + 1) * 64], in0=o3[:, hh, 0:64],
                                                          scalar1=smalls[:, hh:hh + 1], scalar2=None, op0=ALU.mult),
                    [Osb, smalls], [atok])

        def emit_tail_a2(tk):
            ACT(lambda: nc.scalar.activation(out=an[:], in_=atok[:], func=AF.Square, accum_out=smalls[:, 8:9]), [atok], [an, smalls])
            ACT(lambda: nc.scalar.activation(out=smalls[:, 9:10], in_=smalls[:, 8:9], func=AF.Ln, bias=epsc[:, 0:1], scale=float(1.0 / 512)), [smalls, epsc], [smalls])
            ACT(lambda: nc.scalar.activation(out=smalls[:, 10:11], in_=smalls[:, 9:10], func=AF.Exp, scale=-0.5), [smalls], [smalls])
            if g == "S":
                DVE(lambda: nc.vector.scalar_tensor_tensor(out=mrg[:, tk, 0:512], in0=atok[:], scalar=smalls[:, 10:11], in1=gb[:],
                                                           op0=ALU.mult, op1=ALU.mult), [atok, smalls, gb], [mrg])
            else:
                DVE(lambda: nc.vector.tensor_scalar(out=an[:], in0=atok[:], scalar1=smalls[:, 10:11], scalar2=None, op0=ALU.mult), [atok, smalls], [an])

        def emit_tail_b(tk):
            if g == "S":
                return
            for c4 in range(4):
                fw.op("pe", lambda c4=c4: nc.tensor.transpose(psb[:, c4 * 128:(c4 + 1) * 128], an[:, c4 * 128:(c4 + 1) * 128], ident[:]),
                      reads=[an, ident], writes=[psb], inc=(c4 == 3))
            for c4 in range(4):
                DVE(lambda c4=c4: nc.vector.tensor_scalar(out=mrg[:, c4, tk * 128:(tk + 1) * 128], in0=psb[:, c4 * 128:(c4 + 1) * 128],
                                                          scalar1=gains[:, 24 + c4:25 + c4], scalar2=None, op0=ALU.mult), [psb, gains], [mrg])

        Dp = len(PT) - 1
        sched = []
        nj = len(jobs)
        for ji in range(min(Dp, nj)):
            emit_scores(ji)
        for ji in range(nj):
            if ji + Dp < nj:
                emit_scores(ji + Dp)
            emit_pv(ji)
            tk, h = jobs[ji]
            while sched and sched[0][0] <= ji:
                sched.pop(0)[1]()
            if ji + 1 == nj or jobs[ji + 1][0] != tk:
                emit_tail_a(tk)
                sched.append((ji + 2, lambda tk=tk: emit_tail_a2(tk)))
                sched.append((ji + 4, lambda tk=tk: emit_tail_b(tk)))
        while sched:
            sched.pop(0)[1]()
        rot["banks"] = [0, 1, 2, 3, 4, 5]
        rot["i"] = 0
        dead_a = att_t + [hT, Osb]

        tap(g + "mrg", mrg, mrg[:], [128, 8, NTOK], BF16)

        if g == "S":
            yield from s_tail(dead_a + dead_h + dead_in + [gb], mrg)
            return
        xresc = [sb(g + f"xres{c}", [128, NTOK], F32, O_A + c * 4 * KB) for c in range(8)]
        fstg = [sb(g + f"fstg{i}", [128, NTOK], F32, O_E + 44 * KB + i * 4 * KB) for i in range(4)]
        rstd2 = sb(g + "rstd2", [128, NTOK], F32, O_E + 60 * KB)
        sq2 = [sb(g + f"sq2{i}", [128, NTOK], BF16, O_E + 64 * KB + i * 2 * KB) for i in range(2)]
        actT = sb(g + "actT", [128, 22, NTOK], BF16, O_E)
        hT2 = sb(g + "hT2", [128, 8, NTOK], BF16, O_B)
        for t in xresc + [rstd2, actT, hT2] + fstg + sq2:
            fw.alias(t, dead_a + dead_h + dead_in)
        def stats_chunk(pss_, c):
            s_ = sq2[c % 2]
            ACT(lambda: nc.scalar.activation(out=s_[:], in_=xresc[c][:], func=AF.Square), [xresc[c]], [s_])
            for tt in range(2):
                mm(pss_[tt][:], ones[:], s_[:, tt * 512:(tt + 1) * 512], c == 0, c == 7, [ones, s_], [pss_[tt]], True)

        pss6 = [psf[6], psf[5]]
        rot["banks"] = [0, 1, 2, 3, 4]
        rot["i"] = 0
        for b in range(2):
            wt = next_wbuf()
            fw.dma("pool", wt[:], wout_d[:, :, b * 512:(b + 1) * 512], writes=[wt])
            for j in range(4):
                cj = b * 4 + j
                xs = fstg[cj % 2]
                fw.dma("sp", xs[:], x_d[:, cj, :], writes=[xs])
                for tt in range(2):
                    p = nextps()
                    sl = slice(tt * 512, (tt + 1) * 512)
                    for kc in range(8):
                        mm(p[:], wt[:, kc, j * 128:(j + 1) * 128], mrg[:, kc, sl], kc == 0, kc == 7, [wt, mrg], [p], kc == 7)
                    DVE(lambda p=p, cj=cj, sl=sl, xs=xs: nc.vector.scalar_tensor_tensor(
                        out=xresc[cj][:, sl], in0=p[:], scalar=modcol(sset, 2, cj), in1=xs[:, sl], op0=ALU.mult, op1=ALU.add),
                        [p, mods, xs], [xresc[cj]])
                if cj >= 1:
                    stats_chunk(pss6, cj - 1)
        stats_chunk(pss6, 7)

        rot["banks"] = [0, 1, 2, 3, 4, 5]
        rot["i"] = 0
        rstd_from(pss6, rstd2, D)
        for c in range(8):
            ft = fstg[c % 2]
            DVE(lambda c=c, ft=ft: nc.vector.tensor_tensor(out=ft[:], in0=xresc[c][:], in1=rstd2[:], op=ALU.mult), [xresc[c], rstd2], [ft])
            ACT(lambda c=c, ft=ft: nc.scalar.activation(out=hT2[:, c, :], in_=ft[:], func=AF.Identity,
                                                        bias=modcol(sset, 3, c), scale=gm[:, sset, 1, c:c + 1]), [ft, mods, gm], [hT2])

        items = []
        wts = {}
        for blk in range(11):
            for i in range(2):
                jj = 2 * blk + i
                for which, jcol in ((0, i), (1, 2 + i)):
                    fcx = which * 22 + jj
                    sy = fstg[2 * which + (jj % 2)]
                    stt = {}

                    def A(blk=blk, i=i, which=which, jcol=jcol, fcx=fcx, sy=sy, stt=stt):
                        if i == 0 and which == 0:
                            wt = next_wbuf()
                            fw.dma("pool", wt[:], wup_d[:, :, blk * 512:(blk + 1) * 512], writes=[wt])
                            wts[blk] = wt
                        wt = wts[blk]
                        pa, pb2, big = nextpair()
                        for tt, p in enumerate((pa, pb2)):
                            for kc in range(8):
                                mm(p[:], wt[:, kc, jcol * 128:(jcol + 1) * 128], hT2[:, kc, tt * 512:(tt + 1) * 512], kc == 0, kc == 7, [wt, hT2], [p], kc == 7)
                        stt["p"] = (pa, pb2, big)
                        dw_A(pa, pb2, big, ffc[:, fcx, :], sy)

                    def B(fcx=fcx, sy=sy, stt=stt):
                        dw_B(*stt["p"], ffc[:, fcx, :], sy)

                    def C(which=which, jj=jj, sy=sy):
                        if which == 0:
                            ACT(lambda: nc.scalar.activation(out=sy[:], in_=sy[:], func=AF.Silu), [sy], [sy])
                        else:
                            gs = fstg[jj % 2]
                            DVE(lambda: nc.vector.tensor_tensor(out=actT[:, jj, :], in0=gs[:], in1=sy[:], op=ALU.mult), [gs, sy], [actT])
                    items.append([A, B, C])
        run_pipeline(items)

        tap(g + "actT", actT, actT[:], [128, 22, NTOK], BF16)
        pss9 = [psf[6], psf[5]]
        rot["banks"] = [0, 1, 2, 3, 4]
        rot["i"] = 0
        for cj in range(8):
            wt = next_wbuf()
            wflat = wt[:].rearrange("p a b -> p (a b)")
            fw.dma("pool", wflat[:, 0:22 * 128], wdn_d[:, cj, :], writes=[wt])
            for tt in range(2):
                p = nextps()
                sl = slice(tt * 512, (tt + 1) * 512)
                for kk in range(22):
                    mm(p[:], wflat[:, kk * 128:(kk + 1) * 128], actT[:, kk, sl], kk == 0, kk == 21, [wt, actT], [p], kk == 21)
                DVE(lambda p=p, cj=cj, sl=sl: nc.vector.scalar_tensor_tensor(
                    out=xresc[cj][:, sl], in0=p[:], scalar=modcol(sset, 5, cj), in1=xresc[cj][:, sl], op0=ALU.mult, op1=ALU.add),
                    [p, mods, xresc[cj]], [xresc[cj]])
            if cj >= 1:
                stats_chunk(pss9, cj - 1)

        stats_chunk(pss9, 7)
        if g == "S":
            live["S"] = [actT, hT2, mrg] + dead_a + dead_h + dead_in
            prefetch_x("P", live["S"])
            yield "pre_S9"
        rstd_from(pss9, rstd2, D)
        rot["banks"] = [0, 1, 2, 3, 4, 5]
        rot["i"] = 0
        for c in range(8):
            ft = fstg[c % 4]
            DVE(lambda c=c, ft=ft: nc.vector.scalar_tensor_tensor(out=ft[:], in0=xresc[c][:], scalar=gains[:, 16 + c:17 + c],
                                                                 in1=rstd2[:], op0=ALU.mult, op1=ALU.mult), [xresc[c], gains, rstd2], [ft])
            fw.dma("sp" if c % 2 == 0 else "pool", yT_o[g][:, c, :], ft[:], reads=[ft], writes=[OUT])
        final_dead[g] = xresc + [rstd2, actT, hT2, mrg] + fstg + sq2 + dead_a + dead_h

    for _ in range(3):
        mods_early_issue()
    dead = filter_phase(1024, [])
    dead = dead + filter_phase(256, dead)
    fw.alias(Wm[1024], dead)
    fw.dma("sp", Wm[256][:], W_d[256], writes=[Wm[256]])
    fw.dma("sp", Wm[1024][:], W_d[1024], writes=[Wm[1024]])
    mods_final(0)
    gS = group("S", dead)
    assert next(gS) == "post_S1"
    assert next(gS) == "pre_S9"
    gP = group("P", dead + live["S"])
    assert next(gP) == "post_S1"
    for _ in gS:
        pass
    for _ in gP:
        pass
    fw.finish()
    return nc


_CACHE = {}


def _fm(v, nch):
    return np.ascontiguousarray(v.reshape(nch, 128).T)


def _wk(w):
    K, N = w.shape
    return np.ascontiguousarray(w.reshape(K // 128, 128, N).transpose(1, 0, 2))


def _consts():
    if "c" in _CACHE:
        return _CACHE["c"]
    c = {}
    for L in (1024, 256):
        t = np.arange(L, dtype=np.float64)
        f = np.arange(L, dtype=np.float64)
        om = np.pi * (f + 0.5) / L
        Cs = np.cos(np.outer(t + 0.5, om))
        Ss = np.sin(np.outer(t + 0.5, om))
        W = np.concatenate([Cs, -Ss], 1)
        Wu = np.concatenate([np.cos(np.outer(t, om)), np.sin(np.outer(t, om))], 1) / L
        c[f"W{L}"] = _wk(W.astype(np.float32)).astype(ml_dtypes.bfloat16)
        c[f"Wu{L}"] = _wk(Wu.astype(np.float32)).astype(ml_dtypes.bfloat16)
        t32 = np.arange(L, dtype=np.float32) / np.float32(L)
        fr = np.arange(1, 9, dtype=np.float32)
        ang = np.float32(2.0 * np.pi) * t32[:, None] * fr[None, :]
        z = np.concatenate([t32[:, None], np.cos(ang), np.sin(ang)], -1).astype(np.float32)
        zT = np.concatenate([z.T, np.ones((1, L), np.float32)], 0)
        c[f"zT{L}"] = np.ascontiguousarray(zT)
        min_decay = np.log(1e-2) / 1.5
        max_decay = np.log(1e-2) / 0.3
        deltas = np.abs(np.linspace(min_decay, max_decay, 512, dtype=np.float32))
        decay = np.exp(-t32[:, None] * deltas[None, :]).astype(np.float32)
        decayB = decay.copy()
        decayB[0] = 0.0
        dd = np.concatenate([decay, decayB], 1)
        c[f"dec{L}"] = _wk(dd)
    c["ident"] = np.eye(128, dtype=np.float32).astype(ml_dtypes.bfloat16)
    e0 = np.zeros((128, 1), np.float32)
    e0[0, 0] = -1.0
    c["e0n"] = e0
    kc = np.arange(64)[:, None]
    qc = np.arange(64)[None, :]
    cstart = np.clip(qc - 8, 0, 48)
    colin = (kc >= cstart) & (kc < cstart + 16)
    dc = np.clip(kc - qc, -15, 15) + 15
    idxE = np.zeros((128, 16, 64), np.int64)
    idxI = np.zeros((128, 16, 64), np.int64)
    SENT = 15 * 31
    for half in range(2):
        for i in range(16):
            dl = (7 - i) if half == 0 else (8 - i)
            for tab, ok in ((idxE, abs(dl) <= 7), (idxI, -4 <= dl <= 3)):
                if ok:
                    tab[half * 64:(half + 1) * 64, i, :] = np.where(colin, (dl + 7) * 31 + dc, SENT)
                else:
                    tab[half * 64:(half + 1) * 64, i, :] = SENT
    c["idxE"], c["idxI"] = idxE, idxI
    _CACHE["c"] = c
    return c


def _prepare(x_prompt, x_sample, cache_ctx_k, cache_ctx_v, c, c_ctx, w_ada, b_ada, norm1_g,
           w_in, rpb, hy_conv_w, hy_conv_b, filt_w1, filt_b1, filt_w2, filt_b2, filt_w3,
           filt_b3, filt_freq, filt_bias, grp_norm_g, w_out, norm2_g, w_up, ffn_conv_w,
           ffn_conv_b, w_down, final_g):
    f32 = np.float32
    A = lambda a: np.asarray(a, dtype=f32)
    cs = _consts()
    x_prompt, x_sample = A(x_prompt), A(x_sample)
    shared = {}
    shared["w_ada"] = _wk(A(w_ada)[0])
    shared["b_ada"] = _fm(A(b_ada)[0], 48)
    shared["gains"] = np.concatenate([_fm(A(norm1_g)[0], 8), _fm(A(norm2_g)[0], 8), _fm(A(final_g), 8), _fm(A(grp_norm_g)[0], 8)], 1)
    shared["w_in"] = _wk(A(w_in)[0])
    hw, hb = A(hy_conv_w)[0], A(hy_conv_b)[0]
    shared["hyc"] = np.ascontiguousarray(np.stack([_fm(hw[0], 12), _fm(hw[1], 12), _fm(hw[2], 12), _fm(hb, 12)], -1))
    fwc, fbc = A(ffn_conv_w)[0], A(ffn_conv_b)[0]
    shared["ffc"] = np.ascontiguousarray(np.stack([_fm(fwc[0], 44), _fm(fwc[1], 44), _fm(fwc[2], 44), _fm(fbc, 44)], -1))
    shared["w_out"] = _wk(A(w_out)[0])
    wu = A(w_up)[0]
    cols = []
    for blk in range(11):
        for i in range(2):
            cols.append(np.arange((2 * blk + i) * 128, (2 * blk + i + 1) * 128))
        for i in range(2):
            cols.append(DFF + np.arange((2 * blk + i) * 128, (2 * blk + i + 1) * 128))
    shared["w_up"] = _wk(np.ascontiguousarray(wu[:, np.concatenate(cols)]))
    wd = A(w_down)[0]
    wd4 = wd.reshape(22, 128, 8, 128)
    shared["w_down"] = np.ascontiguousarray(wd4.transpose(1, 2, 0, 3).reshape(128, 8, 22 * 128))
    rp = np.concatenate([A(rpb)[0].reshape(8, 15 * 31), np.full((8, 1), NEGM, f32)], 1)
    shared["tabE"] = np.ascontiguousarray(rp[:, cs["idxE"]].transpose(1, 0, 2, 3).reshape(128, 8, 1024))
    shared["tabI"] = np.ascontiguousarray(rp[:, cs["idxI"]].transpose(1, 0, 2, 3).reshape(128, 8, 1024))
    shared["w1b1"] = np.ascontiguousarray(np.concatenate([A(filt_w1)[0], A(filt_b1)[0][None, :]], 0))
    shared["w2"] = np.ascontiguousarray(A(filt_w2)[0])
    shared["fsm"] = np.ascontiguousarray(np.stack([A(filt_b2)[0], A(filt_freq)[0], np.zeros(64, f32), np.zeros(64, f32)], 1))
    shared["w3b3"] = np.ascontiguousarray(np.concatenate([A(filt_w3)[0], A(filt_b3)[0][None, :]], 0))
    shared["fb"] = np.ascontiguousarray(A(filt_bias)[0].reshape(1, 1024))
    shared["gbrow"] = np.ascontiguousarray(A(grp_norm_g)[0][None, 0:512])
    for k in ("zT1024", "zT256", "dec1024", "dec256", "W1024", "W256", "Wu1024", "Wu256", "ident", "e0n"):
        shared[k] = cs[k]
    ck, cv = A(cache_ctx_k), A(cache_ctx_v)
    cc_, cctx = A(c), A(c_ctx)
    in_maps = []
    for core in range(8):
        b = core % 4
        m = dict(shared)
        m["xsT"] = np.ascontiguousarray(x_sample[b].T.reshape(8, 128, NTOK).transpose(1, 0, 2))
        xp = x_prompt[core * 4:(core + 1) * 4].reshape(NTOK, D)
        m["xpT"] = np.ascontiguousarray(xp.T.reshape(8, 128, NTOK).transpose(1, 0, 2))
        m["csil"] = np.ascontiguousarray(np.stack([_fm(cctx, 8), _fm(cc_[b], 8)], -1))
        off = 512 * (core // 4)
        xpad = np.zeros((NTOK + 2, D), f32)
        xpad[1:NTOK + 1] = x_sample[b]
        m["xo"] = np.ascontiguousarray(xpad[off:off + 514].T.reshape(8, 128, 514).transpose(1, 0, 2))
        selm = np.zeros((NTOK, 514), f32)
        ii = np.arange(514)
        tt = off - 1 + ii
        ok = (tt >= 0) & (tt < NTOK)
        selm[tt[ok], ii[ok]] = 1.0
        m["sel"] = np.ascontiguousarray(selm.reshape(8, 128, 514).transpose(1, 0, 2)).astype(ml_dtypes.bfloat16)
        m["mrow"] = np.ascontiguousarray(ok.astype(f32)[None, :])
        kk = ck[b, 0]
        m["ckT"] = np.ascontiguousarray(kk.reshape(4, 2, 256, 64).transpose(1, 3, 0, 2).reshape(128, 4, 256))
        vv = cv[b, 0]
        m["cv"] = np.ascontiguousarray(vv.reshape(8, 2, 128, 64).transpose(2, 1, 0, 3))
        in_maps.append(m)
    return in_maps


def _assemble(R):
    f32 = np.float32
    y_prompt = np.zeros((32, 256, D), f32)
    y_sample = np.zeros((4, 1024, D), f32)
    sk = np.zeros((32, 1, 8, 256, 64), f32)
    sv = np.zeros((32, 1, 8, 256, 64), f32)
    for core in range(8):
        r = R[core]
        yp = np.asarray(r["ypT"]).transpose(1, 0, 2).reshape(D, NTOK).T
        y_prompt[core * 4:(core + 1) * 4] = yp.reshape(4, 256, D)
        off = 512 * (core // 4)
        y_sample[core % 4, off:off + 512] = np.asarray(r["ysT"]).transpose(1, 0, 2).reshape(D, 512).T
        kTo = np.asarray(r["kT_o"]).transpose(1, 0, 2).reshape(512, NTOK)
        sk[core * 4:(core + 1) * 4, 0] = kTo.reshape(8, 64, 4, 256).transpose(2, 0, 3, 1)
        vo = np.asarray(r["v_o"]).reshape(128, 8, 8, 64)
        vo = vo.transpose(1, 0, 2, 3).reshape(4, 256, 8, 64).transpose(0, 2, 1, 3)
        sv[core * 4:(core + 1) * 4, 0] = vo
    return (y_prompt, y_sample, sk, sv)


def kernel(**inputs):
    in_maps = _prepare(**inputs)
    if "nc" not in _CACHE:
        _CACHE["nc"] = build_program()
    res = run_bass_kernel_spmd(_CACHE["nc"], in_maps, core_ids=list(range(8)))
    return _assemble(res.results)
```

```python
import numpy as np
import ml_dtypes
import concourse.bass as bass
import concourse.mybir as mybir
from concourse.bass_utils import run_bass_kernel_spmd

F32 = mybir.dt.float32
BF16 = mybir.dt.bfloat16
I32 = mybir.dt.int32
AF = mybir.ActivationFunctionType
ALU = mybir.AluOpType

D = 1024
NTOK = 1024
DFF = 2816
EPS = 1e-6
NEGM = -30000.0
SAME_ENGINE_SYNC = True


class T:
    __slots__ = ("h", "lw", "rd", "name")

    def __init__(self, h, name=""):
        self.h = h
        self.lw = None
        self.rd = {}
        self.name = name

    def __getitem__(self, idx):
        return self.h[idx]


class FW:
    def __init__(self, nc, n_dma_sems=24):
        self.nc = nc
        self.eng = {"pe": nc.tensor, "act": nc.scalar, "dve": nc.vector,
                    "pool": nc.gpsimd, "sp": nc.sync}
        self.sem = {k: nc.alloc_semaphore(name=f"s_{k}") for k in self.eng}
        self.cnt = {k: 0 for k in self.eng}
        self.seen = {k: {} for k in self.eng}
        self.dsems = [nc.alloc_semaphore(name=f"d_{i}") for i in range(n_dma_sems)]
        self.dcnt = [0] * n_dma_sems
        self.dnext = {"sp": 0, "pool": 0}
        self.drange = {"sp": (0, n_dma_sems // 2), "pool": (n_dma_sems // 2, n_dma_sems)}

    def _wait(self, e, dep):
        kind, k, v = dep
        if kind == "e" and k == e and (e == "pe" or not SAME_ENGINE_SYNC):
            return
        key = (kind, k)
        if self.seen[e].get(key, 0) >= v:
            return
        self.seen[e][key] = v
        s = self.sem[k] if kind == "e" else self.dsems[k]
        self.eng[e].wait_ge(s, v)

    def _deps(self, reads, writes):
        deps = []
        for t in reads:
            if t.lw is not None:
                deps.append(t.lw)
        for t in writes:
            if t.lw is not None:
                deps.append(t.lw)
            deps.extend((k[0], k[1], v) for k, v in t.rd.items())
        return deps

    def op(self, e, fn, reads=(), writes=(), inc=True):
        for d in self._deps(reads, writes):
            self._wait(e, d)
        ins = fn()
        if inc:
            self.cnt[e] += 1
            ins.then_inc(self.sem[e], 1)
            me = ("e", e, self.cnt[e])
        else:
            me = ("e", e, self.cnt[e] + 1)
        self._mark(me, reads, writes)
        return ins

    def _mark(self, me, reads, writes):
        key = (me[0], me[1])
        for t in reads:
            if t.rd.get(key, 0) < me[2]:
                t.rd[key] = me[2]
        for t in writes:
            t.lw = me
            t.rd = {}

    def dma(self, q, out, in_, reads=(), writes=(), **kw):
        for d in self._deps(reads, writes):
            self._wait(q, d)
        lo, hi = self.drange[q]
        i = lo + self.dnext[q]
        self.dnext[q] = (self.dnext[q] + 1) % (hi - lo)
        if self.dcnt[i] > 0:
            self._wait(q, ("d", i, self.dcnt[i]))
        self.dcnt[i] += 16
        ins = self.eng[q].dma_start(out=out, in_=in_, **kw)
        ins.then_inc(self.dsems[i], 16)
        me = ("d", i, self.dcnt[i])
        self._mark(me, reads, writes)
        return me

    def alias(self, new, olds):
        for o in olds:
            ds = list((k[0], k[1], v) for k, v in o.rd.items())
            if o.lw is not None:
                ds.append(o.lw)
            for d in ds:
                key = (d[0], d[1])
                if new.rd.get(key, 0) < d[2]:
                    new.rd[key] = d[2]

    def finish(self):
        for k in ("pe", "act", "dve", "pool"):
            if self.cnt[k] > 0:
                self._wait("sp", ("e", k, self.cnt[k]))
        for i, c in enumerate(self.dcnt):
            if c > 0:
                self._wait("sp", ("d", i, c))


KB = 1024


DEBUG = False
TAPS = []


def build_program():
    nc = bass.Bass("TRN2", target_bir_lowering=False)
    fw = FW(nc)
    del TAPS[:]

    def tap(name, t, ap, shape, dt=F32):
        if not DEBUG:
            return
        d = nc.dram_tensor("tap_" + name, list(shape), dt, kind="ExternalOutput").ap()
        fw.dma("sp", d, ap, reads=[t], writes=[T(None)])
        TAPS.append("tap_" + name)

    def din(name, shape, dt=F32):
        return nc.dram_tensor(name, list(shape), dt, kind="ExternalInput").ap()

    def dout(name, shape):
        return nc.dram_tensor(name, list(shape), F32, kind="ExternalOutput").ap()

    xT = {"S": din("xsT", [128, 8, NTOK]), "P": din("xpT", [128, 8, NTOK])}
    csil_d = din("csil", [128, 8, 2])
    wada_d = din("w_ada", [128, 8, 6144])
    bada_d = din("b_ada", [128, 48])
    gains_d = din("gains", [128, 32])
    win_d = din("w_in", [128, 8, 3072])
    hyc_d = din("hyc", [128, 12, 4])
    ffc_d = din("ffc", [128, 44, 4])
    wout_d = din("w_out", [128, 8, 1024])
    wup_d = din("w_up", [128, 8, 5632])
    wdn_d = din("w_down", [128, 8, 22 * 128])
    ckT_d = din("ckT", [128, 4, 256])
    cv_d = din("cv", [128, 2, 8, 64])
    tabE_d = din("tabE", [128, 8, 1024])
    tabI_d = din("tabI", [128, 8, 1024])
    w1b1_d = din("w1b1", [18, 64])
    w2_d = din("w2", [64, 64])
    fsm_d = din("fsm", [64, 4])
    w3b3_d = din("w3b3", [65, 2048])
    fb_d = din("fb", [1, 1024])
    zT_d = {1024: din("zT1024", [18, 1024]), 256: din("zT256", [18, 256])}
    dec_d = {1024: din("dec1024", [128, 8, 1024]), 256: din("dec256", [128, 2, 1024])}
    W_d = {1024: din("W1024", [128, 8, 2048], BF16), 256: din("W256", [128, 2, 512], BF16)}
    Wu_d = {1024: din("Wu1024", [128, 8, 2048], BF16), 256: din("Wu256", [128, 2, 512], BF16)}
    ident_d = din("ident", [128, 128], BF16)
    e0n_d = din("e0n", [128, 1])
    yT_o = {"S": dout("ysT", [128, 8, 512]), "P": dout("ypT", [128, 8, NTOK])}
    xo_d = din("xo", [128, 8, 514])
    sel_d = din("sel", [128, 8, 514], BF16)
    mrow_d = din("mrow", [1, 514])
    gbrow_d = din("gbrow", [1, 512])
    kT_o = dout("kT_o", [128, 4, NTOK])
    v_o = dout("v_o", [128, 8, 512])
    OUT = T(None, "outs")

    base = (nc.sbuf_base + 63) // 64 * 64
    top = nc.sbuf_top

    def sb(name, shape, dt, off):
        nb = int(np.prod(shape[1:])) * (2 if dt == BF16 else 4)
        assert base + off + nb <= top, (name, off, nb, top - base)
        return T(nc.alloc_sbuf_tensor_at(name, list(shape), dt, offset=base + off), name)

    O_SM = 0
    ident = sb("ident", [128, 128], BF16, 0)
    ones = sb("ones", [128, 128], BF16, 256)
    mods = sb("mods", [128, 2, 48], F32, 512)
    gains = sb("gains", [128, 32], F32, 896)
    gm = sb("gm", [128, 2, 2, 8], F32, 1024)
    hyc = sb("hyc", [128, 12, 4], F32, 1152)
    ffc = sb("ffc", [128, 44, 4], F32, 1344)
    csil = sb("csil", [128, 8, 2], F32, 2048)
    csb = sb("csb", [128, 8, 2], BF16, 2112)
    bada = sb("bada", [128, 48], F32, 2176)
    epsc = sb("epsc", [128, 1], F32, 2368)
    e0n = sb("e0n", [128, 1], F32, 3200)
    fsm = sb("fsm", [64, 8], F32, 2400)
    w1b1 = sb("w1b1", [18, 64], F32, 2432)
    w2s = sb("w2s", [64, 64], F32, 2688)
    smalls = sb("smalls", [128, 64], F32, 2944)
    O_W256 = 5 * KB
    O_K256 = 7 * KB
    O_A = 15 * KB
    O_B = 79 * KB
    O_C = 95 * KB
    O_D = 111 * KB
    O_E = 135 * KB
    Wm = {256: sb("W256", [128, 2, 512], BF16, O_W256), 1024: sb("W1024", [128, 8, 2048], BF16, O_A)}
    Ktab = {256: sb("K256", [128, 2, 2, 2, 512], BF16, O_K256),
            1024: sb("K1024", [128, 8, 2, 2, 512], BF16, O_A + 32 * KB)}
    wbuf = [sb(f"wbuf{i}", [128, 8, 512], BF16, O_D + i * 8 * KB) for i in range(3)]
    wb_i = [0]

    pinned = set()

    def next_wbuf():
        while True:
            t = wbuf[wb_i[0] % 3]
            wb_i[0] += 1
            if t.name not in pinned:
                return t

    class HalfView:
        def __init__(self, big, off):
            self.big, self.off = big, off

        def __getitem__(self, idx):
            if not isinstance(idx, tuple):
                idx = (idx, slice(None))
            ps_, cs_ = idx
            a = 0 if cs_.start is None else cs_.start
            b = 512 if cs_.stop is None else cs_.stop
            return self.big[ps_, self.off + a:self.off + b]

    psd = [nc.alloc_psum_tensor(f"psd{i}", [128, 1024], F32) for i in range(3)]
    psf = [T(HalfView(psd[i // 2], 512 * (i % 2)), f"psf{i}") for i in range(6)]
    psf.append(T(nc.alloc_psum_tensor("psf6", [128, 512], F32), "psf6"))
    psb = T(nc.alloc_psum_tensor("psb", [128, 1024], BF16), "psb")
    rot = {"i": 0, "banks": [0, 1, 2, 3, 4, 5]}

    def nextps():
        b = rot["banks"][rot["i"] % len(rot["banks"])]
        rot["i"] += 1
        return psf[b]

    def nextpair():
        if rot["i"] % 2:
            rot["i"] += 1
        b = rot["banks"][rot["i"] % len(rot["banks"])]
        assert b % 2 == 0
        rot["i"] += 2
        return psf[b], psf[b + 1], psd[b // 2]

    ACT = lambda fn, r, w: fw.op("act", fn, reads=r, writes=w)
    DVE = lambda fn, r, w: fw.op("dve", fn, reads=r, writes=w)
    POOL = lambda fn, r, w: fw.op("pool", fn, reads=r, writes=w)

    def mm(ps_ap, lhsT, rhs, start, stop, reads, writes, last):
        return fw.op("pe", lambda: nc.tensor.matmul(ps_ap, lhsT=lhsT, rhs=rhs, start=start, stop=stop),
                     reads=reads, writes=writes, inc=last)

    fw.dma("sp", ident[:], ident_d, writes=[ident])
    fw.dma("sp", e0n[:], e0n_d, writes=[e0n])
    fw.dma("sp", csil[:], csil_d, writes=[csil])
    fw.dma("sp", bada[:], bada_d, writes=[bada])
    fw.dma("sp", gains[:], gains_d, writes=[gains])
    fw.dma("sp", hyc[:], hyc_d, writes=[hyc])
    fw.dma("sp", ffc[:], ffc_d, writes=[ffc])
    fw.dma("sp", fsm[:, 0:4], fsm_d, writes=[fsm])
    fw.dma("sp", w1b1[:], w1b1_d, writes=[w1b1])
    fw.dma("sp", w2s[:], w2_d, writes=[w2s])
    DVE(lambda: nc.vector.memset(ones[:], 1.0), [], [ones])
    DVE(lambda: nc.vector.memset(epsc[:], EPS), [], [epsc])
    ACT(lambda: nc.scalar.activation(out=csb[:], in_=csil[:], func=AF.Silu), [csil], [csb])
    DVE(lambda: nc.vector.tensor_scalar(out=fsm[:, 4:5], in0=fsm[:, 1:2], scalar1=float(1.0 / (2 * np.pi)),
                                        scalar2=None, op0=ALU.mult), [fsm], [fsm])

    pm = psf[6]
    mods_state = {"blk": 0}

    def mods_mm(blk, wt):
        for j in range(4):
            cj = blk * 4 + j
            for kc in range(8):
                mm(pm[:, cj * 2:cj * 2 + 2], wt[:, kc, j * 128:(j + 1) * 128], csb[:, kc, :],
                   kc == 0, kc == 7, [wt, csb], [pm], kc == 7)

    late = {"slots": None, "pend": []}
    early = {"pend": []}

    def mods_early_issue():
        blk = mods_state["blk"]
        if blk < 4:
            wt = next_wbuf()
            fw.dma("pool", wt[:], wada_d[:, :, blk * 512:(blk + 1) * 512], writes=[wt])
            early["pend"].append((blk, wt))
            mods_state["blk"] += 1

    def mods_early_tick():
        if early["pend"]:
            b0, w0 = early["pend"].pop(0)
            mods_mm(b0, w0)
            mods_early_issue()

    def mods_late_tick():
        blk = mods_state["blk"]
        if len(late["pend"]) == 2 or (blk >= 12 and late["pend"]):
            b0, w0 = late["pend"].pop(0)
            mods_mm(b0, w0)
        if blk < 12:
            wt = late["slots"][blk % 2]
            fw.dma("pool", wt[:], wada_d[:, :, blk * 512:(blk + 1) * 512], writes=[wt])
            late["pend"].append((blk, wt))
            mods_state["blk"] += 1

    def mods_tick(limit=12):
        blk = mods_state["blk"]
        if blk >= limit:
            return
        mods_state["blk"] += 1
        wt = next_wbuf()
        fw.dma("pool", wt[:], wada_d[:, :, blk * 512:(blk + 1) * 512], writes=[wt])
        for j in range(4):
            cj = blk * 4 + j
            for kc in range(8):
                mm(pm[:, cj * 2:cj * 2 + 2], wt[:, kc, j * 128:(j + 1) * 128], csb[:, kc, :],
                   kc == 0, kc == 7, [wt, csb], [pm], kc == 7)

    def mods_finish():
        while mods_state["blk"] < 12:
            mods_tick()
    def mods_final(part):
        if part == 0:
            while early["pend"]:
                mods_early_tick()
            while mods_state["blk"] < 4:
                mods_tick()
            c0, c1 = 0, 16
        else:
            while mods_state["blk"] < 12 or late["pend"]:
                if late["slots"] is not None:
                    mods_late_tick()
                else:
                    mods_tick()
            c0, c1 = 16, 48
        pm3 = pm[:, 0:96].rearrange("p (c s) -> p c s", s=2)
        for s in range(2):
            DVE(lambda s=s: nc.vector.tensor_tensor(out=mods[:, s, c0:c1], in0=pm3[:, c0:c1, s], in1=bada[:, c0:c1], op=ALU.add),
                [pm, bada], [mods])
        w = part
        for s in range(2):
            DVE(lambda s=s, w=w: nc.vector.scalar_tensor_tensor(
                out=gm[:, s, w, :], in0=mods[:, s, (8 + 24 * w):(16 + 24 * w)], scalar=1.0,
                in1=gains[:, 8 * w:8 * w + 8], op0=ALU.add, op1=ALU.mult), [mods, gains], [gm])
        if part == 1:
            tap("mods", mods, mods[:], [128, 2, 48])
            tap("gm", gm, gm[:], [128, 2, 2, 8])

    def modcol(s, j, c):
        return mods[:, s, j * 8 + c:j * 8 + c + 1]

    def filter_phase(L, dead_in):
        Lc = L // 128
        nh = max(1, L // 512)
        dec = sb(f"dec{L}", [128, Lc, 1024], F32, O_B)
        fs = sb(f"fs{L}", [128, Lc, 2, 512], BF16, O_A if L == 1024 else O_E + 60 * KB)
        fd = sb(f"fd{L}", [128, Lc, 2, 512], BF16, O_A + 16 * KB if L == 1024 else O_E + 64 * KB)
        yv = sb(f"yv{L}", [64, L], F32, O_E + 32 * KB)
        ti = sb(f"ti{L}", [64, L], I32, O_E + 36 * KB)
        tf = sb(f"tf{L}", [64, L], F32, O_E + 40 * KB)
        h1 = sb(f"h1{L}", [64, L], F32, O_E + 44 * KB)
        h2 = sb(f"h2{L}", [65, L], BF16, O_E + 48 * KB)
        zT = sb(f"zT{L}", [18, L], F32, O_E + 52 * KB)
        fbs = sb(f"fbs{L}", [128, 1024], F32, O_E + 56 * KB)
        w3 = sb(f"w3{L}", [96, 2048], F32, O_C if L == 256 else O_E + 64 * KB)
        w3c = sb(f"w3c{L}", [96, 2048], BF16, O_C + 8 * KB if L == 256 else O_E + 60 * KB)
        w3b = sb(f"w3b{L}", [96, 2, 512], BF16, O_C + 12 * KB if L == 256 else O_E + 50 * KB)
        t1, t2 = w3c, w3b
        for t in [dec, fs, fd, yv, ti, tf, h1, h2, zT, fbs, w3c, w3b, w3]:
            fw.alias(t, dead_in)
        fw.dma("sp", zT[:], zT_d[L], writes=[zT])
        DVE(lambda: nc.vector.memset(w3[64:96, :], 0.0), [], [w3])
        fw.dma("sp", w3[0:65, :], w3b3_d, writes=[w3])
        for o in range(2):
            wf = w3[:, o * 1024:o * 1024 + 512]
            wb_ = w3[:, o * 1024 + 512:o * 1024 + 1024]
            DVE(lambda o=o, wf=wf, wb_=wb_: nc.vector.tensor_tensor(out=w3c[:, o * 1024:o * 1024 + 512], in0=wf, in1=wb_, op=ALU.add), [w3], [w3c])
            DVE(lambda o=o, wf=wf, wb_=wb_: nc.vector.tensor_tensor(out=w3c[:, o * 1024 + 512:o * 1024 + 1024], in0=wb_, in1=wf, op=ALU.subtract), [w3], [w3c])
            DVE(lambda o=o, wb_=wb_: nc.vector.tensor_copy(out=w3b[:, o, :], in_=wb_), [w3], [w3b])
        fw.dma("sp", dec[:], dec_d[L], writes=[dec])
        fw.dma("sp", fbs[:], fb_d.partition_broadcast(128), writes=[fbs])
        if L == 1024:
            prefetch_x("S", [])
        W = min(L, 512)

        def sin_layer(src_ps_list, dst, add_b2):
            for i, p in enumerate(src_ps_list):
                sl = slice(i * W, (i + 1) * W)
                if add_b2:
                    DVE(lambda p=p, sl=sl: nc.vector.tensor_scalar(out=yv[:, sl], in0=p[0:64, 0:W], scalar1=fsm[:, 0:1],
                                                                   scalar2=fsm[:, 4:5], op0=ALU.add, op1=ALU.mult),
                        [p, fsm], [yv])
                    DVE(lambda sl=sl: nc.vector.tensor_scalar(out=yv[:, sl], in0=yv[:, sl], scalar1=64.0, scalar2=None,
                                                              op0=ALU.add), [yv], [yv])
                else:
                    DVE(lambda p=p, sl=sl: nc.vector.tensor_scalar(out=yv[:, sl], in0=p[0:64, 0:W], scalar1=fsm[:, 4:5],
                                                                   scalar2=64.0, op0=ALU.mult, op1=ALU.add),
                        [p, fsm], [yv])
            DVE(lambda: nc.vector.tensor_copy(out=ti[:], in_=yv[:]), [yv], [ti])
            DVE(lambda: nc.vector.tensor_copy(out=tf[:], in_=ti[:]), [ti], [tf])
            DVE(lambda: nc.vector.tensor_tensor(out=yv[:], in0=yv[:], in1=tf[:], op=ALU.subtract), [yv, tf], [yv])
            DVE(lambda: nc.vector.tensor_scalar(out=tf[:], in0=yv[:], scalar1=0.5, scalar2=None, op0=ALU.is_gt), [yv], [tf])
            DVE(lambda: nc.vector.tensor_tensor(out=yv[:], in0=yv[:], in1=tf[:], op=ALU.subtract), [yv, tf], [yv])
            DVE(lambda: nc.vector.tensor_scalar(out=tf[:], in0=yv[:], scalar1=-0.5, scalar2=None, op0=ALU.is_lt), [yv], [tf])
            DVE(lambda: nc.vector.tensor_tensor(out=yv[:], in0=yv[:], in1=tf[:], op=ALU.add), [yv, tf], [yv])
            ACT(lambda: nc.scalar.activation(out=dst[0:64, :], in_=yv[:], func=AF.Sin, scale=float(2 * np.pi)), [yv], [dst])

        pl = []
        for i in range(nh):
            p = nextps()
            mm(p[0:64, 0:W], w1b1[:], zT[:, i * W:(i + 1) * W], True, True, [w1b1, zT], [p], True)
            pl.append(p)
        sin_layer(pl, h1, False)
        pl = []
        for i in range(nh):
            p = nextps()
            mm(p[0:64, 0:W], w2s[:], h1[:, i * W:(i + 1) * W], True, True, [w2s, h1], [p], True)
            pl.append(p)
        sin_layer(pl, h2, True)
        DVE(lambda: nc.vector.memset(h2[64:65, :], 1.0), [], [h2])
        for tc in range(Lc):
            if tc >= 1:
                mods_early_tick()
            pq = [nextps() for _ in range(4)]
            for q in range(4):
                mm(pq[q][:], h2[:, tc * 128:(tc + 1) * 128], w3c[0:65, q * 512:(q + 1) * 512], True, True, [h2, w3c], [pq[q]], True)
            for o in range(2):
                ps_, pd_ = pq[2 * o], pq[2 * o + 1]
                DVE(lambda ps_=ps_, o=o: nc.vector.tensor_tensor(out=fs[:, tc, o, :], in0=ps_[:], in1=dec[:, tc, 0:512], op=ALU.mult), [ps_, dec], [fs])
                DVE(lambda pd_=pd_, o=o: nc.vector.tensor_tensor(out=fd[:, tc, o, :], in0=pd_[:], in1=dec[:, tc, 0:512], op=ALU.mult), [pd_, dec], [fd])
            if tc == 0:
                for o in range(2):
                    pc = nextps()
                    mm(pc[:], h2[:, 0:128], w3b[0:65, o, :], True, True, [h2, w3b], [pc], True)
                    DVE(lambda o=o, pc=pc: nc.vector.scalar_tensor_tensor(out=fs[:, 0, o, :], in0=pc[:], scalar=e0n[:, 0:1], in1=fs[:, 0, o, :],
                                                                         op0=ALU.mult, op1=ALU.add), [fs, pc, e0n], [fs])
                    DVE(lambda o=o, pc=pc: nc.vector.scalar_tensor_tensor(out=fd[:, 0, o, :], in0=pc[:], scalar=e0n[:, 0:1], in1=fd[:, 0, o, :],
                                                                         op0=ALU.mult, op1=ALU.add), [fd, pc, e0n], [fd])
        Kt = Ktab[L]
        nblk = (2 * L) // 512
        for blk in range(nblk):
            wt = next_wbuf()
            fw.dma("sp", wt[:, 0:Lc, :], Wu_d[L][:, :, blk * 512:(blk + 1) * 512], writes=[wt])
            for j in range(4):
                fr = blk * 4 + j
                isI = fr >= Lc
                fc = fr - Lc if isI else fr
                src = fd if isI else fs
                for o in range(2):
                    p = nextps()
                    for tc in range(Lc):
                        mm(p[:], wt[:, tc, j * 128:(j + 1) * 128], src[:, tc, o, :], tc == 0, tc == Lc - 1, [wt, src], [p], tc == Lc - 1)
                    if isI:
                        ACT(lambda p=p, fc=fc, o=o: nc.scalar.copy(out=Kt[:, fc, 1, o, :], in_=p[:]), [p], [Kt])
                    else:
                        DVE(lambda p=p, fc=fc, o=o: nc.vector.scalar_tensor_tensor(
                            out=Kt[:, fc, 0, o, :], in0=fbs[:, o * 512:(o + 1) * 512], scalar=float(1.0 / L), in1=p[:],
                            op0=ALU.mult, op1=ALU.add), [p, fbs], [Kt])
        tap(f"h1_{L}", h1, h1[:], [64, L])
        tap(f"h2_{L}", h2, h2[:], [65, L])
        tap(f"fs_{L}", fs, fs[:], [128, Lc, 2, 512], BF16)
        tap(f"K_{L}", Kt, Kt[:], [128, Lc, 2, 2, 512], BF16)
        return [dec, fs, fd, yv, ti, tf, h1, h2, zT, fbs, t1, t2, w3]


    def rstd_from(ps_list, dst, dcount):
        for i, p in enumerate(ps_list):
            ACT(lambda p=p, i=i: nc.scalar.activation(out=dst[:, i * 512:(i + 1) * 512], in_=p[:], func=AF.Ln,
                                                      bias=epsc[:, 0:1], scale=float(1.0 / dcount)), [p, epsc], [dst])
        ACT(lambda: nc.scalar.activation(out=dst[:], in_=dst[:], func=AF.Exp, scale=-0.5), [dst], [dst])

    prefetched = {}
    final_dead = {}
    live = {}

    def s_tail(dead_all, mtok):
        sset = 1

        def sba(name, shape, dt, off):
            t = sb("So_" + name, shape, dt, off)
            fw.alias(t, dead_all)
            return t
        x1o = [sba(f"x1o{c}", [128, 514], F32, O_A + c * 2112) for c in range(8)]
        x2o = [sba(f"x2o{c}", [128, 512], F32, O_A + 17 * KB + c * 2048) for c in range(8)]
        xst = [sba(f"xst{i}", [128, 514], F32, O_A + 33 * KB + i * 2112) for i in range(2)]
        rso = sba("rso", [128, 514], F32, O_A + 38 * KB)
        sqo = [sba(f"sqo{i}", [128, 514], BF16, O_A + 41 * KB + i * 1088) for i in range(2)]
        yst = [sba(f"yst{i}", [128, 512], F32, O_A + 44 * KB + i * 2048) for i in range(4)]
        mrow = sba("mrow", [128, 514], F32, O_A + 52 * KB)
        tmpf = [sba(f"tmpf{i}", [128, 514], F32, O_A + 55 * KB + i * 2112) for i in range(2)]
        ost = [sba(f"ost{i}", [128, 512], F32, O_A + 60 * KB + i * 2048) for i in range(2)]
        selT = sba("sel", [128, 8, 514], BF16, O_B + 4 * KB)
        h2o = sba("h2o", [128, 8, 514], BF16, O_B + 4 * KB)
        mrgo = sba("mrgo", [128, 8, 514], BF16, O_E)
        actTo = sba("actTo", [128, 22, 512], BF16, O_E + 9 * KB)
        all_t = x1o + x2o + xst + [rso] + sqo + yst + [mrow] + tmpf + ost + [selT, h2o, mrgo, actTo]
        fw.dma("sp", selT[:], sel_d, writes=[selT])
        fw.dma("sp", mrow[:], mrow_d.partition_broadcast(128), writes=[mrow])
        rot["banks"] = [0, 1, 2, 3]
        rot["i"] = 0
        pst = (psf[4], psf[5], psd[2])

        def mm514(big, pa, pb2, lhs_fn, rhs_t, rhs_fn, n, reads):
            for k in range(n):
                mm(big[:, 0:512], lhs_fn(k), rhs_fn(k, 0, 512), k == 0, k == n - 1, reads, [pa], k == n - 1)
            for k in range(n):
                mm(big[:, 512:514], lhs_fn(k), rhs_fn(k, 512, 514), k == 0, k == n - 1, reads, [pb2], k == n - 1)

        for c in range(8):
            pa, pb2, big = nextpair()
            mm514(big, pa, pb2, lambda tk, c=c: mtok[:, tk, c * 128:(c + 1) * 128], selT,
                  lambda tk, a, b: selT[:, tk, a:b], 8, [mtok, selT])
            ACT(lambda c=c, big=big: nc.scalar.copy(out=mrgo[:, c, :], in_=big[:, 0:514]), [pa, pb2], [mrgo])
        fw.alias(h2o, [selT])

        def stats(c, src_t, width):
            s_ = sqo[c % 2]
            ACT(lambda: nc.scalar.activation(out=s_[:, 0:width], in_=src_t[:, 0:width], func=AF.Square), [src_t], [s_])
            mm(pst[2][:, 0:512], ones[:], s_[:, 0:512], c == 0, c == 7, [ones, s_], [pst[0]], True)
            if width > 512:
                mm(pst[2][:, 512:514], ones[:], s_[:, 512:514], c == 0, c == 7, [ones, s_], [pst[1]], True)

        def rstd_o(width):
            ACT(lambda: nc.scalar.activation(out=rso[:, 0:width], in_=pst[2][:, 0:width], func=AF.Ln, bias=epsc[:, 0:1], scale=float(1.0 / D)),
                [pst[0], pst[1], epsc], [rso])
            ACT(lambda: nc.scalar.activation(out=rso[:, 0:width], in_=rso[:, 0:width], func=AF.Exp, scale=-0.5), [rso], [rso])

        for b in range(2):
            wt = next_wbuf()
            fw.dma("pool", wt[:], wout_d[:, :, b * 512:(b + 1) * 512], writes=[wt])
            for j in range(4):
                cj = b * 4 + j
                xs = xst[cj % 2]
                fw.dma("sp", xs[:], xo_d[:, cj, :], writes=[xs])
                pa, pb2, big = nextpair()
                mm514(big, pa, pb2, lambda kc, j=j, wt=wt: wt[:, kc, j * 128:(j + 1) * 128], mrgo,
                      lambda kc, a, b_: mrgo[:, kc, a:b_], 8, [wt, mrgo])
                DVE(lambda cj=cj, big=big, xs=xs: nc.vector.scalar_tensor_tensor(
                    out=x1o[cj][:], in0=big[:, 0:514], scalar=modcol(sset, 2, cj), in1=xs[:], op0=ALU.mult, op1=ALU.add),
                    [pa, pb2, mods, xs], [x1o[cj]])
                if cj >= 1:
                    stats(cj - 1, x1o[cj - 1], 514)
        stats(7, x1o[7], 514)
        rstd_o(514)
        for c in range(8):
            tf_ = tmpf[c % 2]
            DVE(lambda c=c, tf_=tf_: nc.vector.tensor_tensor(out=tf_[:], in0=x1o[c][:], in1=rso[:], op=ALU.mult), [x1o[c], rso], [tf_])
            ACT(lambda c=c, tf_=tf_: nc.scalar.activation(out=tf_[:], in_=tf_[:], func=AF.Identity,
                                                          bias=modcol(sset, 3, c), scale=gm[:, sset, 1, c:c + 1]), [tf_, mods, gm], [tf_])
            DVE(lambda c=c, tf_=tf_: nc.vector.tensor_tensor(out=h2o[:, c, :], in0=tf_[:], in1=mrow[:], op=ALU.mult), [tf_, mrow], [h2o])
        rot["banks"] = [0, 1, 2, 3, 4, 5]
        rot["i"] = 0
        items = []
        wts = {}
        for blk in range(11):
            for i in range(2):
                jj = 2 * blk + i
                for which, jcol in ((0, i), (1, 2 + i)):
                    fcx = which * 22 + jj
                    sy = yst[2 * which + (jj % 2)]
                    stt = {}
                    wc = ffc[:, fcx, :]

                    def A(blk=blk, i=i, which=which, jcol=jcol, sy=sy, stt=stt, wc=wc):
                        if i == 0 and which == 0:
                            wt = next_wbuf()
                            fw.dma("pool", wt[:], wup_d[:, :, blk * 512:(blk + 1) * 512], writes=[wt])
                            wts[blk] = wt
                        wt = wts[blk]
                        pa, pb2, big = nextpair()
                        mm514(big, pa, pb2, lambda kc: wt[:, kc, jcol * 128:(jcol + 1) * 128], h2o,
                              lambda kc, a, b_: h2o[:, kc, a:b_], 8, [wt, h2o])
                        stt["p"] = (pa, pb2, big)
                        ACT(lambda: nc.scalar.activation(out=sy[:], in_=big[:, 1:513], func=AF.Identity, bias=wc[:, 3:4], scale=wc[:, 1:2]),
                            [pa, pb2, ffc], [sy])

                    def B(sy=sy, stt=stt, wc=wc):
                        pa, pb2, big = stt["p"]
                        DVE(lambda: nc.vector.scalar_tensor_tensor(out=sy[:], in0=big[:, 0:512], scalar=wc[:, 0:1], in1=sy[:],
                                                                   op0=ALU.mult, op1=ALU.add), [sy, pa, ffc], [sy])
                        DVE(lambda: nc.vector.scalar_tensor_tensor(out=sy[:], in0=big[:, 2:514], scalar=wc[:, 2:3], in1=sy[:],
                                                                   op0=ALU.mult, op1=ALU.add), [sy, pa, pb2, ffc], [sy])

                    def C(which=which, jj=jj, sy=sy):
                        if which == 0:
                            ACT(lambda: nc.scalar.activation(out=sy[:], in_=sy[:], func=AF.Silu), [sy], [sy])
                        else:
                            gs = yst[jj % 2]
                            DVE(lambda: nc.vector.tensor_tensor(out=actTo[:, jj, :], in0=gs[:], in1=sy[:], op=ALU.mult), [gs, sy], [actTo])
                    items.append([A, B, C])
        n_it = len(items)
        for t_ in range(n_it + 2):
            for k in (2, 1, 0):
                ii = t_ - k
                if 0 <= ii < n_it:
                    items[ii][k]()
        rot["banks"] = [0, 1, 2, 3]
        rot["i"] = 0
        for cj in range(8):
            wt = next_wbuf()
            wflat = wt[:].rearrange("p a b -> p (a b)")
            fw.dma("pool", wflat[:, 0:22 * 128], wdn_d[:, cj, :], writes=[wt])
            p = nextps()
            for kk in range(22):
                mm(p[:], wflat[:, kk * 128:(kk + 1) * 128], actTo[:, kk, :], kk == 0, kk == 21, [wt, actTo], [p], kk == 21)
            DVE(lambda p=p, cj=cj: nc.vector.scalar_tensor_tensor(
                out=x2o[cj][:], in0=p[:], scalar=modcol(sset, 5, cj), in1=x1o[cj][:, 1:513], op0=ALU.mult, op1=ALU.add),
                [p, mods, x1o[cj]], [x2o[cj]])
            if cj >= 1:
                stats(cj - 1, x2o[cj - 1], 512)
        stats(7, x2o[7], 512)
        live["S"] = all_t + dead_all
        prefetch_x("P", live["S"])
        yield "pre_S9"
        rstd_o(512)
        rot["banks"] = [0, 1, 2, 3, 4, 5]
        rot["i"] = 0
        for c in range(8):
            ft = ost[c % 2]
            DVE(lambda c=c, ft=ft: nc.vector.scalar_tensor_tensor(out=ft[:], in0=x2o[c][:], scalar=gains[:, 16 + c:17 + c],
                                                                 in1=rso[:, 0:512], op0=ALU.mult, op1=ALU.mult), [x2o[c], gains, rso], [ft])
            fw.dma("sp", yT_o["S"][:, c, :], ft[:], reads=[ft], writes=[OUT])
        final_dead["S"] = all_t + dead_all


    def prefetch_x(g2, deadl):
        xc = [sb(g2 + f"xc{c}", [128, NTOK], F32, O_E + c * 4 * KB) for c in range(8)]
        for t in xc:
            fw.alias(t, deadl)
        for c in range(8):
            fw.dma("sp", xc[c][:], xT[g2][:, c, :], writes=[xc[c]])
        prefetched[g2] = xc

    def group(g, dead_in):
        L = 1024 if g == "S" else 256
        nseq = NTOK // L
        Lc = L // 128
        sset = 1 if g == "S" else 0
        x_d = xT[g]
        x1T = sb(g + "x1T", [128, 4, NTOK], BF16, O_E)
        x2T = sb(g + "x2T", [128, 4, NTOK], BF16, O_E + 8 * KB)
        vtok = [sb(g + f"vtok{s}", [128, Lc, 512], BF16, O_E + 16 * KB + s * Lc * KB) for s in range(nseq)]
        nY = 16 // (2 * Lc)
        Ys = [sb(g + f"Y{k}", [128, 2 * Lc, 512], BF16, O_E + 24 * KB + k * 2 * Lc * KB) for k in range(nY)]
        Y = Ys[0]
        ysa = [sb(g + f"ysa{i}", [128, 512], F32, O_E + 40 * KB + i * 2 * KB) for i in range(2)]
        ysb = [sb(g + f"ysb{i}", [128, 512], F32, O_E + 44 * KB + i * 2 * KB) for i in range(2)]
        yt1 = sb(g + "yt1", [128, 512], F32, O_E + 48 * KB)
        yt2 = sb(g + "yt2", [128, 512], F32, O_E + 50 * KB)
        yt3 = sb(g + "yt3", [128, 512], F32, O_E + 66 * KB)
        yt4 = sb(g + "yt4", [128, 512], F32, O_E + 70 * KB)
        ystg = sb(g + "ystg", [128, NTOK], F32, O_E + 52 * KB)
        pstg = sb(g + "pstg", [128, NTOK], F32, O_E + 56 * KB)
        vT = sb(g + "vT", [128, NTOK], BF16, O_E + 60 * KB)
        vT2 = sb(g + "vT2", [128, NTOK], BF16, O_E + 68 * KB)
        vTs = [vT, vT2]
        rstd = sb(g + "rstd", [128, NTOK], F32, O_E + 40 * KB)
        xstg = [sb(g + f"xstg{i}", [128, NTOK], F32, O_E + 24 * KB + i * 4 * KB) for i in range(2)]
        sq = [sb(g + f"sq{i}", [128, NTOK], BF16, O_E + 32 * KB + i * 2 * KB) for i in range(2)]
        hT = sb(g + "hT", [128, 8, NTOK], BF16, O_B)
        mrg = sb(g + "mrg", [128, 8, NTOK], BF16, O_C)
        for t in [x1T, x2T, ystg, pstg, vT, vT2, rstd, hT, mrg] + Ys + vtok + ysa + ysb + [yt1, yt2, yt3, yt4] + xstg + sq:
            fw.alias(t, dead_in)

        if g not in prefetched:
            prefetch_x(g, dead_in)
        xc = prefetched[g]
        pss = [nextps(), nextps()]
        for c in range(8):
            s_ = sq[c % 2]
            ACT(lambda c=c, s_=s_: nc.scalar.activation(out=s_[:], in_=xc[c][:], func=AF.Square), [xc[c]], [s_])
            for tt in range(2):
                mm(pss[tt][:], ones[:], s_[:, tt * 512:(tt + 1) * 512], c == 0, c == 7, [ones, s_], [pss[tt]], True)
        rstd_from(pss, rstd, D)
        for c in range(8):
            DVE(lambda c=c: nc.vector.tensor_tensor(out=xc[c][:], in0=xc[c][:], in1=rstd[:], op=ALU.mult), [xc[c], rstd], [xc[c]])
            ACT(lambda c=c: nc.scalar.activation(out=hT[:, c, :], in_=xc[c][:], func=AF.Identity,
                                                 bias=modcol(sset, 0, c), scale=gm[:, sset, 0, c:c + 1]),
                [xc[c], mods, gm], [hT])
        for t in [x1T, x2T] + Ys + vtok + xstg:
            fw.alias(t, xc)
        for t in ysa:
            fw.alias(t, [rstd])
        yield "post_S1"
        if g == "P":
            for t in [x1T, x2T, ystg, pstg, vT, vT2, rstd, hT, mrg, yt1, yt2, yt3, yt4] + Ys + vtok + ysa + ysb + xstg + sq + xc:
                fw.alias(t, final_dead["S"])
            dead_in = dead_in + final_dead["S"]

        tap(g + "rstd", rstd, rstd[:], [128, NTOK])
        tap(g + "hT", hT, hT[:], [128, 8, NTOK], BF16)
        def dw_A(pa, pb2, big, wcols, stg_y):
            ACT(lambda: nc.scalar.activation(out=stg_y[:], in_=big[:, :], func=AF.Identity,
                                             bias=wcols[:, 3:4], scale=wcols[:, 1:2]), [pa, pb2, hyc, ffc], [stg_y])

        def dw_B(pa, pb2, big, wcols, stg_y):
            y3 = stg_y[:].rearrange("p (s l) -> p s l", l=L)
            p3 = big[:, :].rearrange("p (s l) -> p s l", l=L)
            DVE(lambda: nc.vector.scalar_tensor_tensor(out=y3[:, :, 1:L], in0=p3[:, :, 0:L - 1], scalar=wcols[:, 0:1],
                                                       in1=y3[:, :, 1:L], op0=ALU.mult, op1=ALU.add), [stg_y, pa, pb2, hyc, ffc], [stg_y])
            DVE(lambda: nc.vector.scalar_tensor_tensor(out=y3[:, :, 0:L - 1], in0=p3[:, :, 1:L], scalar=wcols[:, 2:3],
                                                       in1=y3[:, :, 0:L - 1], op0=ALU.mult, op1=ALU.add), [stg_y, pa, pb2, hyc, ffc], [stg_y])

        def run_pipeline(items):
            n = len(items)
            K = max(len(it) for it in items)
            for t in range(n + K - 1):
                for k in range(K - 1, -1, -1):
                    i = t - k
                    if 0 <= i < n and k < len(items[i]):
                        items[i][k]()

        def transposes_to_tok(srcT, src_ap_fn, dst_list, col0):
            for tk in range(8):
                fw.op("pe", lambda tk=tk: nc.tensor.transpose(psb[:, tk * 128:(tk + 1) * 128], src_ap_fn(tk), ident[:]),
                      reads=[srcT, ident], writes=[psb], inc=(tk == 7))
            for s in range(nseq):
                ACT(lambda s=s: nc.scalar.copy(out=dst_list[s][:, :, col0:col0 + 128],
                                               in_=psb[:, s * L:(s + 1) * L].rearrange("p (t c) -> p t c", c=128)),
                    [psb], [dst_list[s]])

        def proj_block(wt, j, writes_ps=None):
            pa, pb2, big = nextpair()
            for tt, p in enumerate((pa, pb2)):
                for kc in range(8):
                    mm(p[:], wt[:, kc, j * 128:(j + 1) * 128], hT[:, kc, tt * 512:(tt + 1) * 512], kc == 0, kc == 7, [wt, hT], [p], kc == 7)
            return pa, pb2, big

        items = []
        wts = {}
        for b in (3, 4, 5):
            for j in range(4):
                hc = (b - 3) * 4 + j
                yb = (ystg, pstg)[hc % 2]
                stt = {}

                def A(b=b, j=j, hc=hc, yb=yb, stt=stt):
                    if j == 0:
                        pinned.clear()
                        wt = next_wbuf()
                        fw.dma("pool", wt[:], win_d[:, :, b * 512:(b + 1) * 512], writes=[wt])
                        wts[b] = wt
                        pinned.add(wt.name)
                    stt["p"] = proj_block(wts[b], j)
                    dw_A(*stt["p"], hyc[:, hc, :], yb)

                def B(hc=hc, yb=yb, stt=stt):
                    dw_B(*stt["p"], hyc[:, hc, :], yb)

                def C(b=b, j=j, yb=yb):
                    if b == 3:
                        ACT(lambda: nc.scalar.copy(out=x1T[:, j, :], in_=yb[:]), [yb], [x1T])
                    elif b == 4:
                        ACT(lambda: nc.scalar.copy(out=x2T[:, j, :], in_=yb[:]), [yb], [x2T])
                    else:
                        vTj = vTs[j % 2]
                        ACT(lambda: nc.scalar.copy(out=vTj[:], in_=yb[:]), [yb], [vTj])
                        transposes_to_tok(vTj, lambda tk: vTj[:, tk * 128:(tk + 1) * 128], vtok, j * 128)
                items.append([A, B, C])
        run_pipeline(items)
        pinned.clear()

        tap(g + "x1T", x1T, x1T[:], [128, 4, NTOK], BF16)
        tap(g + "vtok0", vtok[0], vtok[0][:], [128, Lc, 512], BF16)
        pre_w = []
        for b in (0, 1, 2):
            wt = next_wbuf()
            fw.dma("pool", wt[:], win_d[:, :, b * 512:(b + 1) * 512], writes=[wt])
            pre_w.append(wt)
        def make_s2b(qT, kT, Vaug, kst):
            pieces = []
            for b in (0, 1):
                for j in range(4):
                    def piece(b=b, j=j):
                        wt = pre_w[b]
                        dstT = qT if b == 0 else kT
                        pa, pb2, _big = proj_block(wt, j)
                        for tt, p in enumerate((pa, pb2)):
                            sl = slice(tt * 512, (tt + 1) * 512)
                            if b == 1 and g == "P":
                                ks = kst[(2 * j + tt) % 4]
                                ACT(lambda p=p, ks=ks: nc.scalar.copy(out=ks[:], in_=p[:]), [p], [ks])
                                fw.dma("sp", kT_o[:, j, sl], ks[:], reads=[ks], writes=[OUT])
                                DVE(lambda ks=ks, sl=sl: nc.vector.tensor_copy(out=kT[:, j, sl], in_=ks[:]), [ks], [kT])
                            elif b == 0:
                                ACT(lambda p=p, sl=sl: nc.scalar.mul(out=dstT[:, j, sl], in_=p[:], mul=0.125), [p], [dstT])
                            else:
                                ACT(lambda p=p, sl=sl: nc.scalar.copy(out=dstT[:, j, sl], in_=p[:]), [p], [dstT])
                    pieces.append(piece)
            for tk in range(8):
                def piece(tk=tk):
                    wt = pre_w[2]
                    p = nextps()
                    for kc in range(8):
                        mm(p[:], hT[:, kc, tk * 128:(tk + 1) * 128], wt[:, kc, :], kc == 0, kc == 7, [hT, wt], [p], kc == 7)
                    p3 = p[:].rearrange("p (h d) -> p h d", d=64)
                    if g == "P":
                        ks = kst[tk % 4]
                        ACT(lambda: nc.scalar.copy(out=ks[:], in_=p[:]), [p], [ks])
                        fw.dma("sp", v_o[:, tk, :], ks[:], reads=[ks], writes=[OUT])
                        ACT(lambda: nc.scalar.copy(out=Vaug[:, tk, :, 0:64], in_=ks[:].rearrange("p (h d) -> p h d", d=64)), [ks], [Vaug])
                    else:
                        ACT(lambda: nc.scalar.copy(out=Vaug[:, tk, :, 0:64], in_=p3), [p], [Vaug])
                pieces.append(piece)
            return pieces

        s2b_pieces = []
        early_att = None
        if g == "P":
            curA = [O_A]

            def sba_(name, shape, dt):
                nb = int(np.prod(shape[1:])) * (2 if dt == BF16 else 4)
                t = sb(g + name, shape, dt, curA[0])
                curA[0] += (nb + 63) // 64 * 64
                fw.alias(t, dead_in)
                return t
            qT_ = sba_("qT", [128, 4, NTOK], BF16)
            kT_ = sba_("kT", [128, 4, NTOK], BF16)
            Vaug_ = sba_("Vaug", [128, 8, 8, 65], BF16)
            kst_ = [sba_(f"kst{i}", [128, 512], F32) for i in range(4)]
            POOL(lambda: nc.gpsimd.memset(Vaug_[:, :, :, 64:65], 1.0), [], [Vaug_])
            early_att = (qT_, kT_, Vaug_, kst_)
            s2b_pieces = make_s2b(*early_att)

        def s2b_hook():
            if s2b_pieces:
                s2b_pieces.pop(0)()

        Wt = Wm[L]
        Kt = Ktab[L]

        cm_i = [0]
        cm_t = [(ysa[0], ysb[0], yt1, yt2), (ysa[1], ysb[1], yt3, yt4)]
        if g == "S":
            late["slots"] = [sb("wadaA", [128, 8, 512], BF16, O_C), sb("wadaB", [128, 8, 512], BF16, O_C + 8 * KB)]
            for t in late["slots"]:
                fw.alias(t, dead_in)

        def conv_A(s, o):
            u = vtok[s]
            Y = Ys[s % nY]
            yb0 = 0
            for i in range(Lc):
                if g == "S":
                    mods_late_tick()
                pA, pB = nextps(), nextps()
                for (p, fr) in ((pA, i), (pB, Lc + i)):
                    for tc in range(Lc):
                        mm(p[:], Wt[:, tc, fr * 128:(fr + 1) * 128], u[:, tc, :], tc == 0, tc == Lc - 1, [Wt, u], [p], tc == Lc - 1)
                KR = Kt[:, i, 0, o, :]
                KI = Kt[:, i, 1, o, :]
                cm_i[0] += 1
                tA, tB, tC, tD = cm_t[cm_i[0] % 2]
                DVE(lambda pA=pA, tA=tA: nc.vector.tensor_tensor(out=tA[:], in0=pA[:], in1=KR, op=ALU.mult), [pA, Kt], [tA])
                DVE(lambda pB=pB, tB=tB: nc.vector.tensor_tensor(out=tB[:], in0=pB[:], in1=KI, op=ALU.mult), [pB, Kt], [tB])
                DVE(lambda pA=pA, tC=tC: nc.vector.tensor_tensor(out=tC[:], in0=pA[:], in1=KI, op=ALU.mult), [pA, Kt], [tC])
                DVE(lambda pB=pB, tD=tD: nc.vector.tensor_tensor(out=tD[:], in0=pB[:], in1=KR, op=ALU.mult), [pB, Kt], [tD])
                DVE(lambda i=i, tA=tA, tB=tB: nc.vector.tensor_tensor(out=Y[:, yb0 + i, :], in0=tA[:], in1=tB[:], op=ALU.subtract), [tA, tB], [Y])
                DVE(lambda i=i, tC=tC, tD=tD: nc.vector.tensor_tensor(out=Y[:, yb0 + Lc + i, :], in0=tC[:], in1=tD[:], op=ALU.add), [tC, tD], [Y])

        def conv_B(s, o, mulT):
            Y = Ys[s % nY]
            yb0 = 0
            Nn = min(L, 512)
            for cc in range(4):
                for th in range(L // Nn):
                    p = nextps()
                    for fr in range(2 * Lc):
                        col = (fr // Lc) * L + th * Nn
                        mm(p[:, 0:Nn], Y[:, yb0 + fr, cc * 128:(cc + 1) * 128], Wt[:, fr % Lc, col:col + Nn], fr == 0, fr == 2 * Lc - 1,
                           [Y, Wt], [p], fr == 2 * Lc - 1)
                    t0 = s * L + th * Nn
                    DVE(lambda p=p, cc=cc, t0=t0: nc.vector.tensor_tensor(out=mulT[:, cc, t0:t0 + Nn], in0=p[:, 0:Nn],
                                                                          in1=mulT[:, cc, t0:t0 + Nn], op=ALU.mult), [p, mulT], [mulT])

        def run_convs(o, mulT):
            conv_A(0, o)
            s2b_hook()
            for s_ in range(nseq):
                if s_ + 1 < nseq:
                    conv_A(s_ + 1, o)
                    s2b_hook()
                conv_B(s_, o, mulT)
                s2b_hook()

        run_convs(0, x1T)
        for cc in range(4):
            transposes_to_tok(x1T, lambda tk, cc=cc: x1T[:, cc, tk * 128:(tk + 1) * 128], vtok, cc * 128)
        run_convs(1, x2T)
        if g == "S":
            mods_final(1)
            fw.alias(mrg, late["slots"])
        for t in sq:
            fw.alias(t, Ys + xstg)
        fw.alias(rstd, ysa)
        pss = [nextps(), nextps()]
        for cc in range(4):
            s_ = sq[cc % 2]
            ACT(lambda s_=s_, cc=cc: nc.scalar.activation(out=s_[:], in_=x2T[:, cc, :], func=AF.Square), [x2T], [s_])
            for tt in range(2):
                mm(pss[tt][:], ones[:], s_[:, tt * 512:(tt + 1) * 512], cc == 0, cc == 3, [ones, s_], [pss[tt]], True)
        rstd_from(pss, rstd, 512)
        for cc in range(4):
            if g == "S":
                DVE(lambda cc=cc: nc.vector.scalar_tensor_tensor(out=x2T[:, cc, :], in0=x2T[:, cc, :], scalar=gains[:, 28 + cc:29 + cc],
                                                                 in1=rstd[:], op0=ALU.mult, op1=ALU.mult), [x2T, gains, rstd], [x2T])
                for tk in range(8):
                    fw.op("pe", lambda tk=tk, cc=cc: nc.tensor.transpose(psb[:, tk * 128:(tk + 1) * 128], x2T[:, cc, tk * 128:(tk + 1) * 128], ident[:]),
                          reads=[x2T, ident], writes=[psb], inc=(tk == 7))
                ACT(lambda cc=cc: nc.scalar.copy(out=mrg[:, :, 512 + cc * 128:512 + (cc + 1) * 128],
                                                 in_=psb[:, :].rearrange("p (t c) -> p t c", c=128)), [psb], [mrg])
            else:
                DVE(lambda cc=cc: nc.vector.scalar_tensor_tensor(out=mrg[:, 4 + cc, :], in0=x2T[:, cc, :], scalar=gains[:, 28 + cc:29 + cc],
                                                                 in1=rstd[:], op0=ALU.mult, op1=ALU.mult), [x2T, gains, rstd], [mrg])
        dead_h = [x1T, x2T, ystg, pstg, vT, vT2, rstd, yt1, yt2, yt3, yt4] + Ys + vtok + ysa + ysb + xstg + sq
        if g == "S":
            dead_h += [Wm[1024], Ktab[1024]]

        cur = [O_E]

        def sbe(name, shape, dt):
            nb = int(np.prod(shape[1:])) * (2 if dt == BF16 else 4)
            t = sb(g + name, shape, dt, cur[0])
            cur[0] += (nb + 63) // 64 * 64
            return t
        if early_att is not None:
            qT, kT, Vaug, kst = early_att
        else:
            qT = sbe("qT", [128, 4, NTOK], BF16)
            kT = sbe("kT", [128, 4, NTOK], BF16)
            Vaug = sbe("Vaug", [128, 8, 8, 65], BF16)
            kst = []
        ckT = sbe("ckT", [128, 4, 256], BF16)
        cVaug = sbe("cVaug", [128, 2, 8, 65], BF16)
        PT = [sbe(f"PT{i}", [128, 896], BF16) for i in range(2)]
        atok = sbe("atok", [128, 512], F32)
        an = sbe("an", [128, 512], BF16)
        if g == "S":
            tabs = {"E": sbe("tabE", [128, 8, 1024], BF16), "I": sbe("tabI", [128, 8, 1024], BF16)}
        else:
            PT.append(sbe("PT2", [128, 896], BF16))
            PT.append(sbe("PT3", [128, 896], BF16))
            tabs = {"E": PT[0], "I": PT[0]}
        att_t = [qT, kT, Vaug, ckT, cVaug, atok, an, tabs["E"], tabs["I"]] + PT + kst
        for t in att_t:
            fw.alias(t, dead_h + dead_in)
        if g == "S":
            fw.dma("pool", ckT[:], ckT_d, writes=[ckT])
            fw.dma("pool", cVaug[:, :, :, 0:64], cv_d, writes=[cVaug])
            fw.dma("pool", tabs["E"][:], tabE_d, writes=[tabs["E"]])
            fw.dma("pool", tabs["I"][:], tabI_d, writes=[tabs["I"]])
            POOL(lambda: nc.gpsimd.memset(cVaug[:, :, :, 64:65], 1.0), [], [cVaug])
            for nm in ("E", "I"):
                for hq in range(4):
                    ACT(lambda nm=nm, hq=hq: nc.scalar.activation(out=tabs[nm][:, 2 * hq:2 * hq + 2, :], in_=tabs[nm][:, 2 * hq:2 * hq + 2, :],
                                                                 func=AF.Exp), [tabs[nm]], [tabs[nm]])
        if early_att is None:
            POOL(lambda: nc.gpsimd.memset(Vaug[:, :, :, 64:65], 1.0), [], [Vaug])
            s2b_pieces = make_s2b(qT, kT, Vaug, kst)
        while s2b_pieces:
            s2b_pieces.pop(0)()

        gb = sb(g + "gb", [128, 512], F32, O_B)
        fw.alias(gb, [hT] + dead_in)
        if g == "S":
            fw.dma("sp", gb[:], gbrow_d.partition_broadcast(128), writes=[gb])
        rot["banks"] = [2, 3, 4, 5]
        rot["i"] = 0
        kp_of = {0: [0, 1, 2, 3], 1: [0, 1, 2, 3], 2: [0, 1, 2, 3, 4], 3: [1, 2, 3, 4, 5], 4: [2, 3, 4, 5, 6],
                 5: [3, 4, 5, 6, 7], 6: [4, 5, 6, 7], 7: [4, 5, 6, 7]}
        SC = 1.0
        O = [psf[0], psf[1]]
        Osb = sbe("Osb", [128, 520], F32)
        fw.alias(Osb, dead_h + dead_in)
        if g == "P":
            jobs = [(tk, h) for tk in range(8) for h in (0, 1, 4, 5)]
        else:
            jobs = [(tk, h) for tk in range(8) for h in range(8)]
        st = {}

        def slots_of(tk):
            if g == "P":
                s_ = tk // 2
                return [("k", 2 * s_, 0), ("k", 2 * s_ + 1, 0), ("k", 2 * s_, 2), ("k", 2 * s_ + 1, 2)]
            return [("b", kt, 0) for kt in reversed(kp_of[tk])] + [("c", 0, 0), ("c", 1, 0)]

        def emit_scores(ji):
            tk, h = jobs[ji]
            slots = slots_of(tk)
            ns = len(slots)
            pA = nextps()
            pB = nextps() if ns > 4 else None
            for si, (kind, kt, dh_) in enumerate(slots):
                hh_ = h + dh_
                c, pb_ = hh_ // 2, 64 * (hh_ % 2)
                q_ap = qT[pb_:pb_ + 64, c, tk * 128:(tk + 1) * 128]
                p = pA if si < 4 else pB
                o_ap = p[:, (si % 4) * 128:(si % 4) * 128 + 128]
                if kind == "c":
                    mm(o_ap, ckT[pb_:pb_ + 64, c, kt * 128:(kt + 1) * 128], q_ap, True, True, [ckT, qT], [p], True)
                else:
                    k_ap = kT[pb_:pb_ + 64, c, kt * 128:(kt + 1) * 128]
                    mm(o_ap, k_ap, q_ap, True, True, [kT, qT], [p], True)
            pt = PT[ji % len(PT)]
            n1 = min(ns, 4)
            ACT(lambda: nc.scalar.activation(out=pt[:, 0:n1 * 128], in_=pA[:, 0:n1 * 128], func=AF.Exp, scale=SC), [pA], [pt])
            if ns > 4:
                ACT(lambda: nc.scalar.activation(out=pt[:, 512:ns * 128], in_=pB[:, 0:(ns - 4) * 128], func=AF.Exp, scale=SC), [pB], [pt])
            if g == "S":
                nb = ns - 2
                tab = tabs["E"] if tk in (0, 1, 6, 7) else tabs["I"]
                kt0 = slots[0][1]
                i0 = 7 - (2 * kt0 - 2 * tk)
                DVE(lambda: nc.vector.tensor_tensor(out=pt[:, 0:nb * 128], in0=pt[:, 0:nb * 128],
                                                    in1=tab[:, h, i0 * 64:i0 * 64 + nb * 128], op=ALU.mult), [pt, tab], [pt])
            st[ji] = (pt, slots)

        def emit_pv(ji):
            tk, h = jobs[ji]
            pt, slots = st.pop(ji)
            ns = len(slots)
            for dh_ in sorted(set(sl_[2] for sl_ in slots)):
                hh_ = h + dh_
                ob = O[hh_ // 4]
                o_ap = ob[:, (hh_ % 4) * 65:(hh_ % 4) * 65 + 65]
                idx = [si for si, sl_ in enumerate(slots) if sl_[2] == dh_]
                for n_, si in enumerate(idx):
                    kind, kt, _ = slots[si]
                    if kind == "c":
                        v_ap, vt = cVaug[:, kt, hh_, :], cVaug
                    else:
                        v_ap, vt = Vaug[:, kt, hh_, :], Vaug
                    mm(o_ap, pt[:, si * 128:(si + 1) * 128], v_ap, n_ == 0, n_ == len(idx) - 1, [pt, vt], [ob], n_ == len(idx) - 1)

        def emit_tail_a(tk):
            for hb in range(2):
                ACT(lambda hb=hb: nc.scalar.copy(out=Osb[:, hb * 260:(hb + 1) * 260], in_=O[hb][:, 0:260]), [O[hb]], [Osb])
            o3 = Osb[:].rearrange("p (h d) -> p h d", d=65)
            DVE(lambda: nc.vector.reciprocal(out=smalls[:, 0:8], in_=o3[:, :, 64]), [Osb], [smalls])
            for hh in range(8):
                DVE(lambda hh=hh: nc.vector.tensor_scalar(out=atok[:, hh * 64:(hh + 1) * 64], in0=o3[:, hh, 0:64],
                                                          scalar1=smalls[:, hh:hh + 1], scalar2=None, op0=ALU.mult),
                    [Osb, smalls], [atok])

        def emit_tail_a2(tk):
            ACT(lambda: nc.scalar.activation(out=an[:], in_=atok[:], func=AF.Square, accum_out=smalls[:, 8:9]), [atok], [an, smalls])
            ACT(lambda: nc.scalar.activation(out=smalls[:, 9:10], in_=smalls[:, 8:9], func=AF.Ln, bias=epsc[:, 0:1], scale=float(1.0 / 512)), [smalls, epsc], [smalls])
            ACT(lambda: nc.scalar.activation(out=smalls[:, 10:11], in_=smalls[:, 9:10], func=AF.Exp, scale=-0.5), [smalls], [smalls])
            if g == "S":
                DVE(lambda: nc.vector.scalar_tensor_tensor(out=mrg[:, tk, 0:512], in0=atok[:], scalar=smalls[:, 10:11], in1=gb[:],
                                                           op0=ALU.mult, op1=ALU.mult), [atok, smalls, gb], [mrg])
            else:
                DVE(lambda: nc.vector.tensor_scalar(out=an[:], in0=atok[:], scalar1=smalls[:, 10:11], scalar2=None, op0=ALU.mult), [atok, smalls], [an])

        def emit_tail_b(tk):
            if g == "S":
                return
            for c4 in range(4):
                fw.op("pe", lambda c4=c4: nc.tensor.transpose(psb[:, c4 * 128:(c4 + 1) * 128], an[:, c4 * 128:(c4 + 1) * 128], ident[:]),
                      reads=[an, ident], writes=[psb], inc=(c4 == 3))
            for c4 in range(4):
                DVE(lambda c4=c4: nc.vector.tensor_scalar(out=mrg[:, c4, tk * 128:(tk + 1) * 128], in0=psb[:, c4 * 128:(c4 + 1) * 128],
                                                          scalar1=gains[:, 24 + c4:25 + c4], scalar2=None, op0=ALU.mult), [psb, gains], [mrg])

        Dp = len(PT) - 1
        sched = []
        nj = len(jobs)
        for ji in range(min(Dp, nj)):
            emit_scores(ji)
        for ji in range(nj):
            if ji + Dp < nj:
                emit_scores(ji + Dp)
            emit_pv(ji)
            tk, h = jobs[ji]
            while sched and sched[0][0] <= ji:
                sched.pop(0)[1]()
            if ji + 1 == nj or jobs[ji + 1][0] != tk:
                emit_tail_a(tk)
                sched.append((ji + 2, lambda tk=tk: emit_tail_a2(tk)))
                sched.append((ji + 4, lambda tk=tk: emit_tail_b(tk)))
        while sched:
            sched.pop(0)[1]()
        rot["banks"] = [0, 1, 2, 3, 4, 5]
        rot["i"] = 0
        dead_a = att_t + [hT, Osb]

        tap(g + "mrg", mrg, mrg[:], [128, 8, NTOK], BF16)

        if g == "S":
            yield from s_tail(dead_a + dead_h + dead_in + [gb], mrg)
            return
        xresc = [sb(g + f"xres{c}", [128, NTOK], F32, O_A + c * 4 * KB) for c in range(8)]
        fstg = [sb(g + f"fstg{i}", [128, NTOK], F32, O_E + 44 * KB + i * 4 * KB) for i in range(4)]
        rstd2 = sb(g + "rstd2", [128, NTOK], F32, O_E + 60 * KB)
        sq2 = [sb(g + f"sq2{i}", [128, NTOK], BF16, O_E + 64 * KB + i * 2 * KB) for i in range(2)]
        actT = sb(g + "actT", [128, 22, NTOK], BF16, O_E)
        hT2 = sb(g + "hT2", [128, 8, NTOK], BF16, O_B)
        for t in xresc + [rstd2, actT, hT2] + fstg + sq2:
            fw.alias(t, dead_a + dead_h + dead_in)
        def stats_chunk(pss_, c):
            s_ = sq2[c % 2]
            ACT(lambda: nc.scalar.activation(out=s_[:], in_=xresc[c][:], func=AF.Square), [xresc[c]], [s_])
            for tt in range(2):
                mm(pss_[tt][:], ones[:], s_[:, tt * 512:(tt + 1) * 512], c == 0, c == 7, [ones, s_], [pss_[tt]], True)

        pss6 = [psf[6], psf[5]]
        rot["banks"] = [0, 1, 2, 3, 4]
        rot["i"] = 0
        for b in range(2):
            wt = next_wbuf()
            fw.dma("pool", wt[:], wout_d[:, :, b * 512:(b + 1) * 512], writes=[wt])
            for j in range(4):
                cj = b * 4 + j
                xs = fstg[cj % 2]
                fw.dma("sp", xs[:], x_d[:, cj, :], writes=[xs])
                for tt in range(2):
                    p = nextps()
                    sl = slice(tt * 512, (tt + 1) * 512)
                    for kc in range(8):
                        mm(p[:], wt[:, kc, j * 128:(j + 1) * 128], mrg[:, kc, sl], kc == 0, kc == 7, [wt, mrg], [p], kc == 7)
                    DVE(lambda p=p, cj=cj, sl=sl, xs=xs: nc.vector.scalar_tensor_tensor(
                        out=xresc[cj][:, sl], in0=p[:], scalar=modcol(sset, 2, cj), in1=xs[:, sl], op0=ALU.mult, op1=ALU.add),
                        [p, mods, xs], [xresc[cj]])
                if cj >= 1:
                    stats_chunk(pss6, cj - 1)
        stats_chunk(pss6, 7)

        rot["banks"] = [0, 1, 2, 3, 4, 5]
        rot["i"] = 0
        rstd_from(pss6, rstd2, D)
        for c in range(8):
            ft = fstg[c % 2]
            DVE(lambda c=c, ft=ft: nc.vector.tensor_tensor(out=ft[:], in0=xresc[c][:], in1=rstd2[:], op=ALU.mult), [xresc[c], rstd2], [ft])
            ACT(lambda c=c, ft=ft: nc.scalar.activation(out=hT2[:, c, :], in_=ft[:], func=AF.Identity,
                                                        bias=modcol(sset, 3, c), scale=gm[:, sset, 1, c:c + 1]), [ft, mods, gm], [hT2])

        items = []
        wts = {}
        for blk in range(11):
            for i in range(2):
                jj = 2 * blk + i
                for which, jcol in ((0, i), (1, 2 + i)):
                    fcx = which * 22 + jj
                    sy = fstg[2 * which + (jj % 2)]
                    stt = {}

                    def A(blk=blk, i=i, which=which, jcol=jcol, fcx=fcx, sy=sy, stt=stt):
                        if i == 0 and which == 0:
                            wt = next_wbuf()
                            fw.dma("pool", wt[:], wup_d[:, :, blk * 512:(blk + 1) * 512], writes=[wt])
                            wts[blk] = wt
                        wt = wts[blk]
                        pa, pb2, big = nextpair()
                        for tt, p in enumerate((pa, pb2)):
                            for kc in range(8):
                                mm(p[:], wt[:, kc, jcol * 128:(jcol + 1) * 128], hT2[:, kc, tt * 512:(tt + 1) * 512], kc == 0, kc == 7, [wt, hT2], [p], kc == 7)
                        stt["p"] = (pa, pb2, big)
                        dw_A(pa, pb2, big, ffc[:, fcx, :], sy)

                    def B(fcx=fcx, sy=sy, stt=stt):
                        dw_B(*stt["p"], ffc[:, fcx, :], sy)

                    def C(which=which, jj=jj, sy=sy):
                        if which == 0:
                            ACT(lambda: nc.scalar.activation(out=sy[:], in_=sy[:], func=AF.Silu), [sy], [sy])
                        else:
                            gs = fstg[jj % 2]
                            DVE(lambda: nc.vector.tensor_tensor(out=actT[:, jj, :], in0=gs[:], in1=sy[:], op=ALU.mult), [gs, sy], [actT])
                    items.append([A, B, C])
        run_pipeline(items)

        tap(g + "actT", actT, actT[:], [128, 22, NTOK], BF16)
        pss9 = [psf[6], psf[5]]
        rot["banks"] = [0, 1, 2, 3, 4]
        rot["i"] = 0
        for cj in range(8):
            wt = next_wbuf()
            wflat = wt[:].rearrange("p a b -> p (a b)")
            fw.dma("pool", wflat[:, 0:22 * 128], wdn_d[:, cj, :], writes=[wt])
            for tt in range(2):
                p = nextps()
                sl = slice(tt * 512, (tt + 1) * 512)
                for kk in range(22):
                    mm(p[:], wflat[:, kk * 128:(kk + 1) * 128], actT[:, kk, sl], kk == 0, kk == 21, [wt, actT], [p], kk == 21)
                DVE(lambda p=p, cj=cj, sl=sl: nc.vector.scalar_tensor_tensor(
                    out=xresc[cj][:, sl], in0=p[:], scalar=modcol(sset, 5, cj), in1=xresc[cj][:, sl], op0=ALU.mult, op1=ALU.add),
                    [p, mods, xresc[cj]], [xresc[cj]])
            if cj >= 1:
                stats_chunk(pss9, cj - 1)

        stats_chunk(pss9, 7)
        if g == "S":
            live["S"] = [actT, hT2, mrg] + dead_a + dead_h + dead_in
            prefetch_x("P", live["S"])
            yield "pre_S9"
        rstd_from(pss9, rstd2, D)
        rot["banks"] = [0, 1, 2, 3, 4, 5]
        rot["i"] = 0
        for c in range(8):
            ft = fstg[c % 4]
            DVE(lambda c=c, ft=ft: nc.vector.scalar_tensor_tensor(out=ft[:], in0=xresc[c][:], scalar=gains[:, 16 + c:17 + c],
                                                                 in1=rstd2[:], op0=ALU.mult, op1=ALU.mult), [xresc[c], gains, rstd2], [ft])
            fw.dma("sp", yT_o[g][:, c, :], ft[:], reads=[ft], writes=[OUT])
        final_dead[g] = xresc + [rstd2, actT, hT2, mrg] + fstg + sq2 + dead_a + dead_h

    for _ in range(3):
        mods_early_issue()
    dead = filter_phase(1024, [])
    dead = dead + filter_phase(256, dead)
    fw.alias(Wm[1024], dead)
    fw.dma("sp", Wm[256][:], W_d[256], writes=[Wm[256]])
    fw.dma("sp", Wm[1024][:], W_d[1024], writes=[Wm[1024]])
    mods_final(0)
    gS = group("S", dead)
    assert next(gS) == "post_S1"
    assert next(gS) == "pre_S9"
    gP = group("P", dead + live["S"])
    assert next(gP) == "post_S1"
    for _ in gS:
        pass
    for _ in gP:
        pass
    fw.finish()
    return nc


_CACHE = {}


def _fm(v, nch):
    return np.ascontiguousarray(v.reshape(nch, 128).T)


def _wk(w):
    K, N = w.shape
    return np.ascontiguousarray(w.reshape(K // 128, 128, N).transpose(1, 0, 2))


def _consts():
    if "c" in _CACHE:
        return _CACHE["c"]
    c = {}
    for L in (1024, 256):
        t = np.arange(L, dtype=np.float64)
        f = np.arange(L, dtype=np.float64)
        om = np.pi * (f + 0.5) / L
        Cs = np.cos(np.outer(t + 0.5, om))
        Ss = np.sin(np.outer(t + 0.5, om))
        W = np.concatenate([Cs, -Ss], 1)
        Wu = np.concatenate([np.cos(np.outer(t, om)), np.sin(np.outer(t, om))], 1) / L
        c[f"W{L}"] = _wk(W.astype(np.float32)).astype(ml_dtypes.bfloat16)
        c[f"Wu{L}"] = _wk(Wu.astype(np.float32)).astype(ml_dtypes.bfloat16)
        t32 = np.arange(L, dtype=np.float32) / np.float32(L)
        fr = np.arange(1, 9, dtype=np.float32)
        ang = np.float32(2.0 * np.pi) * t32[:, None] * fr[None, :]
        z = np.concatenate([t32[:, None], np.cos(ang), np.sin(ang)], -1).astype(np.float32)
        zT = np.concatenate([z.T, np.ones((1, L), np.float32)], 0)
        c[f"zT{L}"] = np.ascontiguousarray(zT)
        min_decay = np.log(1e-2) / 1.5
        max_decay = np.log(1e-2) / 0.3
        deltas = np.abs(np.linspace(min_decay, max_decay, 512, dtype=np.float32))
        decay = np.exp(-t32[:, None] * deltas[None, :]).astype(np.float32)
        decayB = decay.copy()
        decayB[0] = 0.0
        dd = np.concatenate([decay, decayB], 1)
        c[f"dec{L}"] = _wk(dd)
    c["ident"] = np.eye(128, dtype=np.float32).astype(ml_dtypes.bfloat16)
    e0 = np.zeros((128, 1), np.float32)
    e0[0, 0] = -1.0
    c["e0n"] = e0
    kc = np.arange(64)[:, None]
    qc = np.arange(64)[None, :]
    cstart = np.clip(qc - 8, 0, 48)
    colin = (kc >= cstart) & (kc < cstart + 16)
    dc = np.clip(kc - qc, -15, 15) + 15
    idxE = np.zeros((128, 16, 64), np.int64)
    idxI = np.zeros((128, 16, 64), np.int64)
    SENT = 15 * 31
    for half in range(2):
        for i in range(16):
            dl = (7 - i) if half == 0 else (8 - i)
            for tab, ok in ((idxE, abs(dl) <= 7), (idxI, -4 <= dl <= 3)):
                if ok:
                    tab[half * 64:(half + 1) * 64, i, :] = np.where(colin, (dl + 7) * 31 + dc, SENT)
                else:
                    tab[half * 64:(half + 1) * 64, i, :] = SENT
    c["idxE"], c["idxI"] = idxE, idxI
    _CACHE["c"] = c
    return c


def _prepare(x_prompt, x_sample, cache_ctx_k, cache_ctx_v, c, c_ctx, w_ada, b_ada, norm1_g,
           w_in, rpb, hy_conv_w, hy_conv_b, filt_w1, filt_b1, filt_w2, filt_b2, filt_w3,
           filt_b3, filt_freq, filt_bias, grp_norm_g, w_out, norm2_g, w_up, ffn_conv_w,
           ffn_conv_b, w_down, final_g):
    f32 = np.float32
    A = lambda a: np.asarray(a, dtype=f32)
    cs = _consts()
    x_prompt, x_sample = A(x_prompt), A(x_sample)
    shared = {}
    shared["w_ada"] = _wk(A(w_ada)[0])
    shared["b_ada"] = _fm(A(b_ada)[0], 48)
    shared["gains"] = np.concatenate([_fm(A(norm1_g)[0], 8), _fm(A(norm2_g)[0], 8), _fm(A(final_g), 8), _fm(A(grp_norm_g)[0], 8)], 1)
    shared["w_in"] = _wk(A(w_in)[0])
    hw, hb = A(hy_conv_w)[0], A(hy_conv_b)[0]
    shared["hyc"] = np.ascontiguousarray(np.stack([_fm(hw[0], 12), _fm(hw[1], 12), _fm(hw[2], 12), _fm(hb, 12)], -1))
    fwc, fbc = A(ffn_conv_w)[0], A(ffn_conv_b)[0]
    shared["ffc"] = np.ascontiguousarray(np.stack([_fm(fwc[0], 44), _fm(fwc[1], 44), _fm(fwc[2], 44), _fm(fbc, 44)], -1))
    shared["w_out"] = _wk(A(w_out)[0])
    wu = A(w_up)[0]
    cols = []
    for blk in range(11):
        for i in range(2):
            cols.append(np.arange((2 * blk + i) * 128, (2 * blk + i + 1) * 128))
        for i in range(2):
            cols.append(DFF + np.arange((2 * blk + i) * 128, (2 * blk + i + 1) * 128))
    shared["w_up"] = _wk(np.ascontiguousarray(wu[:, np.concatenate(cols)]))
    wd = A(w_down)[0]
    wd4 = wd.reshape(22, 128, 8, 128)
    shared["w_down"] = np.ascontiguousarray(wd4.transpose(1, 2, 0, 3).reshape(128, 8, 22 * 128))
    rp = np.concatenate([A(rpb)[0].reshape(8, 15 * 31), np.full((8, 1), NEGM, f32)], 1)
    shared["tabE"] = np.ascontiguousarray(rp[:, cs["idxE"]].transpose(1, 0, 2, 3).reshape(128, 8, 1024))
    shared["tabI"] = np.ascontiguousarray(rp[:, cs["idxI"]].transpose(1, 0, 2, 3).reshape(128, 8, 1024))
    shared["w1b1"] = np.ascontiguousarray(np.concatenate([A(filt_w1)[0], A(filt_b1)[0][None, :]], 0))
    shared["w2"] = np.ascontiguousarray(A(filt_w2)[0])
    shared["fsm"] = np.ascontiguousarray(np.stack([A(filt_b2)[0], A(filt_freq)[0], np.zeros(64, f32), np.zeros(64, f32)], 1))
    shared["w3b3"] = np.ascontiguousarray(np.concatenate([A(filt_w3)[0], A(filt_b3)[0][None, :]], 0))
    shared["fb"] = np.ascontiguousarray(A(filt_bias)[0].reshape(1, 1024))
    shared["gbrow"] = np.ascontiguousarray(A(grp_norm_g)[0][None, 0:512])
    for k in ("zT1024", "zT256", "dec1024", "dec256", "W1024", "W256", "Wu1024", "Wu256", "ident", "e0n"):
        shared[k] = cs[k]
    ck, cv = A(cache_ctx_k), A(cache_ctx_v)
    cc_, cctx = A(c), A(c_ctx)
    in_maps = []
    for core in range(8):
        b = core % 4
        m = dict(shared)
        m["xsT"] = np.ascontiguousarray(x_sample[b].T.reshape(8, 128, NTOK).transpose(1, 0, 2))
        xp = x_prompt[core * 4:(core + 1) * 4].reshape(NTOK, D)
        m["xpT"] = np.ascontiguousarray(xp.T.reshape(8, 128, NTOK).transpose(1, 0, 2))
        m["csil"] = np.ascontiguousarray(np.stack([_fm(cctx, 8), _fm(cc_[b], 8)], -1))
        off = 512 * (core // 4)
        xpad = np.zeros((NTOK + 2, D), f32)
        xpad[1:NTOK + 1] = x_sample[b]
        m["xo"] = np.ascontiguousarray(xpad[off:off + 514].T.reshape(8, 128, 514).transpose(1, 0, 2))
        selm = np.zeros((NTOK, 514), f32)
        ii = np.arange(514)
        tt = off - 1 + ii
        ok = (tt >= 0) & (tt < NTOK)
        selm[tt[ok], ii[ok]] = 1.0
        m["sel"] = np.ascontiguousarray(selm.reshape(8, 128, 514).transpose(1, 0, 2)).astype(ml_dtypes.bfloat16)
        m["mrow"] = np.ascontiguousarray(ok.astype(f32)[None, :])
        kk = ck[b, 0]
        m["ckT"] = np.ascontiguousarray(kk.reshape(4, 2, 256, 64).transpose(1, 3, 0, 2).reshape(128, 4, 256))
        vv = cv[b, 0]
        m["cv"] = np.ascontiguousarray(vv.reshape(8, 2, 128, 64).transpose(2, 1, 0, 3))
        in_maps.append(m)
    return in_maps


def _assemble(R):
    f32 = np.float32
    y_prompt = np.zeros((32, 256, D), f32)
    y_sample = np.zeros((4, 1024, D), f32)
    sk = np.zeros((32, 1, 8, 256, 64), f32)
    sv = np.zeros((32, 1, 8, 256, 64), f32)
    for core in range(8):
        r = R[core]
        yp = np.asarray(r["ypT"]).transpose(1, 0, 2).reshape(D, NTOK).T
        y_prompt[core * 4:(core + 1) * 4] = yp.reshape(4, 256, D)
        off = 512 * (core // 4)
        y_sample[core % 4, off:off + 512] = np.asarray(r["ysT"]).transpose(1, 0, 2).reshape(D, 512).T
        kTo = np.asarray(r["kT_o"]).transpose(1, 0, 2).reshape(512, NTOK)
        sk[core * 4:(core + 1) * 4, 0] = kTo.reshape(8, 64, 4, 256).transpose(2, 0, 3, 1)
        vo = np.asarray(r["v_o"]).reshape(128, 8, 8, 64)
        vo = vo.transpose(1, 0, 2, 3).reshape(4, 256, 8, 64).transpose(0, 2, 1, 3)
        sv[core * 4:(core + 1) * 4, 0] = vo
    return (y_prompt, y_sample, sk, sv)


def kernel(**inputs):
    in_maps = _prepare(**inputs)
    if "nc" not in _CACHE:
        _CACHE["nc"] = build_program()
    res = run_bass_kernel_spmd(_CACHE["nc"], in_maps, core_ids=list(range(8)))
    return _assemble(res.results)
```

```python
import numpy as np
import ml_dtypes
import concourse.bass as bass
import concourse.mybir as mybir
from concourse.bass_utils import run_bass_kernel_spmd

F32 = mybir.dt.float32
BF16 = mybir.dt.bfloat16
I32 = mybir.dt.int32
AF = mybir.ActivationFunctionType
ALU = mybir.AluOpType

D = 1024
NTOK = 1024
DFF = 2816
EPS = 1e-6
NEGM = -30000.0
SAME_ENGINE_SYNC = True


class T:
    __slots__ = ("h", "lw", "rd", "name")

    def __init__(self, h, name=""):
        self.h = h
        self.lw = None
        self.rd = {}
        self.name = name

    def __getitem__(self, idx):
        return self.h[idx]


class FW:
    def __init__(self, nc, n_dma_sems=24):
        self.nc = nc
        self.eng = {"pe": nc.tensor, "act": nc.scalar, "dve": nc.vector,
                    "pool": nc.gpsimd, "sp": nc.sync}
        self.sem = {k: nc.alloc_semaphore(name=f"s_{k}") for k in self.eng}
        self.cnt = {k: 0 for k in self.eng}
        self.seen = {k: {} for k in self.eng}
        self.dsems = [nc.alloc_semaphore(name=f"d_{i}") for i in range(n_dma_sems)]
        self.dcnt = [0] * n_dma_sems
        self.dnext = {"sp": 0, "pool": 0}
        self.drange = {"sp": (0, n_dma_sems // 2), "pool": (n_dma_sems // 2, n_dma_sems)}

    def _wait(self, e, dep):
        kind, k, v = dep
        if kind == "e" and k == e and (e == "pe" or not SAME_ENGINE_SYNC):
            return
        key = (kind, k)
        if self.seen[e].get(key, 0) >= v:
            return
        self.seen[e][key] = v
        s = self.sem[k] if kind == "e" else self.dsems[k]
        self.eng[e].wait_ge(s, v)

    def _deps(self, reads, writes):
        deps = []
        for t in reads:
            if t.lw is not None:
                deps.append(t.lw)
        for t in writes:
            if t.lw is not None:
                deps.append(t.lw)
            deps.extend((k[0], k[1], v) for k, v in t.rd.items())
        return deps

    def op(self, e, fn, reads=(), writes=(), inc=True):
        for d in self._deps(reads, writes):
            self._wait(e, d)
        ins = fn()
        if inc:
            self.cnt[e] += 1
            ins.then_inc(self.sem[e], 1)
            me = ("e", e, self.cnt[e])
        else:
            me = ("e", e, self.cnt[e] + 1)
        self._mark(me, reads, writes)
        return ins

    def _mark(self, me, reads, writes):
        key = (me[0], me[1])
        for t in reads:
            if t.rd.get(key, 0) < me[2]:
                t.rd[key] = me[2]
        for t in writes:
            t.lw = me
            t.rd = {}

    def dma(self, q, out, in_, reads=(), writes=(), **kw):
        for d in self._deps(reads, writes):
            self._wait(q, d)
        lo, hi = self.drange[q]
        i = lo + self.dnext[q]
        self.dnext[q] = (self.dnext[q] + 1) % (hi - lo)
        if self.dcnt[i] > 0:
            self._wait(q, ("d", i, self.dcnt[i]))
        self.dcnt[i] += 16
        ins = self.eng[q].dma_start(out=out, in_=in_, **kw)
        ins.then_inc(self.dsems[i], 16)
        me = ("d", i, self.dcnt[i])
        self._mark(me, reads, writes)
        return me

    def alias(self, new, olds):
        for o in olds:
            ds = list((k[0], k[1], v) for k, v in o.rd.items())
            if o.lw is not None:
                ds.append(o.lw)
            for d in ds:
                key = (d[0], d[1])
                if new.rd.get(key, 0) < d[2]:
                    new.rd[key] = d[2]

    def finish(self):
        for k in ("pe", "act", "dve", "pool"):
            if self.cnt[k] > 0:
                self._wait("sp", ("e", k, self.cnt[k]))
        for i, c in enumerate(self.dcnt):
            if c > 0:
                self._wait("sp", ("d", i, c))


KB = 1024


DEBUG = False
TAPS = []


def build_program():
    nc = bass.Bass("TRN2", target_bir_lowering=False)
    fw = FW(nc)
    del TAPS[:]

    def tap(name, t, ap, shape, dt=F32):
        if not DEBUG:
            return
        d = nc.dram_tensor("tap_" + name, list(shape), dt, kind="ExternalOutput").ap()
        fw.dma("sp", d, ap, reads=[t], writes=[T(None)])
        TAPS.append("tap_" + name)

    def din(name, shape, dt=F32):
        return nc.dram_tensor(name, list(shape), dt, kind="ExternalInput").ap()

    def dout(name, shape):
        return nc.dram_tensor(name, list(shape), F32, kind="ExternalOutput").ap()

    xT = {"S": din("xsT", [128, 8, NTOK]), "P": din("xpT", [128, 8, NTOK])}
    csil_d = din("csil", [128, 8, 2])
    wada_d = din("w_ada", [128, 8, 6144])
    bada_d = din("b_ada", [128, 48])
    gains_d = din("gains", [128, 32])
    win_d = din("w_in", [128, 8, 3072])
    hyc_d = din("hyc", [128, 12, 4])
    ffc_d = din("ffc", [128, 44, 4])
    wout_d = din("w_out", [128, 8, 1024])
    wup_d = din("w_up", [128, 8, 5632])
    wdn_d = din("w_down", [128, 8, 22 * 128])
    ckT_d = din("ckT", [128, 4, 256])
    cv_d = din("cv", [128, 2, 8, 64])
    tabE_d = din("tabE", [128, 8, 1024])
    tabI_d = din("tabI", [128, 8, 1024])
    w1b1_d = din("w1b1", [18, 64])
    w2_d = din("w2", [64, 64])
    fsm_d = din("fsm", [64, 4])
    w3b3_d = din("w3b3", [65, 2048])
    fb_d = din("fb", [1, 1024])
    zT_d = {1024: din("zT1024", [18, 1024]), 256: din("zT256", [18, 256])}
    dec_d = {1024: din("dec1024", [128, 8, 1024]), 256: din("dec256", [128, 2, 1024])}
    W_d = {1024: din("W1024", [128, 8, 2048], BF16), 256: din("W256", [128, 2, 512], BF16)}
    Wu_d = {1024: din("Wu1024", [128, 8, 2048], BF16), 256: din("Wu256", [128, 2, 512], BF16)}
    ident_d = din("ident", [128, 128], BF16)
    e0n_d = din("e0n", [128, 1])
    yT_o = {"S": dout("ysT", [128, 8, 512]), "P": dout("ypT", [128, 8, NTOK])}
    xo_d = din("xo", [128, 8, 514])
    sel_d = din("sel", [128, 8, 514], BF16)
    mrow_d = din("mrow", [1, 514])
    gbrow_d = din("gbrow", [1, 512])
    kT_o = dout("kT_o", [128, 4, NTOK])
    v_o = dout("v_o", [128, 8, 512])
    OUT = T(None, "outs")

    base = (nc.sbuf_base + 63) // 64 * 64
    top = nc.sbuf_top

    def sb(name, shape, dt, off):
        nb = int(np.prod(shape[1:])) * (2 if dt == BF16 else 4)
        assert base + off + nb <= top, (name, off, nb, top - base)
        return T(nc.alloc_sbuf_tensor_at(name, list(shape), dt, offset=base + off), name)

    O_SM = 0
    ident = sb("ident", [128, 128], BF16, 0)
    ones = sb("ones", [128, 128], BF16, 256)
    mods = sb("mods", [128, 2, 48], F32, 512)
    gains = sb("gains", [128, 32], F32, 896)
    gm = sb("gm", [128, 2, 2, 8], F32, 1024)
    hyc = sb("hyc", [128, 12, 4], F32, 1152)
    ffc = sb("ffc", [128, 44, 4], F32, 1344)
    csil = sb("csil", [128, 8, 2], F32, 2048)
    csb = sb("csb", [128, 8, 2], BF16, 2112)
    bada = sb("bada", [128, 48], F32, 2176)
    epsc = sb("epsc", [128, 1], F32, 2368)
    e0n = sb("e0n", [128, 1], F32, 3200)
    fsm = sb("fsm", [64, 8], F32, 2400)
    w1b1 = sb("w1b1", [18, 64], F32, 2432)
    w2s = sb("w2s", [64, 64], F32, 2688)
    smalls = sb("smalls", [128, 64], F32, 2944)
    O_W256 = 5 * KB
    O_K256 = 7 * KB
    O_A = 15 * KB
    O_B = 79 * KB
    O_C = 95 * KB
    O_D = 111 * KB
    O_E = 135 * KB
    Wm = {256: sb("W256", [128, 2, 512], BF16, O_W256), 1024: sb("W1024", [128, 8, 2048], BF16, O_A)}
    Ktab = {256: sb("K256", [128, 2, 2, 2, 512], BF16, O_K256),
            1024: sb("K1024", [128, 8, 2, 2, 512], BF16, O_A + 32 * KB)}
    wbuf = [sb(f"wbuf{i}", [128, 8, 512], BF16, O_D + i * 8 * KB) for i in range(3)]
    wb_i = [0]

    pinned = set()

    def next_wbuf():
        while True:
            t = wbuf[wb_i[0] % 3]
            wb_i[0] += 1
            if t.name not in pinned:
                return t

    class HalfView:
        def __init__(self, big, off):
            self.big, self.off = big, off

        def __getitem__(self, idx):
            if not isinstance(idx, tuple):
                idx = (idx, slice(None))
            ps_, cs_ = idx
            a = 0 if cs_.start is None else cs_.start
            b = 512 if cs_.stop is None else cs_.stop
            return self.big[ps_, self.off + a:self.off + b]

    psd = [nc.alloc_psum_tensor(f"psd{i}", [128, 1024], F32) for i in range(3)]
    psf = [T(HalfView(psd[i // 2], 512 * (i % 2)), f"psf{i}") for i in range(6)]
    psf.append(T(nc.alloc_psum_tensor("psf6", [128, 512], F32), "psf6"))
    psb = T(nc.alloc_psum_tensor("psb", [128, 1024], BF16), "psb")
    rot = {"i": 0, "banks": [0, 1, 2, 3, 4, 5]}

    def nextps():
        b = rot["banks"][rot["i"] % len(rot["banks"])]
        rot["i"] += 1
        return psf[b]

    def nextpair():
        if rot["i"] % 2:
            rot["i"] += 1
        b = rot["banks"][rot["i"] % len(rot["banks"])]
        assert b % 2 == 0
        rot["i"] += 2
        return psf[b], psf[b + 1], psd[b // 2]

    ACT = lambda fn, r, w: fw.op("act", fn, reads=r, writes=w)
    DVE = lambda fn, r, w: fw.op("dve", fn, reads=r, writes=w)
    POOL = lambda fn, r, w: fw.op("pool", fn, reads=r, writes=w)

    def mm(ps_ap, lhsT, rhs, start, stop, reads, writes, last):
        return fw.op("pe", lambda: nc.tensor.matmul(ps_ap, lhsT=lhsT, rhs=rhs, start=start, stop=stop),
                     reads=reads, writes=writes, inc=last)

    fw.dma("sp", ident[:], ident_d, writes=[ident])
    fw.dma("sp", e0n[:], e0n_d, writes=[e0n])
    fw.dma("sp", csil[:], csil_d, writes=[csil])
    fw.dma("sp", bada[:], bada_d, writes=[bada])
    fw.dma("sp", gains[:], gains_d, writes=[gains])
    fw.dma("sp", hyc[:], hyc_d, writes=[hyc])
    fw.dma("sp", ffc[:], ffc_d, writes=[ffc])
    fw.dma("sp", fsm[:, 0:4], fsm_d, writes=[fsm])
    fw.dma("sp", w1b1[:], w1b1_d, writes=[w1b1])
    fw.dma("sp", w2s[:], w2_d, writes=[w2s])
    DVE(lambda: nc.vector.memset(ones[:], 1.0), [], [ones])
    DVE(lambda: nc.vector.memset(epsc[:], EPS), [], [epsc])
    ACT(lambda: nc.scalar.activation(out=csb[:], in_=csil[:], func=AF.Silu), [csil], [csb])
    DVE(lambda: nc.vector.tensor_scalar(out=fsm[:, 4:5], in0=fsm[:, 1:2], scalar1=float(1.0 / (2 * np.pi)),
                                        scalar2=None, op0=ALU.mult), [fsm], [fsm])

    pm = psf[6]
    mods_state = {"blk": 0}

    def mods_mm(blk, wt):
        for j in range(4):
            cj = blk * 4 + j
            for kc in range(8):
                mm(pm[:, cj * 2:cj * 2 + 2], wt[:, kc, j * 128:(j + 1) * 128], csb[:, kc, :],
                   kc == 0, kc == 7, [wt, csb], [pm], kc == 7)

    late = {"slots": None, "pend": []}
    early = {"pend": []}

    def mods_early_issue():
        blk = mods_state["blk"]
        if blk < 4:
            wt = next_wbuf()
            fw.dma("pool", wt[:], wada_d[:, :, blk * 512:(blk + 1) * 512], writes=[wt])
            early["pend"].append((blk, wt))
            mods_state["blk"] += 1

    def mods_early_tick():
        if early["pend"]:
            b0, w0 = early["pend"].pop(0)
            mods_mm(b0, w0)
            mods_early_issue()

    def mods_late_tick():
        blk = mods_state["blk"]
        if len(late["pend"]) == 2 or (blk >= 12 and late["pend"]):
            b0, w0 = late["pend"].pop(0)
            mods_mm(b0, w0)
        if blk < 12:
            wt = late["slots"][blk % 2]
            fw.dma("pool", wt[:], wada_d[:, :, blk * 512:(blk + 1) * 512], writes=[wt])
            late["pend"].append((blk, wt))
            mods_state["blk"] += 1

    def mods_tick(limit=12):
        blk = mods_state["blk"]
        if blk >= limit:
            return
        mods_state["blk"] += 1
        wt = next_wbuf()
        fw.dma("pool", wt[:], wada_d[:, :, blk * 512:(blk + 1) * 512], writes=[wt])
        for j in range(4):
            cj = blk * 4 + j
            for kc in range(8):
                mm(pm[:, cj * 2:cj * 2 + 2], wt[:, kc, j * 128:(j + 1) * 128], csb[:, kc, :],
                   kc == 0, kc == 7, [wt, csb], [pm], kc == 7)

    def mods_finish():
        while mods_state["blk"] < 12:
            mods_tick()
    def mods_final(part):
        if part == 0:
            while early["pend"]:
                mods_early_tick()
            while mods_state["blk"] < 4:
                mods_tick()
            c0, c1 = 0, 16
        else:
            while mods_state["blk"] < 12 or late["pend"]:
                if late["slots"] is not None:
                    mods_late_tick()
                else:
                    mods_tick()
            c0, c1 = 16, 48
        pm3 = pm[:, 0:96].rearrange("p (c s) -> p c s", s=2)
        for s in range(2):
            DVE(lambda s=s: nc.vector.tensor_tensor(out=mods[:, s, c0:c1], in0=pm3[:, c0:c1, s], in1=bada[:, c0:c1], op=ALU.add),
                [pm, bada], [mods])
        w = part
        for s in range(2):
            DVE(lambda s=s, w=w: nc.vector.scalar_tensor_tensor(
                out=gm[:, s, w, :], in0=mods[:, s, (8 + 24 * w):(16 + 24 * w)], scalar=1.0,
                in1=gains[:, 8 * w:8 * w + 8], op0=ALU.add, op1=ALU.mult), [mods, gains], [gm])
        if part == 1:
            tap("mods", mods, mods[:], [128, 2, 48])
            tap("gm", gm, gm[:], [128, 2, 2, 8])

    def modcol(s, j, c):
        return mods[:, s, j * 8 + c:j * 8 + c + 1]

    def filter_phase(L, dead_in):
        Lc = L // 128
        nh = max(1, L // 512)
        dec = sb(f"dec{L}", [128, Lc, 1024], F32, O_B)
        fs = sb(f"fs{L}", [128, Lc, 2, 512], BF16, O_A if L == 1024 else O_E + 60 * KB)
        fd = sb(f"fd{L}", [128, Lc, 2, 512], BF16, O_A + 16 * KB if L == 1024 else O_E + 64 * KB)
        yv = sb(f"yv{L}", [64, L], F32, O_E + 32 * KB)
        ti = sb(f"ti{L}", [64, L], I32, O_E + 36 * KB)
        tf = sb(f"tf{L}", [64, L], F32, O_E + 40 * KB)
        h1 = sb(f"h1{L}", [64, L], F32, O_E + 44 * KB)
        h2 = sb(f"h2{L}", [65, L], BF16, O_E + 48 * KB)
        zT = sb(f"zT{L}", [18, L], F32, O_E + 52 * KB)
        fbs = sb(f"fbs{L}", [128, 1024], F32, O_E + 56 * KB)
        w3 = sb(f"w3{L}", [96, 2048], F32, O_C if L == 256 else O_E + 64 * KB)
        w3c = sb(f"w3c{L}", [96, 2048], BF16, O_C + 8 * KB if L == 256 else O_E + 60 * KB)
        w3b = sb(f"w3b{L}", [96, 2, 512], BF16, O_C + 12 * KB if L == 256 else O_E + 50 * KB)
        t1, t2 = w3c, w3b
        for t in [dec, fs, fd, yv, ti, tf, h1, h2, zT, fbs, w3c, w3b, w3]:
            fw.alias(t, dead_in)
        fw.dma("sp", zT[:], zT_d[L], writes=[zT])
        DVE(lambda: nc.vector.memset(w3[64:96, :], 0.0), [], [w3])
        fw.dma("sp", w3[0:65, :], w3b3_d, writes=[w3])
        for o in range(2):
            wf = w3[:, o * 1024:o * 1024 + 512]
            wb_ = w3[:, o * 1024 + 512:o * 1024 + 1024]
            DVE(lambda o=o, wf=wf, wb_=wb_: nc.vector.tensor_tensor(out=w3c[:, o * 1024:o * 1024 + 512], in0=wf, in1=wb_, op=ALU.add), [w3], [w3c])
            DVE(lambda o=o, wf=wf, wb_=wb_: nc.vector.tensor_tensor(out=w3c[:, o * 1024 + 512:o * 1024 + 1024], in0=wb_, in1=wf, op=ALU.subtract), [w3], [w3c])
            DVE(lambda o=o, wb_=wb_: nc.vector.tensor_copy(out=w3b[:, o, :], in_=wb_), [w3], [w3b])
        fw.dma("sp", dec[:], dec_d[L], writes=[dec])
        fw.dma("sp", fbs[:], fb_d.partition_broadcast(128), writes=[fbs])
        if L == 1024:
            prefetch_x("S", [])
        W = min(L, 512)

        def sin_layer(src_ps_list, dst, add_b2):
            for i, p in enumerate(src_ps_list):
                sl = slice(i * W, (i + 1) * W)
                if add_b2:
                    DVE(lambda p=p, sl=sl: nc.vector.tensor_scalar(out=yv[:, sl], in0=p[0:64, 0:W], scalar1=fsm[:, 0:1],
                                                                   scalar2=fsm[:, 4:5], op0=ALU.add, op1=ALU.mult),
                        [p, fsm], [yv])
                    DVE(lambda sl=sl: nc.vector.tensor_scalar(out=yv[:, sl], in0=yv[:, sl], scalar1=64.0, scalar2=None,
                                                              op0=ALU.add), [yv], [yv])
                else:
                    DVE(lambda p=p, sl=sl: nc.vector.tensor_scalar(out=yv[:, sl], in0=p[0:64, 0:W], scalar1=fsm[:, 4:5],
                                                                   scalar2=64.0, op0=ALU.mult, op1=ALU.add),
                        [p, fsm], [yv])
            DVE(lambda: nc.vector.tensor_copy(out=ti[:], in_=yv[:]), [yv], [ti])
            DVE(lambda: nc.vector.tensor_copy(out=tf[:], in_=ti[:]), [ti], [tf])
            DVE(lambda: nc.vector.tensor_tensor(out=yv[:], in0=yv[:], in1=tf[:], op=ALU.subtract), [yv, tf], [yv])
            DVE(lambda: nc.vector.tensor_scalar(out=tf[:], in0=yv[:], scalar1=0.5, scalar2=None, op0=ALU.is_gt), [yv], [tf])
            DVE(lambda: nc.vector.tensor_tensor(out=yv[:], in0=yv[:], in1=tf[:], op=ALU.subtract), [yv, tf], [yv])
            DVE(lambda: nc.vector.tensor_scalar(out=tf[:], in0=yv[:], scalar1=-0.5, scalar2=None, op0=ALU.is_lt), [yv], [tf])
            DVE(lambda: nc.vector.tensor_tensor(out=yv[:], in0=yv[:], in1=tf[:], op=ALU.add), [yv, tf], [yv])
            ACT(lambda: nc.scalar.activation(out=dst[0:64, :], in_=yv[:], func=AF.Sin, scale=float(2 * np.pi)), [yv], [dst])

        pl = []
        for i in range(nh):
            p = nextps()
            mm(p[0:64, 0:W], w1b1[:], zT[:, i * W:(i + 1) * W], True, True, [w1b1, zT], [p], True)
            pl.append(p)
        sin_layer(pl, h1, False)
        pl = []
        for i in range(nh):
            p = nextps()
            mm(p[0:64, 0:W], w2s[:], h1[:, i * W:(i + 1) * W], True, True, [w2s, h1], [p], True)
            pl.append(p)
        sin_layer(pl, h2, True)
        DVE(lambda: nc.vector.memset(h2[64:65, :], 1.0), [], [h2])
        for tc in range(Lc):
            if tc >= 1:
                mods_early_tick()
            pq = [nextps() for _ in range(4)]
            for q in range(4):
                mm(pq[q][:], h2[:, tc * 128:(tc + 1) * 128], w3c[0:65, q * 512:(q + 1) * 512], True, True, [h2, w3c], [pq[q]], True)
            for o in range(2):
                ps_, pd_ = pq[2 * o], pq[2 * o + 1]
                DVE(lambda ps_=ps_, o=o: nc.vector.tensor_tensor(out=fs[:, tc, o, :], in0=ps_[:], in1=dec[:, tc, 0:512], op=ALU.mult), [ps_, dec], [fs])
                DVE(lambda pd_=pd_, o=o: nc.vector.tensor_tensor(out=fd[:, tc, o, :], in0=pd_[:], in1=dec[:, tc, 0:512], op=ALU.mult), [pd_, dec], [fd])
            if tc == 0:
                for o in range(2):
                    pc = nextps()
                    mm(pc[:], h2[:, 0:128], w3b[0:65, o, :], True, True, [h2, w3b], [pc], True)
                    DVE(lambda o=o, pc=pc: nc.vector.scalar_tensor_tensor(out=fs[:, 0, o, :], in0=pc[:], scalar=e0n[:, 0:1], in1=fs[:, 0, o, :],
                                                                         op0=ALU.mult, op1=ALU.add), [fs, pc, e0n], [fs])
                    DVE(lambda o=o, pc=pc: nc.vector.scalar_tensor_tensor(out=fd[:, 0, o, :], in0=pc[:], scalar=e0n[:, 0:1], in1=fd[:, 0, o, :],
                                                                         op0=ALU.mult, op1=ALU.add), [fd, pc, e0n], [fd])
        Kt = Ktab[L]
        nblk = (2 * L) // 512
        for blk in range(nblk):
            wt = next_wbuf()
            fw.dma("sp", wt[:, 0:Lc, :], Wu_d[L][:, :, blk * 512:(blk + 1) * 512], writes=[wt])
            for j in range(4):
                fr = blk * 4 + j
                isI = fr >= Lc
                fc = fr - Lc if isI else fr
                src = fd if isI else fs
                for o in range(2):
                    p = nextps()
                    for tc in range(Lc):
                        mm(p[:], wt[:, tc, j * 128:(j + 1) * 128], src[:, tc, o, :], tc == 0, tc == Lc - 1, [wt, src], [p], tc == Lc - 1)
                    if isI:
                        ACT(lambda p=p, fc=fc, o=o: nc.scalar.copy(out=Kt[:, fc, 1, o, :], in_=p[:]), [p], [Kt])
                    else:
                        DVE(lambda p=p, fc=fc, o=o: nc.vector.scalar_tensor_tensor(
                            out=Kt[:, fc, 0, o, :], in0=fbs[:, o * 512:(o + 1) * 512], scalar=float(1.0 / L), in1=p[:],
                            op0=ALU.mult, op1=ALU.add), [p, fbs], [Kt])
        tap(f"h1_{L}", h1, h1[:], [64, L])
        tap(f"h2_{L}", h2, h2[:], [65, L])
        tap(f"fs_{L}", fs, fs[:], [128, Lc, 2, 512], BF16)
        tap(f"K_{L}", Kt, Kt[:], [128, Lc, 2, 2, 512], BF16)
        return [dec, fs, fd, yv, ti, tf, h1, h2, zT, fbs, t1, t2, w3]


    def rstd_from(ps_list, dst, dcount):
        for i, p in enumerate(ps_list):
            ACT(lambda p=p, i=i: nc.scalar.activation(out=dst[:, i * 512:(i + 1) * 512], in_=p[:], func=AF.Ln,
                                                      bias=epsc[:, 0:1], scale=float(1.0 / dcount)), [p, epsc], [dst])
        ACT(lambda: nc.scalar.activation(out=dst[:], in_=dst[:], func=AF.Exp, scale=-0.5), [dst], [dst])

    prefetched = {}
    final_dead = {}
    live = {}

    def s_tail(dead_all, mtok):
        sset = 1

        def sba(name, shape, dt, off):
            t = sb("So_" + name, shape, dt, off)
            fw.alias(t, dead_all)
            return t
        x1o = [sba(f"x1o{c}", [128, 514], F32, O_A + c * 2112) for c in range(8)]
        x2o = [sba(f"x2o{c}", [128, 512], F32, O_A + 17 * KB + c * 2048) for c in range(8)]
        xst = [sba(f"xst{i}", [128, 514], F32, O_A + 33 * KB + i * 2112) for i in range(2)]
        rso = sba("rso", [128, 514], F32, O_A + 38 * KB)
        sqo = [sba(f"sqo{i}", [128, 514], BF16, O_A + 41 * KB + i * 1088) for i in range(2)]
        yst = [sba(f"yst{i}", [128, 512], F32, O_A + 44 * KB + i * 2048) for i in range(4)]
        mrow = sba("mrow", [128, 514], F32, O_A + 52 * KB)
        tmpf = [sba(f"tmpf{i}", [128, 514], F32, O_A + 55 * KB + i * 2112) for i in range(2)]
        ost = [sba(f"ost{i}", [128, 512], F32, O_A + 60 * KB + i * 2048) for i in range(2)]
        selT = sba("sel", [128, 8, 514], BF16, O_B + 4 * KB)
        h2o = sba("h2o", [128, 8, 514], BF16, O_B + 4 * KB)
        mrgo = sba("mrgo", [128, 8, 514], BF16, O_E)
        actTo = sba("actTo", [128, 22, 512], BF16, O_E + 9 * KB)
        all_t = x1o + x2o + xst + [rso] + sqo + yst + [mrow] + tmpf + ost + [selT, h2o, mrgo, actTo]
        fw.dma("sp", selT[:], sel_d, writes=[selT])
        fw.dma("sp", mrow[:], mrow_d.partition_broadcast(128), writes=[mrow])
        rot["banks"] = [0, 1, 2, 3]
        rot["i"] = 0
        pst = (psf[4], psf[5], psd[2])

        def mm514(big, pa, pb2, lhs_fn, rhs_t, rhs_fn, n, reads):
            for k in range(n):
                mm(big[:, 0:512], lhs_fn(k), rhs_fn(k, 0, 512), k == 0, k == n - 1, reads, [pa], k == n - 1)
            for k in range(n):
                mm(big[:, 512:514], lhs_fn(k), rhs_fn(k, 512, 514), k == 0, k == n - 1, reads, [pb2], k == n - 1)

        rot["banks"] = [0, 1, 2, 3, 4, 5]
        rot["i"] = 0
        for c in range(8):
            pa, pb2, big = nextpair()
            mm514(big, pa, pb2, lambda tk, c=c: mtok[:, tk, c * 128:(c + 1) * 128], selT,
                  lambda tk, a, b: selT[:, tk, a:b], 8, [mtok, selT])
            ACT(lambda c=c, big=big: nc.scalar.copy(out=mrgo[:, c, :], in_=big[:, 0:514]), [pa, pb2], [mrgo])
        fw.alias(h2o, [selT])

        def stats(c, src_t, width):
            s_ = sqo[c % 2]
            ACT(lambda: nc.scalar.activation(out=s_[:, 0:width], in_=src_t[:, 0:width], func=AF.Square), [src_t], [s_])
            mm(pst[2][:, 0:512], ones[:], s_[:, 0:512], c == 0, c == 7, [ones, s_], [pst[0]], True)
            if width > 512:
                mm(pst[2][:, 512:514], ones[:], s_[:, 512:514], c == 0, c == 7, [ones, s_], [pst[1]], True)

        def rstd_o(width):
            ACT(lambda: nc.scalar.activation(out=rso[:, 0:width], in_=pst[2][:, 0:width], func=AF.Ln, bias=epsc[:, 0:1], scale=float(1.0 / D)),
                [pst[0], pst[1], epsc], [rso])
            ACT(lambda: nc.scalar.activation(out=rso[:, 0:width], in_=rso[:, 0:width], func=AF.Exp, scale=-0.5), [rso], [rso])

        rot["banks"] = [0, 1, 2, 3]
        rot["i"] = 0
        for b in range(2):
            wt = next_wbuf()
            fw.dma("pool", wt[:], wout_d[:, :, b * 512:(b + 1) * 512], writes=[wt])
            for j in range(4):
                cj = b * 4 + j
                xs = xst[cj % 2]
                fw.dma("sp", xs[:], xo_d[:, cj, :], writes=[xs])
                pa, pb2, big = nextpair()
                mm514(big, pa, pb2, lambda kc, j=j, wt=wt: wt[:, kc, j * 128:(j + 1) * 128], mrgo,
                      lambda kc, a, b_: mrgo[:, kc, a:b_], 8, [wt, mrgo])
                DVE(lambda cj=cj, big=big, xs=xs: nc.vector.scalar_tensor_tensor(
                    out=x1o[cj][:], in0=big[:, 0:514], scalar=modcol(sset, 2, cj), in1=xs[:], op0=ALU.mult, op1=ALU.add),
                    [pa, pb2, mods, xs], [x1o[cj]])
                if cj >= 1:
                    stats(cj - 1, x1o[cj - 1], 514)
        stats(7, x1o[7], 514)
        rstd_o(514)
        for c in range(8):
            tf_ = tmpf[c % 2]
            DVE(lambda c=c, tf_=tf_: nc.vector.tensor_tensor(out=tf_[:], in0=x1o[c][:], in1=rso[:], op=ALU.mult), [x1o[c], rso], [tf_])
            ACT(lambda c=c, tf_=tf_: nc.scalar.activation(out=tf_[:], in_=tf_[:], func=AF.Identity,
                                                          bias=modcol(sset, 3, c), scale=gm[:, sset, 1, c:c + 1]), [tf_, mods, gm], [tf_])
            DVE(lambda c=c, tf_=tf_: nc.vector.tensor_tensor(out=h2o[:, c, :], in0=tf_[:], in1=mrow[:], op=ALU.mult), [tf_, mrow], [h2o])
        rot["banks"] = [0, 1, 2, 3, 4, 5]
        rot["i"] = 0
        items = []
        wts = {}
        for blk in range(11):
            for i in range(2):
                jj = 2 * blk + i
                for which, jcol in ((0, i), (1, 2 + i)):
                    fcx = which * 22 + jj
                    sy = yst[2 * which + (jj % 2)]
                    stt = {}
                    wc = ffc[:, fcx, :]

                    def A(blk=blk, i=i, which=which, jcol=jcol, sy=sy, stt=stt, wc=wc):
                        if i == 0 and which == 0:
                            wt = next_wbuf()
                            fw.dma("pool", wt[:], wup_d[:, :, blk * 512:(blk + 1) * 512], writes=[wt])
                            wts[blk] = wt
                        wt = wts[blk]
                        pa, pb2, big = nextpair()
                        mm514(big, pa, pb2, lambda kc: wt[:, kc, jcol * 128:(jcol + 1) * 128], h2o,
                              lambda kc, a, b_: h2o[:, kc, a:b_], 8, [wt, h2o])
                        stt["p"] = (pa, pb2, big)
                        ACT(lambda: nc.scalar.activation(out=sy[:], in_=big[:, 1:513], func=AF.Identity, bias=wc[:, 3:4], scale=wc[:, 1:2]),
                            [pa, pb2, ffc], [sy])

                    def B(sy=sy, stt=stt, wc=wc):
                        pa, pb2, big = stt["p"]
                        DVE(lambda: nc.vector.scalar_tensor_tensor(out=sy[:], in0=big[:, 0:512], scalar=wc[:, 0:1], in1=sy[:],
                                                                   op0=ALU.mult, op1=ALU.add), [sy, pa, ffc], [sy])
                        DVE(lambda: nc.vector.scalar_tensor_tensor(out=sy[:], in0=big[:, 2:514], scalar=wc[:, 2:3], in1=sy[:],
                                                                   op0=ALU.mult, op1=ALU.add), [sy, pa, pb2, ffc], [sy])

                    def C(which=which, jj=jj, sy=sy):
                        if which == 0:
                            ACT(lambda: nc.scalar.activation(out=sy[:], in_=sy[:], func=AF.Silu), [sy], [sy])
                        else:
                            gs = yst[jj % 2]
                            DVE(lambda: nc.vector.tensor_tensor(out=actTo[:, jj, :], in0=gs[:], in1=sy[:], op=ALU.mult), [gs, sy], [actTo])
                    items.append([A, B, C])
        n_it = len(items)
        for t_ in range(n_it + 2):
            for k in (2, 1, 0):
                ii = t_ - k
                if 0 <= ii < n_it:
                    items[ii][k]()
        rot["banks"] = [0, 1, 2, 3]
        rot["i"] = 0
        for cj in range(8):
            wt = next_wbuf()
            wflat = wt[:].rearrange("p a b -> p (a b)")
            fw.dma("pool", wflat[:, 0:22 * 128], wdn_d[:, cj, :], writes=[wt])
            p = nextps()
            for kk in range(22):
                mm(p[:], wflat[:, kk * 128:(kk + 1) * 128], actTo[:, kk, :], kk == 0, kk == 21, [wt, actTo], [p], kk == 21)
            DVE(lambda p=p, cj=cj: nc.vector.scalar_tensor_tensor(
                out=x2o[cj][:], in0=p[:], scalar=modcol(sset, 5, cj), in1=x1o[cj][:, 1:513], op0=ALU.mult, op1=ALU.add),
                [p, mods, x1o[cj]], [x2o[cj]])
            if cj >= 1:
                stats(cj - 1, x2o[cj - 1], 512)
        stats(7, x2o[7], 512)
        live["S"] = all_t + dead_all
        prefetch_x("P", live["S"])
        yield "pre_S9"
        rstd_o(512)
        rot["banks"] = [0, 1, 2, 3, 4, 5]
        rot["i"] = 0
        for c in range(8):
            ft = ost[c % 2]
            DVE(lambda c=c, ft=ft: nc.vector.scalar_tensor_tensor(out=ft[:], in0=x2o[c][:], scalar=gains[:, 16 + c:17 + c],
                                                                 in1=rso[:, 0:512], op0=ALU.mult, op1=ALU.mult), [x2o[c], gains, rso], [ft])
            fw.dma("sp", yT_o["S"][:, c, :], ft[:], reads=[ft], writes=[OUT])
        final_dead["S"] = all_t + dead_all


    def prefetch_x(g2, deadl):
        xc = [sb(g2 + f"xc{c}", [128, NTOK], F32, O_E + c * 4 * KB) for c in range(8)]
        for t in xc:
            fw.alias(t, deadl)
        for c in range(8):
            fw.dma("sp", xc[c][:], xT[g2][:, c, :], writes=[xc[c]])
        prefetched[g2] = xc

    def group(g, dead_in):
        L = 1024 if g == "S" else 256
        nseq = NTOK // L
        Lc = L // 128
        sset = 1 if g == "S" else 0
        x_d = xT[g]
        x1T = sb(g + "x1T", [128, 4, NTOK], BF16, O_E)
        x2T = sb(g + "x2T", [128, 4, NTOK], BF16, O_E + 8 * KB)
        vtok = [sb(g + f"vtok{s}", [128, Lc, 512], BF16, O_E + 16 * KB + s * Lc * KB) for s in range(nseq)]
        nY = 16 // (2 * Lc)
        Ys = [sb(g + f"Y{k}", [128, 2 * Lc, 512], BF16, O_E + 24 * KB + k * 2 * Lc * KB) for k in range(nY)]
        Y = Ys[0]
        ysa = [sb(g + f"ysa{i}", [128, 512], F32, O_E + 40 * KB + i * 2 * KB) for i in range(2)]
        ysb = [sb(g + f"ysb{i}", [128, 512], F32, O_E + 44 * KB + i * 2 * KB) for i in range(2)]
        yt1 = sb(g + "yt1", [128, 512], F32, O_E + 48 * KB)
        yt2 = sb(g + "yt2", [128, 512], F32, O_E + 50 * KB)
        yt3 = sb(g + "yt3", [128, 512], F32, O_E + 66 * KB)
        yt4 = sb(g + "yt4", [128, 512], F32, O_E + 70 * KB)
        ystg = sb(g + "ystg", [128, NTOK], F32, O_E + 52 * KB)
        pstg = sb(g + "pstg", [128, NTOK], F32, O_E + 56 * KB)
        vT = sb(g + "vT", [128, NTOK], BF16, O_E + 60 * KB)
        vT2 = sb(g + "vT2", [128, NTOK], BF16, O_E + 68 * KB)
        vTs = [vT, vT2]
        rstd = sb(g + "rstd", [128, NTOK], F32, O_E + 40 * KB)
        xstg = [sb(g + f"xstg{i}", [128, NTOK], F32, O_E + 24 * KB + i * 4 * KB) for i in range(2)]
        sq = [sb(g + f"sq{i}", [128, NTOK], BF16, O_E + 32 * KB + i * 2 * KB) for i in range(2)]
        hT = sb(g + "hT", [128, 8, NTOK], BF16, O_B)
        mrg = sb(g + "mrg", [128, 8, NTOK], BF16, O_C)
        for t in [x1T, x2T, ystg, pstg, vT, vT2, rstd, hT, mrg] + Ys + vtok + ysa + ysb + [yt1, yt2, yt3, yt4] + xstg + sq:
            fw.alias(t, dead_in)

        if g not in prefetched:
            prefetch_x(g, dead_in)
        xc = prefetched[g]
        pss = [nextps(), nextps()]
        for c in range(8):
            s_ = sq[c % 2]
            ACT(lambda c=c, s_=s_: nc.scalar.activation(out=s_[:], in_=xc[c][:], func=AF.Square), [xc[c]], [s_])
            for tt in range(2):
                mm(pss[tt][:], ones[:], s_[:, tt * 512:(tt + 1) * 512], c == 0, c == 7, [ones, s_], [pss[tt]], True)
        rstd_from(pss, rstd, D)
        for c in range(8):
            DVE(lambda c=c: nc.vector.tensor_tensor(out=xc[c][:], in0=xc[c][:], in1=rstd[:], op=ALU.mult), [xc[c], rstd], [xc[c]])
            ACT(lambda c=c: nc.scalar.activation(out=hT[:, c, :], in_=xc[c][:], func=AF.Identity,
                                                 bias=modcol(sset, 0, c), scale=gm[:, sset, 0, c:c + 1]),
                [xc[c], mods, gm], [hT])
        for t in [x1T, x2T] + Ys + vtok + xstg:
            fw.alias(t, xc)
        for t in ysa:
            fw.alias(t, [rstd])
        yield "post_S1"
        if g == "P":
            for t in [x1T, x2T, ystg, pstg, vT, vT2, rstd, hT, mrg, yt1, yt2, yt3, yt4] + Ys + vtok + ysa + ysb + xstg + sq + xc:
                fw.alias(t, final_dead["S"])
            dead_in = dead_in + final_dead["S"]

        tap(g + "rstd", rstd, rstd[:], [128, NTOK])
        tap(g + "hT", hT, hT[:], [128, 8, NTOK], BF16)
        def dw_A(pa, pb2, big, wcols, stg_y):
            ACT(lambda: nc.scalar.activation(out=stg_y[:], in_=big[:, :], func=AF.Identity,
                                             bias=wcols[:, 3:4], scale=wcols[:, 1:2]), [pa, pb2, hyc, ffc], [stg_y])

        def dw_B(pa, pb2, big, wcols, stg_y):
            y3 = stg_y[:].rearrange("p (s l) -> p s l", l=L)
            p3 = big[:, :].rearrange("p (s l) -> p s l", l=L)
            DVE(lambda: nc.vector.scalar_tensor_tensor(out=y3[:, :, 1:L], in0=p3[:, :, 0:L - 1], scalar=wcols[:, 0:1],
                                                       in1=y3[:, :, 1:L], op0=ALU.mult, op1=ALU.add), [stg_y, pa, pb2, hyc, ffc], [stg_y])
            DVE(lambda: nc.vector.scalar_tensor_tensor(out=y3[:, :, 0:L - 1], in0=p3[:, :, 1:L], scalar=wcols[:, 2:3],
                                                       in1=y3[:, :, 0:L - 1], op0=ALU.mult, op1=ALU.add), [stg_y, pa, pb2, hyc, ffc], [stg_y])

        def run_pipeline(items):
            n = len(items)
            K = max(len(it) for it in items)
            for t in range(n + K - 1):
                for k in range(K - 1, -1, -1):
                    i = t - k
                    if 0 <= i < n and k < len(items[i]):
                        items[i][k]()

        def transposes_to_tok(srcT, src_ap_fn, dst_list, col0):
            for tk in range(8):
                fw.op("pe", lambda tk=tk: nc.tensor.transpose(psb[:, tk * 128:(tk + 1) * 128], src_ap_fn(tk), ident[:]),
                      reads=[srcT, ident], writes=[psb], inc=(tk == 7))
            for s in range(nseq):
                ACT(lambda s=s: nc.scalar.copy(out=dst_list[s][:, :, col0:col0 + 128],
                                               in_=psb[:, s * L:(s + 1) * L].rearrange("p (t c) -> p t c", c=128)),
                    [psb], [dst_list[s]])

        def proj_block(wt, j, writes_ps=None):
            pa, pb2, big = nextpair()
            for tt, p in enumerate((pa, pb2)):
                for kc in range(8):
                    mm(p[:], wt[:, kc, j * 128:(j + 1) * 128], hT[:, kc, tt * 512:(tt + 1) * 512], kc == 0, kc == 7, [wt, hT], [p], kc == 7)
            return pa, pb2, big

        items = []
        wts = {}
        for b in (3, 4, 5):
            for j in range(4):
                hc = (b - 3) * 4 + j
                yb = (ystg, pstg)[hc % 2]
                stt = {}

                def A(b=b, j=j, hc=hc, yb=yb, stt=stt):
                    if j == 0:
                        pinned.clear()
                        wt = next_wbuf()
                        fw.dma("pool", wt[:], win_d[:, :, b * 512:(b + 1) * 512], writes=[wt])
                        wts[b] = wt
                        pinned.add(wt.name)
                    stt["p"] = proj_block(wts[b], j)
                    dw_A(*stt["p"], hyc[:, hc, :], yb)

                def B(hc=hc, yb=yb, stt=stt):
                    dw_B(*stt["p"], hyc[:, hc, :], yb)

                def C(b=b, j=j, yb=yb):
                    if b == 3:
                        ACT(lambda: nc.scalar.copy(out=x1T[:, j, :], in_=yb[:]), [yb], [x1T])
                    elif b == 4:
                        ACT(lambda: nc.scalar.copy(out=x2T[:, j, :], in_=yb[:]), [yb], [x2T])
                    else:
                        vTj = vTs[j % 2]
                        ACT(lambda: nc.scalar.copy(out=vTj[:], in_=yb[:]), [yb], [vTj])
                        transposes_to_tok(vTj, lambda tk: vTj[:, tk * 128:(tk + 1) * 128], vtok, j * 128)
                items.append([A, B, C])
        run_pipeline(items)
        pinned.clear()

        tap(g + "x1T", x1T, x1T[:], [128, 4, NTOK], BF16)
        tap(g + "vtok0", vtok[0], vtok[0][:], [128, Lc, 512], BF16)
        pre_w = []
        for b in (0, 1, 2):
            wt = next_wbuf()
            fw.dma("pool", wt[:], win_d[:, :, b * 512:(b + 1) * 512], writes=[wt])
            pre_w.append(wt)
        def make_s2b(qT, kT, Vaug, kst):
            pieces = []
            for b in (0, 1):
                for j in range(4):
                    def piece(b=b, j=j):
                        wt = pre_w[b]
                        dstT = qT if b == 0 else kT
                        pa, pb2, _big = proj_block(wt, j)
                        for tt, p in enumerate((pa, pb2)):
                            sl = slice(tt * 512, (tt + 1) * 512)
                            if b == 1 and g == "P":
                                ks = kst[(2 * j + tt) % 4]
                                ACT(lambda p=p, ks=ks: nc.scalar.copy(out=ks[:], in_=p[:]), [p], [ks])
                                fw.dma("sp", kT_o[:, j, sl], ks[:], reads=[ks], writes=[OUT])
                                DVE(lambda ks=ks, sl=sl: nc.vector.tensor_copy(out=kT[:, j, sl], in_=ks[:]), [ks], [kT])
                            elif b == 0:
                                ACT(lambda p=p, sl=sl: nc.scalar.mul(out=dstT[:, j, sl], in_=p[:], mul=0.125), [p], [dstT])
                            else:
                                ACT(lambda p=p, sl=sl: nc.scalar.copy(out=dstT[:, j, sl], in_=p[:]), [p], [dstT])
                    pieces.append(piece)
            for tk in range(8):
                def piece(tk=tk):
                    wt = pre_w[2]
                    p = nextps()
                    for kc in range(8):
                        mm(p[:], hT[:, kc, tk * 128:(tk + 1) * 128], wt[:, kc, :], kc == 0, kc == 7, [hT, wt], [p], kc == 7)
                    p3 = p[:].rearrange("p (h d) -> p h d", d=64)
                    if g == "P":
                        ks = kst[tk % 4]
                        ACT(lambda: nc.scalar.copy(out=ks[:], in_=p[:]), [p], [ks])
                        fw.dma("sp", v_o[:, tk, :], ks[:], reads=[ks], writes=[OUT])
                        ACT(lambda: nc.scalar.copy(out=Vaug[:, tk, :, 0:64], in_=ks[:].rearrange("p (h d) -> p h d", d=64)), [ks], [Vaug])
                    else:
                        ACT(lambda: nc.scalar.copy(out=Vaug[:, tk, :, 0:64], in_=p3), [p], [Vaug])
                pieces.append(piece)
            return pieces

        s2b_pieces = []
        early_att = None
        if g == "P":
            curA = [O_A]

            def sba_(name, shape, dt):
                nb = int(np.prod(shape[1:])) * (2 if dt == BF16 else 4)
                t = sb(g + name, shape, dt, curA[0])
                curA[0] += (nb + 63) // 64 * 64
                fw.alias(t, dead_in)
                return t
            qT_ = sba_("qT", [128, 4, NTOK], BF16)
            kT_ = sba_("kT", [128, 4, NTOK], BF16)
            Vaug_ = sba_("Vaug", [128, 8, 8, 65], BF16)
            kst_ = [sba_(f"kst{i}", [128, 512], F32) for i in range(4)]
            POOL(lambda: nc.gpsimd.memset(Vaug_[:, :, :, 64:65], 1.0), [], [Vaug_])
            early_att = (qT_, kT_, Vaug_, kst_)
            s2b_pieces = make_s2b(*early_att)

        def s2b_hook():
            if s2b_pieces:
                s2b_pieces.pop(0)()

        Wt = Wm[L]
        Kt = Ktab[L]

        cm_i = [0]
        cm_t = [(ysa[0], ysb[0], yt1, yt2), (ysa[1], ysb[1], yt3, yt4)]
        if g == "S":
            late["slots"] = [sb("wadaA", [128, 8, 512], BF16, O_C), sb("wadaB", [128, 8, 512], BF16, O_C + 8 * KB)]
            for t in late["slots"]:
                fw.alias(t, dead_in)

        def conv_A(s, o):
            u = vtok[s]
            Y = Ys[s % nY]
            yb0 = 0
            for i in range(Lc):
                if g == "S":
                    mods_late_tick()
                pA, pB = nextps(), nextps()
                for (p, fr) in ((pA, i), (pB, Lc + i)):
                    for tc in range(Lc):
                        mm(p[:], Wt[:, tc, fr * 128:(fr + 1) * 128], u[:, tc, :], tc == 0, tc == Lc - 1, [Wt, u], [p], tc == Lc - 1)
                KR = Kt[:, i, 0, o, :]
                KI = Kt[:, i, 1, o, :]
                cm_i[0] += 1
                tA, tB, tC, tD = cm_t[cm_i[0] % 2]
                DVE(lambda pA=pA, tA=tA: nc.vector.tensor_tensor(out=tA[:], in0=pA[:], in1=KR, op=ALU.mult), [pA, Kt], [tA])
                DVE(lambda pB=pB, tB=tB: nc.vector.tensor_tensor(out=tB[:], in0=pB[:], in1=KI, op=ALU.mult), [pB, Kt], [tB])
                DVE(lambda pA=pA, tC=tC: nc.vector.tensor_tensor(out=tC[:], in0=pA[:], in1=KI, op=ALU.mult), [pA, Kt], [tC])
                DVE(lambda pB=pB, tD=tD: nc.vector.tensor_tensor(out=tD[:], in0=pB[:], in1=KR, op=ALU.mult), [pB, Kt], [tD])
                DVE(lambda i=i, tA=tA, tB=tB: nc.vector.tensor_tensor(out=Y[:, yb0 + i, :], in0=tA[:], in1=tB[:], op=ALU.subtract), [tA, tB], [Y])
                DVE(lambda i=i, tC=tC, tD=tD: nc.vector.tensor_tensor(out=Y[:, yb0 + Lc + i, :], in0=tC[:], in1=tD[:], op=ALU.add), [tC, tD], [Y])

        def conv_B(s, o, mulT):
            Y = Ys[s % nY]
            yb0 = 0
            Nn = min(L, 512)
            for cc in range(4):
                for th in range(L // Nn):
                    p = nextps()
                    for fr in range(2 * Lc):
                        col = (fr // Lc) * L + th * Nn
                        mm(p[:, 0:Nn], Y[:, yb0 + fr, cc * 128:(cc + 1) * 128], Wt[:, fr % Lc, col:col + Nn], fr == 0, fr == 2 * Lc - 1,
                           [Y, Wt], [p], fr == 2 * Lc - 1)
                    t0 = s * L + th * Nn
                    DVE(lambda p=p, cc=cc, t0=t0: nc.vector.tensor_tensor(out=mulT[:, cc, t0:t0 + Nn], in0=p[:, 0:Nn],
                                                                          in1=mulT[:, cc, t0:t0 + Nn], op=ALU.mult), [p, mulT], [mulT])

        def run_convs(o, mulT):
            conv_A(0, o)
            s2b_hook()
            for s_ in range(nseq):
                if s_ + 1 < nseq:
                    conv_A(s_ + 1, o)
                    s2b_hook()
                conv_B(s_, o, mulT)
                s2b_hook()

        run_convs(0, x1T)
        for cc in range(4):
            transposes_to_tok(x1T, lambda tk, cc=cc: x1T[:, cc, tk * 128:(tk + 1) * 128], vtok, cc * 128)
        run_convs(1, x2T)
        if g == "S":
            mods_final(1)
            fw.alias(mrg, late["slots"])
        for t in sq:
            fw.alias(t, Ys + xstg)
        fw.alias(rstd, ysa)
        pss = [nextps(), nextps()]
        for cc in range(4):
            s_ = sq[cc % 2]
            ACT(lambda s_=s_, cc=cc: nc.scalar.activation(out=s_[:], in_=x2T[:, cc, :], func=AF.Square), [x2T], [s_])
            for tt in range(2):
                mm(pss[tt][:], ones[:], s_[:, tt * 512:(tt + 1) * 512], cc == 0, cc == 3, [ones, s_], [pss[tt]], True)
        rstd_from(pss, rstd, 512)
        for cc in range(4):
            if g == "S":
                DVE(lambda cc=cc: nc.vector.scalar_tensor_tensor(out=x2T[:, cc, :], in0=x2T[:, cc, :], scalar=gains[:, 28 + cc:29 + cc],
                                                                 in1=rstd[:], op0=ALU.mult, op1=ALU.mult), [x2T, gains, rstd], [x2T])
                for tk in range(8):
                    fw.op("pe", lambda tk=tk, cc=cc: nc.tensor.transpose(psb[:, tk * 128:(tk + 1) * 128], x2T[:, cc, tk * 128:(tk + 1) * 128], ident[:]),
                          reads=[x2T, ident], writes=[psb], inc=(tk == 7))
                ACT(lambda cc=cc: nc.scalar.copy(out=mrg[:, :, 512 + cc * 128:512 + (cc + 1) * 128],
                                                 in_=psb[:, :].rearrange("p (t c) -> p t c", c=128)), [psb], [mrg])
            else:
                DVE(lambda cc=cc: nc.vector.scalar_tensor_tensor(out=mrg[:, 4 + cc, :], in0=x2T[:, cc, :], scalar=gains[:, 28 + cc:29 + cc],
                                                                 in1=rstd[:], op0=ALU.mult, op1=ALU.mult), [x2T, gains, rstd], [mrg])
        dead_h = [x1T, x2T, ystg, pstg, vT, vT2, rstd, yt1, yt2, yt3, yt4] + Ys + vtok + ysa + ysb + xstg + sq
        if g == "S":
            dead_h += [Wm[1024], Ktab[1024]]

        cur = [O_E]

        def sbe(name, shape, dt):
            nb = int(np.prod(shape[1:])) * (2 if dt == BF16 else 4)
            t = sb(g + name, shape, dt, cur[0])
            cur[0] += (nb + 63) // 64 * 64
            return t
        if early_att is not None:
            qT, kT, Vaug, kst = early_att
        else:
            qT = sbe("qT", [128, 4, NTOK], BF16)
            kT = sbe("kT", [128, 4, NTOK], BF16)
            Vaug = sbe("Vaug", [128, 8, 8, 65], BF16)
            kst = []
        ckT = sbe("ckT", [128, 4, 256], BF16)
        cVaug = sbe("cVaug", [128, 2, 8, 65], BF16)
        PT = [sbe(f"PT{i}", [128, 896], BF16) for i in range(2)]
        atok = sbe("atok", [128, 512], F32)
        an = sbe("an", [128, 512], BF16)
        if g == "S":
            tabs = {"E": sbe("tabE", [128, 8, 1024], BF16), "I": sbe("tabI", [128, 8, 1024], BF16)}
        else:
            PT.append(sbe("PT2", [128, 896], BF16))
            PT.append(sbe("PT3", [128, 896], BF16))
            tabs = {"E": PT[0], "I": PT[0]}
        att_t = [qT, kT, Vaug, ckT, cVaug, atok, an, tabs["E"], tabs["I"]] + PT + kst
        for t in att_t:
            fw.alias(t, dead_h + dead_in)
        if g == "S":
            fw.dma("pool", ckT[:], ckT_d, writes=[ckT])
            fw.dma("pool", cVaug[:, :, :, 0:64], cv_d, writes=[cVaug])
            fw.dma("pool", tabs["E"][:], tabE_d, writes=[tabs["E"]])
            fw.dma("pool", tabs["I"][:], tabI_d, writes=[tabs["I"]])
            POOL(lambda: nc.gpsimd.memset(cVaug[:, :, :, 64:65], 1.0), [], [cVaug])
            for nm in ("E", "I"):
                for hq in range(4):
                    ACT(lambda nm=nm, hq=hq: nc.scalar.activation(out=tabs[nm][:, 2 * hq:2 * hq + 2, :], in_=tabs[nm][:, 2 * hq:2 * hq + 2, :],
                                                                 func=AF.Exp), [tabs[nm]], [tabs[nm]])
        if early_att is None:
            POOL(lambda: nc.gpsimd.memset(Vaug[:, :, :, 64:65], 1.0), [], [Vaug])
            s2b_pieces = make_s2b(qT, kT, Vaug, kst)
        while s2b_pieces:
            s2b_pieces.pop(0)()

        gb = sb(g + "gb", [128, 512], F32, O_B)
        fw.alias(gb, [hT] + dead_in)
        if g == "S":
            fw.dma("sp", gb[:], gbrow_d.partition_broadcast(128), writes=[gb])
        rot["banks"] = [2, 3, 4, 5]
        rot["i"] = 0
        kp_of = {0: [0, 1, 2, 3], 1: [0, 1, 2, 3], 2: [0, 1, 2, 3, 4], 3: [1, 2, 3, 4, 5], 4: [2, 3, 4, 5, 6],
                 5: [3, 4, 5, 6, 7], 6: [4, 5, 6, 7], 7: [4, 5, 6, 7]}
        SC = 1.0
        O = [psf[0], psf[1]]
        Osb = sbe("Osb", [128, 520], F32)
        fw.alias(Osb, dead_h + dead_in)
        if g == "P":
            jobs = [(tk, h) for tk in range(8) for h in (0, 1, 4, 5)]
        else:
            jobs = [(tk, h) for tk in range(8) for h in range(8)]
        st = {}

        def slots_of(tk):
            if g == "P":
                s_ = tk // 2
                return [("k", 2 * s_, 0), ("k", 2 * s_ + 1, 0), ("k", 2 * s_, 2), ("k", 2 * s_ + 1, 2)]
            return [("b", kt, 0) for kt in reversed(kp_of[tk])] + [("c", 0, 0), ("c", 1, 0)]

        def emit_scores(ji):
            tk, h = jobs[ji]
            slots = slots_of(tk)
            ns = len(slots)
            pA = nextps()
            pB = nextps() if ns > 4 else None
            for si, (kind, kt, dh_) in enumerate(slots):
                hh_ = h + dh_
                c, pb_ = hh_ // 2, 64 * (hh_ % 2)
                q_ap = qT[pb_:pb_ + 64, c, tk * 128:(tk + 1) * 128]
                p = pA if si < 4 else pB
                o_ap = p[:, (si % 4) * 128:(si % 4) * 128 + 128]
                if kind == "c":
                    mm(o_ap, ckT[pb_:pb_ + 64, c, kt * 128:(kt + 1) * 128], q_ap, True, True, [ckT, qT], [p], True)
                else:
                    k_ap = kT[pb_:pb_ + 64, c, kt * 128:(kt + 1) * 128]
                    mm(o_ap, k_ap, q_ap, True, True, [kT, qT], [p], True)
            pt = PT[ji % len(PT)]
            n1 = min(ns, 4)
            ACT(lambda: nc.scalar.activation(out=pt[:, 0:n1 * 128], in_=pA[:, 0:n1 * 128], func=AF.Exp, scale=SC), [pA], [pt])
            if ns > 4:
                ACT(lambda: nc.scalar.activation(out=pt[:, 512:ns * 128], in_=pB[:, 0:(ns - 4) * 128], func=AF.Exp, scale=SC), [pB], [pt])
            if g == "S":
                nb = ns - 2
                tab = tabs["E"] if tk in (0, 1, 6, 7) else tabs["I"]
                kt0 = slots[0][1]
                i0 = 7 - (2 * kt0 - 2 * tk)
                DVE(lambda: nc.vector.tensor_tensor(out=pt[:, 0:nb * 128], in0=pt[:, 0:nb * 128],
                                                    in1=tab[:, h, i0 * 64:i0 * 64 + nb * 128], op=ALU.mult), [pt, tab], [pt])
            st[ji] = (pt, slots)

        def emit_pv(ji):
            tk, h = jobs[ji]
            pt, slots = st.pop(ji)
            ns = len(slots)
            for dh_ in sorted(set(sl_[2] for sl_ in slots)):
                hh_ = h + dh_
                ob = O[hh_ // 4]
                o_ap = ob[:, (hh_ % 4) * 65:(hh_ % 4) * 65 + 65]
                idx = [si for si, sl_ in enumerate(slots) if sl_[2] == dh_]
                for n_, si in enumerate(idx):
                    kind, kt, _ = slots[si]
                    if kind == "c":
                        v_ap, vt = cVaug[:, kt, hh_, :], cVaug
                    else:
                        v_ap, vt = Vaug[:, kt, hh_, :], Vaug
                    mm(o_ap, pt[:, si * 128:(si + 1) * 128], v_ap, n_ == 0, n_ == len(idx) - 1, [pt, vt], [ob], n_ == len(idx) - 1)

        def emit_tail_a(tk):
            for hb in range(2):
                ACT(lambda hb=hb: nc.scalar.copy(out=Osb[:, hb * 260:(hb + 1) * 260], in_=O[hb][:, 0:260]), [O[hb]], [Osb])
            o3 = Osb[:].rearrange("p (h d) -> p h d", d=65)
            DVE(lambda: nc.vector.reciprocal(out=smalls[:, 0:8], in_=o3[:, :, 64]), [Osb], [smalls])
            for hh in range(8):
                DVE(lambda hh=hh: nc.vector.tensor_scalar(out=atok[:, hh * 64:(hh + 1) * 64], in0=o3[:, hh, 0:64],
                                                          scalar1=smalls[:, hh:hh + 1], scalar2=None, op0=ALU.mult),
                    [Osb, smalls], [atok])

        def emit_tail_a2(tk):
            ACT(lambda: nc.scalar.activation(out=an[:], in_=atok[:], func=AF.Square, accum_out=smalls[:, 8:9]), [atok], [an, smalls])
            ACT(lambda: nc.scalar.activation(out=smalls[:, 9:10], in_=smalls[:, 8:9], func=AF.Ln, bias=epsc[:, 0:1], scale=float(1.0 / 512)), [smalls, epsc], [smalls])
            ACT(lambda: nc.scalar.activation(out=smalls[:, 10:11], in_=smalls[:, 9:10], func=AF.Exp, scale=-0.5), [smalls], [smalls])
            if g == "S":
                DVE(lambda: nc.vector.scalar_tensor_tensor(out=mrg[:, tk, 0:512], in0=atok[:], scalar=smalls[:, 10:11], in1=gb[:],
                                                           op0=ALU.mult, op1=ALU.mult), [atok, smalls, gb], [mrg])
            else:
                DVE(lambda: nc.vector.tensor_scalar(out=an[:], in0=atok[:], scalar1=smalls[:, 10:11], scalar2=None, op0=ALU.mult), [atok, smalls], [an])

        def emit_tail_b(tk):
            if g == "S":
                return
            for c4 in range(4):
                fw.op("pe", lambda c4=c4: nc.tensor.transpose(psb[:, c4 * 128:(c4 + 1) * 128], an[:, c4 * 128:(c4 + 1) * 128], ident[:]),
                      reads=[an, ident], writes=[psb], inc=(c4 == 3))
            for c4 in range(4):
                DVE(lambda c4=c4: nc.vector.tensor_scalar(out=mrg[:, c4, tk * 128:(tk + 1) * 128], in0=psb[:, c4 * 128:(c4 + 1) * 128],
                                                          scalar1=gains[:, 24 + c4:25 + c4], scalar2=None, op0=ALU.mult), [psb, gains], [mrg])

        Dp = len(PT) - 1
        sched = []
        nj = len(jobs)
        for ji in range(min(Dp, nj)):
            emit_scores(ji)
        for ji in range(nj):
            if ji + Dp < nj:
                emit_scores(ji + Dp)
            emit_pv(ji)
            tk, h = jobs[ji]
            while sched and sched[0][0] <= ji:
                sched.pop(0)[1]()
            if ji + 1 == nj or jobs[ji + 1][0] != tk:
                emit_tail_a(tk)
                sched.append((ji + 2, lambda tk=tk: emit_tail_a2(tk)))
                sched.append((ji + 4, lambda tk=tk: emit_tail_b(tk)))
        while sched:
            sched.pop(0)[1]()
        rot["banks"] = [0, 1, 2, 3, 4, 5]
        rot["i"] = 0
        dead_a = att_t + [hT, Osb]

        tap(g + "mrg", mrg, mrg[:], [128, 8, NTOK], BF16)

        if g == "S":
            yield from s_tail(dead_a + dead_h + dead_in + [gb], mrg)
            return
        xresc = [sb(g + f"xres{c}", [128, NTOK], F32, O_A + c * 4 * KB) for c in range(8)]
        fstg = [sb(g + f"fstg{i}", [128, NTOK], F32, O_E + 44 * KB + i * 4 * KB) for i in range(4)]
        rstd2 = sb(g + "rstd2", [128, NTOK], F32, O_E + 60 * KB)
        sq2 = [sb(g + f"sq2{i}", [128, NTOK], BF16, O_E + 64 * KB + i * 2 * KB) for i in range(2)]
        actT = sb(g + "actT", [128, 22, NTOK], BF16, O_E)
        hT2 = sb(g + "hT2", [128, 8, NTOK], BF16, O_B)
        for t in xresc + [rstd2, actT, hT2] + fstg + sq2:
            fw.alias(t, dead_a + dead_h + dead_in)
        def stats_chunk(pss_, c):
            s_ = sq2[c % 2]
            ACT(lambda: nc.scalar.activation(out=s_[:], in_=xresc[c][:], func=AF.Square), [xresc[c]], [s_])
            for tt in range(2):
                mm(pss_[tt][:], ones[:], s_[:, tt * 512:(tt + 1) * 512], c == 0, c == 7, [ones, s_], [pss_[tt]], True)

        pss6 = [psf[6], psf[5]]
        rot["banks"] = [0, 1, 2, 3, 4]
        rot["i"] = 0
        for b in range(2):
            wt = next_wbuf()
            fw.dma("pool", wt[:], wout_d[:, :, b * 512:(b + 1) * 512], writes=[wt])
            for j in range(4):
                cj = b * 4 + j
                xs = fstg[cj % 2]
                fw.dma("sp", xs[:], x_d[:, cj, :], writes=[xs])
                for tt in range(2):
                    p = nextps()
                    sl = slice(tt * 512, (tt + 1) * 512)
                    for kc in range(8):
                        mm(p[:], wt[:, kc, j * 128:(j + 1) * 128], mrg[:, kc, sl], kc == 0, kc == 7, [wt, mrg], [p], kc == 7)
                    DVE(lambda p=p, cj=cj, sl=sl, xs=xs: nc.vector.scalar_tensor_tensor(
                        out=xresc[cj][:, sl], in0=p[:], scalar=modcol(sset, 2, cj), in1=xs[:, sl], op0=ALU.mult, op1=ALU.add),
                        [p, mods, xs], [xresc[cj]])
                if cj >= 1:
                    stats_chunk(pss6, cj - 1)
        stats_chunk(pss6, 7)

        rot["banks"] = [0, 1, 2, 3, 4, 5]
        rot["i"] = 0
        rstd_from(pss6, rstd2, D)
        for c in range(8):
            ft = fstg[c % 2]
            DVE(lambda c=c, ft=ft: nc.vector.tensor_tensor(out=ft[:], in0=xresc[c][:], in1=rstd2[:], op=ALU.mult), [xresc[c], rstd2], [ft])
            ACT(lambda c=c, ft=ft: nc.scalar.activation(out=hT2[:, c, :], in_=ft[:], func=AF.Identity,
                                                        bias=modcol(sset, 3, c), scale=gm[:, sset, 1, c:c + 1]), [ft, mods, gm], [hT2])

        items = []
        wts = {}
        for blk in range(11):
            for i in range(2):
                jj = 2 * blk + i
                for which, jcol in ((0, i), (1, 2 + i)):
                    fcx = which * 22 + jj
                    sy = fstg[2 * which + (jj % 2)]
                    stt = {}

                    def A(blk=blk, i=i, which=which, jcol=jcol, fcx=fcx, sy=sy, stt=stt):
                        if i == 0 and which == 0:
                            wt = next_wbuf()
                            fw.dma("pool", wt[:], wup_d[:, :, blk * 512:(blk + 1) * 512], writes=[wt])
                            wts[blk] = wt
                        wt = wts[blk]
                        pa, pb2, big = nextpair()
                        for tt, p in enumerate((pa, pb2)):
                            for kc in range(8):
                                mm(p[:], wt[:, kc, jcol * 128:(jcol + 1) * 128], hT2[:, kc, tt * 512:(tt + 1) * 512], kc == 0, kc == 7, [wt, hT2], [p], kc == 7)
                        stt["p"] = (pa, pb2, big)
                        dw_A(pa, pb2, big, ffc[:, fcx, :], sy)

                    def B(fcx=fcx, sy=sy, stt=stt):
                        dw_B(*stt["p"], ffc[:, fcx, :], sy)

                    def C(which=which, jj=jj, sy=sy):
                        if which == 0:
                            ACT(lambda: nc.scalar.activation(out=sy[:], in_=sy[:], func=AF.Silu), [sy], [sy])
                        else:
                            gs = fstg[jj % 2]
                            DVE(lambda: nc.vector.tensor_tensor(out=actT[:, jj, :], in0=gs[:], in1=sy[:], op=ALU.mult), [gs, sy], [actT])
                    items.append([A, B, C])
        run_pipeline(items)

        tap(g + "actT", actT, actT[:], [128, 22, NTOK], BF16)
        pss9 = [psf[6], psf[5]]
        rot["banks"] = [0, 1, 2, 3, 4]
        rot["i"] = 0
        for cj in range(8):
            wt = next_wbuf()
            wflat = wt[:].rearrange("p a b -> p (a b)")
            fw.dma("pool", wflat[:, 0:22 * 128], wdn_d[:, cj, :], writes=[wt])
            for tt in range(2):
                p = nextps()
                sl = slice(tt * 512, (tt + 1) * 512)
                for kk in range(22):
                    mm(p[:], wflat[:, kk * 128:(kk + 1) * 128], actT[:, kk, sl], kk == 0, kk == 21, [wt, actT], [p], kk == 21)
                DVE(lambda p=p, cj=cj, sl=sl: nc.vector.scalar_tensor_tensor(
                    out=xresc[cj][:, sl], in0=p[:], scalar=modcol(sset, 5, cj), in1=xresc[cj][:, sl], op0=ALU.mult, op1=ALU.add),
                    [p, mods, xresc[cj]], [xresc[cj]])
            if cj >= 1:
                stats_chunk(pss9, cj - 1)

        stats_chunk(pss9, 7)
        if g == "S":
            live["S"] = [actT, hT2, mrg] + dead_a + dead_h + dead_in
            prefetch_x("P", live["S"])
            yield "pre_S9"
        rstd_from(pss9, rstd2, D)
        rot["banks"] = [0, 1, 2, 3, 4, 5]
        rot["i"] = 0
        for c in range(8):
            ft = fstg[c % 4]
            DVE(lambda c=c, ft=ft: nc.vector.scalar_tensor_tensor(out=ft[:], in0=xresc[c][:], scalar=gains[:, 16 + c:17 + c],
                                                                 in1=rstd2[:], op0=ALU.mult, op1=ALU.mult), [xresc[c], gains, rstd2], [ft])
            fw.dma("sp", yT_o[g][:, c, :], ft[:], reads=[ft], writes=[OUT])
        final_dead[g] = xresc + [rstd2, actT, hT2, mrg] + fstg + sq2 + dead_a + dead_h

    for _ in range(3):
        mods_early_issue()
    dead = filter_phase(1024, [])
    dead = dead + filter_phase(256, dead)
    fw.alias(Wm[1024], dead)
    fw.dma("sp", Wm[256][:], W_d[256], writes=[Wm[256]])
    fw.dma("sp", Wm[1024][:], W_d[1024], writes=[Wm[1024]])
    mods_final(0)
    gS = group("S", dead)
    assert next(gS) == "post_S1"
    assert next(gS) == "pre_S9"
    gP = group("P", dead + live["S"])
    assert next(gP) == "post_S1"
    for _ in gS:
        pass
    for _ in gP:
        pass
    fw.finish()
    return nc


_CACHE = {}


def _fm(v, nch):
    return np.ascontiguousarray(v.reshape(nch, 128).T)


def _wk(w):
    K, N = w.shape
    return np.ascontiguousarray(w.reshape(K // 128, 128, N).transpose(1, 0, 2))


def _consts():
    if "c" in _CACHE:
        return _CACHE["c"]
    c = {}
    for L in (1024, 256):
        t = np.arange(L, dtype=np.float64)
        f = np.arange(L, dtype=np.float64)
        om = np.pi * (f + 0.5) / L
        Cs = np.cos(np.outer(t + 0.5, om))
        Ss = np.sin(np.outer(t + 0.5, om))
        W = np.concatenate([Cs, -Ss], 1)
        Wu = np.concatenate([np.cos(np.outer(t, om)), np.sin(np.outer(t, om))], 1) / L
        c[f"W{L}"] = _wk(W.astype(np.float32)).astype(ml_dtypes.bfloat16)
        c[f"Wu{L}"] = _wk(Wu.astype(np.float32)).astype(ml_dtypes.bfloat16)
        t32 = np.arange(L, dtype=np.float32) / np.float32(L)
        fr = np.arange(1, 9, dtype=np.float32)
        ang = np.float32(2.0 * np.pi) * t32[:, None] * fr[None, :]
        z = np.concatenate([t32[:, None], np.cos(ang), np.sin(ang)], -1).astype(np.float32)
        zT = np.concatenate([z.T, np.ones((1, L), np.float32)], 0)
        c[f"zT{L}"] = np.ascontiguousarray(zT)
        min_decay = np.log(1e-2) / 1.5
        max_decay = np.log(1e-2) / 0.3
        deltas = np.abs(np.linspace(min_decay, max_decay, 512, dtype=np.float32))
        decay = np.exp(-t32[:, None] * deltas[None, :]).astype(np.float32)
        decayB = decay.copy()
        decayB[0] = 0.0
        dd = np.concatenate([decay, decayB], 1)
        c[f"dec{L}"] = _wk(dd)
    c["ident"] = np.eye(128, dtype=np.float32).astype(ml_dtypes.bfloat16)
    e0 = np.zeros((128, 1), np.float32)
    e0[0, 0] = -1.0
    c["e0n"] = e0
    kc = np.arange(64)[:, None]
    qc = np.arange(64)[None, :]
    cstart = np.clip(qc - 8, 0, 48)
    colin = (kc >= cstart) & (kc < cstart + 16)
    dc = np.clip(kc - qc, -15, 15) + 15
    idxE = np.zeros((128, 16, 64), np.int64)
    idxI = np.zeros((128, 16, 64), np.int64)
    SENT = 15 * 31
    for half in range(2):
        for i in range(16):
            dl = (7 - i) if half == 0 else (8 - i)
            for tab, ok in ((idxE, abs(dl) <= 7), (idxI, -4 <= dl <= 3)):
                if ok:
                    tab[half * 64:(half + 1) * 64, i, :] = np.where(colin, (dl + 7) * 31 + dc, SENT)
                else:
                    tab[half * 64:(half + 1) * 64, i, :] = SENT
    c["idxE"], c["idxI"] = idxE, idxI
    _CACHE["c"] = c
    return c


def _prepare(x_prompt, x_sample, cache_ctx_k, cache_ctx_v, c, c_ctx, w_ada, b_ada, norm1_g,
           w_in, rpb, hy_conv_w, hy_conv_b, filt_w1, filt_b1, filt_w2, filt_b2, filt_w3,
           filt_b3, filt_freq, filt_bias, grp_norm_g, w_out, norm2_g, w_up, ffn_conv_w,
           ffn_conv_b, w_down, final_g):
    f32 = np.float32
    A = lambda a: np.asarray(a, dtype=f32)
    cs = _consts()
    x_prompt, x_sample = A(x_prompt), A(x_sample)
    shared = {}
    shared["w_ada"] = _wk(A(w_ada)[0])
    shared["b_ada"] = _fm(A(b_ada)[0], 48)
    shared["gains"] = np.concatenate([_fm(A(norm1_g)[0], 8), _fm(A(norm2_g)[0], 8), _fm(A(final_g), 8), _fm(A(grp_norm_g)[0], 8)], 1)
    shared["w_in"] = _wk(A(w_in)[0])
    hw, hb = A(hy_conv_w)[0], A(hy_conv_b)[0]
    shared["hyc"] = np.ascontiguousarray(np.stack([_fm(hw[0], 12), _fm(hw[1], 12), _fm(hw[2], 12), _fm(hb, 12)], -1))
    fwc, fbc = A(ffn_conv_w)[0], A(ffn_conv_b)[0]
    shared["ffc"] = np.ascontiguousarray(np.stack([_fm(fwc[0], 44), _fm(fwc[1], 44), _fm(fwc[2], 44), _fm(fbc, 44)], -1))
    shared["w_out"] = _wk(A(w_out)[0])
    wu = A(w_up)[0]
    cols = []
    for blk in range(11):
        for i in range(2):
            cols.append(np.arange((2 * blk + i) * 128, (2 * blk + i + 1) * 128))
        for i in range(2):
            cols.append(DFF + np.arange((2 * blk + i) * 128, (2 * blk + i + 1) * 128))
    shared["w_up"] = _wk(np.ascontiguousarray(wu[:, np.concatenate(cols)]))
    wd = A(w_down)[0]
    wd4 = wd.reshape(22, 128, 8, 128)
    shared["w_down"] = np.ascontiguousarray(wd4.transpose(1, 2, 0, 3).reshape(128, 8, 22 * 128))
    rp = np.concatenate([A(rpb)[0].reshape(8, 15 * 31), np.full((8, 1), NEGM, f32)], 1)
    shared["tabE"] = np.ascontiguousarray(rp[:, cs["idxE"]].transpose(1, 0, 2, 3).reshape(128, 8, 1024))
    shared["tabI"] = np.ascontiguousarray(rp[:, cs["idxI"]].transpose(1, 0, 2, 3).reshape(128, 8, 1024))
    shared["w1b1"] = np.ascontiguousarray(np.concatenate([A(filt_w1)[0], A(filt_b1)[0][None, :]], 0))
    shared["w2"] = np.ascontiguousarray(A(filt_w2)[0])
    shared["fsm"] = np.ascontiguousarray(np.stack([A(filt_b2)[0], A(filt_freq)[0], np.zeros(64, f32), np.zeros(64, f32)], 1))
    shared["w3b3"] = np.ascontiguousarray(np.concatenate([A(filt_w3)[0], A(filt_b3)[0][None, :]], 0))
    shared["fb"] = np.ascontiguousarray(A(filt_bias)[0].reshape(1, 1024))
    shared["gbrow"] = np.ascontiguousarray(A(grp_norm_g)[0][None, 0:512])
    for k in ("zT1024", "zT256", "dec1024", "dec256", "W1024", "W256", "Wu1024", "Wu256", "ident", "e0n"):
        shared[k] = cs[k]
    ck, cv = A(cache_ctx_k), A(cache_ctx_v)
    cc_, cctx = A(c), A(c_ctx)
    in_maps = []
    for core in range(8):
        b = core % 4
        m = dict(shared)
        m["xsT"] = np.ascontiguousarray(x_sample[b].T.reshape(8, 128, NTOK).transpose(1, 0, 2))
        xp = x_prompt[core * 4:(core + 1) * 4].reshape(NTOK, D)
        m["xpT"] = np.ascontiguousarray(xp.T.reshape(8, 128, NTOK).transpose(1, 0, 2))
        m["csil"] = np.ascontiguousarray(np.stack([_fm(cctx, 8), _fm(cc_[b], 8)], -1))
        off = 512 * (core // 4)
        xpad = np.zeros((NTOK + 2, D), f32)
        xpad[1:NTOK + 1] = x_sample[b]
        m["xo"] = np.ascontiguousarray(xpad[off:off + 514].T.reshape(8, 128, 514).transpose(1, 0, 2))
        selm = np.zeros((NTOK, 514), f32)
        ii = np.arange(514)
        tt = off - 1 + ii
        ok = (tt >= 0) & (tt < NTOK)
        selm[tt[ok], ii[ok]] = 1.0
        m["sel"] = np.ascontiguousarray(selm.reshape(8, 128, 514).transpose(1, 0, 2)).astype(ml_dtypes.bfloat16)
        m["mrow"] = np.ascontiguousarray(ok.astype(f32)[None, :])
        kk = ck[b, 0]
        m["ckT"] = np.ascontiguousarray(kk.reshape(4, 2, 256, 64).transpose(1, 3, 0, 2).reshape(128, 4, 256))
        vv = cv[b, 0]
        m["cv"] = np.ascontiguousarray(vv.reshape(8, 2, 128, 64).transpose(2, 1, 0, 3))
        in_maps.append(m)
    return in_maps


def _assemble(R):
    f32 = np.float32
    y_prompt = np.zeros((32, 256, D), f32)
    y_sample = np.zeros((4, 1024, D), f32)
    sk = np.zeros((32, 1, 8, 256, 64), f32)
    sv = np.zeros((32, 1, 8, 256, 64), f32)
    for core in range(8):
        r = R[core]
        yp = np.asarray(r["ypT"]).transpose(1, 0, 2).reshape(D, NTOK).T
        y_prompt[core * 4:(core + 1) * 4] = yp.reshape(4, 256, D)
        off = 512 * (core // 4)
        y_sample[core % 4, off:off + 512] = np.asarray(r["ysT"]).transpose(1, 0, 2).reshape(D, 512).T
        kTo = np.asarray(r["kT_o"]).transpose(1, 0, 2).reshape(512, NTOK)
        sk[core * 4:(core + 1) * 4, 0] = kTo.reshape(8, 64, 4, 256).transpose(2, 0, 3, 1)
        vo = np.asarray(r["v_o"]).reshape(128, 8, 8, 64)
        vo = vo.transpose(1, 0, 2, 3).reshape(4, 256, 8, 64).transpose(0, 2, 1, 3)
        sv[core * 4:(core + 1) * 4, 0] = vo
    return (y_prompt, y_sample, sk, sv)


def kernel(**inputs):
    in_maps = _prepare(**inputs)
    if "nc" not in _CACHE:
        _CACHE["nc"] = build_program()
    res = run_bass_kernel_spmd(_CACHE["nc"], in_maps, core_ids=list(range(8)))
    return _assemble(res.results)
```

```python
import numpy as np
import ml_dtypes
import concourse.bass as bass
import concourse.mybir as mybir
from concourse.bass_utils import run_bass_kernel_spmd

F32 = mybir.dt.float32
BF16 = mybir.dt.bfloat16
I32 = mybir.dt.int32
AF = mybir.ActivationFunctionType
ALU = mybir.AluOpType

D = 1024
NTOK = 1024
DFF = 2816
EPS = 1e-6
NEGM = -30000.0
SAME_ENGINE_SYNC = True


class T:
    __slots__ = ("h", "lw", "rd", "name")

    def __init__(self, h, name=""):
        self.h = h
        self.lw = None
        self.rd = {}
        self.name = name

    def __getitem__(self, idx):
        return self.h[idx]


class FW:
    def __init__(self, nc, n_dma_sems=24):
        self.nc = nc
        self.eng = {"pe": nc.tensor, "act": nc.scalar, "dve": nc.vector,
                    "pool": nc.gpsimd, "sp": nc.sync}
        self.sem = {k: nc.alloc_semaphore(name=f"s_{k}") for k in self.eng}
        self.cnt = {k: 0 for k in self.eng}
        self.seen = {k: {} for k in self.eng}
        self.dsems = [nc.alloc_semaphore(name=f"d_{i}") for i in range(n_dma_sems)]
        self.dcnt = [0] * n_dma_sems
        self.dnext = {"sp": 0, "pool": 0}
        self.drange = {"sp": (0, n_dma_sems // 2), "pool": (n_dma_sems // 2, n_dma_sems)}

    def _wait(self, e, dep):
        kind, k, v = dep
        if kind == "e" and k == e and (e == "pe" or not SAME_ENGINE_SYNC):
            return
        key = (kind, k)
        if self.seen[e].get(key, 0) >= v:
            return
        self.seen[e][key] = v
        s = self.sem[k] if kind == "e" else self.dsems[k]
        self.eng[e].wait_ge(s, v)

    def _deps(self, reads, writes):
        deps = []
        for t in reads:
            if t.lw is not None:
                deps.append(t.lw)
        for t in writes:
            if t.lw is not None:
                deps.append(t.lw)
            deps.extend((k[0], k[1], v) for k, v in t.rd.items())
        return deps

    def op(self, e, fn, reads=(), writes=(), inc=True):
        for d in self._deps(reads, writes):
            self._wait(e, d)
        ins = fn()
        if inc:
            self.cnt[e] += 1
            ins.then_inc(self.sem[e], 1)
            me = ("e", e, self.cnt[e])
        else:
            me = ("e", e, self.cnt[e] + 1)
        self._mark(me, reads, writes)
        return ins

    def _mark(self, me, reads, writes):
        key = (me[0], me[1])
        for t in reads:
            if t.rd.get(key, 0) < me[2]:
                t.rd[key] = me[2]
        for t in writes:
            t.lw = me
            t.rd = {}

    def dma(self, q, out, in_, reads=(), writes=(), **kw):
        for d in self._deps(reads, writes):
            self._wait(q, d)
        lo, hi = self.drange[q]
        i = lo + self.dnext[q]
        self.dnext[q] = (self.dnext[q] + 1) % (hi - lo)
        if self.dcnt[i] > 0:
            self._wait(q, ("d", i, self.dcnt[i]))
        self.dcnt[i] += 16
        ins = self.eng[q].dma_start(out=out, in_=in_, **kw)
        ins.then_inc(self.dsems[i], 16)
        me = ("d", i, self.dcnt[i])
        self._mark(me, reads, writes)
        return me

    def alias(self, new, olds):
        for o in olds:
            ds = list((k[0], k[1], v) for k, v in o.rd.items())
            if o.lw is not None:
                ds.append(o.lw)
            for d in ds:
                key = (d[0], d[1])
                if new.rd.get(key, 0) < d[2]:
                    new.rd[key] = d[2]

    def finish(self):
        for k in ("pe", "act", "dve", "pool"):
            if self.cnt[k] > 0:
                self._wait("sp", ("e", k, self.cnt[k]))
        for i, c in enumerate(self.dcnt):
            if c > 0:
                self._wait("sp", ("d", i, c))


KB = 1024


DEBUG = False
TAPS = []


def build_program():
    nc = bass.Bass("TRN2", target_bir_lowering=False)
    fw = FW(nc)
    del TAPS[:]

    def tap(name, t, ap, shape, dt=F32):
        if not DEBUG:
            return
        d = nc.dram_tensor("tap_" + name, list(shape), dt, kind="ExternalOutput").ap()
        fw.dma("sp", d, ap, reads=[t], writes=[T(None)])
        TAPS.append("tap_" + name)

    def din(name, shape, dt=F32):
        return nc.dram_tensor(name, list(shape), dt, kind="ExternalInput").ap()

    def dout(name, shape):
        return nc.dram_tensor(name, list(shape), F32, kind="ExternalOutput").ap()

    xT = {"S": din("xsT", [128, 8, NTOK]), "P": din("xpT", [128, 8, NTOK])}
    csil_d = din("csil", [128, 8, 2])
    wada_d = din("w_ada", [128, 8, 6144])
    bada_d = din("b_ada", [128, 48])
    gains_d = din("gains", [128, 32])
    win_d = din("w_in", [128, 8, 3072])
    hyc_d = din("hyc", [128, 12, 4])
    ffc_d = din("ffc", [128, 44, 4])
    wout_d = din("w_out", [128, 8, 1024])
    wup_d = din("w_up", [128, 8, 5632])
    wdn_d = din("w_down", [128, 8, 22 * 128])
    ckT_d = din("ckT", [128, 4, 256])
    cv_d = din("cv", [128, 2, 8, 64])
    tabE_d = din("tabE", [128, 8, 1024])
    tabI_d = din("tabI", [128, 8, 1024])
    w1b1_d = din("w1b1", [18, 64])
    w2_d = din("w2", [64, 64])
    fsm_d = din("fsm", [64, 4])
    w3b3_d = din("w3b3", [65, 2048])
    fb_d = din("fb", [1, 1024])
    zT_d = {1024: din("zT1024", [18, 1024]), 256: din("zT256", [18, 256])}
    dec_d = {1024: din("dec1024", [128, 8, 1024]), 256: din("dec256", [128, 2, 1024])}
    W_d = {1024: din("W1024", [128, 8, 2048], BF16), 256: din("W256", [128, 2, 512], BF16)}
    Wu_d = {1024: din("Wu1024", [128, 8, 2048], BF16), 256: din("Wu256", [128, 2, 512], BF16)}
    ident_d = din("ident", [128, 128], BF16)
    e0n_d = din("e0n", [128, 1])
    yT_o = {"S": dout("ysT", [128, 8, 512]), "P": dout("ypT", [128, 8, NTOK])}
    xo_d = din("xo", [128, 8, 514])
    sel_d = din("sel", [128, 8, 514], BF16)
    mrow_d = din("mrow", [1, 514])
    gbrow_d = din("gbrow", [1, 512])
    kT_o = dout("kT_o", [128, 4, NTOK])
    v_o = dout("v_o", [128, 8, 512])
    OUT = T(None, "outs")

    base = (nc.sbuf_base + 63) // 64 * 64
    top = nc.sbuf_top

    def sb(name, shape, dt, off):
        nb = int(np.prod(shape[1:])) * (2 if dt == BF16 else 4)
        assert base + off + nb <= top, (name, off, nb, top - base)
        return T(nc.alloc_sbuf_tensor_at(name, list(shape), dt, offset=base + off), name)

    O_SM = 0
    ident = sb("ident", [128, 128], BF16, 0)
    ones = sb("ones", [128, 128], BF16, 256)
    mods = sb("mods", [128, 2, 48], F32, 512)
    gains = sb("gains", [128, 32], F32, 896)
    gm = sb("gm", [128, 2, 2, 8], F32, 1024)
    hyc = sb("hyc", [128, 12, 4], F32, 1152)
    ffc = sb("ffc", [128, 44, 4], F32, 1344)
    csil = sb("csil", [128, 8, 2], F32, 2048)
    csb = sb("csb", [128, 8, 2], BF16, 2112)
    bada = sb("bada", [128, 48], F32, 2176)
    epsc = sb("epsc", [128, 1], F32, 2368)
    e0n = sb("e0n", [128, 1], F32, 3200)
    fsm = sb("fsm", [64, 8], F32, 2400)
    w1b1 = sb("w1b1", [18, 64], F32, 2432)
    w2s = sb("w2s", [64, 64], F32, 2688)
    smalls = sb("smalls", [128, 64], F32, 2944)
    O_W256 = 5 * KB
    O_K256 = 7 * KB
    O_A = 15 * KB
    O_B = 79 * KB
    O_C = 95 * KB
    O_D = 111 * KB
    O_E = 135 * KB
    Wm = {256: sb("W256", [128, 2, 512], BF16, O_W256), 1024: sb("W1024", [128, 8, 2048], BF16, O_A)}
    Ktab = {256: sb("K256", [128, 2, 2, 2, 512], BF16, O_K256),
            1024: sb("K1024", [128, 8, 2, 2, 512], BF16, O_A + 32 * KB)}
    wbuf = [sb(f"wbuf{i}", [128, 8, 512], BF16, O_D + i * 8 * KB) for i in range(3)]
    wb_i = [0]

    pinned = set()

    def next_wbuf():
        while True:
            t = wbuf[wb_i[0] % 3]
            wb_i[0] += 1
            if t.name not in pinned:
                return t

    class HalfView:
        def __init__(self, big, off):
            self.big, self.off = big, off

        def __getitem__(self, idx):
            if not isinstance(idx, tuple):
                idx = (idx, slice(None))
            ps_, cs_ = idx
            a = 0 if cs_.start is None else cs_.start
            b = 512 if cs_.stop is None else cs_.stop
            return self.big[ps_, self.off + a:self.off + b]

    psd = [nc.alloc_psum_tensor(f"psd{i}", [128, 1024], F32) for i in range(3)]
    psf = [T(HalfView(psd[i // 2], 512 * (i % 2)), f"psf{i}") for i in range(6)]
    psf.append(T(nc.alloc_psum_tensor("psf6", [128, 512], F32), "psf6"))
    psb = T(nc.alloc_psum_tensor("psb", [128, 1024], BF16), "psb")
    rot = {"i": 0, "banks": [0, 1, 2, 3, 4, 5]}

    def nextps():
        b = rot["banks"][rot["i"] % len(rot["banks"])]
        rot["i"] += 1
        return psf[b]

    def nextpair():
        if rot["i"] % 2:
            rot["i"] += 1
        b = rot["banks"][rot["i"] % len(rot["banks"])]
        assert b % 2 == 0
        rot["i"] += 2
        return psf[b], psf[b + 1], psd[b // 2]

    ACT = lambda fn, r, w: fw.op("act", fn, reads=r, writes=w)
    DVE = lambda fn, r, w: fw.op("dve", fn, reads=r, writes=w)
    POOL = lambda fn, r, w: fw.op("pool", fn, reads=r, writes=w)

    def mm(ps_ap, lhsT, rhs, start, stop, reads, writes, last):
        return fw.op("pe", lambda: nc.tensor.matmul(ps_ap, lhsT=lhsT, rhs=rhs, start=start, stop=stop),
                     reads=reads, writes=writes, inc=last)

    fw.dma("sp", ident[:], ident_d, writes=[ident])
    fw.dma("sp", e0n[:], e0n_d, writes=[e0n])
    fw.dma("sp", csil[:], csil_d, writes=[csil])
    fw.dma("sp", bada[:], bada_d, writes=[bada])
    fw.dma("sp", gains[:], gains_d, writes=[gains])
    fw.dma("sp", hyc[:], hyc_d, writes=[hyc])
    fw.dma("sp", ffc[:], ffc_d, writes=[ffc])
    fw.dma("sp", fsm[:, 0:4], fsm_d, writes=[fsm])
    fw.dma("sp", w1b1[:], w1b1_d, writes=[w1b1])
    fw.dma("sp", w2s[:], w2_d, writes=[w2s])
    DVE(lambda: nc.vector.memset(ones[:], 1.0), [], [ones])
    DVE(lambda: nc.vector.memset(epsc[:], EPS), [], [epsc])
    ACT(lambda: nc.scalar.activation(out=csb[:], in_=csil[:], func=AF.Silu), [csil], [csb])
    DVE(lambda: nc.vector.tensor_scalar(out=fsm[:, 4:5], in0=fsm[:, 1:2], scalar1=float(1.0 / (2 * np.pi)),
                                        scalar2=None, op0=ALU.mult), [fsm], [fsm])

    pm = psf[6]
    mods_state = {"blk": 0}

    def mods_mm(blk, wt):
        for j in range(4):
            cj = blk * 4 + j
            for kc in range(8):
                mm(pm[:, cj * 2:cj * 2 + 2], wt[:, kc, j * 128:(j + 1) * 128], csb[:, kc, :],
                   kc == 0, kc == 7, [wt, csb], [pm], kc == 7)

    late = {"slots": None, "pend": []}
    early = {"pend": []}

    def mods_early_issue():
        blk = mods_state["blk"]
        if blk < 4:
            wt = next_wbuf()
            fw.dma("pool", wt[:], wada_d[:, :, blk * 512:(blk + 1) * 512], writes=[wt])
            early["pend"].append((blk, wt))
            mods_state["blk"] += 1

    def mods_early_tick():
        if early["pend"]:
            b0, w0 = early["pend"].pop(0)
            mods_mm(b0, w0)
            mods_early_issue()

    def mods_late_tick():
        blk = mods_state["blk"]
        if len(late["pend"]) == 2 or (blk >= 12 and late["pend"]):
            b0, w0 = late["pend"].pop(0)
            mods_mm(b0, w0)
        if blk < 12:
            wt = late["slots"][blk % 2]
            fw.dma("pool", wt[:], wada_d[:, :, blk * 512:(blk + 1) * 512], writes=[wt])
            late["pend"].append((blk, wt))
            mods_state["blk"] += 1

    def mods_tick(limit=12):
        blk = mods_state["blk"]
        if blk >= limit:
            return
        mods_state["blk"] += 1
        wt = next_wbuf()
        fw.dma("pool", wt[:], wada_d[:, :, blk * 512:(blk + 1) * 512], writes=[wt])
        for j in range(4):
            cj = blk * 4 + j
            for kc in range(8):
                mm(pm[:, cj * 2:cj * 2 + 2], wt[:, kc, j * 128:(j + 1) * 128], csb[:, kc, :],
                   kc == 0, kc == 7, [wt, csb], [pm], kc == 7)

    def mods_finish():
        while mods_state["blk"] < 12:
            mods_tick()
    def mods_final(part):
        if part == 0:
            while early["pend"]:
                mods_early_tick()
            while mods_state["blk"] < 4:
                mods_tick()
            c0, c1 = 0, 16
        else:
            while mods_state["blk"] < 12 or late["pend"]:
                if late["slots"] is not None:
                    mods_late_tick()
                else:
                    mods_tick()
            c0, c1 = 16, 48
        pm3 = pm[:, 0:96].rearrange("p (c s) -> p c s", s=2)
        for s in range(2):
            DVE(lambda s=s: nc.vector.tensor_tensor(out=mods[:, s, c0:c1], in0=pm3[:, c0:c1, s], in1=bada[:, c0:c1], op=ALU.add),
                [pm, bada], [mods])
        w = part
        for s in range(2):
            DVE(lambda s=s, w=w: nc.vector.scalar_tensor_tensor(
                out=gm[:, s, w, :], in0=mods[:, s, (8 + 24 * w):(16 + 24 * w)], scalar=1.0,
                in1=gains[:, 8 * w:8 * w + 8], op0=ALU.add, op1=ALU.mult), [mods, gains], [gm])
        if part == 1:
            tap("mods", mods, mods[:], [128, 2, 48])
            tap("gm", gm, gm[:], [128, 2, 2, 8])

    def modcol(s, j, c):
        return mods[:, s, j * 8 + c:j * 8 + c + 1]

    def filter_phase(L, dead_in):
        Lc = L // 128
        nh = max(1, L // 512)
        dec = sb(f"dec{L}", [128, Lc, 1024], F32, O_B)
        fs = sb(f"fs{L}", [128, Lc, 2, 512], BF16, O_A if L == 1024 else O_E + 60 * KB)
        fd = sb(f"fd{L}", [128, Lc, 2, 512], BF16, O_A + 16 * KB if L == 1024 else O_E + 64 * KB)
        yv = sb(f"yv{L}", [64, L], F32, O_E + 32 * KB)
        ti = sb(f"ti{L}", [64, L], I32, O_E + 36 * KB)
        tf = sb(f"tf{L}", [64, L], F32, O_E + 40 * KB)
        h1 = sb(f"h1{L}", [64, L], F32, O_E + 44 * KB)
        h2 = sb(f"h2{L}", [65, L], BF16, O_E + 48 * KB)
        zT = sb(f"zT{L}", [18, L], F32, O_E + 52 * KB)
        fbs = sb(f"fbs{L}", [128, 1024], F32, O_E + 56 * KB)
        w3 = sb(f"w3{L}", [96, 2048], F32, O_C if L == 256 else O_E + 64 * KB)
        w3c = sb(f"w3c{L}", [96, 2048], BF16, O_C + 8 * KB if L == 256 else O_E + 60 * KB)
        w3b = sb(f"w3b{L}", [96, 2, 512], BF16, O_C + 12 * KB if L == 256 else O_E + 50 * KB)
        t1, t2 = w3c, w3b
        for t in [dec, fs, fd, yv, ti, tf, h1, h2, zT, fbs, w3c, w3b, w3]:
            fw.alias(t, dead_in)
        fw.dma("sp", zT[:], zT_d[L], writes=[zT])
        DVE(lambda: nc.vector.memset(w3[64:96, :], 0.0), [], [w3])
        fw.dma("sp", w3[0:65, :], w3b3_d, writes=[w3])
        for o in range(2):
            wf = w3[:, o * 1024:o * 1024 + 512]
            wb_ = w3[:, o * 1024 + 512:o * 1024 + 1024]
            DVE(lambda o=o, wf=wf, wb_=wb_: nc.vector.tensor_tensor(out=w3c[:, o * 1024:o * 1024 + 512], in0=wf, in1=wb_, op=ALU.add), [w3], [w3c])
            DVE(lambda o=o, wf=wf, wb_=wb_: nc.vector.tensor_tensor(out=w3c[:, o * 1024 + 512:o * 1024 + 1024], in0=wb_, in1=wf, op=ALU.subtract), [w3], [w3c])
            DVE(lambda o=o, wb_=wb_: nc.vector.tensor_copy(out=w3b[:, o, :], in_=wb_), [w3], [w3b])
        fw.dma("sp", dec[:], dec_d[L], writes=[dec])
        fw.dma("sp", fbs[:], fb_d.partition_broadcast(128), writes=[fbs])
        if L == 1024:
            prefetch_x("S", [])
        W = min(L, 512)

        def sin_layer(src_ps_list, dst, add_b2):
            for i, p in enumerate(src_ps_list):
                sl = slice(i * W, (i + 1) * W)
                if add_b2:
                    DVE(lambda p=p, sl=sl: nc.vector.tensor_scalar(out=yv[:, sl], in0=p[0:64, 0:W], scalar1=fsm[:, 0:1],
                                                                   scalar2=fsm[:, 4:5], op0=ALU.add, op1=ALU.mult),
                        [p, fsm], [yv])
                    DVE(lambda sl=sl: nc.vector.tensor_scalar(out=yv[:, sl], in0=yv[:, sl], scalar1=64.0, scalar2=None,
                                                              op0=ALU.add), [yv], [yv])
                else:
                    DVE(lambda p=p, sl=sl: nc.vector.tensor_scalar(out=yv[:, sl], in0=p[0:64, 0:W], scalar1=fsm[:, 4:5],
                                                                   scalar2=64.0, op0=ALU.mult, op1=ALU.add),
                        [p, fsm], [yv])
            DVE(lambda: nc.vector.tensor_copy(out=ti[:], in_=yv[:]), [yv], [ti])
            DVE(lambda: nc.vector.tensor_copy(out=tf[:], in_=ti[:]), [ti], [tf])
            DVE(lambda: nc.vector.tensor_tensor(out=yv[:], in0=yv[:], in1=tf[:], op=ALU.subtract), [yv, tf], [yv])
            DVE(lambda: nc.vector.tensor_scalar(out=tf[:], in0=yv[:], scalar1=0.5, scalar2=None, op0=ALU.is_gt), [yv], [tf])
            DVE(lambda: nc.vector.tensor_tensor(out=yv[:], in0=yv[:], in1=tf[:], op=ALU.subtract), [yv, tf], [yv])
            DVE(lambda: nc.vector.tensor_scalar(out=tf[:], in0=yv[:], scalar1=-0.5, scalar2=None, op0=ALU.is_lt), [yv], [tf])
            DVE(lambda: nc.vector.tensor_tensor(out=yv[:], in0=yv[:], in1=tf[:], op=ALU.add), [yv, tf], [yv])
            ACT(lambda: nc.scalar.activation(out=dst[0:64, :], in_=yv[:], func=AF.Sin, scale=float(2 * np.pi)), [yv], [dst])

        pl = []
        for i in range(nh):
            p = nextps()
            mm(p[0:64, 0:W], w1b1[:], zT[:, i * W:(i + 1) * W], True, True, [w1b1, zT], [p], True)
            pl.append(p)
        sin_layer(pl, h1, False)
        pl = []
        for i in range(nh):
            p = nextps()
            mm(p[0:64, 0:W], w2s[:], h1[:, i * W:(i + 1) * W], True, True, [w2s, h1], [p], True)
            pl.append(p)
        sin_layer(pl, h2, True)
        DVE(lambda: nc.vector.memset(h2[64:65, :], 1.0), [], [h2])
        for tc in range(Lc):
            if tc >= 1:
                mods_early_tick()
            pq = [nextps() for _ in range(4)]
            for q in range(4):
                mm(pq[q][:], h2[:, tc * 128:(tc + 1) * 128], w3c[0:65, q * 512:(q + 1) * 512], True, True, [h2, w3c], [pq[q]], True)
            for o in range(2):
                ps_, pd_ = pq[2 * o], pq[2 * o + 1]
                DVE(lambda ps_=ps_, o=o: nc.vector.tensor_tensor(out=fs[:, tc, o, :], in0=ps_[:], in1=dec[:, tc, 0:512], op=ALU.mult), [ps_, dec], [fs])
                DVE(lambda pd_=pd_, o=o: nc.vector.tensor_tensor(out=fd[:, tc, o, :], in0=pd_[:], in1=dec[:, tc, 0:512], op=ALU.mult), [pd_, dec], [fd])
            if tc == 0:
                for o in range(2):
                    pc = nextps()
                    mm(pc[:], h2[:, 0:128], w3b[0:65, o, :], True, True, [h2, w3b], [pc], True)
                    DVE(lambda o=o, pc=pc: nc.vector.scalar_tensor_tensor(out=fs[:, 0, o, :], in0=pc[:], scalar=e0n[:, 0:1], in1=fs[:, 0, o, :],
                                                                         op0=ALU.mult, op1=ALU.add), [fs, pc, e0n], [fs])
                    DVE(lambda o=o, pc=pc: nc.vector.scalar_tensor_tensor(out=fd[:, 0, o, :], in0=pc[:], scalar=e0n[:, 0:1], in1=fd[:, 0, o, :],
                                                                         op0=ALU.mult, op1=ALU.add), [fd, pc, e0n], [fd])
        Kt = Ktab[L]
        nblk = (2 * L) // 512
        for blk in range(nblk):
            wt = next_wbuf()
            fw.dma("sp", wt[:, 0:Lc, :], Wu_d[L][:, :, blk * 512:(blk + 1) * 512], writes=[wt])
            for j in range(4):
                fr = blk * 4 + j
                isI = fr >= Lc
                fc = fr - Lc if isI else fr
                src = fd if isI else fs
                for o in range(2):
                    p = nextps()
                    for tc in range(Lc):
                        mm(p[:], wt[:, tc, j * 128:(j + 1) * 128], src[:, tc, o, :], tc == 0, tc == Lc - 1, [wt, src], [p], tc == Lc - 1)
                    if isI:
                        ACT(lambda p=p, fc=fc, o=o: nc.scalar.copy(out=Kt[:, fc, 1, o, :], in_=p[:]), [p], [Kt])
                    else:
                        DVE(lambda p=p, fc=fc, o=o: nc.vector.scalar_tensor_tensor(
                            out=Kt[:, fc, 0, o, :], in0=fbs[:, o * 512:(o + 1) * 512], scalar=float(1.0 / L), in1=p[:],
                            op0=ALU.mult, op1=ALU.add), [p, fbs], [Kt])
        tap(f"h1_{L}", h1, h1[:], [64, L])
        tap(f"h2_{L}", h2, h2[:], [65, L])
        tap(f"fs_{L}", fs, fs[:], [128, Lc, 2, 512], BF16)
        tap(f"K_{L}", Kt, Kt[:], [128, Lc, 2, 2, 512], BF16)
        return [dec, fs, fd, yv, ti, tf, h1, h2, zT, fbs, t1, t2, w3]


    def rstd_from(ps_list, dst, dcount):
        for i, p in enumerate(ps_list):
            ACT(lambda p=p, i=i: nc.scalar.activation(out=dst[:, i * 512:(i + 1) * 512], in_=p[:], func=AF.Ln,
                                                      bias=epsc[:, 0:1], scale=float(1.0 / dcount)), [p, epsc], [dst])
        ACT(lambda: nc.scalar.activation(out=dst[:], in_=dst[:], func=AF.Exp, scale=-0.5), [dst], [dst])

    prefetched = {}
    final_dead = {}
    live = {}

    def s_tail(dead_all, mtok):
        sset = 1

        def sba(name, shape, dt, off):
            t = sb("So_" + name, shape, dt, off)
            fw.alias(t, dead_all)
            return t
        x1o = [sba(f"x1o{c}", [128, 514], F32, O_A + c * 2112) for c in range(8)]
        x2o = [sba(f"x2o{c}", [128, 512], F32, O_A + 17 * KB + c * 2048) for c in range(8)]
        xst = [sba(f"xst{i}", [128, 514], F32, O_A + 33 * KB + i * 2112) for i in range(2)]
        rso = sba("rso", [128, 514], F32, O_A + 38 * KB)
        sqo = [sba(f"sqo{i}", [128, 514], BF16, O_A + 41 * KB + i * 1088) for i in range(2)]
        yst = [sba(f"yst{i}", [128, 512], F32, O_A + 44 * KB + i * 2048) for i in range(4)]
        mrow = sba("mrow", [128, 514], F32, O_A + 52 * KB)
        tmpf = [sba(f"tmpf{i}", [128, 514], F32, O_A + 55 * KB + i * 2112) for i in range(2)]
        ost = [sba(f"ost{i}", [128, 512], F32, O_A + 60 * KB + i * 2048) for i in range(2)]
        selT = sba("sel", [128, 8, 514], BF16, O_B + 4 * KB)
        h2o = sba("h2o", [128, 8, 514], BF16, O_B + 4 * KB)
        mrgo = sba("mrgo", [128, 8, 514], BF16, O_E)
        actTo = sba("actTo", [128, 22, 512], BF16, O_E + 9 * KB)
        all_t = x1o + x2o + xst + [rso] + sqo + yst + [mrow] + tmpf + ost + [selT, h2o, mrgo, actTo]
        fw.dma("sp", selT[:], sel_d, writes=[selT])
        fw.dma("sp", mrow[:], mrow_d.partition_broadcast(128), writes=[mrow])
        rot["banks"] = [0, 1, 2, 3]
        rot["i"] = 0
        pst = (psf[4], psf[5], psd[2])

        def mm514(big, pa, pb2, lhs_fn, rhs_t, rhs_fn, n, reads):
            for k in range(n):
                mm(big[:, 0:512], lhs_fn(k), rhs_fn(k, 0, 512), k == 0, k == n - 1, reads, [pa], k == n - 1)
            for k in range(n):
                mm(big[:, 512:514], lhs_fn(k), rhs_fn(k, 512, 514), k == 0, k == n - 1, reads, [pb2], k == n - 1)

        rot["banks"] = [0, 1, 2, 3, 4, 5]
        rot["i"] = 0
        for c in range(8):
            pa, pb2, big = nextpair()
            mm514(big, pa, pb2, lambda tk, c=c: mtok[:, tk, c * 128:(c + 1) * 128], selT,
                  lambda tk, a, b: selT[:, tk, a:b], 8, [mtok, selT])
            ACT(lambda c=c, big=big: nc.scalar.copy(out=mrgo[:, c, :], in_=big[:, 0:514]), [pa, pb2], [mrgo])
        fw.alias(h2o, [selT])

        def stats(c, src_t, width):
            s_ = sqo[c % 2]
            ACT(lambda: nc.scalar.activation(out=s_[:, 0:width], in_=src_t[:, 0:width], func=AF.Square), [src_t], [s_])
            mm(pst[2][:, 0:512], ones[:], s_[:, 0:512], c == 0, c == 7, [ones, s_], [pst[0]], True)
            if width > 512:
                mm(pst[2][:, 512:514], ones[:], s_[:, 512:514], c == 0, c == 7, [ones, s_], [pst[1]], True)

        def rstd_o(width):
            ACT(lambda: nc.scalar.activation(out=rso[:, 0:width], in_=pst[2][:, 0:width], func=AF.Ln, bias=epsc[:, 0:1], scale=float(1.0 / D)),
                [pst[0], pst[1], epsc], [rso])
            ACT(lambda: nc.scalar.activation(out=rso[:, 0:width], in_=rso[:, 0:width], func=AF.Exp, scale=-0.5), [rso], [rso])

        rot["banks"] = [0, 1, 2, 3]
        rot["i"] = 0
        for b in range(2):
            wt = next_wbuf()
            fw.dma("pool", wt[:], wout_d[:, :, b * 512:(b + 1) * 512], writes=[wt])
            for j in range(4):
                cj = b * 4 + j
                xs = xst[cj % 2]
                fw.dma("sp", xs[:], xo_d[:, cj, :], writes=[xs])
                pa, pb2, big = nextpair()
                mm514(big, pa, pb2, lambda kc, j=j, wt=wt: wt[:, kc, j * 128:(j + 1) * 128], mrgo,
                      lambda kc, a, b_: mrgo[:, kc, a:b_], 8, [wt, mrgo])
                DVE(lambda cj=cj, big=big, xs=xs: nc.vector.scalar_tensor_tensor(
                    out=x1o[cj][:], in0=big[:, 0:514], scalar=modcol(sset, 2, cj), in1=xs[:], op0=ALU.mult, op1=ALU.add),
                    [pa, pb2, mods, xs], [x1o[cj]])
                if cj >= 1:
                    stats(cj - 1, x1o[cj - 1], 514)
        stats(7, x1o[7], 514)
        rstd_o(514)
        for c in range(8):
            tf_ = tmpf[c % 2]
            DVE(lambda c=c, tf_=tf_: nc.vector.tensor_tensor(out=tf_[:], in0=x1o[c][:], in1=rso[:], op=ALU.mult), [x1o[c], rso], [tf_])
            ACT(lambda c=c, tf_=tf_: nc.scalar.activation(out=tf_[:], in_=tf_[:], func=AF.Identity,
                                                          bias=modcol(sset, 3, c), scale=gm[:, sset, 1, c:c + 1]), [tf_, mods, gm], [tf_])
            DVE(lambda c=c, tf_=tf_: nc.vector.tensor_tensor(out=h2o[:, c, :], in0=tf_[:], in1=mrow[:], op=ALU.mult), [tf_, mrow], [h2o])
        rot["banks"] = [0, 1, 2, 3, 4, 5]
        rot["i"] = 0
        items = []
        wts = {}
        for blk in range(11):
            for i in range(2):
                jj = 2 * blk + i
                for which, jcol in ((0, i), (1, 2 + i)):
                    fcx = which * 22 + jj
                    sy = yst[2 * which + (jj % 2)]
                    stt = {}
                    wc = ffc[:, fcx, :]

                    def A(blk=blk, i=i, which=which, jcol=jcol, sy=sy, stt=stt, wc=wc):
                        if i == 0 and which == 0:
                            wt = next_wbuf()
                            fw.dma("pool", wt[:], wup_d[:, :, blk * 512:(blk + 1) * 512], writes=[wt])
                            wts[blk] = wt
                        wt = wts[blk]
                        pa, pb2, big = nextpair()
                        mm514(big, pa, pb2, lambda kc: wt[:, kc, jcol * 128:(jcol + 1) * 128], h2o,
                              lambda kc, a, b_: h2o[:, kc, a:b_], 8, [wt, h2o])
                        stt["p"] = (pa, pb2, big)
                        ACT(lambda: nc.scalar.activation(out=sy[:], in_=big[:, 1:513], func=AF.Identity, bias=wc[:, 3:4], scale=wc[:, 1:2]),
                            [pa, pb2, ffc], [sy])

                    def B(sy=sy, stt=stt, wc=wc):
                        pa, pb2, big = stt["p"]
                        DVE(lambda: nc.vector.scalar_tensor_tensor(out=sy[:], in0=big[:, 0:512], scalar=wc[:, 0:1], in1=sy[:],
                                                                   op0=ALU.mult, op1=ALU.add), [sy, pa, ffc], [sy])
                        DVE(lambda: nc.vector.scalar_tensor_tensor(out=sy[:], in0=big[:, 2:514], scalar=wc[:, 2:3], in1=sy[:],
                                                                   op0=ALU.mult, op1=ALU.add), [sy, pa, pb2, ffc], [sy])

                    def C(which=which, jj=jj, sy=sy):
                        if which == 0:
                            ACT(lambda: nc.scalar.activation(out=sy[:], in_=sy[:], func=AF.Silu), [sy], [sy])
                        else:
                            gs = yst[jj % 2]
                            DVE(lambda: nc.vector.tensor_tensor(out=actTo[:, jj, :], in0=gs[:], in1=sy[:], op=ALU.mult), [gs, sy], [actTo])
                    items.append([A, B, C])
        n_it = len(items)
        for t_ in range(n_it + 2):
            for k in (2, 1, 0):
                ii = t_ - k
                if 0 <= ii < n_it:
                    items[ii][k]()
        rot["banks"] = [0, 1, 2, 3]
        rot["i"] = 0
        for cj in range(8):
            wt = next_wbuf()
            wflat = wt[:].rearrange("p a b -> p (a b)")
            fw.dma("pool", wflat[:, 0:22 * 128], wdn_d[:, cj, :], writes=[wt])
            p = nextps()
            for kk in range(22):
                mm(p[:], wflat[:, kk * 128:(kk + 1) * 128], actTo[:, kk, :], kk == 0, kk == 21, [wt, actTo], [p], kk == 21)
            DVE(lambda p=p, cj=cj: nc.vector.scalar_tensor_tensor(
                out=x2o[cj][:], in0=p[:], scalar=modcol(sset, 5, cj), in1=x1o[cj][:, 1:513], op0=ALU.mult, op1=ALU.add),
                [p, mods, x1o[cj]], [x2o[cj]])
            if cj >= 1:
                stats(cj - 1, x2o[cj - 1], 512)
        stats(7, x2o[7], 512)
        live["S"] = all_t + dead_all
        prefetch_x("P", live["S"])
        yield "pre_S9"
        rstd_o(512)
        rot["banks"] = [0, 1, 2, 3, 4, 5]
        rot["i"] = 0
        for c in range(8):
            ft = ost[c % 2]
            DVE(lambda c=c, ft=ft: nc.vector.scalar_tensor_tensor(out=ft[:], in0=x2o[c][:], scalar=gains[:, 16 + c:17 + c],
                                                                 in1=rso[:, 0:512], op0=ALU.mult, op1=ALU.mult), [x2o[c], gains, rso], [ft])
            fw.dma("sp", yT_o["S"][:, c, :], ft[:], reads=[ft], writes=[OUT])
        final_dead["S"] = all_t + dead_all


    def prefetch_x(g2, deadl):
        xc = [sb(g2 + f"xc{c}", [128, NTOK], F32, O_E + c * 4 * KB) for c in range(8)]
        for t in xc:
            fw.alias(t, deadl)
        for c in range(8):
            fw.dma("sp", xc[c][:], xT[g2][:, c, :], writes=[xc[c]])
        prefetched[g2] = xc

    def group(g, dead_in):
        L = 1024 if g == "S" else 256
        nseq = NTOK // L
        Lc = L // 128
        sset = 1 if g == "S" else 0
        x_d = xT[g]
        x1T = sb(g + "x1T", [128, 4, NTOK], BF16, O_E)
        x2T = sb(g + "x2T", [128, 4, NTOK], BF16, O_E + 8 * KB)
        vtok = [sb(g + f"vtok{s}", [128, Lc, 512], BF16, O_E + 16 * KB + s * Lc * KB) for s in range(nseq)]
        nY = 16 // (2 * Lc)
        Ys = [sb(g + f"Y{k}", [128, 2 * Lc, 512], BF16, O_E + 24 * KB + k * 2 * Lc * KB) for k in range(nY)]
        Y = Ys[0]
        ysa = [sb(g + f"ysa{i}", [128, 512], F32, O_E + 40 * KB + i * 2 * KB) for i in range(2)]
        ysb = [sb(g + f"ysb{i}", [128, 512], F32, O_E + 44 * KB + i * 2 * KB) for i in range(2)]
        yt1 = sb(g + "yt1", [128, 512], F32, O_E + 48 * KB)
        yt2 = sb(g + "yt2", [128, 512], F32, O_E + 50 * KB)
        yt3 = sb(g + "yt3", [128, 512], F32, O_E + 66 * KB)
        yt4 = sb(g + "yt4", [128, 512], F32, O_E + 70 * KB)
        ystg = sb(g + "ystg", [128, NTOK], F32, O_E + 52 * KB)
        pstg = sb(g + "pstg", [128, NTOK], F32, O_E + 56 * KB)
        vT = sb(g + "vT", [128, NTOK], BF16, O_E + 60 * KB)
        vT2 = sb(g + "vT2", [128, NTOK], BF16, O_E + 68 * KB)
        vTs = [vT, vT2]
        rstd = sb(g + "rstd", [128, NTOK], F32, O_E + 40 * KB)
        xstg = [sb(g + f"xstg{i}", [128, NTOK], F32, O_E + 24 * KB + i * 4 * KB) for i in range(2)]
        sq = [sb(g + f"sq{i}", [128, NTOK], BF16, O_E + 32 * KB + i * 2 * KB) for i in range(2)]
        hT = sb(g + "hT", [128, 8, NTOK], BF16, O_B)
        mrg = sb(g + "mrg", [128, 8, NTOK], BF16, O_C)
        for t in [x1T, x2T, ystg, pstg, vT, vT2, rstd, hT, mrg] + Ys + vtok + ysa + ysb + [yt1, yt2, yt3, yt4] + xstg + sq:
            fw.alias(t, dead_in)

        if g not in prefetched:
            prefetch_x(g, dead_in)
        xc = prefetched[g]
        pss = [nextps(), nextps()]
        for c in range(8):
            s_ = sq[c % 2]
            ACT(lambda c=c, s_=s_: nc.scalar.activation(out=s_[:], in_=xc[c][:], func=AF.Square), [xc[c]], [s_])
            for tt in range(2):
                mm(pss[tt][:], ones[:], s_[:, tt * 512:(tt + 1) * 512], c == 0, c == 7, [ones, s_], [pss[tt]], True)
        rstd_from(pss, rstd, D)
        for c in range(8):
            DVE(lambda c=c: nc.vector.tensor_tensor(out=xc[c][:], in0=xc[c][:], in1=rstd[:], op=ALU.mult), [xc[c], rstd], [xc[c]])
            ACT(lambda c=c: nc.scalar.activation(out=hT[:, c, :], in_=xc[c][:], func=AF.Identity,
                                                 bias=modcol(sset, 0, c), scale=gm[:, sset, 0, c:c + 1]),
                [xc[c], mods, gm], [hT])
        for t in [x1T, x2T] + Ys + vtok + xstg:
            fw.alias(t, xc)
        for t in ysa:
            fw.alias(t, [rstd])
        yield "post_S1"
        if g == "P":
            for t in [x1T, x2T, ystg, pstg, vT, vT2, rstd, hT, mrg, yt1, yt2, yt3, yt4] + Ys + vtok + ysa + ysb + xstg + sq + xc:
                fw.alias(t, final_dead["S"])
            dead_in = dead_in + final_dead["S"]

        tap(g + "rstd", rstd, rstd[:], [128, NTOK])
        tap(g + "hT", hT, hT[:], [128, 8, NTOK], BF16)
        def dw_A(pa, pb2, big, wcols, stg_y):
            ACT(lambda: nc.scalar.activation(out=stg_y[:], in_=big[:, :], func=AF.Identity,
                                             bias=wcols[:, 3:4], scale=wcols[:, 1:2]), [pa, pb2, hyc, ffc], [stg_y])

        def dw_B(pa, pb2, big, wcols, stg_y):
            y3 = stg_y[:].rearrange("p (s l) -> p s l", l=L)
            p3 = big[:, :].rearrange("p (s l) -> p s l", l=L)
            DVE(lambda: nc.vector.scalar_tensor_tensor(out=y3[:, :, 1:L], in0=p3[:, :, 0:L - 1], scalar=wcols[:, 0:1],
                                                       in1=y3[:, :, 1:L], op0=ALU.mult, op1=ALU.add), [stg_y, pa, pb2, hyc, ffc], [stg_y])
            DVE(lambda: nc.vector.scalar_tensor_tensor(out=y3[:, :, 0:L - 1], in0=p3[:, :, 1:L], scalar=wcols[:, 2:3],
                                                       in1=y3[:, :, 0:L - 1], op0=ALU.mult, op1=ALU.add), [stg_y, pa, pb2, hyc, ffc], [stg_y])

        def run_pipeline(items):
            n = len(items)
            K = max(len(it) for it in items)
            for t in range(n + K - 1):
                for k in range(K - 1, -1, -1):
                    i = t - k
                    if 0 <= i < n and k < len(items[i]):
                        items[i][k]()

        def transposes_to_tok(srcT, src_ap_fn, dst_list, col0):
            for tk in range(8):
                fw.op("pe", lambda tk=tk: nc.tensor.transpose(psb[:, tk * 128:(tk + 1) * 128], src_ap_fn(tk), ident[:]),
                      reads=[srcT, ident], writes=[psb], inc=(tk == 7))
            for s in range(nseq):
                ACT(lambda s=s: nc.scalar.copy(out=dst_list[s][:, :, col0:col0 + 128],
                                               in_=psb[:, s * L:(s + 1) * L].rearrange("p (t c) -> p t c", c=128)),
                    [psb], [dst_list[s]])

        def proj_block(wt, j, writes_ps=None):
            pa, pb2, big = nextpair()
            for tt, p in enumerate((pa, pb2)):
                for kc in range(8):
                    mm(p[:], wt[:, kc, j * 128:(j + 1) * 128], hT[:, kc, tt * 512:(tt + 1) * 512], kc == 0, kc == 7, [wt, hT], [p], kc == 7)
            return pa, pb2, big

        items = []
        wts = {}
        for b in (3, 4, 5):
            for j in range(4):
                hc = (b - 3) * 4 + j
                yb = (ystg, pstg)[hc % 2]
                stt = {}

                def A(b=b, j=j, hc=hc, yb=yb, stt=stt):
                    if j == 0:
                        pinned.clear()
                        wt = next_wbuf()
                        fw.dma("pool", wt[:], win_d[:, :, b * 512:(b + 1) * 512], writes=[wt])
                        wts[b] = wt
                        pinned.add(wt.name)
                    stt["p"] = proj_block(wts[b], j)
                    dw_A(*stt["p"], hyc[:, hc, :], yb)

                def B(hc=hc, yb=yb, stt=stt):
                    dw_B(*stt["p"], hyc[:, hc, :], yb)

                def C(b=b, j=j, yb=yb):
                    if b == 3:
                        ACT(lambda: nc.scalar.copy(out=x1T[:, j, :], in_=yb[:]), [yb], [x1T])
                    elif b == 4:
                        ACT(lambda: nc.scalar.copy(out=x2T[:, j, :], in_=yb[:]), [yb], [x2T])
                    else:
                        vTj = vTs[j % 2]
                        ACT(lambda: nc.scalar.copy(out=vTj[:], in_=yb[:]), [yb], [vTj])
                        transposes_to_tok(vTj, lambda tk: vTj[:, tk * 128:(tk + 1) * 128], vtok, j * 128)
                items.append([A, B, C])
        run_pipeline(items)
        pinned.clear()

        tap(g + "x1T", x1T, x1T[:], [128, 4, NTOK], BF16)
        tap(g + "vtok0", vtok[0], vtok[0][:], [128, Lc, 512], BF16)
        pre_w = []
        for b in (0, 1, 2):
            wt = next_wbuf()
            fw.dma("pool", wt[:], win_d[:, :, b * 512:(b + 1) * 512], writes=[wt])
            pre_w.append(wt)
        def make_s2b(qT, kT, Vaug, kst):
            pieces = []
            for b in (0, 1):
                for j in range(4):
                    def piece(b=b, j=j):
                        wt = pre_w[b]
                        dstT = qT if b == 0 else kT
                        pa, pb2, _big = proj_block(wt, j)
                        for tt, p in enumerate((pa, pb2)):
                            sl = slice(tt * 512, (tt + 1) * 512)
                            if b == 1 and g == "P":
                                ks = kst[(2 * j + tt) % 4]
                                ACT(lambda p=p, ks=ks: nc.scalar.copy(out=ks[:], in_=p[:]), [p], [ks])
                                fw.dma("sp", kT_o[:, j, sl], ks[:], reads=[ks], writes=[OUT])
                                DVE(lambda ks=ks, sl=sl: nc.vector.tensor_copy(out=kT[:, j, sl], in_=ks[:]), [ks], [kT])
                            elif b == 0:
                                ACT(lambda p=p, sl=sl: nc.scalar.mul(out=dstT[:, j, sl], in_=p[:], mul=0.125), [p], [dstT])
                            else:
                                ACT(lambda p=p, sl=sl: nc.scalar.copy(out=dstT[:, j, sl], in_=p[:]), [p], [dstT])
                    pieces.append(piece)
            for tk in range(8):
                def piece(tk=tk):
                    wt = pre_w[2]
                    p = nextps()
                    for kc in range(8):
                        mm(p[:], hT[:, kc, tk * 128:(tk + 1) * 128], wt[:, kc, :], kc == 0, kc == 7, [hT, wt], [p], kc == 7)
                    p3 = p[:].rearrange("p (h d) -> p h d", d=64)
                    if g == "P":
                        ks = kst[tk % 4]
                        ACT(lambda: nc.scalar.copy(out=ks[:], in_=p[:]), [p], [ks])
                        fw.dma("sp", v_o[:, tk, :], ks[:], reads=[ks], writes=[OUT])
                        ACT(lambda: nc.scalar.copy(out=Vaug[:, tk, :, 0:64], in_=ks[:].rearrange("p (h d) -> p h d", d=64)), [ks], [Vaug])
                    else:
                        ACT(lambda: nc.scalar.copy(out=Vaug[:, tk, :, 0:64], in_=p3), [p], [Vaug])
                pieces.append(piece)
            return pieces

        s2b_pieces = []
        early_att = None
        if g == "P":
            curA = [O_A]

            def sba_(name, shape, dt):
                nb = int(np.prod(shape[1:])) * (2 if dt == BF16 else 4)
                t = sb(g + name, shape, dt, curA[0])
                curA[0] += (nb + 63) // 64 * 64
                fw.alias(t, dead_in)
                return t
            qT_ = sba_("qT", [128, 4, NTOK], BF16)
            kT_ = sba_("kT", [128, 4, NTOK], BF16)
            Vaug_ = sba_("Vaug", [128, 8, 8, 65], BF16)
            kst_ = [sba_(f"kst{i}", [128, 512], F32) for i in range(4)]
            POOL(lambda: nc.gpsimd.memset(Vaug_[:, :, :, 64:65], 1.0), [], [Vaug_])
            early_att = (qT_, kT_, Vaug_, kst_)
            s2b_pieces = make_s2b(*early_att)

        def s2b_hook():
            if s2b_pieces:
                s2b_pieces.pop(0)()

        Wt = Wm[L]
        Kt = Ktab[L]

        cm_i = [0]
        cm_t = [(ysa[0], ysb[0], yt1, yt2), (ysa[1], ysb[1], yt3, yt4)]
        if g == "S":
            late["slots"] = [sb("wadaA", [128, 8, 512], BF16, O_C), sb("wadaB", [128, 8, 512], BF16, O_C + 8 * KB)]
            for t in late["slots"]:
                fw.alias(t, dead_in)

        def conv_A(s, o):
            u = vtok[s]
            Y = Ys[s % nY]
            yb0 = 0
            for i in range(Lc):
                if g == "S":
                    mods_late_tick()
                pA, pB = nextps(), nextps()
                for (p, fr) in ((pA, i), (pB, Lc + i)):
                    for tc in range(Lc):
                        mm(p[:], Wt[:, tc, fr * 128:(fr + 1) * 128], u[:, tc, :], tc == 0, tc == Lc - 1, [Wt, u], [p], tc == Lc - 1)
                KR = Kt[:, i, 0, o, :]
                KI = Kt[:, i, 1, o, :]
                cm_i[0] += 1
                tA, tB, tC, tD = cm_t[cm_i[0] % 2]
                DVE(lambda pA=pA, tA=tA: nc.vector.tensor_tensor(out=tA[:], in0=pA[:], in1=KR, op=ALU.mult), [pA, Kt], [tA])
                DVE(lambda pB=pB, tB=tB: nc.vector.tensor_tensor(out=tB[:], in0=pB[:], in1=KI, op=ALU.mult), [pB, Kt], [tB])
                DVE(lambda pA=pA, tC=tC: nc.vector.tensor_tensor(out=tC[:], in0=pA[:], in1=KI, op=ALU.mult), [pA, Kt], [tC])
                DVE(lambda pB=pB, tD=tD: nc.vector.tensor_tensor(out=tD[:], in0=pB[:], in1=KR, op=ALU.mult), [pB, Kt], [tD])
                DVE(lambda i=i, tA=tA, tB=tB: nc.vector.tensor_tensor(out=Y[:, yb0 + i, :], in0=tA[:], in1=tB[:], op=ALU.subtract), [tA, tB], [Y])
                DVE(lambda i=i, tC=tC, tD=tD: nc.vector.tensor_tensor(out=Y[:, yb0 + Lc + i, :], in0=tC[:], in1=tD[:], op=ALU.add), [tC, tD], [Y])

        def conv_B(s, o, mulT):
            Y = Ys[s % nY]
            yb0 = 0
            Nn = min(L, 512)
            for cc in range(4):
                for th in range(L // Nn):
                    p = nextps()
                    for fr in range(2 * Lc):
                        col = (fr // Lc) * L + th * Nn
                        mm(p[:, 0:Nn], Y[:, yb0 + fr, cc * 128:(cc + 1) * 128], Wt[:, fr % Lc, col:col + Nn], fr == 0, fr == 2 * Lc - 1,
                           [Y, Wt], [p], fr == 2 * Lc - 1)
                    t0 = s * L + th * Nn
                    DVE(lambda p=p, cc=cc, t0=t0: nc.vector.tensor_tensor(out=mulT[:, cc, t0:t0 + Nn], in0=p[:, 0:Nn],
                                                                          in1=mulT[:, cc, t0:t0 + Nn], op=ALU.mult), [p, mulT], [mulT])

        def run_convs(o, mulT):
            conv_A(0, o)
            s2b_hook()
            for s_ in range(nseq):
                if s_ + 1 < nseq:
                    conv_A(s_ + 1, o)
                    s2b_hook()
                conv_B(s_, o, mulT)
                s2b_hook()

        run_convs(0, x1T)
        for cc in range(4):
            transposes_to_tok(x1T, lambda tk, cc=cc: x1T[:, cc, tk * 128:(tk + 1) * 128], vtok, cc * 128)
        run_convs(1, x2T)
        if g == "S":
            mods_final(1)
            fw.alias(mrg, late["slots"])
        for t in sq:
            fw.alias(t, Ys + xstg)
        fw.alias(rstd, ysa)
        pss = [nextps(), nextps()]
        for cc in range(4):
            s_ = sq[cc % 2]
            ACT(lambda s_=s_, cc=cc: nc.scalar.activation(out=s_[:], in_=x2T[:, cc, :], func=AF.Square), [x2T], [s_])
            for tt in range(2):
                mm(pss[tt][:], ones[:], s_[:, tt * 512:(tt + 1) * 512], cc == 0, cc == 3, [ones, s_], [pss[tt]], True)
        rstd_from(pss, rstd, 512)
        for cc in range(4):
            if g == "S":
                DVE(lambda cc=cc: nc.vector.scalar_tensor_tensor(out=x2T[:, cc, :], in0=x2T[:, cc, :], scalar=gains[:, 28 + cc:29 + cc],
                                                                 in1=rstd[:], op0=ALU.mult, op1=ALU.mult), [x2T, gains, rstd], [x2T])
                for tk in range(8):
                    fw.op("pe", lambda tk=tk, cc=cc: nc.tensor.transpose(psb[:, tk * 128:(tk + 1) * 128], x2T[:, cc, tk * 128:(tk + 1) * 128], ident[:]),
                          reads=[x2T, ident], writes=[psb], inc=(tk == 7))
                ACT(lambda cc=cc: nc.scalar.copy(out=mrg[:, :, 512 + cc * 128:512 + (cc + 1) * 128],
                                                 in_=psb[:, :].rearrange("p (t c) -> p t c", c=128)), [psb], [mrg])
            else:
                DVE(lambda cc=cc: nc.vector.scalar_tensor_tensor(out=mrg[:, 4 + cc, :], in0=x2T[:, cc, :], scalar=gains[:, 28 + cc:29 + cc],
                                                                 in1=rstd[:], op0=ALU.mult, op1=ALU.mult), [x2T, gains, rstd], [mrg])
        dead_h = [x1T, x2T, ystg, pstg, vT, vT2, rstd, yt1, yt2, yt3, yt4] + Ys + vtok + ysa + ysb + xstg + sq
        if g == "S":
            dead_h += [Wm[1024], Ktab[1024]]

        cur = [O_E]

        def sbe(name, shape, dt):
            nb = int(np.prod(shape[1:])) * (2 if dt == BF16 else 4)
            t = sb(g + name, shape, dt, cur[0])
            cur[0] += (nb + 63) // 64 * 64
            return t
        if early_att is not None:
            qT, kT, Vaug, kst = early_att
        else:
            qT = sbe("qT", [128, 4, NTOK], BF16)
            kT = sbe("kT", [128, 4, NTOK], BF16)
            Vaug = sbe("Vaug", [128, 8, 8, 65], BF16)
            kst = []
        ckT = sbe("ckT", [128, 4, 256], BF16)
        cVaug = sbe("cVaug", [128, 2, 8, 65], BF16)
        PT = [sbe(f"PT{i}", [128, 896], BF16) for i in range(2)]
        atok = sbe("atok", [128, 512], F32)
        an = sbe("an", [128, 512], BF16)
        if g == "S":
            tabs = {"E": sbe("tabE", [128, 8, 1024], BF16), "I": sbe("tabI", [128, 8, 1024], BF16)}
        else:
            PT.append(sbe("PT2", [128, 896], BF16))
            PT.append(sbe("PT3", [128, 896], BF16))
            tabs = {"E": PT[0], "I": PT[0]}
        att_t = [qT, kT, Vaug, ckT, cVaug, atok, an, tabs["E"], tabs["I"]] + PT + kst
        for t in att_t:
            fw.alias(t, dead_h + dead_in)
        if g == "S":
            fw.dma("pool", ckT[:], ckT_d, writes=[ckT])
            fw.dma("pool", cVaug[:, :, :, 0:64], cv_d, writes=[cVaug])
            fw.dma("pool", tabs["E"][:], tabE_d, writes=[tabs["E"]])
            fw.dma("pool", tabs["I"][:], tabI_d, writes=[tabs["I"]])
            POOL(lambda: nc.gpsimd.memset(cVaug[:, :, :, 64:65], 1.0), [], [cVaug])
            for nm in ("E", "I"):
                for hq in range(4):
                    ACT(lambda nm=nm, hq=hq: nc.scalar.activation(out=tabs[nm][:, 2 * hq:2 * hq + 2, :], in_=tabs[nm][:, 2 * hq:2 * hq + 2, :],
                                                                 func=AF.Exp), [tabs[nm]], [tabs[nm]])
        if early_att is None:
            POOL(lambda: nc.gpsimd.memset(Vaug[:, :, :, 64:65], 1.0), [], [Vaug])
            s2b_pieces = make_s2b(qT, kT, Vaug, kst)
        while s2b_pieces:
            s2b_pieces.pop(0)()

        gb = sb(g + "gb", [128, 512], F32, O_B)
        fw.alias(gb, [hT] + dead_in)
        if g == "S":
            fw.dma("sp", gb[:], gbrow_d.partition_broadcast(128), writes=[gb])
        rot["banks"] = [2, 3, 4, 5, 6]
        rot["i"] = 0
        kp_of = {0: [0, 1, 2, 3], 1: [0, 1, 2, 3], 2: [0, 1, 2, 3, 4], 3: [1, 2, 3, 4, 5], 4: [2, 3, 4, 5, 6],
                 5: [3, 4, 5, 6, 7], 6: [4, 5, 6, 7], 7: [4, 5, 6, 7]}
        SC = 1.0
        O = [psf[0], psf[1]]
        Osb = sbe("Osb", [128, 520], F32)
        fw.alias(Osb, dead_h + dead_in)
        if g == "P":
            jobs = [(tk, h) for tk in range(8) for h in (0, 1, 4, 5)]
        else:
            jobs = [(tk, h) for tk in range(8) for h in range(8)]
        st = {}

        def slots_of(tk):
            if g == "P":
                s_ = tk // 2
                return [("k", 2 * s_, 0), ("k", 2 * s_ + 1, 0), ("k", 2 * s_, 2), ("k", 2 * s_ + 1, 2)]
            return [("b", kt, 0) for kt in reversed(kp_of[tk])] + [("c", 0, 0), ("c", 1, 0)]

        def emit_scores(ji):
            tk, h = jobs[ji]
            slots = slots_of(tk)
            ns = len(slots)
            pA = nextps()
            pB = nextps() if ns > 4 else None
            for si, (kind, kt, dh_) in enumerate(slots):
                hh_ = h + dh_
                c, pb_ = hh_ // 2, 64 * (hh_ % 2)
                q_ap = qT[pb_:pb_ + 64, c, tk * 128:(tk + 1) * 128]
                p = pA if si < 4 else pB
                o_ap = p[:, (si % 4) * 128:(si % 4) * 128 + 128]
                if kind == "c":
                    mm(o_ap, ckT[pb_:pb_ + 64, c, kt * 128:(kt + 1) * 128], q_ap, True, True, [ckT, qT], [p], True)
                else:
                    k_ap = kT[pb_:pb_ + 64, c, kt * 128:(kt + 1) * 128]
                    mm(o_ap, k_ap, q_ap, True, True, [kT, qT], [p], True)
            pt = PT[ji % len(PT)]
            n1 = min(ns, 4)
            ACT(lambda: nc.scalar.activation(out=pt[:, 0:n1 * 128], in_=pA[:, 0:n1 * 128], func=AF.Exp, scale=SC), [pA], [pt])
            if ns > 4:
                ACT(lambda: nc.scalar.activation(out=pt[:, 512:ns * 128], in_=pB[:, 0:(ns - 4) * 128], func=AF.Exp, scale=SC), [pB], [pt])
            if g == "S":
                nb = ns - 2
                tab = tabs["E"] if tk in (0, 1, 6, 7) else tabs["I"]
                kt0 = slots[0][1]
                i0 = 7 - (2 * kt0 - 2 * tk)
                DVE(lambda: nc.vector.tensor_tensor(out=pt[:, 0:nb * 128], in0=pt[:, 0:nb * 128],
                                                    in1=tab[:, h, i0 * 64:i0 * 64 + nb * 128], op=ALU.mult), [pt, tab], [pt])
            st[ji] = (pt, slots)

        def emit_pv(ji):
            tk, h = jobs[ji]
            pt, slots = st.pop(ji)
            ns = len(slots)
            for dh_ in sorted(set(sl_[2] for sl_ in slots)):
                hh_ = h + dh_
                ob = O[hh_ // 4]
                o_ap = ob[:, (hh_ % 4) * 65:(hh_ % 4) * 65 + 65]
                idx = [si for si, sl_ in enumerate(slots) if sl_[2] == dh_]
                for n_, si in enumerate(idx):
                    kind, kt, _ = slots[si]
                    if kind == "c":
                        v_ap, vt = cVaug[:, kt, hh_, :], cVaug
                    else:
                        v_ap, vt = Vaug[:, kt, hh_, :], Vaug
                    mm(o_ap, pt[:, si * 128:(si + 1) * 128], v_ap, n_ == 0, n_ == len(idx) - 1, [pt, vt], [ob], n_ == len(idx) - 1)

        def emit_tail_a(tk):
            for hb in range(2):
                ACT(lambda hb=hb: nc.scalar.copy(out=Osb[:, hb * 260:(hb + 1) * 260], in_=O[hb][:, 0:260]), [O[hb]], [Osb])
            o3 = Osb[:].rearrange("p (h d) -> p h d", d=65)
            DVE(lambda: nc.vector.reciprocal(out=smalls[:, 0:8], in_=o3[:, :, 64]), [Osb], [smalls])
            for hh in range(8):
                DVE(lambda hh=hh: nc.vector.tensor_scalar(out=atok[:, hh * 64:(hh + 1) * 64], in0=o3[:, hh, 0:64],
                                                          scalar1=smalls[:, hh:hh + 1], scalar2=None, op0=ALU.mult),
                    [Osb, smalls], [atok])

        def emit_tail_a2(tk):
            ACT(lambda: nc.scalar.activation(out=an[:], in_=atok[:], func=AF.Square, accum_out=smalls[:, 8:9]), [atok], [an, smalls])
            ACT(lambda: nc.scalar.activation(out=smalls[:, 9:10], in_=smalls[:, 8:9], func=AF.Ln, bias=epsc[:, 0:1], scale=float(1.0 / 512)), [smalls, epsc], [smalls])
            ACT(lambda: nc.scalar.activation(out=smalls[:, 10:11], in_=smalls[:, 9:10], func=AF.Exp, scale=-0.5), [smalls], [smalls])
            if g == "S":
                DVE(lambda: nc.vector.scalar_tensor_tensor(out=mrg[:, tk, 0:512], in0=atok[:], scalar=smalls[:, 10:11], in1=gb[:],
                                                           op0=ALU.mult, op1=ALU.mult), [atok, smalls, gb], [mrg])
            else:
                DVE(lambda: nc.vector.tensor_scalar(out=an[:], in0=atok[:], scalar1=smalls[:, 10:11], scalar2=None, op0=ALU.mult), [atok, smalls], [an])

        def emit_tail_b(tk):
            if g == "S":
                return
            for c4 in range(4):
                fw.op("pe", lambda c4=c4: nc.tensor.transpose(psb[:, c4 * 128:(c4 + 1) * 128], an[:, c4 * 128:(c4 + 1) * 128], ident[:]),
                      reads=[an, ident], writes=[psb], inc=(c4 == 3))
            for c4 in range(4):
                DVE(lambda c4=c4: nc.vector.tensor_scalar(out=mrg[:, c4, tk * 128:(tk + 1) * 128], in0=psb[:, c4 * 128:(c4 + 1) * 128],
                                                          scalar1=gains[:, 24 + c4:25 + c4], scalar2=None, op0=ALU.mult), [psb, gains], [mrg])

        Dp = len(PT) - 1
        sched = []
        nj = len(jobs)
        for ji in range(min(Dp, nj)):
            emit_scores(ji)
        for ji in range(nj):
            if ji + Dp < nj:
                emit_scores(ji + Dp)
            emit_pv(ji)
            tk, h = jobs[ji]
            while sched and sched[0][0] <= ji:
                sched.pop(0)[1]()
            if ji + 1 == nj or jobs[ji + 1][0] != tk:
                emit_tail_a(tk)
                sched.append((ji + 2, lambda tk=tk: emit_tail_a2(tk)))
                sched.append((ji + 4, lambda tk=tk: emit_tail_b(tk)))
        while sched:
            sched.pop(0)[1]()
        rot["banks"] = [0, 1, 2, 3, 4, 5]
        rot["i"] = 0
        dead_a = att_t + [hT, Osb]

        tap(g + "mrg", mrg, mrg[:], [128, 8, NTOK], BF16)

        if g == "S":
            yield from s_tail(dead_a + dead_h + dead_in + [gb], mrg)
            return
        xresc = [sb(g + f"xres{c}", [128, NTOK], F32, O_A + c * 4 * KB) for c in range(8)]
        fstg = [sb(g + f"fstg{i}", [128, NTOK], F32, O_E + 44 * KB + i * 4 * KB) for i in range(4)]
        rstd2 = sb(g + "rstd2", [128, NTOK], F32, O_E + 60 * KB)
        sq2 = [sb(g + f"sq2{i}", [128, NTOK], BF16, O_E + 64 * KB + i * 2 * KB) for i in range(2)]
        actT = sb(g + "actT", [128, 22, NTOK], BF16, O_E)
        hT2 = sb(g + "hT2", [128, 8, NTOK], BF16, O_B)
        for t in xresc + [rstd2, actT, hT2] + fstg + sq2:
            fw.alias(t, dead_a + dead_h + dead_in)
        def stats_chunk(pss_, c):
            s_ = sq2[c % 2]
            ACT(lambda: nc.scalar.activation(out=s_[:], in_=xresc[c][:], func=AF.Square), [xresc[c]], [s_])
            for tt in range(2):
                mm(pss_[tt][:], ones[:], s_[:, tt * 512:(tt + 1) * 512], c == 0, c == 7, [ones, s_], [pss_[tt]], True)

        pss6 = [psf[6], psf[5]]
        rot["banks"] = [0, 1, 2, 3, 4]
        rot["i"] = 0
        for b in range(2):
            wt = next_wbuf()
            fw.dma("pool", wt[:], wout_d[:, :, b * 512:(b + 1) * 512], writes=[wt])
            for j in range(4):
                cj = b * 4 + j
                xs = fstg[cj % 2]
                fw.dma("sp", xs[:], x_d[:, cj, :], writes=[xs])
                for tt in range(2):
                    p = nextps()
                    sl = slice(tt * 512, (tt + 1) * 512)
                    for kc in range(8):
                        mm(p[:], wt[:, kc, j * 128:(j + 1) * 128], mrg[:, kc, sl], kc == 0, kc == 7, [wt, mrg], [p], kc == 7)
                    DVE(lambda p=p, cj=cj, sl=sl, xs=xs: nc.vector.scalar_tensor_tensor(
                        out=xresc[cj][:, sl], in0=p[:], scalar=modcol(sset, 2, cj), in1=xs[:, sl], op0=ALU.mult, op1=ALU.add),
                        [p, mods, xs], [xresc[cj]])
                if cj >= 1:
                    stats_chunk(pss6, cj - 1)
        stats_chunk(pss6, 7)

        rot["banks"] = [0, 1, 2, 3, 4, 5]
        rot["i"] = 0
        rstd_from(pss6, rstd2, D)
        for c in range(8):
            ft = fstg[c % 2]
            DVE(lambda c=c, ft=ft: nc.vector.tensor_tensor(out=ft[:], in0=xresc[c][:], in1=rstd2[:], op=ALU.mult), [xresc[c], rstd2], [ft])
            ACT(lambda c=c, ft=ft: nc.scalar.activation(out=hT2[:, c, :], in_=ft[:], func=AF.Identity,
                                                        bias=modcol(sset, 3, c), scale=gm[:, sset, 1, c:c + 1]), [ft, mods, gm], [hT2])

        items = []
        wts = {}
        for blk in range(11):
            for i in range(2):
                jj = 2 * blk + i
                for which, jcol in ((0, i), (1, 2 + i)):
                    fcx = which * 22 + jj
                    sy = fstg[2 * which + (jj % 2)]
                    stt = {}

                    def A(blk=blk, i=i, which=which, jcol=jcol, fcx=fcx, sy=sy, stt=stt):
                        if i == 0 and which == 0:
                            wt = next_wbuf()
                            fw.dma("pool", wt[:], wup_d[:, :, blk * 512:(blk + 1) * 512], writes=[wt])
                            wts[blk] = wt
                        wt = wts[blk]
                        pa, pb2, big = nextpair()
                        for tt, p in enumerate((pa, pb2)):
                            for kc in range(8):
                                mm(p[:], wt[:, kc, jcol * 128:(jcol + 1) * 128], hT2[:, kc, tt * 512:(tt + 1) * 512], kc == 0, kc == 7, [wt, hT2], [p], kc == 7)
                        stt["p"] = (pa, pb2, big)
                        dw_A(pa, pb2, big, ffc[:, fcx, :], sy)

                    def B(fcx=fcx, sy=sy, stt=stt):
                        dw_B(*stt["p"], ffc[:, fcx, :], sy)

                    def C(which=which, jj=jj, sy=sy):
                        if which == 0:
                            ACT(lambda: nc.scalar.activation(out=sy[:], in_=sy[:], func=AF.Silu), [sy], [sy])
                        else:
                            gs = fstg[jj % 2]
                            DVE(lambda: nc.vector.tensor_tensor(out=actT[:, jj, :], in0=gs[:], in1=sy[:], op=ALU.mult), [gs, sy], [actT])
                    items.append([A, B, C])
        run_pipeline(items)

        tap(g + "actT", actT, actT[:], [128, 22, NTOK], BF16)
        pss9 = [psf[6], psf[5]]
        rot["banks"] = [0, 1, 2, 3, 4]
        rot["i"] = 0
        for cj in range(8):
            wt = next_wbuf()
            wflat = wt[:].rearrange("p a b -> p (a b)")
            fw.dma("pool", wflat[:, 0:22 * 128], wdn_d[:, cj, :], writes=[wt])
            for tt in range(2):
                p = nextps()
                sl = slice(tt * 512, (tt + 1) * 512)
                for kk in range(22):
                    mm(p[:], wflat[:, kk * 128:(kk + 1) * 128], actT[:, kk, sl], kk == 0, kk == 21, [wt, actT], [p], kk == 21)
                DVE(lambda p=p, cj=cj, sl=sl: nc.vector.scalar_tensor_tensor(
                    out=xresc[cj][:, sl], in0=p[:], scalar=modcol(sset, 5, cj), in1=xresc[cj][:, sl], op0=ALU.mult, op1=ALU.add),
                    [p, mods, xresc[cj]], [xresc[cj]])
            if cj >= 1:
                stats_chunk(pss9, cj - 1)

        stats_chunk(pss9, 7)
        if g == "S":
            live["S"] = [actT, hT2, mrg] + dead_a + dead_h + dead_in
            prefetch_x("P", live["S"])
            yield "pre_S9"
        rstd_from(pss9, rstd2, D)
        rot["banks"] = [0, 1, 2, 3, 4, 5]
        rot["i"] = 0
        for c in range(8):
            ft = fstg[c % 4]
            DVE(lambda c=c, ft=ft: nc.vector.scalar_tensor_tensor(out=ft[:], in0=xresc[c][:], scalar=gains[:, 16 + c:17 + c],
                                                                 in1=rstd2[:], op0=ALU.mult, op1=ALU.mult), [xresc[c], gains, rstd2], [ft])
            fw.dma("sp", yT_o[g][:, c, :], ft[:], reads=[ft], writes=[OUT])
        final_dead[g] = xresc + [rstd2, actT, hT2, mrg] + fstg + sq2 + dead_a + dead_h

    for _ in range(3):
        mods_early_issue()
    dead = filter_phase(1024, [])
    dead = dead + filter_phase(256, dead)
    fw.alias(Wm[1024], dead)
    fw.dma("sp", Wm[256][:], W_d[256], writes=[Wm[256]])
    fw.dma("sp", Wm[1024][:], W_d[1024], writes=[Wm[1024]])
    mods_final(0)
    gS = group("S", dead)
    assert next(gS) == "post_S1"
    assert next(gS) == "pre_S9"
    gP = group("P", dead + live["S"])
    assert next(gP) == "post_S1"
    for _ in gS:
        pass
    for _ in gP:
        pass
    fw.finish()
    return nc


_CACHE = {}


def _fm(v, nch):
    return np.ascontiguousarray(v.reshape(nch, 128).T)


def _wk(w):
    K, N = w.shape
    return np.ascontiguousarray(w.reshape(K // 128, 128, N).transpose(1, 0, 2))


def _consts():
    if "c" in _CACHE:
        return _CACHE["c"]
    c = {}
    for L in (1024, 256):
        t = np.arange(L, dtype=np.float64)
        f = np.arange(L, dtype=np.float64)
        om = np.pi * (f + 0.5) / L
        Cs = np.cos(np.outer(t + 0.5, om))
        Ss = np.sin(np.outer(t + 0.5, om))
        W = np.concatenate([Cs, -Ss], 1)
        Wu = np.concatenate([np.cos(np.outer(t, om)), np.sin(np.outer(t, om))], 1) / L
        c[f"W{L}"] = _wk(W.astype(np.float32)).astype(ml_dtypes.bfloat16)
        c[f"Wu{L}"] = _wk(Wu.astype(np.float32)).astype(ml_dtypes.bfloat16)
        t32 = np.arange(L, dtype=np.float32) / np.float32(L)
        fr = np.arange(1, 9, dtype=np.float32)
        ang = np.float32(2.0 * np.pi) * t32[:, None] * fr[None, :]
        z = np.concatenate([t32[:, None], np.cos(ang), np.sin(ang)], -1).astype(np.float32)
        zT = np.concatenate([z.T, np.ones((1, L), np.float32)], 0)
        c[f"zT{L}"] = np.ascontiguousarray(zT)
        min_decay = np.log(1e-2) / 1.5
        max_decay = np.log(1e-2) / 0.3
        deltas = np.abs(np.linspace(min_decay, max_decay, 512, dtype=np.float32))
        decay = np.exp(-t32[:, None] * deltas[None, :]).astype(np.float32)
        decayB = decay.copy()
        decayB[0] = 0.0
        dd = np.concatenate([decay, decayB], 1)
        c[f"dec{L}"] = _wk(dd)
    c["ident"] = np.eye(128, dtype=np.float32).astype(ml_dtypes.bfloat16)
    e0 = np.zeros((128, 1), np.float32)
    e0[0, 0] = -1.0
    c["e0n"] = e0
    kc = np.arange(64)[:, None]
    qc = np.arange(64)[None, :]
    cstart = np.clip(qc - 8, 0, 48)
    colin = (kc >= cstart) & (kc < cstart + 16)
    dc = np.clip(kc - qc, -15, 15) + 15
    idxE = np.zeros((128, 16, 64), np.int64)
    idxI = np.zeros((128, 16, 64), np.int64)
    SENT = 15 * 31
    for half in range(2):
        for i in range(16):
            dl = (7 - i) if half == 0 else (8 - i)
            for tab, ok in ((idxE, abs(dl) <= 7), (idxI, -4 <= dl <= 3)):
                if ok:
                    tab[half * 64:(half + 1) * 64, i, :] = np.where(colin, (dl + 7) * 31 + dc, SENT)
                else:
                    tab[half * 64:(half + 1) * 64, i, :] = SENT
    c["idxE"], c["idxI"] = idxE, idxI
    _CACHE["c"] = c
    return c


def _prepare(x_prompt, x_sample, cache_ctx_k, cache_ctx_v, c, c_ctx, w_ada, b_ada, norm1_g,
           w_in, rpb, hy_conv_w, hy_conv_b, filt_w1, filt_b1, filt_w2, filt_b2, filt_w3,
           filt_b3, filt_freq, filt_bias, grp_norm_g, w_out, norm2_g, w_up, ffn_conv_w,
           ffn_conv_b, w_down, final_g):
    f32 = np.float32
    A = lambda a: np.asarray(a, dtype=f32)
    cs = _consts()
    x_prompt, x_sample = A(x_prompt), A(x_sample)
    shared = {}
    shared["w_ada"] = _wk(A(w_ada)[0])
    shared["b_ada"] = _fm(A(b_ada)[0], 48)
    shared["gains"] = np.concatenate([_fm(A(norm1_g)[0], 8), _fm(A(norm2_g)[0], 8), _fm(A(final_g), 8), _fm(A(grp_norm_g)[0], 8)], 1)
    shared["w_in"] = _wk(A(w_in)[0])
    hw, hb = A(hy_conv_w)[0], A(hy_conv_b)[0]
    shared["hyc"] = np.ascontiguousarray(np.stack([_fm(hw[0], 12), _fm(hw[1], 12), _fm(hw[2], 12), _fm(hb, 12)], -1))
    fwc, fbc = A(ffn_conv_w)[0], A(ffn_conv_b)[0]
    shared["ffc"] = np.ascontiguousarray(np.stack([_fm(fwc[0], 44), _fm(fwc[1], 44), _fm(fwc[2], 44), _fm(fbc, 44)], -1))
    shared["w_out"] = _wk(A(w_out)[0])
    wu = A(w_up)[0]
    cols = []
    for blk in range(11):
        for i in range(2):
            cols.append(np.arange((2 * blk + i) * 128, (2 * blk + i + 1) * 128))
        for i in range(2):
            cols.append(DFF + np.arange((2 * blk + i) * 128, (2 * blk + i + 1) * 128))
    shared["w_up"] = _wk(np.ascontiguousarray(wu[:, np.concatenate(cols)]))
    wd = A(w_down)[0]
    wd4 = wd.reshape(22, 128, 8, 128)
    shared["w_down"] = np.ascontiguousarray(wd4.transpose(1, 2, 0, 3).reshape(128, 8, 22 * 128))
    rp = np.concatenate([A(rpb)[0].reshape(8, 15 * 31), np.full((8, 1), NEGM, f32)], 1)
    shared["tabE"] = np.ascontiguousarray(rp[:, cs["idxE"]].transpose(1, 0, 2, 3).reshape(128, 8, 1024))
    shared["tabI"] = np.ascontiguousarray(rp[:, cs["idxI"]].transpose(1, 0, 2, 3).reshape(128, 8, 1024))
    shared["w1b1"] = np.ascontiguousarray(np.concatenate([A(filt_w1)[0], A(filt_b1)[0][None, :]], 0))
    shared["w2"] = np.ascontiguousarray(A(filt_w2)[0])
    shared["fsm"] = np.ascontiguousarray(np.stack([A(filt_b2)[0], A(filt_freq)[0], np.zeros(64, f32), np.zeros(64, f32)], 1))
    shared["w3b3"] = np.ascontiguousarray(np.concatenate([A(filt_w3)[0], A(filt_b3)[0][None, :]], 0))
    shared["fb"] = np.ascontiguousarray(A(filt_bias)[0].reshape(1, 1024))
    shared["gbrow"] = np.ascontiguousarray(A(grp_norm_g)[0][None, 0:512])
    for k in ("zT1024", "zT256", "dec1024", "dec256", "W1024", "W256", "Wu1024", "Wu256", "ident", "e0n"):
        shared[k] = cs[k]
    ck, cv = A(cache_ctx_k), A(cache_ctx_v)
    cc_, cctx = A(c), A(c_ctx)
    in_maps = []
    for core in range(8):
        b = core % 4
        m = dict(shared)
        m["xsT"] = np.ascontiguousarray(x_sample[b].T.reshape(8, 128, NTOK).transpose(1, 0, 2))
        xp = x_prompt[core * 4:(core + 1) * 4].reshape(NTOK, D)
        m["xpT"] = np.ascontiguousarray(xp.T.reshape(8, 128, NTOK).transpose(1, 0, 2))
        m["csil"] = np.ascontiguousarray(np.stack([_fm(cctx, 8), _fm(cc_[b], 8)], -1))
        off = 512 * (core // 4)
        xpad = np.zeros((NTOK + 2, D), f32)
        xpad[1:NTOK + 1] = x_sample[b]
        m["xo"] = np.ascontiguousarray(xpad[off:off + 514].T.reshape(8, 128, 514).transpose(1, 0, 2))
        selm = np.zeros((NTOK, 514), f32)
        ii = np.arange(514)
        tt = off - 1 + ii
        ok = (tt >= 0) & (tt < NTOK)
        selm[tt[ok], ii[ok]] = 1.0
        m["sel"] = np.ascontiguousarray(selm.reshape(8, 128, 514).transpose(1, 0, 2)).astype(ml_dtypes.bfloat16)
        m["mrow"] = np.ascontiguousarray(ok.astype(f32)[None, :])
        kk = ck[b, 0]
        m["ckT"] = np.ascontiguousarray(kk.reshape(4, 2, 256, 64).transpose(1, 3, 0, 2).reshape(128, 4, 256))
        vv = cv[b, 0]
        m["cv"] = np.ascontiguousarray(vv.reshape(8, 2, 128, 64).transpose(2, 1, 0, 3))
        in_maps.append(m)
    return in_maps


def _assemble(R):
    f32 = np.float32
    y_prompt = np.zeros((32, 256, D), f32)
    y_sample = np.zeros((4, 1024, D), f32)
    sk = np.zeros((32, 1, 8, 256, 64), f32)
    sv = np.zeros((32, 1, 8, 256, 64), f32)
    for core in range(8):
        r = R[core]
        yp = np.asarray(r["ypT"]).transpose(1, 0, 2).reshape(D, NTOK).T
        y_prompt[core * 4:(core + 1) * 4] = yp.reshape(4, 256, D)
        off = 512 * (core // 4)
        y_sample[core % 4, off:off + 512] = np.asarray(r["ysT"]).transpose(1, 0, 2).reshape(D, 512).T
        kTo = np.asarray(r["kT_o"]).transpose(1, 0, 2).reshape(512, NTOK)
        sk[core * 4:(core + 1) * 4, 0] = kTo.reshape(8, 64, 4, 256).transpose(2, 0, 3, 1)
        vo = np.asarray(r["v_o"]).reshape(128, 8, 8, 64)
        vo = vo.transpose(1, 0, 2, 3).reshape(4, 256, 8, 64).transpose(0, 2, 1, 3)
        sv[core * 4:(core + 1) * 4, 0] = vo
    return (y_prompt, y_sample, sk, sv)


def kernel(**inputs):
    in_maps = _prepare(**inputs)
    if "nc" not in _CACHE:
        _CACHE["nc"] = build_program()
    res = run_bass_kernel_spmd(_CACHE["nc"], in_maps, core_ids=list(range(8)))
    return _assemble(res.results)
```

```python
import numpy as np
import ml_dtypes
import concourse.bass as bass
import concourse.mybir as mybir
from concourse.bass_utils import run_bass_kernel_spmd

F32 = mybir.dt.float32
BF16 = mybir.dt.bfloat16
I32 = mybir.dt.int32
AF = mybir.ActivationFunctionType
ALU = mybir.AluOpType

D = 1024
NTOK = 1024
DFF = 2816
EPS = 1e-6
NEGM = -30000.0
SAME_ENGINE_SYNC = True


class T:
    __slots__ = ("h", "lw", "rd", "name")

    def __init__(self, h, name=""):
        self.h = h
        self.lw = None
        self.rd = {}
        self.name = name

    def __getitem__(self, idx):
        return self.h[idx]


class FW:
    def __init__(self, nc, n_dma_sems=24):
        self.nc = nc
        self.eng = {"pe": nc.tensor, "act": nc.scalar, "dve": nc.vector,
                    "pool": nc.gpsimd, "sp": nc.sync}
        self.sem = {k: nc.alloc_semaphore(name=f"s_{k}") for k in self.eng}
        self.cnt = {k: 0 for k in self.eng}
        self.seen = {k: {} for k in self.eng}
        self.dsems = [nc.alloc_semaphore(name=f"d_{i}") for i in range(n_dma_sems)]
        self.dcnt = [0] * n_dma_sems
        self.dnext = {"sp": 0, "pool": 0}
        self.drange = {"sp": (0, n_dma_sems // 2), "pool": (n_dma_sems // 2, n_dma_sems)}

    def _wait(self, e, dep):
        kind, k, v = dep
        if kind == "e" and k == e and (e == "pe" or not SAME_ENGINE_SYNC):
            return
        key = (kind, k)
        if self.seen[e].get(key, 0) >= v:
            return
        self.seen[e][key] = v
        s = self.sem[k] if kind == "e" else self.dsems[k]
        self.eng[e].wait_ge(s, v)

    def _deps(self, reads, writes):
        deps = []
        for t in reads:
            if t.lw is not None:
                deps.append(t.lw)
        for t in writes:
            if t.lw is not None:
                deps.append(t.lw)
            deps.extend((k[0], k[1], v) for k, v in t.rd.items())
        return deps

    def op(self, e, fn, reads=(), writes=(), inc=True):
        for d in self._deps(reads, writes):
            self._wait(e, d)
        ins = fn()
        if inc:
            self.cnt[e] += 1
            ins.then_inc(self.sem[e], 1)
            me = ("e", e, self.cnt[e])
        else:
            me = ("e", e, self.cnt[e] + 1)
        self._mark(me, reads, writes)
        return ins

    def _mark(self, me, reads, writes):
        key = (me[0], me[1])
        for t in reads:
            if t.rd.get(key, 0) < me[2]:
                t.rd[key] = me[2]
        for t in writes:
            t.lw = me
            t.rd = {}

    def dma(self, q, out, in_, reads=(), writes=(), **kw):
        for d in self._deps(reads, writes):
            self._wait(q, d)
        lo, hi = self.drange[q]
        i = lo + self.dnext[q]
        self.dnext[q] = (self.dnext[q] + 1) % (hi - lo)
        if self.dcnt[i] > 0:
            self._wait(q, ("d", i, self.dcnt[i]))
        self.dcnt[i] += 16
        ins = self.eng[q].dma_start(out=out, in_=in_, **kw)
        ins.then_inc(self.dsems[i], 16)
        me = ("d", i, self.dcnt[i])
        self._mark(me, reads, writes)
        return me

    def alias(self, new, olds):
        for o in olds:
            ds = list((k[0], k[1], v) for k, v in o.rd.items())
            if o.lw is not None:
                ds.append(o.lw)
            for d in ds:
                key = (d[0], d[1])
                if new.rd.get(key, 0) < d[2]:
                    new.rd[key] = d[2]

    def finish(self):
        for k in ("pe", "act", "dve", "pool"):
            if self.cnt[k] > 0:
                self._wait("sp", ("e", k, self.cnt[k]))
        for i, c in enumerate(self.dcnt):
            if c > 0:
                self._wait("sp", ("d", i, c))


KB = 1024


DEBUG = False
TAPS = []


def build_program():
    nc = bass.Bass("TRN2", target_bir_lowering=False)
    fw = FW(nc)
    del TAPS[:]

    def tap(name, t, ap, shape, dt=F32):
        if not DEBUG:
            return
        d = nc.dram_tensor("tap_" + name, list(shape), dt, kind="ExternalOutput").ap()
        fw.dma("sp", d, ap, reads=[t], writes=[T(None)])
        TAPS.append("tap_" + name)

    def din(name, shape, dt=F32):
        return nc.dram_tensor(name, list(shape), dt, kind="ExternalInput").ap()

    def dout(name, shape):
        return nc.dram_tensor(name, list(shape), F32, kind="ExternalOutput").ap()

    xT = {"S": din("xsT", [128, 8, NTOK]), "P": din("xpT", [128, 8, NTOK])}
    csil_d = din("csil", [128, 8, 2])
    wada_d = din("w_ada", [128, 8, 6144])
    bada_d = din("b_ada", [128, 48])
    gains_d = din("gains", [128, 32])
    win_d = din("w_in", [128, 8, 3072])
    hyc_d = din("hyc", [128, 12, 4])
    ffc_d = din("ffc", [128, 44, 4])
    wout_d = din("w_out", [128, 8, 1024])
    wup_d = din("w_up", [128, 8, 5632])
    wdn_d = din("w_down", [128, 8, 22 * 128])
    ckT_d = din("ckT", [128, 4, 256])
    cv_d = din("cv", [128, 2, 8, 64])
    tabE_d = din("tabE", [128, 8, 1024])
    tabI_d = din("tabI", [128, 8, 1024])
    w1b1_d = din("w1b1", [18, 64])
    w2_d = din("w2", [64, 64])
    fsm_d = din("fsm", [64, 4])
    w3b3_d = din("w3b3", [65, 2048])
    fb_d = din("fb", [1, 1024])
    zT_d = {1024: din("zT1024", [18, 1024]), 256: din("zT256", [18, 256])}
    dec_d = {1024: din("dec1024", [128, 8, 1024]), 256: din("dec256", [128, 2, 1024])}
    W_d = {1024: din("W1024", [128, 8, 2048], BF16), 256: din("W256", [128, 2, 512], BF16)}
    Wu_d = {1024: din("Wu1024", [128, 8, 2048], BF16), 256: din("Wu256", [128, 2, 512], BF16)}
    ident_d = din("ident", [128, 128], BF16)
    e0n_d = din("e0n", [128, 1])
    yT_o = {"S": dout("ysT", [128, 8, 512]), "P": dout("ypT", [128, 8, NTOK])}
    xo_d = din("xo", [128, 8, 514])
    sel_d = din("sel", [128, 8, 514], BF16)
    mrow_d = din("mrow", [1, 514])
    gbrow_d = din("gbrow", [1, 512])
    kT_o = dout("kT_o", [128, 4, NTOK])
    v_o = dout("v_o", [128, 8, 512])
    OUT = T(None, "outs")

    base = (nc.sbuf_base + 63) // 64 * 64
    top = nc.sbuf_top

    def sb(name, shape, dt, off):
        nb = int(np.prod(shape[1:])) * (2 if dt == BF16 else 4)
        assert base + off + nb <= top, (name, off, nb, top - base)
        return T(nc.alloc_sbuf_tensor_at(name, list(shape), dt, offset=base + off), name)

    O_SM = 0
    ident = sb("ident", [128, 128], BF16, 0)
    ones = sb("ones", [128, 128], BF16, 256)
    mods = sb("mods", [128, 2, 48], F32, 512)
    gains = sb("gains", [128, 32], F32, 896)
    gm = sb("gm", [128, 2, 2, 8], F32, 1024)
    hyc = sb("hyc", [128, 12, 4], F32, 1152)
    ffc = sb("ffc", [128, 44, 4], F32, 1344)
    csil = sb("csil", [128, 8, 2], F32, 2048)
    csb = sb("csb", [128, 8, 2], BF16, 2112)
    bada = sb("bada", [128, 48], F32, 2176)
    epsc = sb("epsc", [128, 1], F32, 2368)
    e0n = sb("e0n", [128, 1], F32, 3200)
    fsm = sb("fsm", [64, 8], F32, 2400)
    w1b1 = sb("w1b1", [18, 64], F32, 2432)
    w2s = sb("w2s", [64, 64], F32, 2688)
    smalls = sb("smalls", [128, 64], F32, 2944)
    O_W256 = 5 * KB
    O_K256 = 7 * KB
    O_A = 15 * KB
    O_B = 79 * KB
    O_C = 95 * KB
    O_D = 111 * KB
    O_E = 135 * KB
    Wm = {256: sb("W256", [128, 2, 512], BF16, O_W256), 1024: sb("W1024", [128, 8, 2048], BF16, O_A)}
    Ktab = {256: sb("K256", [128, 2, 2, 2, 512], BF16, O_K256),
            1024: sb("K1024", [128, 8, 2, 2, 512], BF16, O_A + 32 * KB)}
    wbuf = [sb(f"wbuf{i}", [128, 8, 512], BF16, O_D + i * 8 * KB) for i in range(3)]
    wb_i = [0]

    pinned = set()

    def next_wbuf():
        while True:
            t = wbuf[wb_i[0] % 3]
            wb_i[0] += 1
            if t.name not in pinned:
                return t

    class HalfView:
        def __init__(self, big, off):
            self.big, self.off = big, off

        def __getitem__(self, idx):
            if not isinstance(idx, tuple):
                idx = (idx, slice(None))
            ps_, cs_ = idx
            a = 0 if cs_.start is None else cs_.start
            b = 512 if cs_.stop is None else cs_.stop
            return self.big[ps_, self.off + a:self.off + b]

    psd = [nc.alloc_psum_tensor(f"psd{i}", [128, 1024], F32) for i in range(3)]
    psf = [T(HalfView(psd[i // 2], 512 * (i % 2)), f"psf{i}") for i in range(6)]
    psf.append(T(nc.alloc_psum_tensor("psf6", [128, 512], F32), "psf6"))
    psb = T(nc.alloc_psum_tensor("psb", [128, 1024], BF16), "psb")
    rot = {"i": 0, "banks": [0, 1, 2, 3, 4, 5]}

    def nextps():
        b = rot["banks"][rot["i"] % len(rot["banks"])]
        rot["i"] += 1
        return psf[b]

    def nextpair():
        if rot["i"] % 2:
            rot["i"] += 1
        b = rot["banks"][rot["i"] % len(rot["banks"])]
        assert b % 2 == 0
        rot["i"] += 2
        return psf[b], psf[b + 1], psd[b // 2]

    ACT = lambda fn, r, w: fw.op("act", fn, reads=r, writes=w)
    DVE = lambda fn, r, w: fw.op("dve", fn, reads=r, writes=w)
    POOL = lambda fn, r, w: fw.op("pool", fn, reads=r, writes=w)

    def mm(ps_ap, lhsT, rhs, start, stop, reads, writes, last):
        return fw.op("pe", lambda: nc.tensor.matmul(ps_ap, lhsT=lhsT, rhs=rhs, start=start, stop=stop),
                     reads=reads, writes=writes, inc=last)

    fw.dma("sp", ident[:], ident_d, writes=[ident])
    fw.dma("sp", e0n[:], e0n_d, writes=[e0n])
    fw.dma("sp", csil[:], csil_d, writes=[csil])
    fw.dma("sp", bada[:], bada_d, writes=[bada])
    fw.dma("sp", gains[:], gains_d, writes=[gains])
    fw.dma("sp", hyc[:], hyc_d, writes=[hyc])
    fw.dma("sp", ffc[:], ffc_d, writes=[ffc])
    fw.dma("sp", fsm[:, 0:4], fsm_d, writes=[fsm])
    fw.dma("sp", w1b1[:], w1b1_d, writes=[w1b1])
    fw.dma("sp", w2s[:], w2_d, writes=[w2s])
    DVE(lambda: nc.vector.memset(ones[:], 1.0), [], [ones])
    DVE(lambda: nc.vector.memset(epsc[:], EPS), [], [epsc])
    ACT(lambda: nc.scalar.activation(out=csb[:], in_=csil[:], func=AF.Silu), [csil], [csb])
    DVE(lambda: nc.vector.tensor_scalar(out=fsm[:, 4:5], in0=fsm[:, 1:2], scalar1=float(1.0 / (2 * np.pi)),
                                        scalar2=None, op0=ALU.mult), [fsm], [fsm])

    pm = psf[6]
    mods_state = {"blk": 0}

    def mods_mm(blk, wt):
        for j in range(4):
            cj = blk * 4 + j
            for kc in range(8):
                mm(pm[:, cj * 2:cj * 2 + 2], wt[:, kc, j * 128:(j + 1) * 128], csb[:, kc, :],
                   kc == 0, kc == 7, [wt, csb], [pm], kc == 7)

    late = {"slots": None, "pend": []}
    early = {"pend": []}

    def mods_early_issue():
        blk = mods_state["blk"]
        if blk < 4:
            wt = next_wbuf()
            fw.dma("pool", wt[:], wada_d[:, :, blk * 512:(blk + 1) * 512], writes=[wt])
            early["pend"].append((blk, wt))
            mods_state["blk"] += 1

    def mods_early_tick():
        if early["pend"]:
            b0, w0 = early["pend"].pop(0)
            mods_mm(b0, w0)
            mods_early_issue()

    def mods_late_tick():
        blk = mods_state["blk"]
        if len(late["pend"]) == 2 or (blk >= 12 and late["pend"]):
            b0, w0 = late["pend"].pop(0)
            mods_mm(b0, w0)
        if blk < 12:
            wt = late["slots"][blk % 2]
            fw.dma("pool", wt[:], wada_d[:, :, blk * 512:(blk + 1) * 512], writes=[wt])
            late["pend"].append((blk, wt))
            mods_state["blk"] += 1

    def mods_tick(limit=12):
        blk = mods_state["blk"]
        if blk >= limit:
            return
        mods_state["blk"] += 1
        wt = next_wbuf()
        fw.dma("pool", wt[:], wada_d[:, :, blk * 512:(blk + 1) * 512], writes=[wt])
        for j in range(4):
            cj = blk * 4 + j
            for kc in range(8):
                mm(pm[:, cj * 2:cj * 2 + 2], wt[:, kc, j * 128:(j + 1) * 128], csb[:, kc, :],
                   kc == 0, kc == 7, [wt, csb], [pm], kc == 7)

    def mods_finish():
        while mods_state["blk"] < 12:
            mods_tick()
    def mods_final(part):
        if part == 0:
            while early["pend"]:
                mods_early_tick()
            while mods_state["blk"] < 4:
                mods_tick()
            c0, c1 = 0, 16
        else:
            while mods_state["blk"] < 12 or late["pend"]:
                if late["slots"] is not None:
                    mods_late_tick()
                else:
                    mods_tick()
            c0, c1 = 16, 48
        pm3 = pm[:, 0:96].rearrange("p (c s) -> p c s", s=2)
        for s in range(2):
            DVE(lambda s=s: nc.vector.tensor_tensor(out=mods[:, s, c0:c1], in0=pm3[:, c0:c1, s], in1=bada[:, c0:c1], op=ALU.add),
                [pm, bada], [mods])
        w = part
        for s in range(2):
            DVE(lambda s=s, w=w: nc.vector.scalar_tensor_tensor(
                out=gm[:, s, w, :], in0=mods[:, s, (8 + 24 * w):(16 + 24 * w)], scalar=1.0,
                in1=gains[:, 8 * w:8 * w + 8], op0=ALU.add, op1=ALU.mult), [mods, gains], [gm])
        if part == 1:
            tap("mods", mods, mods[:], [128, 2, 48])
            tap("gm", gm, gm[:], [128, 2, 2, 8])

    def modcol(s, j, c):
        return mods[:, s, j * 8 + c:j * 8 + c + 1]

    def filter_phase(L, dead_in):
        Lc = L // 128
        nh = max(1, L // 512)
        dec = sb(f"dec{L}", [128, Lc, 1024], F32, O_B)
        fs = sb(f"fs{L}", [128, Lc, 2, 512], BF16, O_A if L == 1024 else O_E + 60 * KB)
        fd = sb(f"fd{L}", [128, Lc, 2, 512], BF16, O_A + 16 * KB if L == 1024 else O_E + 64 * KB)
        yv = sb(f"yv{L}", [64, L], F32, O_E + 32 * KB)
        ti = sb(f"ti{L}", [64, L], I32, O_E + 36 * KB)
        tf = sb(f"tf{L}", [64, L], F32, O_E + 40 * KB)
        h1 = sb(f"h1{L}", [64, L], F32, O_E + 44 * KB)
        h2 = sb(f"h2{L}", [65, L], BF16, O_E + 48 * KB)
        zT = sb(f"zT{L}", [18, L], F32, O_E + 52 * KB)
        fbs = sb(f"fbs{L}", [128, 1024], F32, O_E + 56 * KB)
        w3 = sb(f"w3{L}", [96, 2048], F32, O_C if L == 256 else O_E + 64 * KB)
        w3c = sb(f"w3c{L}", [96, 2048], BF16, O_C + 8 * KB if L == 256 else O_E + 60 * KB)
        w3b = sb(f"w3b{L}", [96, 2, 512], BF16, O_C + 12 * KB if L == 256 else O_E + 50 * KB)
        t1, t2 = w3c, w3b
        for t in [dec, fs, fd, yv, ti, tf, h1, h2, zT, fbs, w3c, w3b, w3]:
            fw.alias(t, dead_in)
        fw.dma("sp", zT[:], zT_d[L], writes=[zT])
        DVE(lambda: nc.vector.memset(w3[64:96, :], 0.0), [], [w3])
        fw.dma("sp", w3[0:65, :], w3b3_d, writes=[w3])
        for o in range(2):
            wf = w3[:, o * 1024:o * 1024 + 512]
            wb_ = w3[:, o * 1024 + 512:o * 1024 + 1024]
            DVE(lambda o=o, wf=wf, wb_=wb_: nc.vector.tensor_tensor(out=w3c[:, o * 1024:o * 1024 + 512], in0=wf, in1=wb_, op=ALU.add), [w3], [w3c])
            DVE(lambda o=o, wf=wf, wb_=wb_: nc.vector.tensor_tensor(out=w3c[:, o * 1024 + 512:o * 1024 + 1024], in0=wb_, in1=wf, op=ALU.subtract), [w3], [w3c])
            DVE(lambda o=o, wb_=wb_: nc.vector.tensor_copy(out=w3b[:, o, :], in_=wb_), [w3], [w3b])
        fw.dma("sp", dec[:], dec_d[L], writes=[dec])
        fw.dma("sp", fbs[:], fb_d.partition_broadcast(128), writes=[fbs])
        if L == 1024:
            prefetch_x("S", [])
        W = min(L, 512)

        def sin_layer(src_ps_list, dst, add_b2):
            for i, p in enumerate(src_ps_list):
                sl = slice(i * W, (i + 1) * W)
                if add_b2:
                    DVE(lambda p=p, sl=sl: nc.vector.tensor_scalar(out=yv[:, sl], in0=p[0:64, 0:W], scalar1=fsm[:, 0:1],
                                                                   scalar2=fsm[:, 4:5], op0=ALU.add, op1=ALU.mult),
                        [p, fsm], [yv])
                    DVE(lambda sl=sl: nc.vector.tensor_scalar(out=yv[:, sl], in0=yv[:, sl], scalar1=64.0, scalar2=None,
                                                              op0=ALU.add), [yv], [yv])
                else:
                    DVE(lambda p=p, sl=sl: nc.vector.tensor_scalar(out=yv[:, sl], in0=p[0:64, 0:W], scalar1=fsm[:, 4:5],
                                                                   scalar2=64.0, op0=ALU.mult, op1=ALU.add),
                        [p, fsm], [yv])
            DVE(lambda: nc.vector.tensor_copy(out=ti[:], in_=yv[:]), [yv], [ti])
            DVE(lambda: nc.vector.tensor_copy(out=tf[:], in_=ti[:]), [ti], [tf])
            DVE(lambda: nc.vector.tensor_tensor(out=yv[:], in0=yv[:], in1=tf[:], op=ALU.subtract), [yv, tf], [yv])
            DVE(lambda: nc.vector.tensor_scalar(out=tf[:], in0=yv[:], scalar1=0.5, scalar2=None, op0=ALU.is_gt), [yv], [tf])
            DVE(lambda: nc.vector.tensor_tensor(out=yv[:], in0=yv[:], in1=tf[:], op=ALU.subtract), [yv, tf], [yv])
            DVE(lambda: nc.vector.tensor_scalar(out=tf[:], in0=yv[:], scalar1=-0.5, scalar2=None, op0=ALU.is_lt), [yv], [tf])
            DVE(lambda: nc.vector.tensor_tensor(out=yv[:], in0=yv[:], in1=tf[:], op=ALU.add), [yv, tf], [yv])
            ACT(lambda: nc.scalar.activation(out=dst[0:64, :], in_=yv[:], func=AF.Sin, scale=float(2 * np.pi)), [yv], [dst])

        pl = []
        for i in range(nh):
            p = nextps()
            mm(p[0:64, 0:W], w1b1[:], zT[:, i * W:(i + 1) * W], True, True, [w1b1, zT], [p], True)
            pl.append(p)
        sin_layer(pl, h1, False)
        pl = []
        for i in range(nh):
            p = nextps()
            mm(p[0:64, 0:W], w2s[:], h1[:, i * W:(i + 1) * W], True, True, [w2s, h1], [p], True)
            pl.append(p)
        sin_layer(pl, h2, True)
        DVE(lambda: nc.vector.memset(h2[64:65, :], 1.0), [], [h2])
        for tc in range(Lc):
            if tc >= 1:
                mods_early_tick()
            pq = [nextps() for _ in range(4)]
            for q in range(4):
                mm(pq[q][:], h2[:, tc * 128:(tc + 1) * 128], w3c[0:65, q * 512:(q + 1) * 512], True, True, [h2, w3c], [pq[q]], True)
            for o in range(2):
                ps_, pd_ = pq[2 * o], pq[2 * o + 1]
                DVE(lambda ps_=ps_, o=o: nc.vector.tensor_tensor(out=fs[:, tc, o, :], in0=ps_[:], in1=dec[:, tc, 0:512], op=ALU.mult), [ps_, dec], [fs])
                DVE(lambda pd_=pd_, o=o: nc.vector.tensor_tensor(out=fd[:, tc, o, :], in0=pd_[:], in1=dec[:, tc, 0:512], op=ALU.mult), [pd_, dec], [fd])
            if tc == 0:
                for o in range(2):
                    pc = nextps()
                    mm(pc[:], h2[:, 0:128], w3b[0:65, o, :], True, True, [h2, w3b], [pc], True)
                    DVE(lambda o=o, pc=pc: nc.vector.scalar_tensor_tensor(out=fs[:, 0, o, :], in0=pc[:], scalar=e0n[:, 0:1], in1=fs[:, 0, o, :],
                                                                         op0=ALU.mult, op1=ALU.add), [fs, pc, e0n], [fs])
                    DVE(lambda o=o, pc=pc: nc.vector.scalar_tensor_tensor(out=fd[:, 0, o, :], in0=pc[:], scalar=e0n[:, 0:1], in1=fd[:, 0, o, :],
                                                                         op0=ALU.mult, op1=ALU.add), [fd, pc, e0n], [fd])
        Kt = Ktab[L]
        nblk = (2 * L) // 512
        for blk in range(nblk):
            wt = next_wbuf()
            fw.dma("sp", wt[:, 0:Lc, :], Wu_d[L][:, :, blk * 512:(blk + 1) * 512], writes=[wt])
            for j in range(4):
                fr = blk * 4 + j
                isI = fr >= Lc
                fc = fr - Lc if isI else fr
                src = fd if isI else fs
                for o in range(2):
                    p = nextps()
                    for tc in range(Lc):
                        mm(p[:], wt[:, tc, j * 128:(j + 1) * 128], src[:, tc, o, :], tc == 0, tc == Lc - 1, [wt, src], [p], tc == Lc - 1)
                    if isI:
                        ACT(lambda p=p, fc=fc, o=o: nc.scalar.copy(out=Kt[:, fc, 1, o, :], in_=p[:]), [p], [Kt])
                    else:
                        DVE(lambda p=p, fc=fc, o=o: nc.vector.scalar_tensor_tensor(
                            out=Kt[:, fc, 0, o, :], in0=fbs[:, o * 512:(o + 1) * 512], scalar=float(1.0 / L), in1=p[:],
                            op0=ALU.mult, op1=ALU.add), [p, fbs], [Kt])
        tap(f"h1_{L}", h1, h1[:], [64, L])
        tap(f"h2_{L}", h2, h2[:], [65, L])
        tap(f"fs_{L}", fs, fs[:], [128, Lc, 2, 512], BF16)
        tap(f"K_{L}", Kt, Kt[:], [128, Lc, 2, 2, 512], BF16)
        return [dec, fs, fd, yv, ti, tf, h1, h2, zT, fbs, t1, t2, w3]


    def rstd_from(ps_list, dst, dcount):
        for i, p in enumerate(ps_list):
            ACT(lambda p=p, i=i: nc.scalar.activation(out=dst[:, i * 512:(i + 1) * 512], in_=p[:], func=AF.Ln,
                                                      bias=epsc[:, 0:1], scale=float(1.0 / dcount)), [p, epsc], [dst])
        ACT(lambda: nc.scalar.activation(out=dst[:], in_=dst[:], func=AF.Exp, scale=-0.5), [dst], [dst])

    prefetched = {}
    final_dead = {}
    live = {}

    def s_tail(dead_all, mtok):
        sset = 1

        def sba(name, shape, dt, off):
            t = sb("So_" + name, shape, dt, off)
            fw.alias(t, dead_all)
            return t
        x1o = [sba(f"x1o{c}", [128, 514], F32, O_A + c * 2112) for c in range(8)]
        x2o = [sba(f"x2o{c}", [128, 512], F32, O_A + 17 * KB + c * 2048) for c in range(8)]
        xst = [sba(f"xst{i}", [128, 514], F32, O_A + 33 * KB + i * 2112) for i in range(2)]
        rso = sba("rso", [128, 514], F32, O_A + 38 * KB)
        sqo = [sba(f"sqo{i}", [128, 514], BF16, O_A + 41 * KB + i * 1088) for i in range(2)]
        yst = [sba(f"yst{i}", [128, 512], F32, O_A + 44 * KB + i * 2048) for i in range(4)]
        mrow = sba("mrow", [128, 514], F32, O_A + 52 * KB)
        tmpf = [sba(f"tmpf{i}", [128, 514], F32, O_A + 55 * KB + i * 2112) for i in range(2)]
        ost = [sba(f"ost{i}", [128, 512], F32, O_A + 60 * KB + i * 2048) for i in range(2)]
        selT = sba("sel", [128, 8, 514], BF16, O_B + 4 * KB)
        h2o = sba("h2o", [128, 8, 514], BF16, O_B + 4 * KB)
        mrgo = sba("mrgo", [128, 8, 514], BF16, O_E)
        actTo = sba("actTo", [128, 22, 512], BF16, O_E + 9 * KB)
        all_t = x1o + x2o + xst + [rso] + sqo + yst + [mrow] + tmpf + ost + [selT, h2o, mrgo, actTo]
        fw.dma("sp", selT[:], sel_d, writes=[selT])
        fw.dma("sp", mrow[:], mrow_d.partition_broadcast(128), writes=[mrow])
        rot["banks"] = [0, 1, 2, 3]
        rot["i"] = 0
        pst = (psf[4], psf[5], psd[2])

        def mm514(big, pa, pb2, lhs_fn, rhs_t, rhs_fn, n, reads):
            for k in range(n):
                mm(big[:, 0:512], lhs_fn(k), rhs_fn(k, 0, 512), k == 0, k == n - 1, reads, [pa], k == n - 1)
            for k in range(n):
                mm(big[:, 512:514], lhs_fn(k), rhs_fn(k, 512, 514), k == 0, k == n - 1, reads, [pb2], k == n - 1)

        rot["banks"] = [0, 1, 2, 3, 4, 5]
        rot["i"] = 0
        for c in range(8):
            pa, pb2, big = nextpair()
            mm514(big, pa, pb2, lambda tk, c=c: mtok[:, tk, c * 128:(c + 1) * 128], selT,
                  lambda tk, a, b: selT[:, tk, a:b], 8, [mtok, selT])
            ACT(lambda c=c, big=big: nc.scalar.copy(out=mrgo[:, c, :], in_=big[:, 0:514]), [pa, pb2], [mrgo])
        fw.alias(h2o, [selT])

        def stats(c, src_t, width):
            s_ = sqo[c % 2]
            ACT(lambda: nc.scalar.activation(out=s_[:, 0:width], in_=src_t[:, 0:width], func=AF.Square), [src_t], [s_])
            mm(pst[2][:, 0:512], ones[:], s_[:, 0:512], c == 0, c == 7, [ones, s_], [pst[0]], True)
            if width > 512:
                mm(pst[2][:, 512:514], ones[:], s_[:, 512:514], c == 0, c == 7, [ones, s_], [pst[1]], True)

        def rstd_o(width):
            ACT(lambda: nc.scalar.activation(out=rso[:, 0:width], in_=pst[2][:, 0:width], func=AF.Ln, bias=epsc[:, 0:1], scale=float(1.0 / D)),
                [pst[0], pst[1], epsc], [rso])
            ACT(lambda: nc.scalar.activation(out=rso[:, 0:width], in_=rso[:, 0:width], func=AF.Exp, scale=-0.5), [rso], [rso])

        rot["banks"] = [0, 1, 2, 3]
        rot["i"] = 0
        for b in range(2):
            wt = next_wbuf()
            fw.dma("pool", wt[:], wout_d[:, :, b * 512:(b + 1) * 512], writes=[wt])
            for j in range(4):
                cj = b * 4 + j
                xs = xst[cj % 2]
                fw.dma("sp", xs[:], xo_d[:, cj, :], writes=[xs])
                pa, pb2, big = nextpair()
                mm514(big, pa, pb2, lambda kc, j=j, wt=wt: wt[:, kc, j * 128:(j + 1) * 128], mrgo,
                      lambda kc, a, b_: mrgo[:, kc, a:b_], 8, [wt, mrgo])
                DVE(lambda cj=cj, big=big, xs=xs: nc.vector.scalar_tensor_tensor(
                    out=x1o[cj][:], in0=big[:, 0:514], scalar=modcol(sset, 2, cj), in1=xs[:], op0=ALU.mult, op1=ALU.add),
                    [pa, pb2, mods, xs], [x1o[cj]])
                if cj >= 1:
                    stats(cj - 1, x1o[cj - 1], 514)
        stats(7, x1o[7], 514)
        rstd_o(514)
        for c in range(8):
            tf_ = tmpf[c % 2]
            DVE(lambda c=c, tf_=tf_: nc.vector.tensor_tensor(out=tf_[:], in0=x1o[c][:], in1=rso[:], op=ALU.mult), [x1o[c], rso], [tf_])
            ACT(lambda c=c, tf_=tf_: nc.scalar.activation(out=tf_[:], in_=tf_[:], func=AF.Identity,
                                                          bias=modcol(sset, 3, c), scale=gm[:, sset, 1, c:c + 1]), [tf_, mods, gm], [tf_])
            DVE(lambda c=c, tf_=tf_: nc.vector.tensor_tensor(out=h2o[:, c, :], in0=tf_[:], in1=mrow[:], op=ALU.mult), [tf_, mrow], [h2o])
        rot["banks"] = [0, 1, 2, 3, 4, 5]
        rot["i"] = 0
        items = []
        wts = {}
        for blk in range(11):
            for i in range(2):
                jj = 2 * blk + i
                for which, jcol in ((0, i), (1, 2 + i)):
                    fcx = which * 22 + jj
                    sy = yst[2 * which + (jj % 2)]
                    stt = {}
                    wc = ffc[:, fcx, :]

                    def A(blk=blk, i=i, which=which, jcol=jcol, sy=sy, stt=stt, wc=wc):
                        if i == 0 and which == 0:
                            wt = next_wbuf()
                            fw.dma("pool", wt[:], wup_d[:, :, blk * 512:(blk + 1) * 512], writes=[wt])
                            wts[blk] = wt
                        wt = wts[blk]
                        pa, pb2, big = nextpair()
                        mm514(big, pa, pb2, lambda kc: wt[:, kc, jcol * 128:(jcol + 1) * 128], h2o,
                              lambda kc, a, b_: h2o[:, kc, a:b_], 8, [wt, h2o])
                        stt["p"] = (pa, pb2, big)
                        ACT(lambda: nc.scalar.activation(out=sy[:], in_=big[:, 1:513], func=AF.Identity, bias=wc[:, 3:4], scale=wc[:, 1:2]),
                            [pa, pb2, ffc], [sy])

                    def B(sy=sy, stt=stt, wc=wc):
                        pa, pb2, big = stt["p"]
                        DVE(lambda: nc.vector.scalar_tensor_tensor(out=sy[:], in0=big[:, 0:512], scalar=wc[:, 0:1], in1=sy[:],
                                                                   op0=ALU.mult, op1=ALU.add), [sy, pa, ffc], [sy])
                        DVE(lambda: nc.vector.scalar_tensor_tensor(out=sy[:], in0=big[:, 2:514], scalar=wc[:, 2:3], in1=sy[:],
                                                                   op0=ALU.mult, op1=ALU.add), [sy, pa, pb2, ffc], [sy])

                    def C(which=which, jj=jj, sy=sy):
                        if which == 0:
                            ACT(lambda: nc.scalar.activation(out=sy[:], in_=sy[:], func=AF.Silu), [sy], [sy])
                        else:
                            gs = yst[jj % 2]
                            DVE(lambda: nc.vector.tensor_tensor(out=actTo[:, jj, :], in0=gs[:], in1=sy[:], op=ALU.mult), [gs, sy], [actTo])
                    items.append([A, B, C])
        n_it = len(items)
        for t_ in range(n_it + 2):
            for k in (2, 1, 0):
                ii = t_ - k
                if 0 <= ii < n_it:
                    items[ii][k]()
        rot["banks"] = [0, 1, 2, 3]
        rot["i"] = 0
        for cj in range(8):
            wt = next_wbuf()
            wflat = wt[:].rearrange("p a b -> p (a b)")
            fw.dma("pool", wflat[:, 0:22 * 128], wdn_d[:, cj, :], writes=[wt])
            p = nextps()
            for kk in range(22):
                mm(p[:], wflat[:, kk * 128:(kk + 1) * 128], actTo[:, kk, :], kk == 0, kk == 21, [wt, actTo], [p], kk == 21)
            DVE(lambda p=p, cj=cj: nc.vector.scalar_tensor_tensor(
                out=x2o[cj][:], in0=p[:], scalar=modcol(sset, 5, cj), in1=x1o[cj][:, 1:513], op0=ALU.mult, op1=ALU.add),
                [p, mods, x1o[cj]], [x2o[cj]])
            if cj >= 1:
                stats(cj - 1, x2o[cj - 1], 512)
        stats(7, x2o[7], 512)
        live["S"] = all_t + dead_all
        prefetch_x("P", live["S"])
        yield "pre_S9"
        rstd_o(512)
        rot["banks"] = [0, 1, 2, 3, 4, 5]
        rot["i"] = 0
        for c in range(8):
            ft = ost[c % 2]
            DVE(lambda c=c, ft=ft: nc.vector.scalar_tensor_tensor(out=ft[:], in0=x2o[c][:], scalar=gains[:, 16 + c:17 + c],
                                                                 in1=rso[:, 0:512], op0=ALU.mult, op1=ALU.mult), [x2o[c], gains, rso], [ft])
            fw.dma("sp", yT_o["S"][:, c, :], ft[:], reads=[ft], writes=[OUT])
        final_dead["S"] = all_t + dead_all


    def prefetch_x(g2, deadl):
        xc = [sb(g2 + f"xc{c}", [128, NTOK], F32, O_E + c * 4 * KB) for c in range(8)]
        for t in xc:
            fw.alias(t, deadl)
        for c in range(8):
            fw.dma("sp", xc[c][:], xT[g2][:, c, :], writes=[xc[c]])
        prefetched[g2] = xc

    def group(g, dead_in):
        L = 1024 if g == "S" else 256
        nseq = NTOK // L
        Lc = L // 128
        sset = 1 if g == "S" else 0
        x_d = xT[g]
        x1T = sb(g + "x1T", [128, 4, NTOK], BF16, O_E)
        x2T = sb(g + "x2T", [128, 4, NTOK], BF16, O_E + 8 * KB)
        vtok = [sb(g + f"vtok{s}", [128, Lc, 512], BF16, O_E + 16 * KB + s * Lc * KB) for s in range(nseq)]
        nY = 16 // (2 * Lc)
        Ys = [sb(g + f"Y{k}", [128, 2 * Lc, 512], BF16, O_E + 24 * KB + k * 2 * Lc * KB) for k in range(nY)]
        Y = Ys[0]
        ysa = [sb(g + f"ysa{i}", [128, 512], F32, O_E + 40 * KB + i * 2 * KB) for i in range(2)]
        ysb = [sb(g + f"ysb{i}", [128, 512], F32, O_E + 44 * KB + i * 2 * KB) for i in range(2)]
        yt1 = sb(g + "yt1", [128, 512], F32, O_E + 48 * KB)
        yt2 = sb(g + "yt2", [128, 512], F32, O_E + 50 * KB)
        yt3 = sb(g + "yt3", [128, 512], F32, O_E + 66 * KB)
        yt4 = sb(g + "yt4", [128, 512], F32, O_E + 70 * KB)
        ystg = sb(g + "ystg", [128, NTOK], F32, O_E + 52 * KB)
        pstg = sb(g + "pstg", [128, NTOK], F32, O_E + 56 * KB)
        vT = sb(g + "vT", [128, NTOK], BF16, O_E + 60 * KB)
        vT2 = sb(g + "vT2", [128, NTOK], BF16, O_E + 68 * KB)
        vTs = [vT, vT2]
        rstd = sb(g + "rstd", [128, NTOK], F32, O_E + 40 * KB)
        xstg = [sb(g + f"xstg{i}", [128, NTOK], F32, O_E + 24 * KB + i * 4 * KB) for i in range(2)]
        sq = [sb(g + f"sq{i}", [128, NTOK], BF16, O_E + 32 * KB + i * 2 * KB) for i in range(2)]
        hT = sb(g + "hT", [128, 8, NTOK], BF16, O_B)
        mrg = sb(g + "mrg", [128, 8, NTOK], BF16, O_C)
        for t in [x1T, x2T, ystg, pstg, vT, vT2, rstd, hT, mrg] + Ys + vtok + ysa + ysb + [yt1, yt2, yt3, yt4] + xstg + sq:
            fw.alias(t, dead_in)

        if g not in prefetched:
            prefetch_x(g, dead_in)
        xc = prefetched[g]
        pss = [nextps(), nextps()]
        for c in range(8):
            s_ = sq[c % 2]
            ACT(lambda c=c, s_=s_: nc.scalar.activation(out=s_[:], in_=xc[c][:], func=AF.Square), [xc[c]], [s_])
            for tt in range(2):
                mm(pss[tt][:], ones[:], s_[:, tt * 512:(tt + 1) * 512], c == 0, c == 7, [ones, s_], [pss[tt]], True)
        rstd_from(pss, rstd, D)
        for c in range(8):
            DVE(lambda c=c: nc.vector.tensor_tensor(out=xc[c][:], in0=xc[c][:], in1=rstd[:], op=ALU.mult), [xc[c], rstd], [xc[c]])
            ACT(lambda c=c: nc.scalar.activation(out=hT[:, c, :], in_=xc[c][:], func=AF.Identity,
                                                 bias=modcol(sset, 0, c), scale=gm[:, sset, 0, c:c + 1]),
                [xc[c], mods, gm], [hT])
        for t in [x1T, x2T] + Ys + vtok + xstg:
            fw.alias(t, xc)
        for t in ysa:
            fw.alias(t, [rstd])
        yield "post_S1"
        if g == "P":
            for t in [x1T, x2T, ystg, pstg, vT, vT2, rstd, hT, mrg, yt1, yt2, yt3, yt4] + Ys + vtok + ysa + ysb + xstg + sq + xc:
                fw.alias(t, final_dead["S"])
            dead_in = dead_in + final_dead["S"]

        tap(g + "rstd", rstd, rstd[:], [128, NTOK])
        tap(g + "hT", hT, hT[:], [128, 8, NTOK], BF16)
        def dw_A(pa, pb2, big, wcols, stg_y):
            ACT(lambda: nc.scalar.activation(out=stg_y[:], in_=big[:, :], func=AF.Identity,
                                             bias=wcols[:, 3:4], scale=wcols[:, 1:2]), [pa, pb2, hyc, ffc], [stg_y])

        def dw_B(pa, pb2, big, wcols, stg_y):
            y3 = stg_y[:].rearrange("p (s l) -> p s l", l=L)
            p3 = big[:, :].rearrange("p (s l) -> p s l", l=L)
            DVE(lambda: nc.vector.scalar_tensor_tensor(out=y3[:, :, 1:L], in0=p3[:, :, 0:L - 1], scalar=wcols[:, 0:1],
                                                       in1=y3[:, :, 1:L], op0=ALU.mult, op1=ALU.add), [stg_y, pa, pb2, hyc, ffc], [stg_y])
            DVE(lambda: nc.vector.scalar_tensor_tensor(out=y3[:, :, 0:L - 1], in0=p3[:, :, 1:L], scalar=wcols[:, 2:3],
                                                       in1=y3[:, :, 0:L - 1], op0=ALU.mult, op1=ALU.add), [stg_y, pa, pb2, hyc, ffc], [stg_y])

        def run_pipeline(items):
            n = len(items)
            K = max(len(it) for it in items)
            for t in range(n + K - 1):
                for k in range(K - 1, -1, -1):
                    i = t - k
                    if 0 <= i < n and k < len(items[i]):
                        items[i][k]()

        def transposes_to_tok(srcT, src_ap_fn, dst_list, col0):
            for tk in range(8):
                fw.op("pe", lambda tk=tk: nc.tensor.transpose(psb[:, tk * 128:(tk + 1) * 128], src_ap_fn(tk), ident[:]),
                      reads=[srcT, ident], writes=[psb], inc=(tk == 7))
            for s in range(nseq):
                ACT(lambda s=s: nc.scalar.copy(out=dst_list[s][:, :, col0:col0 + 128],
                                               in_=psb[:, s * L:(s + 1) * L].rearrange("p (t c) -> p t c", c=128)),
                    [psb], [dst_list[s]])

        def proj_block(wt, j, writes_ps=None):
            pa, pb2, big = nextpair()
            for tt, p in enumerate((pa, pb2)):
                for kc in range(8):
                    mm(p[:], wt[:, kc, j * 128:(j + 1) * 128], hT[:, kc, tt * 512:(tt + 1) * 512], kc == 0, kc == 7, [wt, hT], [p], kc == 7)
            return pa, pb2, big

        items = []
        wts = {}
        for b in (3, 4, 5):
            for j in range(4):
                hc = (b - 3) * 4 + j
                yb = (ystg, pstg)[hc % 2]
                stt = {}

                def A(b=b, j=j, hc=hc, yb=yb, stt=stt):
                    if j == 0:
                        pinned.clear()
                        wt = next_wbuf()
                        fw.dma("pool", wt[:], win_d[:, :, b * 512:(b + 1) * 512], writes=[wt])
                        wts[b] = wt
                        pinned.add(wt.name)
                    stt["p"] = proj_block(wts[b], j)
                    dw_A(*stt["p"], hyc[:, hc, :], yb)

                def B(hc=hc, yb=yb, stt=stt):
                    dw_B(*stt["p"], hyc[:, hc, :], yb)

                def C(b=b, j=j, yb=yb):
                    if b == 3:
                        ACT(lambda: nc.scalar.copy(out=x1T[:, j, :], in_=yb[:]), [yb], [x1T])
                    elif b == 4:
                        ACT(lambda: nc.scalar.copy(out=x2T[:, j, :], in_=yb[:]), [yb], [x2T])
                    else:
                        vTj = vTs[j % 2]
                        ACT(lambda: nc.scalar.copy(out=vTj[:], in_=yb[:]), [yb], [vTj])
                        transposes_to_tok(vTj, lambda tk: vTj[:, tk * 128:(tk + 1) * 128], vtok, j * 128)
                items.append([A, B, C])
        run_pipeline(items)
        pinned.clear()

        tap(g + "x1T", x1T, x1T[:], [128, 4, NTOK], BF16)
        tap(g + "vtok0", vtok[0], vtok[0][:], [128, Lc, 512], BF16)
        pre_w = []
        for b in (0, 1, 2):
            wt = next_wbuf()
            fw.dma("pool", wt[:], win_d[:, :, b * 512:(b + 1) * 512], writes=[wt])
            pre_w.append(wt)
        def make_s2b(qT, kT, Vaug, kst):
            pieces = []
            for b in (0, 1):
                for j in range(4):
                    def piece(b=b, j=j):
                        wt = pre_w[b]
                        dstT = qT if b == 0 else kT
                        pa, pb2, _big = proj_block(wt, j)
                        for tt, p in enumerate((pa, pb2)):
                            sl = slice(tt * 512, (tt + 1) * 512)
                            if b == 1 and g == "P":
                                ks = kst[(2 * j + tt) % 4]
                                ACT(lambda p=p, ks=ks: nc.scalar.copy(out=ks[:], in_=p[:]), [p], [ks])
                                fw.dma("sp", kT_o[:, j, sl], ks[:], reads=[ks], writes=[OUT])
                                DVE(lambda ks=ks, sl=sl: nc.vector.tensor_copy(out=kT[:, j, sl], in_=ks[:]), [ks], [kT])
                            elif b == 0:
                                ACT(lambda p=p, sl=sl: nc.scalar.mul(out=dstT[:, j, sl], in_=p[:], mul=0.125), [p], [dstT])
                            else:
                                ACT(lambda p=p, sl=sl: nc.scalar.copy(out=dstT[:, j, sl], in_=p[:]), [p], [dstT])
                    pieces.append(piece)
            for tk in range(8):
                def piece(tk=tk):
                    wt = pre_w[2]
                    p = nextps()
                    for kc in range(8):
                        mm(p[:], hT[:, kc, tk * 128:(tk + 1) * 128], wt[:, kc, :], kc == 0, kc == 7, [hT, wt], [p], kc == 7)
                    p3 = p[:].rearrange("p (h d) -> p h d", d=64)
                    if g == "P":
                        ks = kst[tk % 4]
                        ACT(lambda: nc.scalar.copy(out=ks[:], in_=p[:]), [p], [ks])
                        fw.dma("sp", v_o[:, tk, :], ks[:], reads=[ks], writes=[OUT])
                        ACT(lambda: nc.scalar.copy(out=Vaug[:, tk, :, 0:64], in_=ks[:].rearrange("p (h d) -> p h d", d=64)), [ks], [Vaug])
                    else:
                        ACT(lambda: nc.scalar.copy(out=Vaug[:, tk, :, 0:64], in_=p3), [p], [Vaug])
                pieces.append(piece)
            return pieces

        s2b_pieces = []
        early_att = None
        if g == "P":
            curA = [O_A]

            def sba_(name, shape, dt):
                nb = int(np.prod(shape[1:])) * (2 if dt == BF16 else 4)
                t = sb(g + name, shape, dt, curA[0])
                curA[0] += (nb + 63) // 64 * 64
                fw.alias(t, dead_in)
                return t
            qT_ = sba_("qT", [128, 4, NTOK], BF16)
            kT_ = sba_("kT", [128, 4, NTOK], BF16)
            Vaug_ = sba_("Vaug", [128, 8, 8, 65], BF16)
            kst_ = [sba_(f"kst{i}", [128, 512], F32) for i in range(4)]
            POOL(lambda: nc.gpsimd.memset(Vaug_[:, :, :, 64:65], 1.0), [], [Vaug_])
            early_att = (qT_, kT_, Vaug_, kst_)
            s2b_pieces = make_s2b(*early_att)

        def s2b_hook():
            if s2b_pieces:
                s2b_pieces.pop(0)()

        Wt = Wm[L]
        Kt = Ktab[L]

        cm_i = [0]
        cm_t = [(ysa[0], ysb[0], yt1, yt2), (ysa[1], ysb[1], yt3, yt4)]
        if g == "S":
            late["slots"] = [sb("wadaA", [128, 8, 512], BF16, O_C), sb("wadaB", [128, 8, 512], BF16, O_C + 8 * KB)]
            for t in late["slots"]:
                fw.alias(t, dead_in)

        def conv_A(s, o):
            u = vtok[s]
            Y = Ys[s % nY]
            yb0 = 0
            for i in range(Lc):
                if g == "S":
                    mods_late_tick()
                pA, pB = nextps(), nextps()
                for (p, fr) in ((pA, i), (pB, Lc + i)):
                    for tc in range(Lc):
                        mm(p[:], Wt[:, tc, fr * 128:(fr + 1) * 128], u[:, tc, :], tc == 0, tc == Lc - 1, [Wt, u], [p], tc == Lc - 1)
                KR = Kt[:, i, 0, o, :]
                KI = Kt[:, i, 1, o, :]
                cm_i[0] += 1
                tA, tB, tC, tD = cm_t[cm_i[0] % 2]
                DVE(lambda pA=pA, tA=tA: nc.vector.tensor_tensor(out=tA[:], in0=pA[:], in1=KR, op=ALU.mult), [pA, Kt], [tA])
                DVE(lambda pB=pB, tB=tB: nc.vector.tensor_tensor(out=tB[:], in0=pB[:], in1=KI, op=ALU.mult), [pB, Kt], [tB])
                DVE(lambda pA=pA, tC=tC: nc.vector.tensor_tensor(out=tC[:], in0=pA[:], in1=KI, op=ALU.mult), [pA, Kt], [tC])
                DVE(lambda pB=pB, tD=tD: nc.vector.tensor_tensor(out=tD[:], in0=pB[:], in1=KR, op=ALU.mult), [pB, Kt], [tD])
                DVE(lambda i=i, tA=tA, tB=tB: nc.vector.tensor_tensor(out=Y[:, yb0 + i, :], in0=tA[:], in1=tB[:], op=ALU.subtract), [tA, tB], [Y])
                DVE(lambda i=i, tC=tC, tD=tD: nc.vector.tensor_tensor(out=Y[:, yb0 + Lc + i, :], in0=tC[:], in1=tD[:], op=ALU.add), [tC, tD], [Y])

        def conv_B(s, o, mulT):
            Y = Ys[s % nY]
            yb0 = 0
            Nn = min(L, 512)
            for cc in range(4):
                for th in range(L // Nn):
                    p = nextps()
                    for fr in range(2 * Lc):
                        col = (fr // Lc) * L + th * Nn
                        mm(p[:, 0:Nn], Y[:, yb0 + fr, cc * 128:(cc + 1) * 128], Wt[:, fr % Lc, col:col + Nn], fr == 0, fr == 2 * Lc - 1,
                           [Y, Wt], [p], fr == 2 * Lc - 1)
                    t0 = s * L + th * Nn
                    DVE(lambda p=p, cc=cc, t0=t0: nc.vector.tensor_tensor(out=mulT[:, cc, t0:t0 + Nn], in0=p[:, 0:Nn],
                                                                          in1=mulT[:, cc, t0:t0 + Nn], op=ALU.mult), [p, mulT], [mulT])

        def run_convs(o, mulT):
            conv_A(0, o)
            s2b_hook()
            for s_ in range(nseq):
                if s_ + 1 < nseq:
                    conv_A(s_ + 1, o)
                    s2b_hook()
                conv_B(s_, o, mulT)
                s2b_hook()

        run_convs(0, x1T)
        for cc in range(4):
            transposes_to_tok(x1T, lambda tk, cc=cc: x1T[:, cc, tk * 128:(tk + 1) * 128], vtok, cc * 128)
        run_convs(1, x2T)
        if g == "S":
            mods_final(1)
            fw.alias(mrg, late["slots"])
        for t in sq:
            fw.alias(t, Ys + xstg)
        fw.alias(rstd, ysa)
        pss = [nextps(), nextps()]
        for cc in range(4):
            s_ = sq[cc % 2]
            ACT(lambda s_=s_, cc=cc: nc.scalar.activation(out=s_[:], in_=x2T[:, cc, :], func=AF.Square), [x2T], [s_])
            for tt in range(2):
                mm(pss[tt][:], ones[:], s_[:, tt * 512:(tt + 1) * 512], cc == 0, cc == 3, [ones, s_], [pss[tt]], True)
        rstd_from(pss, rstd, 512)
        for cc in range(4):
            if g == "S":
                DVE(lambda cc=cc: nc.vector.scalar_tensor_tensor(out=x2T[:, cc, :], in0=x2T[:, cc, :], scalar=gains[:, 28 + cc:29 + cc],
                                                                 in1=rstd[:], op0=ALU.mult, op1=ALU.mult), [x2T, gains, rstd], [x2T])
                for tk in range(8):
                    fw.op("pe", lambda tk=tk, cc=cc: nc.tensor.transpose(psb[:, tk * 128:(tk + 1) * 128], x2T[:, cc, tk * 128:(tk + 1) * 128], ident[:]),
                          reads=[x2T, ident], writes=[psb], inc=(tk == 7))
                ACT(lambda cc=cc: nc.scalar.copy(out=mrg[:, :, 512 + cc * 128:512 + (cc + 1) * 128],
                                                 in_=psb[:, :].rearrange("p (t c) -> p t c", c=128)), [psb], [mrg])
            else:
                DVE(lambda cc=cc: nc.vector.scalar_tensor_tensor(out=mrg[:, 4 + cc, :], in0=x2T[:, cc, :], scalar=gains[:, 28 + cc:29 + cc],
                                                                 in1=rstd[:], op0=ALU.mult, op1=ALU.mult), [x2T, gains, rstd], [mrg])
        dead_h = [x1T, x2T, ystg, pstg, vT, vT2, rstd, yt1, yt2, yt3, yt4] + Ys + vtok + ysa + ysb + xstg + sq
        if g == "S":
            dead_h += [Wm[1024], Ktab[1024]]

        cur = [O_E]

        def sbe(name, shape, dt):
            nb = int(np.prod(shape[1:])) * (2 if dt == BF16 else 4)
            t = sb(g + name, shape, dt, cur[0])
            cur[0] += (nb + 63) // 64 * 64
            return t
        if early_att is not None:
            qT, kT, Vaug, kst = early_att
        else:
            qT = sbe("qT", [128, 4, NTOK], BF16)
            kT = sbe("kT", [128, 4, NTOK], BF16)
            Vaug = sbe("Vaug", [128, 8, 8, 65], BF16)
            kst = []
        ckT = sbe("ckT", [128, 4, 256], BF16)
        cVaug = sbe("cVaug", [128, 2, 8, 65], BF16)
        PT = [sbe(f"PT{i}", [128, 896], BF16) for i in range(3 if g == "S" else 2)]
        atok = sbe("atok", [128, 512], F32)
        an = sbe("an", [128, 512], BF16)
        if g == "S":
            tabs = {"E": sbe("tabE", [128, 8, 1024], BF16), "I": sbe("tabI", [128, 8, 1024], BF16)}
        else:
            PT.append(sbe("PT2", [128, 896], BF16))
            PT.append(sbe("PT3", [128, 896], BF16))
            tabs = {"E": PT[0], "I": PT[0]}
        att_t = [qT, kT, Vaug, ckT, cVaug, atok, an, tabs["E"], tabs["I"]] + PT + kst
        for t in att_t:
            fw.alias(t, dead_h + dead_in)
        if g == "S":
            fw.dma("pool", ckT[:], ckT_d, writes=[ckT])
            fw.dma("pool", cVaug[:, :, :, 0:64], cv_d, writes=[cVaug])
            fw.dma("pool", tabs["E"][:], tabE_d, writes=[tabs["E"]])
            fw.dma("pool", tabs["I"][:], tabI_d, writes=[tabs["I"]])
            POOL(lambda: nc.gpsimd.memset(cVaug[:, :, :, 64:65], 1.0), [], [cVaug])
            for nm in ("E", "I"):
                for hq in range(4):
                    ACT(lambda nm=nm, hq=hq: nc.scalar.activation(out=tabs[nm][:, 2 * hq:2 * hq + 2, :], in_=tabs[nm][:, 2 * hq:2 * hq + 2, :],
                                                                 func=AF.Exp), [tabs[nm]], [tabs[nm]])
        if early_att is None:
            POOL(lambda: nc.gpsimd.memset(Vaug[:, :, :, 64:65], 1.0), [], [Vaug])
            s2b_pieces = make_s2b(qT, kT, Vaug, kst)
        while s2b_pieces:
            s2b_pieces.pop(0)()

        gb = sb(g + "gb", [128, 512], F32, O_B)
        fw.alias(gb, [hT] + dead_in)
        if g == "S":
            fw.dma("sp", gb[:], gbrow_d.partition_broadcast(128), writes=[gb])
        rot["banks"] = [2, 3, 4, 5, 6]
        rot["i"] = 0
        kp_of = {0: [0, 1, 2, 3], 1: [0, 1, 2, 3], 2: [0, 1, 2, 3, 4], 3: [1, 2, 3, 4, 5], 4: [2, 3, 4, 5, 6],
                 5: [3, 4, 5, 6, 7], 6: [4, 5, 6, 7], 7: [4, 5, 6, 7]}
        SC = 1.0
        O = [psf[0], psf[1]]
        Osb = sbe("Osb", [128, 520], F32)
        fw.alias(Osb, dead_h + dead_in)
        if g == "P":
            jobs = [(tk, h) for tk in range(8) for h in (0, 1, 4, 5)]
        else:
            jobs = [(tk, h) for tk in range(8) for h in range(8)]
        st = {}

        def slots_of(tk):
            if g == "P":
                s_ = tk // 2
                return [("k", 2 * s_, 0), ("k", 2 * s_ + 1, 0), ("k", 2 * s_, 2), ("k", 2 * s_ + 1, 2)]
            return [("b", kt, 0) for kt in reversed(kp_of[tk])] + [("c", 0, 0), ("c", 1, 0)]

        def emit_scores(ji):
            tk, h = jobs[ji]
            slots = slots_of(tk)
            ns = len(slots)
            pA = nextps()
            pB = nextps() if ns > 4 else None
            for si, (kind, kt, dh_) in enumerate(slots):
                hh_ = h + dh_
                c, pb_ = hh_ // 2, 64 * (hh_ % 2)
                q_ap = qT[pb_:pb_ + 64, c, tk * 128:(tk + 1) * 128]
                p = pA if si < 4 else pB
                o_ap = p[:, (si % 4) * 128:(si % 4) * 128 + 128]
                if kind == "c":
                    mm(o_ap, ckT[pb_:pb_ + 64, c, kt * 128:(kt + 1) * 128], q_ap, True, True, [ckT, qT], [p], True)
                else:
                    k_ap = kT[pb_:pb_ + 64, c, kt * 128:(kt + 1) * 128]
                    mm(o_ap, k_ap, q_ap, True, True, [kT, qT], [p], True)
            pt = PT[ji % len(PT)]
            n1 = min(ns, 4)
            ACT(lambda: nc.scalar.activation(out=pt[:, 0:n1 * 128], in_=pA[:, 0:n1 * 128], func=AF.Exp, scale=SC), [pA], [pt])
            if ns > 4:
                ACT(lambda: nc.scalar.activation(out=pt[:, 512:ns * 128], in_=pB[:, 0:(ns - 4) * 128], func=AF.Exp, scale=SC), [pB], [pt])
            if g == "S":
                nb = ns - 2
                tab = tabs["E"] if tk in (0, 1, 6, 7) else tabs["I"]
                kt0 = slots[0][1]
                i0 = 7 - (2 * kt0 - 2 * tk)
                DVE(lambda: nc.vector.tensor_tensor(out=pt[:, 0:nb * 128], in0=pt[:, 0:nb * 128],
                                                    in1=tab[:, h, i0 * 64:i0 * 64 + nb * 128], op=ALU.mult), [pt, tab], [pt])
            st[ji] = (pt, slots)

        def emit_pv(ji):
            tk, h = jobs[ji]
            pt, slots = st.pop(ji)
            ns = len(slots)
            for dh_ in sorted(set(sl_[2] for sl_ in slots)):
                hh_ = h + dh_
                ob = O[hh_ // 4]
                o_ap = ob[:, (hh_ % 4) * 65:(hh_ % 4) * 65 + 65]
                idx = [si for si, sl_ in enumerate(slots) if sl_[2] == dh_]
                for n_, si in enumerate(idx):
                    kind, kt, _ = slots[si]
                    if kind == "c":
                        v_ap, vt = cVaug[:, kt, hh_, :], cVaug
                    else:
                        v_ap, vt = Vaug[:, kt, hh_, :], Vaug
                    mm(o_ap, pt[:, si * 128:(si + 1) * 128], v_ap, n_ == 0, n_ == len(idx) - 1, [pt, vt], [ob], n_ == len(idx) - 1)

        def emit_tail_a(tk):
            for hb in range(2):
                ACT(lambda hb=hb: nc.scalar.copy(out=Osb[:, hb * 260:(hb + 1) * 260], in_=O[hb][:, 0:260]), [O[hb]], [Osb])
            o3 = Osb[:].rearrange("p (h d) -> p h d", d=65)
            DVE(lambda: nc.vector.reciprocal(out=smalls[:, 0:8], in_=o3[:, :, 64]), [Osb], [smalls])
            for hh in range(8):
                DVE(lambda hh=hh: nc.vector.tensor_scalar(out=atok[:, hh * 64:(hh + 1) * 64], in0=o3[:, hh, 0:64],
                                                          scalar1=smalls[:, hh:hh + 1], scalar2=None, op0=ALU.mult),
                    [Osb, smalls], [atok])

        def emit_tail_a2(tk):
            ACT(lambda: nc.scalar.activation(out=an[:], in_=atok[:], func=AF.Square, accum_out=smalls[:, 8:9]), [atok], [an, smalls])
            ACT(lambda: nc.scalar.activation(out=smalls[:, 9:10], in_=smalls[:, 8:9], func=AF.Ln, bias=epsc[:, 0:1], scale=float(1.0 / 512)), [smalls, epsc], [smalls])
            ACT(lambda: nc.scalar.activation(out=smalls[:, 10:11], in_=smalls[:, 9:10], func=AF.Exp, scale=-0.5), [smalls], [smalls])
            if g == "S":
                DVE(lambda: nc.vector.scalar_tensor_tensor(out=mrg[:, tk, 0:512], in0=atok[:], scalar=smalls[:, 10:11], in1=gb[:],
                                                           op0=ALU.mult, op1=ALU.mult), [atok, smalls, gb], [mrg])
            else:
                DVE(lambda: nc.vector.tensor_scalar(out=an[:], in0=atok[:], scalar1=smalls[:, 10:11], scalar2=None, op0=ALU.mult), [atok, smalls], [an])

        def emit_tail_b(tk):
            if g == "S":
                return
            for c4 in range(4):
                fw.op("pe", lambda c4=c4: nc.tensor.transpose(psb[:, c4 * 128:(c4 + 1) * 128], an[:, c4 * 128:(c4 + 1) * 128], ident[:]),
                      reads=[an, ident], writes=[psb], inc=(c4 == 3))
            for c4 in range(4):
                DVE(lambda c4=c4: nc.vector.tensor_scalar(out=mrg[:, c4, tk * 128:(tk + 1) * 128], in0=psb[:, c4 * 128:(c4 + 1) * 128],
                                                          scalar1=gains[:, 24 + c4:25 + c4], scalar2=None, op0=ALU.mult), [psb, gains], [mrg])

        Dp = len(PT) - 1
        sched = []
        nj = len(jobs)
        for ji in range(min(Dp, nj)):
            emit_scores(ji)
        for ji in range(nj):
            if ji + Dp < nj:
                emit_scores(ji + Dp)
            emit_pv(ji)
            tk, h = jobs[ji]
            while sched and sched[0][0] <= ji:
                sched.pop(0)[1]()
            if ji + 1 == nj or jobs[ji + 1][0] != tk:
                emit_tail_a(tk)
                sched.append((ji + 2, lambda tk=tk: emit_tail_a2(tk)))
                sched.append((ji + 4, lambda tk=tk: emit_tail_b(tk)))
        while sched:
            sched.pop(0)[1]()
        rot["banks"] = [0, 1, 2, 3, 4, 5]
        rot["i"] = 0
        dead_a = att_t + [hT, Osb]

        tap(g + "mrg", mrg, mrg[:], [128, 8, NTOK], BF16)

        if g == "S":
            yield from s_tail(dead_a + dead_h + dead_in + [gb], mrg)
            return
        xresc = [sb(g + f"xres{c}", [128, NTOK], F32, O_A + c * 4 * KB) for c in range(8)]
        fstg = [sb(g + f"fstg{i}", [128, NTOK], F32, O_E + 44 * KB + i * 4 * KB) for i in range(4)]
        rstd2 = sb(g + "rstd2", [128, NTOK], F32, O_E + 60 * KB)
        sq2 = [sb(g + f"sq2{i}", [128, NTOK], BF16, O_E + 64 * KB + i * 2 * KB) for i in range(2)]
        actT = sb(g + "actT", [128, 22, NTOK], BF16, O_E)
        hT2 = sb(g + "hT2", [128, 8, NTOK], BF16, O_B)
        for t in xresc + [rstd2, actT, hT2] + fstg + sq2:
            fw.alias(t, dead_a + dead_h + dead_in)
        def stats_chunk(pss_, c):
            s_ = sq2[c % 2]
            ACT(lambda: nc.scalar.activation(out=s_[:], in_=xresc[c][:], func=AF.Square), [xresc[c]], [s_])
            for tt in range(2):
                mm(pss_[tt][:], ones[:], s_[:, tt * 512:(tt + 1) * 512], c == 0, c == 7, [ones, s_], [pss_[tt]], True)

        pss6 = [psf[6], psf[5]]
        rot["banks"] = [0, 1, 2, 3, 4]
        rot["i"] = 0
        for b in range(2):
            wt = next_wbuf()
            fw.dma("pool", wt[:], wout_d[:, :, b * 512:(b + 1) * 512], writes=[wt])
            for j in range(4):
                cj = b * 4 + j
                xs = fstg[cj % 2]
                fw.dma("sp", xs[:], x_d[:, cj, :], writes=[xs])
                for tt in range(2):
                    p = nextps()
                    sl = slice(tt * 512, (tt + 1) * 512)
                    for kc in range(8):
                        mm(p[:], wt[:, kc, j * 128:(j + 1) * 128], mrg[:, kc, sl], kc == 0, kc == 7, [wt, mrg], [p], kc == 7)
                    DVE(lambda p=p, cj=cj, sl=sl, xs=xs: nc.vector.scalar_tensor_tensor(
                        out=xresc[cj][:, sl], in0=p[:], scalar=modcol(sset, 2, cj), in1=xs[:, sl], op0=ALU.mult, op1=ALU.add),
                        [p, mods, xs], [xresc[cj]])
                if cj >= 1:
                    stats_chunk(pss6, cj - 1)
        stats_chunk(pss6, 7)

        rot["banks"] = [0, 1, 2, 3, 4, 5]
        rot["i"] = 0
        rstd_from(pss6, rstd2, D)
        for c in range(8):
            ft = fstg[c % 2]
            DVE(lambda c=c, ft=ft: nc.vector.tensor_tensor(out=ft[:], in0=xresc[c][:], in1=rstd2[:], op=ALU.mult), [xresc[c], rstd2], [ft])
            ACT(lambda c=c, ft=ft: nc.scalar.activation(out=hT2[:, c, :], in_=ft[:], func=AF.Identity,
                                                        bias=modcol(sset, 3, c), scale=gm[:, sset, 1, c:c + 1]), [ft, mods, gm], [hT2])

        items = []
        wts = {}
        for blk in range(11):
            for i in range(2):
                jj = 2 * blk + i
                for which, jcol in ((0, i), (1, 2 + i)):
                    fcx = which * 22 + jj
                    sy = fstg[2 * which + (jj % 2)]
                    stt = {}

                    def A(blk=blk, i=i, which=which, jcol=jcol, fcx=fcx, sy=sy, stt=stt):
                        if i == 0 and which == 0:
                            wt = next_wbuf()
                            fw.dma("pool", wt[:], wup_d[:, :, blk * 512:(blk + 1) * 512], writes=[wt])
                            wts[blk] = wt
                        wt = wts[blk]
                        pa, pb2, big = nextpair()
                        for tt, p in enumerate((pa, pb2)):
                            for kc in range(8):
                                mm(p[:], wt[:, kc, jcol * 128:(jcol + 1) * 128], hT2[:, kc, tt * 512:(tt + 1) * 512], kc == 0, kc == 7, [wt, hT2], [p], kc == 7)
                        stt["p"] = (pa, pb2, big)
                        dw_A(pa, pb2, big, ffc[:, fcx, :], sy)

                    def B(fcx=fcx, sy=sy, stt=stt):
                        dw_B(*stt["p"], ffc[:, fcx, :], sy)

                    def C(which=which, jj=jj, sy=sy):
                        if which == 0:
                            ACT(lambda: nc.scalar.activation(out=sy[:], in_=sy[:], func=AF.Silu), [sy], [sy])
                        else:
                            gs = fstg[jj % 2]
                            DVE(lambda: nc.vector.tensor_tensor(out=actT[:, jj, :], in0=gs[:], in1=sy[:], op=ALU.mult), [gs, sy], [actT])
                    items.append([A, B, C])
        run_pipeline(items)

        tap(g + "actT", actT, actT[:], [128, 22, NTOK], BF16)
        pss9 = [psf[6], psf[5]]
        rot["banks"] = [0, 1, 2, 3, 4]
        rot["i"] = 0
        for cj in range(8):
            wt = next_wbuf()
            wflat = wt[:].rearrange("p a b -> p (a b)")
            fw.dma("pool", wflat[:, 0:22 * 128], wdn_d[:, cj, :], writes=[wt])
            for tt in range(2):
                p = nextps()
                sl = slice(tt * 512, (tt + 1) * 512)
                for kk in range(22):
                    mm(p[:], wflat[:, kk * 128:(kk + 1) * 128], actT[:, kk, sl], kk == 0, kk == 21, [wt, actT], [p], kk == 21)
                DVE(lambda p=p, cj=cj, sl=sl: nc.vector.scalar_tensor_tensor(
                    out=xresc[cj][:, sl], in0=p[:], scalar=modcol(sset, 5, cj), in1=xresc[cj][:, sl], op0=ALU.mult, op1=ALU.add),
                    [p, mods, xresc[cj]], [xresc[cj]])
            if cj >= 1:
                stats_chunk(pss9, cj - 1)

        stats_chunk(pss9, 7)
        if g == "S":
            live["S"] = [actT, hT2, mrg] + dead_a + dead_h + dead_in
            prefetch_x("P", live["S"])
            yield "pre_S9"
        rstd_from(pss9, rstd2, D)
        rot["banks"] = [0, 1, 2, 3, 4, 5]
        rot["i"] = 0
        for c in range(8):
            ft = fstg[c % 4]
            DVE(lambda c=c, ft=ft: nc.vector.scalar_tensor_tensor(out=ft[:], in0=xresc[c][:], scalar=gains[:, 16 + c:17 + c],
                                                                 in1=rstd2[:], op0=ALU.mult, op1=ALU.mult), [xresc[c], gains, rstd2], [ft])
            fw.dma("sp", yT_o[g][:, c, :], ft[:], reads=[ft], writes=[OUT])
        final_dead[g] = xresc + [rstd2, actT, hT2, mrg] + fstg + sq2 + dead_a + dead_h

    for _ in range(3):
        mods_early_issue()
    dead = filter_phase(1024, [])
    dead = dead + filter_phase(256, dead)
    fw.alias(Wm[1024], dead)
    fw.dma("sp", Wm[256][:], W_d[256], writes=[Wm[256]])
    fw.dma("sp", Wm[1024][:], W_d[1024], writes=[Wm[1024]])
    mods_final(0)
    gS = group("S", dead)
    assert next(gS) == "post_S1"
    assert next(gS) == "pre_S9"
    gP = group("P", dead + live["S"])
    assert next(gP) == "post_S1"
    for _ in gS:
        pass
    for _ in gP:
        pass
    fw.finish()
    return nc


_CACHE = {}


def _fm(v, nch):
    return np.ascontiguousarray(v.reshape(nch, 128).T)


def _wk(w):
    K, N = w.shape
    return np.ascontiguousarray(w.reshape(K // 128, 128, N).transpose(1, 0, 2))


def _consts():
    if "c" in _CACHE:
        return _CACHE["c"]
    c = {}
    for L in (1024, 256):
        t = np.arange(L, dtype=np.float64)
        f = np.arange(L, dtype=np.float64)
        om = np.pi * (f + 0.5) / L
        Cs = np.cos(np.outer(t + 0.5, om))
        Ss = np.sin(np.outer(t + 0.5, om))
        W = np.concatenate([Cs, -Ss], 1)
        Wu = np.concatenate([np.cos(np.outer(t, om)), np.sin(np.outer(t, om))], 1) / L
        c[f"W{L}"] = _wk(W.astype(np.float32)).astype(ml_dtypes.bfloat16)
        c[f"Wu{L}"] = _wk(Wu.astype(np.float32)).astype(ml_dtypes.bfloat16)
        t32 = np.arange(L, dtype=np.float32) / np.float32(L)
        fr = np.arange(1, 9, dtype=np.float32)
        ang = np.float32(2.0 * np.pi) * t32[:, None] * fr[None, :]
        z = np.concatenate([t32[:, None], np.cos(ang), np.sin(ang)], -1).astype(np.float32)
        zT = np.concatenate([z.T, np.ones((1, L), np.float32)], 0)
        c[f"zT{L}"] = np.ascontiguousarray(zT)
        min_decay = np.log(1e-2) / 1.5
        max_decay = np.log(1e-2) / 0.3
        deltas = np.abs(np.linspace(min_decay, max_decay, 512, dtype=np.float32))
        decay = np.exp(-t32[:, None] * deltas[None, :]).astype(np.float32)
        decayB = decay.copy()
        decayB[0] = 0.0
        dd = np.concatenate([decay, decayB], 1)
        c[f"dec{L}"] = _wk(dd)
    c["ident"] = np.eye(128, dtype=np.float32).astype(ml_dtypes.bfloat16)
    e0 = np.zeros((128, 1), np.float32)
    e0[0, 0] = -1.0
    c["e0n"] = e0
    kc = np.arange(64)[:, None]
    qc = np.arange(64)[None, :]
    cstart = np.clip(qc - 8, 0, 48)
    colin = (kc >= cstart) & (kc < cstart + 16)
    dc = np.clip(kc - qc, -15, 15) + 15
    idxE = np.zeros((128, 16, 64), np.int64)
    idxI = np.zeros((128, 16, 64), np.int64)
    SENT = 15 * 31
    for half in range(2):
        for i in range(16):
            dl = (7 - i) if half == 0 else (8 - i)
            for tab, ok in ((idxE, abs(dl) <= 7), (idxI, -4 <= dl <= 3)):
                if ok:
                    tab[half * 64:(half + 1) * 64, i, :] = np.where(colin, (dl + 7) * 31 + dc, SENT)
                else:
                    tab[half * 64:(half + 1) * 64, i, :] = SENT
    c["idxE"], c["idxI"] = idxE, idxI
    _CACHE["c"] = c
    return c


def _prepare(x_prompt, x_sample, cache_ctx_k, cache_ctx_v, c, c_ctx, w_ada, b_ada, norm1_g,
           w_in, rpb, hy_conv_w, hy_conv_b, filt_w1, filt_b1, filt_w2, filt_b2, filt_w3,
           filt_b3, filt_freq, filt_bias, grp_norm_g, w_out, norm2_g, w_up, ffn_conv_w,
           ffn_conv_b, w_down, final_g):
    f32 = np.float32
    A = lambda a: np.asarray(a, dtype=f32)
    cs = _consts()
    x_prompt, x_sample = A(x_prompt), A(x_sample)
    shared = {}
    shared["w_ada"] = _wk(A(w_ada)[0])
    shared["b_ada"] = _fm(A(b_ada)[0], 48)
    shared["gains"] = np.concatenate([_fm(A(norm1_g)[0], 8), _fm(A(norm2_g)[0], 8), _fm(A(final_g), 8), _fm(A(grp_norm_g)[0], 8)], 1)
    shared["w_in"] = _wk(A(w_in)[0])
    hw, hb = A(hy_conv_w)[0], A(hy_conv_b)[0]
    shared["hyc"] = np.ascontiguousarray(np.stack([_fm(hw[0], 12), _fm(hw[1], 12), _fm(hw[2], 12), _fm(hb, 12)], -1))
    fwc, fbc = A(ffn_conv_w)[0], A(ffn_conv_b)[0]
    shared["ffc"] = np.ascontiguousarray(np.stack([_fm(fwc[0], 44), _fm(fwc[1], 44), _fm(fwc[2], 44), _fm(fbc, 44)], -1))
    shared["w_out"] = _wk(A(w_out)[0])
    wu = A(w_up)[0]
    cols = []
    for blk in range(11):
        for i in range(2):
            cols.append(np.arange((2 * blk + i) * 128, (2 * blk + i + 1) * 128))
        for i in range(2):
            cols.append(DFF + np.arange((2 * blk + i) * 128, (2 * blk + i + 1) * 128))
    shared["w_up"] = _wk(np.ascontiguousarray(wu[:, np.concatenate(cols)]))
    wd = A(w_down)[0]
    wd4 = wd.reshape(22, 128, 8, 128)
    shared["w_down"] = np.ascontiguousarray(wd4.transpose(1, 2, 0, 3).reshape(128, 8, 22 * 128))
    rp = np.concatenate([A(rpb)[0].reshape(8, 15 * 31), np.full((8, 1), NEGM, f32)], 1)
    shared["tabE"] = np.ascontiguousarray(rp[:, cs["idxE"]].transpose(1, 0, 2, 3).reshape(128, 8, 1024))
    shared["tabI"] = np.ascontiguousarray(rp[:, cs["idxI"]].transpose(1, 0, 2, 3).reshape(128, 8, 1024))
    shared["w1b1"] = np.ascontiguousarray(np.concatenate([A(filt_w1)[0], A(filt_b1)[0][None, :]], 0))
    shared["w2"] = np.ascontiguousarray(A(filt_w2)[0])
    shared["fsm"] = np.ascontiguousarray(np.stack([A(filt_b2)[0], A(filt_freq)[0], np.zeros(64, f32), np.zeros(64, f32)], 1))
    shared["w3b3"] = np.ascontiguousarray(np.concatenate([A(filt_w3)[0], A(filt_b3)[0][None, :]], 0))
    shared["fb"] = np.ascontiguousarray(A(filt_bias)[0].reshape(1, 1024))
    shared["gbrow"] = np.ascontiguousarray(A(grp_norm_g)[0][None, 0:512])
    for k in ("zT1024", "zT256", "dec1024", "dec256", "W1024", "W256", "Wu1024", "Wu256", "ident", "e0n"):
        shared[k] = cs[k]
    ck, cv = A(cache_ctx_k), A(cache_ctx_v)
    cc_, cctx = A(c), A(c_ctx)
    in_maps = []
    for core in range(8):
        b = core % 4
        m = dict(shared)
        m["xsT"] = np.ascontiguousarray(x_sample[b].T.reshape(8, 128, NTOK).transpose(1, 0, 2))
        xp = x_prompt[core * 4:(core + 1) * 4].reshape(NTOK, D)
        m["xpT"] = np.ascontiguousarray(xp.T.reshape(8, 128, NTOK).transpose(1, 0, 2))
        m["csil"] = np.ascontiguousarray(np.stack([_fm(cctx, 8), _fm(cc_[b], 8)], -1))
        off = 512 * (core // 4)
        xpad = np.zeros((NTOK + 2, D), f32)
        xpad[1:NTOK + 1] = x_sample[b]
        m["xo"] = np.ascontiguousarray(xpad[off:off + 514].T.reshape(8, 128, 514).transpose(1, 0, 2))
        selm = np.zeros((NTOK, 514), f32)
        ii = np.arange(514)
        tt = off - 1 + ii
        ok = (tt >= 0) & (tt < NTOK)
        selm[tt[ok], ii[ok]] = 1.0
        m["sel"] = np.ascontiguousarray(selm.reshape(8, 128, 514).transpose(1, 0, 2)).astype(ml_dtypes.bfloat16)
        m["mrow"] = np.ascontiguousarray(ok.astype(f32)[None, :])
        kk = ck[b, 0]
        m["ckT"] = np.ascontiguousarray(kk.reshape(4, 2, 256, 64).transpose(1, 3, 0, 2).reshape(128, 4, 256))
        vv = cv[b, 0]
        m["cv"] = np.ascontiguousarray(vv.reshape(8, 2, 128, 64).transpose(2, 1, 0, 3))
        in_maps.append(m)
    return in_maps


def _assemble(R):
    f32 = np.float32
    y_prompt = np.zeros((32, 256, D), f32)
    y_sample = np.zeros((4, 1024, D), f32)
    sk = np.zeros((32, 1, 8, 256, 64), f32)
    sv = np.zeros((32, 1, 8, 256, 64), f32)
    for core in range(8):
        r = R[core]
        yp = np.asarray(r["ypT"]).transpose(1, 0, 2).reshape(D, NTOK).T
        y_prompt[core * 4:(core + 1) * 4] = yp.reshape(4, 256, D)
        off = 512 * (core // 4)
        y_sample[core % 4, off:off + 512] = np.asarray(r["ysT"]).transpose(1, 0, 2).reshape(D, 512).T
        kTo = np.asarray(r["kT_o"]).transpose(1, 0, 2).reshape(512, NTOK)
        sk[core * 4:(core + 1) * 4, 0] = kTo.reshape(8, 64, 4, 256).transpose(2, 0, 3, 1)
        vo = np.asarray(r["v_o"]).reshape(128, 8, 8, 64)
        vo = vo.transpose(1, 0, 2, 3).reshape(4, 256, 8, 64).transpose(0, 2, 1, 3)
        sv[core * 4:(core + 1) * 4, 0] = vo
    return (y_prompt, y_sample, sk, sv)


def kernel(**inputs):
    in_maps = _prepare(**inputs)
    if "nc" not in _CACHE:
        _CACHE["nc"] = build_program()
    res = run_bass_kernel_spmd(_CACHE["nc"], in_maps, core_ids=list(range(8)))
    return _assemble(res.results)
```
